# Optimizing a Trainium2 kernel written in Bass

```python
import math
import jax, jax.numpy as jnp
from jax import lax
import numpy as np

D_MODEL = 2048
BATCH = 2
SEQ = 8192
DEPTH = 4

N_EVEN = (DEPTH + 1) // 2
N_ODD = DEPTH // 2

S5_WIDTH = D_MODEL // 2
S5_GROUP = 16
S5_GROUPS = S5_WIDTH // S5_GROUP
S5_STATE = 64
DIFF_WIDTH = D_MODEL // 2
DIFF_HEAD_DIM = 128
DIFF_VDIM = 2 * DIFF_HEAD_DIM
DIFF_HEADS = DIFF_WIDTH // DIFF_VDIM

C_HEAD_DIM = 128
C_HEADS = D_MODEL // C_HEAD_DIM
DILATED_BRANCHES = ((128, 1), (512, 4), (2048, 16))

MEM_LEN = 256
XA_HEADS = 4
XA_HEAD_DIM = D_MODEL // XA_HEADS

FFN_HIDDEN = -(-8 * D_MODEL // (3 * 256)) * 256

ROPE_THETA = 500000.0
ROPE_DIM = 128 // 4
Q_BLOCK = 128
NORM_EPS = 1e-6
MASK_VALUE = -1e30

kernel_name = 'hybrid_s5_diffattn_dilated_encoder'

f32 = jnp.float32


def rmsnorm(x, g):
    xf = x.astype(f32)
    y = xf * lax.rsqrt(jnp.mean(xf * xf, axis=-1, keepdims=True) + NORM_EPS)
    return (y * g.astype(f32)).astype(x.dtype)


def rope_tables(seq):
    pos = jnp.arange(seq, dtype=f32)
    inv = ROPE_THETA ** (-jnp.arange(0, ROPE_DIM, 2, dtype=f32) / ROPE_DIM)
    ang = pos[:, None] * inv[None, :]
    return jnp.cos(ang), jnp.sin(ang)


def apply_partial_rope(t, cos, sin):
    seq = t.shape[1]
    bshape = (1, seq) + (1,) * (t.ndim - 3) + (ROPE_DIM // 2,)
    c = cos.reshape(bshape)
    s = sin.reshape(bshape)
    tf = t.astype(f32)
    x1 = tf[..., :ROPE_DIM // 2]
    x2 = tf[..., ROPE_DIM // 2:ROPE_DIM]
    out = jnp.concatenate([x1 * c - x2 * s, x2 * c + x1 * s, tf[..., ROPE_DIM:]], axis=-1)
    return out.astype(t.dtype)


def _ssm_combine(e1, e2):
    a1, b1 = e1
    a2, b2 = e2
    return a1 * a2, a2 * b1 + b2


def s5_mixer(u, lam_re, lam_im, log_step, b_re, b_im, c_re, c_im, d_skip, glu_w, glu_b):
    bsz, seq, _ = u.shape
    uf = u.astype(f32)
    ug = uf.reshape(bsz, seq, S5_GROUPS, S5_GROUP)
    lam = lax.complex(lam_re.astype(f32), lam_im.astype(f32))
    step = jnp.exp(log_step.astype(f32))[..., None]
    lam_bar = jnp.exp(lam * step)
    bmat = lax.complex(b_re.astype(f32), b_im.astype(f32))
    b_bar = ((lam_bar - 1.0) / lam)[..., None] * bmat
    cmat = lax.complex(c_re.astype(f32), c_im.astype(f32))
    y = ug * d_skip.astype(f32).reshape(S5_GROUPS, S5_GROUP)
    for direction in (0, 1):
        bu = jnp.einsum('bsgn,gpn->bsgp', ug, b_bar[direction])
        a = jnp.broadcast_to(lam_bar[direction], bu.shape)
        _, state = lax.associative_scan(_ssm_combine, (a, bu), reverse=(direction == 1), axis=1)
        y = y + jnp.einsum('bsgp,gnp->bsgn', state, cmat[direction]).real
    y = y.reshape(bsz, seq, S5_WIDTH)
    g = jax.nn.gelu(y)
    out = g * jax.nn.sigmoid(g @ glu_w.astype(f32) + glu_b.astype(f32))
    return out.astype(u.dtype)


def differential_attention(q, k, v, lam_vec, subln_g, lambda_init, cos, sin):
    bsz, seq, _ = q.shape
    H, dh = DIFF_HEADS, DIFF_HEAD_DIM
    q = apply_partial_rope(q.reshape(bsz, seq, H, 2, dh), cos, sin)
    k = apply_partial_rope(k.reshape(bsz, seq, H, 2, dh), cos, sin)
    v = v.reshape(bsz, seq, H, DIFF_VDIM)
    lv = lam_vec.astype(f32)
    lam = jnp.exp(jnp.dot(lv[0], lv[1])) - jnp.exp(jnp.dot(lv[2], lv[3])) + lambda_init
    scale = dh ** -0.5
    nb = seq // Q_BLOCK
    qb = jnp.moveaxis(q.reshape(bsz, nb, Q_BLOCK, H, 2, dh), 1, 0)

    def block(qblk):
        s = jnp.einsum('bqhcd,bkhcd->bhcqk', qblk, k, preferred_element_type=f32) * scale
        p = jax.nn.softmax(s, axis=-1)
        w = p[:, :, 0] - lam * p[:, :, 1]
        return jnp.einsum('bhqk,bkhe->bqhe', w.astype(v.dtype), v, preferred_element_type=f32)

    o = lax.map(block, qb)
    o = jnp.moveaxis(o, 0, 1).reshape(bsz, seq, H, DIFF_VDIM)
    o = rmsnorm(o, subln_g) * (1.0 - lambda_init)
    return o.reshape(bsz, seq, DIFF_WIDTH).astype(v.dtype)


def even_mixer(hn, w_in, w_out, lam_re, lam_im, log_step, b_re, b_im, c_re, c_im,
               d_skip, glu_w, glu_b, lam_vec, subln_g, lambda_init, cos, sin):
    proj = hn @ w_in
    u, q, k, v = jnp.split(proj, [S5_WIDTH, S5_WIDTH + DIFF_WIDTH, S5_WIDTH + 2 * DIFF_WIDTH], axis=-1)
    y_s5 = s5_mixer(u, lam_re, lam_im, log_step, b_re, b_im, c_re, c_im, d_skip, glu_w, glu_b)
    y_diff = differential_attention(q, k, v, lam_vec, subln_g, lambda_init, cos, sin)
    return jnp.concatenate([y_s5.astype(hn.dtype), y_diff.astype(hn.dtype)], axis=-1) @ w_out


def dilated_window_branch(q, k, v, dil, half):
    bsz, seq, H, dh = q.shape
    M = seq // dil
    nb = -(-M // Q_BLOCK)
    Mp = nb * Q_BLOCK
    qs = q.reshape(bsz, M, dil, H, dh)
    ks = k.reshape(bsz, M, dil, H, dh)
    vs = v.reshape(bsz, M, dil, H, dh)
    qp = jnp.pad(qs, ((0, 0), (0, Mp - M), (0, 0), (0, 0), (0, 0)))
    kpad = ((0, 0), (half, Mp - M + Q_BLOCK - half), (0, 0), (0, 0), (0, 0))
    kp = jnp.pad(ks, kpad).reshape(bsz, nb + 1, Q_BLOCK, dil, H, dh)
    vp = jnp.pad(vs, kpad).reshape(bsz, nb + 1, Q_BLOCK, dil, H, dh)
    kslab = jnp.concatenate([kp[:, :-1], kp[:, 1:]], axis=2)
    vslab = jnp.concatenate([vp[:, :-1], vp[:, 1:]], axis=2)
    qb = qp.reshape(bsz, nb, Q_BLOCK, dil, H, dh)
    s = jnp.einsum('bnqrhd,bnkrhd->bnrhqk', qb, kslab, preferred_element_type=f32) * (dh ** -0.5)
    i = jnp.arange(Q_BLOCK)[:, None]
    t = jnp.arange(2 * Q_BLOCK)[None, :]
    offset = t - half - i
    kpos = jnp.arange(nb)[:, None, None] * Q_BLOCK + t[None] - half
    valid = (jnp.abs(offset) <= half)[None] & (kpos >= 0) & (kpos < M)
    s = jnp.where(valid[None, :, None, None], s, MASK_VALUE)
    m = jnp.max(s, axis=-1, keepdims=True)
    e = jnp.exp(s - m)
    den = jnp.sum(e, axis=-1, keepdims=True)
    o = jnp.einsum('bnrhqk,bnkrhd->bnqrhd', (e / den).astype(v.dtype), vslab, preferred_element_type=f32)
    lse = (m + jnp.log(den))[..., 0]
    o = o.reshape(bsz, Mp, dil, H, dh)[:, :M].reshape(bsz, seq, H, dh)
    lse = jnp.moveaxis(lse, -1, 2).reshape(bsz, Mp, dil, H)[:, :M].reshape(bsz, seq, H)
    return o, lse


def odd_mixer(hn, w_qkv, w_out, cos, sin):
    bsz, seq, _ = hn.shape
    qkv = (hn @ w_qkv).reshape(bsz, seq, 3, C_HEADS, C_HEAD_DIM)
    q = apply_partial_rope(qkv[:, :, 0], cos, sin)
    k = apply_partial_rope(qkv[:, :, 1], cos, sin)
    v = qkv[:, :, 2]
    outs, lses = [], []
    for window, dil in DILATED_BRANCHES:
        o, lse = dilated_window_branch(q, k, v, dil, window // (2 * dil))
        outs.append(o)
        lses.append(lse)
    alpha = jax.nn.softmax(jnp.stack(lses, axis=0), axis=0)
    o = jnp.sum(alpha[..., None] * jnp.stack(outs, axis=0), axis=0)
    return o.reshape(bsz, seq, D_MODEL).astype(hn.dtype) @ w_out


def memory_cross_attention(hn, mem_n, wq, wkv, wo):
    bsz, seq, _ = hn.shape
    q = (hn @ wq).reshape(bsz, seq, XA_HEADS, XA_HEAD_DIM)
    kv = (mem_n @ wkv).reshape(bsz, mem_n.shape[1], 2, XA_HEADS, XA_HEAD_DIM)
    s = jnp.einsum('bqhd,bkhd->bhqk', q, kv[:, :, 0], preferred_element_type=f32) * (XA_HEAD_DIM ** -0.5)
    p = jax.nn.softmax(s, axis=-1)
    o = jnp.einsum('bhqk,bkhd->bqhd', p.astype(hn.dtype), kv[:, :, 1])
    return o.reshape(bsz, seq, D_MODEL) @ wo


def swiglu(hn, w13, w2):
    a, b = jnp.split(hn @ w13, 2, axis=-1)
    return (jax.nn.silu(a) * b) @ w2


def setup_inputs(seed: int = 0) -> dict:
    key = jax.random.key(seed)
    keys = iter(jax.random.split(key, 40))
    D, G, P, N = D_MODEL, S5_GROUPS, S5_STATE, S5_GROUP

    def nrm(shape, scale):
        return jax.random.normal(next(keys), shape, f32) * scale

    def gain(shape):
        return 1.0 + nrm(shape, 0.02)

    n_idx = jnp.arange(P, dtype=f32)
    inp = {}
    inp['x'] = nrm((BATCH, SEQ, D), 1.0)
    inp['mem'] = nrm((BATCH, MEM_LEN, D), 1.0)
    inp['norm_mix_g'] = gain((DEPTH, D))
    inp['norm_xa_g'] = gain((DEPTH, D))
    inp['norm_mem_g'] = gain((DEPTH, D))
    inp['xa_wq'] = nrm((DEPTH, D, D), D ** -0.5)
    inp['xa_wkv'] = nrm((DEPTH, D, 2 * D), D ** -0.5)
    inp['xa_wo'] = nrm((DEPTH, D, D), D ** -0.5)
    inp['norm_ffn_g'] = gain((DEPTH, D))
    inp['ffn_w13'] = nrm((DEPTH, D, 2 * FFN_HIDDEN), D ** -0.5)
    inp['ffn_w2'] = nrm((DEPTH, FFN_HIDDEN, D), FFN_HIDDEN ** -0.5)
    inp['ab_w_in'] = nrm((N_EVEN, D, S5_WIDTH + 3 * DIFF_WIDTH), D ** -0.5)
    inp['ab_w_out'] = nrm((N_EVEN, S5_WIDTH + DIFF_WIDTH, D), (S5_WIDTH + DIFF_WIDTH) ** -0.5)
    inp['s5_lambda_re'] = -0.5 + nrm((N_EVEN, 2, G, P), 0.01)
    inp['s5_lambda_im'] = math.pi * n_idx + nrm((N_EVEN, 2, G, P), 0.01)
    inp['s5_log_step'] = jax.random.uniform(next(keys), (N_EVEN, 2, G), f32, math.log(1e-3), math.log(1e-1))
    inp['s5_b_re'] = nrm((N_EVEN, 2, G, P, N), (2 * N) ** -0.5)
    inp['s5_b_im'] = nrm((N_EVEN, 2, G, P, N), (2 * N) ** -0.5)
    inp['s5_c_re'] = nrm((N_EVEN, 2, G, N, P), 2.0 * (2 * P) ** -0.5)
    inp['s5_c_im'] = nrm((N_EVEN, 2, G, N, P), 2.0 * (2 * P) ** -0.5)
    inp['s5_d'] = nrm((N_EVEN, S5_WIDTH), 1.0)
    inp['s5_glu_w'] = nrm((N_EVEN, S5_WIDTH, S5_WIDTH), S5_WIDTH ** -0.5)
    inp['s5_glu_b'] = nrm((N_EVEN, S5_WIDTH), 0.02)
    inp['diff_lambda'] = nrm((N_EVEN, 4, DIFF_HEAD_DIM), 0.1)
    inp['diff_subln_g'] = gain((N_EVEN, DIFF_VDIM))
    inp['c_w_qkv'] = nrm((N_ODD, D, 3 * D), D ** -0.5)
    inp['c_w_out'] = nrm((N_ODD, D, D), D ** -0.5)
    inp['final_norm_g'] = gain((D,))
    return inp


def reference(x, mem, norm_mix_g, norm_xa_g, norm_mem_g, xa_wq, xa_wkv, xa_wo, norm_ffn_g,
              ffn_w13, ffn_w2, ab_w_in, ab_w_out, s5_lambda_re, s5_lambda_im, s5_log_step,
              s5_b_re, s5_b_im, s5_c_re, s5_c_im, s5_d, s5_glu_w, s5_glu_b, diff_lambda,
              diff_subln_g, c_w_qkv, c_w_out, final_norm_g):
    cos, sin = rope_tables(x.shape[1])
    h = x
    for layer in range(DEPTH):
        i = layer // 2
        hn = rmsnorm(h, norm_mix_g[layer])
        if layer % 2 == 0:
            lambda_init = 0.8 - 0.6 * math.exp(-0.3 * layer)
            mix = even_mixer(hn, ab_w_in[i], ab_w_out[i], s5_lambda_re[i], s5_lambda_im[i],
                             s5_log_step[i], s5_b_re[i], s5_b_im[i], s5_c_re[i], s5_c_im[i],
                             s5_d[i], s5_glu_w[i], s5_glu_b[i], diff_lambda[i], diff_subln_g[i],
                             lambda_init, cos, sin)
        else:
            mix = odd_mixer(hn, c_w_qkv[i], c_w_out[i], cos, sin)
        h = h + mix
        h = h + memory_cross_attention(rmsnorm(h, norm_xa_g[layer]), rmsnorm(mem, norm_mem_g[layer]),
                                       xa_wq[layer], xa_wkv[layer], xa_wo[layer])
        h = h + swiglu(rmsnorm(h, norm_ffn_g[layer]), ffn_w13[layer], ffn_w2[layer])
    return rmsnorm(h, final_norm_g)
```

```python
import contextlib
import math
import numpy as np
import ml_dtypes
import concourse.bass as bass
import concourse.mybir as mybir
from concourse.bass_utils import run_bass_kernel_spmd

F32 = mybir.dt.float32
BF16 = mybir.dt.bfloat16
I32 = mybir.dt.int32
ALU = mybir.AluOpType
AF = mybir.ActivationFunctionType
NPBF = ml_dtypes.bfloat16

D = 2048
SEQ = 8192
NB = 2
NCORE = 8
TOK = 2048
TG = 512
FFN = 5632
EPS = 1e-6
SKIP = set()
NGRP_OVERRIDE = 0


class Prog:
    DMA_K = 6

    def __init__(self, nc):
        self.nc = nc
        self.ops = []
        self.last_w = {}
        self.readers = {}

    def add(self, eng, fn, R=(), W=(), dma=False):
        i = len(self.ops)
        deps = set()
        for k in list(R) + list(W):
            if k in self.last_w:
                deps.add(self.last_w[k])
        for k in W:
            for r in self.readers.get(k, ()):
                deps.add(r)
        for k in R:
            lst = self.readers.setdefault(k, [])
            if not dma:
                lst[:] = [r for r in lst if self.ops[r]["dma"] or self.ops[r]["eng"] != eng]
            lst.append(i)
        for k in W:
            self.last_w[k] = i
            self.readers[k] = []
        self.ops.append(dict(eng=eng, fn=fn, deps=deps, dma=dma, sig=dma, signal=None))
        return i

    def dma(self, eng, out, in_, R=(), W=()):
        return self.add(eng, lambda e: e.dma_start(out=out, in_=in_), R, W, dma=True)

    def emit(self):
        nc = self.nc
        ops = self.ops
        for i, o in enumerate(ops):
            nd = set()
            for d in o["deps"]:
                if ops[d]["eng"] == "tensor" and o["eng"] == "tensor" and not ops[d]["dma"] and not o["dma"]:
                    continue
                nd.add(d)
            o["deps"] = nd
            for d in nd:
                ops[d]["sig"] = True
        engs = ["tensor", "vector", "scalar", "gpsimd", "sync"]
        pfx = getattr(self, "pfx", "")
        self.sems = []

        def mk(name):
            h = nc.alloc_semaphore(name=pfx + name)
            self.sems.append(h)
            return h
        csem = {e: mk("c_" + e) for e in engs}
        dsem = {e: [mk("d_%s%d" % (e, k)) for k in range(self.DMA_K)] for e in ["sync", "gpsimd"]}
        ccount = {e: 0 for e in engs}
        dcount = {e: 0 for e in dsem}
        streams = {e: [] for e in engs}
        waited = {e: {} for e in engs}
        final = {}
        for i, o in enumerate(ops):
            E = o["eng"]
            st = streams[E]
            w = waited[E]
            for d in sorted(o["deps"]):
                sem, val = ops[d]["signal"]
                if w.get(id(sem), 0) < val:
                    w[id(sem)] = val
                    st.append((lambda e, sem=sem, val=val: e.wait_ge(sem, val)))
            if o["dma"]:
                n = dcount[E]
                dcount[E] += 1
                sem = dsem[E][n % self.DMA_K]
                val = 16 * (n // self.DMA_K + 1)
                if n >= self.DMA_K and w.get(id(sem), 0) < val - 16:
                    w[id(sem)] = val - 16
                    st.append((lambda e, sem=sem, val=val: e.wait_ge(sem, val - 16)))
                o["signal"] = (sem, val)
                final[id(sem)] = (sem, val)
                st.append((lambda e, fn=o["fn"], sem=sem: fn(e).then_inc(sem, 16)))
            elif o["sig"]:
                ccount[E] += 1
                sem = csem[E]
                val = ccount[E]
                o["signal"] = (sem, val)
                st.append((lambda e, fn=o["fn"], sem=sem: fn(e).then_inc(sem, 1)))
            else:
                st.append((lambda e, fn=o["fn"]: fn(e)))
        for sem, val in final.values():
            if waited["sync"].get(id(sem), 0) < val:
                streams["sync"].append((lambda e, sem=sem, val=val: e.wait_ge(sem, val)))
        with nc.Block() as block:
            @block.tensor
            def _(e):
                for f in streams["tensor"]:
                    f(e)

            @block.vector
            def _(e):
                for f in streams["vector"]:
                    f(e)

            @block.scalar
            def _(e):
                for f in streams["scalar"]:
                    f(e)

            @block.gpsimd
            def _(e):
                for f in streams["gpsimd"]:
                    f(e)

            @block.sync
            def _(e):
                for f in streams["sync"]:
                    f(e)
        self.stats = {e: len(streams[e]) for e in engs}
        self.counts = (dict(ccount), dict(dcount))


class Ctx:
    count = 0

    def __init__(self, nc=None, io=None):
        self.io = io
        Ctx.count += 1
        self.pfx = "p%d_" % Ctx.count
        self.nc = nc if nc is not None else bass.Bass("TRN2", target_bir_lowering=False)
        self.P = Prog(self.nc)
        self.S = contextlib.ExitStack()
        self.P.stack = self.S
        self.P.pfx = self.pfx
        self.nps = 0
        self.ps = []
        self.rot = {}
        self.cast_i = 0

    def din(self, name, shape, dt=F32):
        if self.io is not None:
            ap = self.io[name]
            assert list(ap.shape) == list(shape), (name, ap.shape, shape)
            return ap
        return self.nc.dram_tensor(name, list(shape), dt, kind="ExternalInput").ap()

    def dout(self, name, shape, dt=F32):
        if self.io is not None:
            ap = self.io[name]
            assert list(ap.shape) == list(shape), (name, ap.shape, shape)
            return ap
        return self.nc.dram_tensor(name, list(shape), dt, kind="ExternalOutput").ap()

    def finish(self):
        self.P.emit()
        if self.io is not None:
            self.nc.all_engine_barrier()
            self.nc.clear_and_free_semaphores(self.P.sems)
            self.nc.all_engine_barrier()
        self.S.close()
        return self.nc

    def sb(self, name, shape, dt):
        return self.S.enter_context(self.nc.sbuf_tensor(self.pfx + name, list(shape), dt))

    def alloc_psum(self, n=8):
        self.ps = [self.S.enter_context(self.nc.psum_tensor(self.pfx + "ps%d" % i, [128, 512], F32)) for i in range(n)]

    def next_ps(self, group="main", banks=None):
        banks = banks if banks is not None else list(range(len(self.ps)))
        i = self.rot.get(group, 0)
        self.rot[group] = i + 1
        b = banks[i % len(banks)]
        return self.ps[b], ("ps", b)

    def nxt(self, group, n):
        i = self.rot.get(group, 0)
        self.rot[group] = i + 1
        return i % n


class BgW:
    count = 0

    def __init__(self, nc, specs):
        BgW.count += 1
        self.nc = nc
        self.jobs = []
        self.out = {}
        self.pos = 0
        for key, W, blk in specs:
            K_, N_ = W.shape
            nci, nb = K_ // 128, N_ // blk
            Wb = nc.dram_tensor("bgw%d_%s" % (BgW.count, key), [nb, 128, nci, blk], BF16).ap()
            for j in range(nb):
                for ci0 in range(0, nci, 16):
                    n = min(16, nci - ci0)
                    src = W[ci0 * 128:(ci0 + n) * 128, j * blk:(j + 1) * blk].rearrange("(c p) o -> p c o", p=128)
                    self.jobs.append((src, Wb[j, :, ci0:ci0 + n, :], n, blk))
            self.out[key] = (Wb, blk)

    def remaining(self):
        return len(self.jobs) - self.pos

    def emit(self, C, count=1, cast_engines=("scalar",)):
        if not hasattr(C, "bg_st"):
            C.bg_st = [C.sb("bgst%d" % i, [128, 16, 128], F32) for i in range(2)]
            C.bg_bf = [C.sb("bgbf%d" % i, [128, 16, 128], BF16) for i in range(2)]
            C.bg_k = 0
        P = C.P
        for _ in range(count):
            if self.pos >= len(self.jobs):
                return
            src, dst, n, blk = self.jobs[self.pos]
            self.pos += 1
            k = C.bg_k
            C.bg_k += 1
            s = k % 2
            st, bf = C.bg_st[s], C.bg_bf[s]
            P.dma("sync", st[:, 0:n, 0:blk], src, W=[("bgst", s)])
            ce = cast_engines[k % len(cast_engines)]
            if ce == "scalar":
                P.add("scalar", lambda e, st=st, bf=bf, n=n, blk=blk: e.activation(out=bf[:, 0:n, 0:blk], in_=st[:, 0:n, 0:blk], func=AF.Copy),
                      R=[("bgst", s)], W=[("bgbf", s)])
            else:
                P.add(ce, lambda e, st=st, bf=bf, n=n, blk=blk: e.tensor_copy(out=bf[:, 0:n, 0:blk], in_=st[:, 0:n, 0:blk]),
                      R=[("bgst", s)], W=[("bgbf", s)])
            P.dma("gpsimd", dst, bf[:, 0:n, 0:blk], R=[("bgbf", s)])

    def flush(self):
        if self.remaining() == 0:
            return
        C = Ctx(self.nc, io={})
        self.emit(C, self.remaining(), cast_engines=("vector", "scalar"))
        C.finish()


def emit_W(nc, specs):
    C = Ctx(nc, io={})
    P = C.P
    NS = 4
    wst = [C.sb("wst%d" % i, [128, 16, 128], F32) for i in range(NS)]
    wbf = [C.sb("wbf%d" % i, [128, 16, 128], BF16) for i in range(NS)]
    out = {}
    k = 0
    for key, W, blk in specs:
        K_, N_ = W.shape
        nci, nb = K_ // 128, N_ // blk
        Wb = nc.dram_tensor(C.pfx + "wb_" + key, [nb, 128, nci, blk], BF16).ap()
        for j in range(nb):
            for ci0 in range(0, nci, 16):
                n = min(16, nci - ci0)
                s = k % NS
                src = W[ci0 * 128:(ci0 + n) * 128, j * blk:(j + 1) * blk].rearrange("(c p) o -> p c o", p=128)
                P.dma("sync", wst[s][:, 0:n, 0:blk], src, W=[("wst", s)])
                ce = ["vector", "scalar"][k % 2]
                if ce == "scalar":
                    P.add("scalar", lambda e, s=s, n=n, blk=blk: e.activation(out=wbf[s][:, 0:n, 0:blk], in_=wst[s][:, 0:n, 0:blk], func=AF.Copy),
                          R=[("wst", s)], W=[("wbf", s)])
                else:
                    P.add(ce, lambda e, s=s, n=n, blk=blk: e.tensor_copy(out=wbf[s][:, 0:n, 0:blk], in_=wst[s][:, 0:n, 0:blk]),
                          R=[("wst", s)], W=[("wbf", s)])
                P.dma("sync" if k % 2 else "gpsimd", Wb[j, :, ci0:ci0 + n, :], wbf[s][:, 0:n, 0:blk], R=[("wbf", s)])
                k += 1
        out[key] = (Wb, blk)
    C.finish()
    return out


def t_weight_specs(prev, nxt, io):
    specs = []
    if prev is not None:
        if prev == "even":
            specs.append(("glu_w", io["glu_w"], 128))
        for nm in ("w_out", "wq", "wkv", "wo", "w13", "w2"):
            specs.append((nm, io[nm], 128))
    if nxt is not None:
        specs.append(("w_in", io["w_in"], 128))
        specs.append(("w_sw", io["w_sw"], 32))
    return specs


def build_T(prev, nxt, final, nc=None, io=None, wb=None):
    C = Ctx(nc, io)
    nc, P = C.nc, C.P
    add, dma = P.add, P.dma
    ngrp = NGRP_OVERRIDE or (TOK // TG)

    hT_in = C.din("hT_in", [D, TOK])
    if prev is not None:
        if prev == "even":
            ys5T = C.din("ys5T", [1024, TOK])
            ydiffT = C.din("ydiffT", [1024, TOK], BF16)
            glu_w = C.din("glu_w", [1024, 1024])
            glu_b = C.din("glu_b", [128, 8])
        else:
            oT_in = C.din("oT_in", [D, TOK], BF16)
        w_out = C.din("w_out", [D, D])
        memT = C.din("memT", [D, 256])
        g_xa = C.din("g_xa", [128, 16])
        g_mem = C.din("g_mem", [128, 16])
        wq = C.din("wq", [D, D])
        wkv = C.din("wkv", [D, 2 * D])
        wo = C.din("wo", [D, D])
        g_ffn = C.din("g_ffn", [128, 16])
        w13 = C.din("w13", [D, 2 * FFN])
        w2 = C.din("w2", [FFN, D])
    if nxt is not None:
        g_mix = C.din("g_mix", [128, 16])
        ropeC = C.din("ropeC", [32, TOK])
        ropeS = C.din("ropeS", [32, TOK])
        if nxt == "even":
            w_in = C.din("w_in", [D, 4096])
            w_sw = C.din("w_sw", [D, 16 * 32])
            uT_out = C.dout("uT_out", [1024, TOK], BF16)
            qT_out = C.dout("qT_out", [8, 128, TOK], BF16)
            kT_out = C.dout("kT_out", [8, 128, TOK], BF16)
            v_out = C.dout("v_out", [TOK, 1024], BF16)
        else:
            w_in = C.din("w_in", [D, 6144])
            w_sw = C.din("w_sw", [D, 32 * 32])
            qT_out = C.dout("qT_out", [16, 128, TOK], BF16)
            kT_out = C.dout("kT_out", [16, 128, TOK], BF16)
            v_out = C.dout("v_out", [TOK, 2048], BF16)
        hT_out = C.dout("hT_out", [D, TOK])
    if final:
        g_fin = C.din("g_fin", [128, 16])
        outT = C.dout("outT", [D, TOK])

    if wb is None:
        wio = {}
        if prev is not None:
            wio.update({"w_out": w_out, "wq": wq, "wkv": wkv, "wo": wo, "w13": w13, "w2": w2})
            if prev == "even":
                wio["glu_w"] = glu_w
        if nxt is not None:
            wio.update({"w_in": w_in, "w_sw": w_sw})
        wb = emit_W(nc, t_weight_specs(prev, nxt, wio))
    wbmap = {}
    if prev is not None:
        for nm, ap in (("w_out", w_out), ("wq", wq), ("wkv", wkv), ("wo", wo), ("w13", w13), ("w2", w2)):
            wbmap[id(ap)] = wb[nm]
        if prev == "even":
            wbmap[id(glu_w)] = wb["glu_w"]
    if nxt is not None:
        wbmap[id(w_in)] = wb["w_in"]
        wbmap[id(w_sw)] = wb["w_sw"]

    C.alloc_psum(8)
    hT = C.sb("hT", [128, 16, TG], F32)
    hn = C.sb("hn", [128, 16, TG], BF16)
    sq = [C.sb("sq%d" % i, [128, TG], BF16) for i in range(2)]
    rstd = C.sb("rstd", [128, TG], F32)
    ones = C.sb("ones", [128, 128], BF16)
    epsc = C.sb("epsc", [128, 1], F32)
    NW = 6
    wbf = [C.sb("wbf%d" % i, [128, 16, 128], BF16) for i in range(NW)]
    gvec = C.sb("gvec", [128, 5, 16], F32)
    add("gpsimd", lambda e: e.memset(ones[:], 1.0), W=["ones"])
    add("gpsimd", lambda e: e.memset(epsc[:], EPS), W=["epsc"])

    def load_g(slot, src):
        dma("sync", gvec[:, slot, :], src[:, :], W=[("gvec", slot)])

    def load_w(Wd, ci0, n, c0, ncol=128):
        Wb, blk = wbmap[id(Wd)]
        assert ncol == blk and c0 % blk == 0
        s = C.nxt("w", NW)
        dma("sync", wbf[s][:, 0:n, 0:ncol], Wb[c0 // blk, :, ci0:ci0 + n, :], W=[("wbf", s)])
        return wbf[s], ("wbf", s)

    def linear(Wd, n_ci, cols, x_tile, x_keys, evac, ntok=TG, banks=None):
        for c0 in cols:
            ps, pk = C.next_ps("main", banks)
            blocks = []
            ci = 0
            while ci < n_ci:
                n = min(16, n_ci - ci)
                blocks.append((ci, n))
                ci += n
            first = True
            for (ci0, n) in blocks:
                wt, wk = load_w(Wd, ci0, n, c0)
                for j in range(n):
                    last = (ci0 + j == n_ci - 1)
                    xk = [("hn", ci0 + j)] if list(x_keys) == ["hn"] else list(x_keys)
                    add("tensor", lambda e, wt=wt, j=j, cj=ci0 + j, ps=ps, first=first, last=last:
                        e.matmul(ps[:, 0:ntok], lhsT=wt[:, j, :], rhs=x_tile[:, cj, 0:ntok], start=first, stop=last),
                        R=[wk] + xk, W=[pk])
                    first = False
            evac(c0, ps, pk)

    def rmsnorm(src_tile, src_keys, gslot, dst_tile, dst_key, ntok=TG, nchunk=16):
        ps, pk = C.next_ps("main")
        for c in range(nchunk):
            s = C.nxt("sq", 2)
            add("scalar", lambda e, s=s, c=c: e.activation(out=sq[s][:, 0:ntok], in_=src_tile[:, c, 0:ntok], func=AF.Square),
                R=list(src_keys), W=[("sq", s)])
            add("tensor", lambda e, s=s, c=c, ps=ps: e.matmul(ps[:, 0:ntok], lhsT=ones[:], rhs=sq[s][:, 0:ntok],
                                                               start=(c == 0), stop=(c == nchunk - 1)),
                R=[("sq", s), "ones"], W=[pk])
        add("scalar", lambda e, ps=ps: e.activation(out=rstd[:, 0:ntok], in_=ps[:, 0:ntok], func=AF.Sqrt,
                                                     scale=1.0 / D, bias=epsc[:, 0:1]), R=[pk, "epsc"], W=["rstd"])
        add("vector", lambda e: e.reciprocal(out=rstd[:, 0:ntok], in_=rstd[:, 0:ntok]), R=["rstd"], W=["rstd"])
        for c in range(nchunk):
            add("vector", lambda e, c=c: e.scalar_tensor_tensor(out=dst_tile[:, c, 0:ntok], in0=src_tile[:, c, 0:ntok],
                                                                 scalar=gvec[:, gslot, c:c + 1], in1=rstd[:, 0:ntok],
                                                                 op0=ALU.mult, op1=ALU.mult),
                R=list(src_keys) + ["rstd", ("gvec", gslot)], W=[(dst_key, c) if dst_key == "hn" else dst_key])

    def resid_evac(c0, ps, pk):
        c = c0 // 128
        add("vector", lambda e: e.tensor_tensor(out=hT[:, c, :], in0=hT[:, c, :], in1=ps[:, :], op=ALU.add),
            R=[pk, "hT"], W=["hT"])

    if prev is not None:
        load_g(0, g_xa)
        load_g(1, g_mem)
        load_g(2, g_ffn)
        memt = hT
        memn = hn
        kmT = C.sb("kmT", [128, 16, 256], BF16)
        vm = C.sb("vm", [128, 2, 2048], BF16)
        dma("sync", memt[:, :, 0:256], memT.rearrange("(c p) t -> p c t", p=128), W=["hT"])
        rmsnorm(memt, ["hT"], 1, memn, "hn", ntok=256)

        def k_evac(c0, ps, pk):
            c = c0 // 128
            add("vector", lambda e: e.tensor_copy(out=kmT[:, c, :], in_=ps[:, 0:256]), R=[pk], W=["kmT"])
        linear(wkv, 16, [c * 128 for c in range(16)], memn, ["hn"], k_evac, ntok=256)
        for c in range(16):
            wt, wk = load_w(wkv, 0, 16, D + c * 128)
            for kt in range(2):
                ps, pk = C.next_ps("main")
                for j in range(16):
                    add("tensor", lambda e, wt=wt, j=j, ps=ps, kt=kt: e.matmul(ps[:, 0:128], lhsT=memn[:, j, kt * 128:(kt + 1) * 128],
                                                                                rhs=wt[:, j, :], start=(j == 0), stop=(j == 15)),
                        R=[wk, ("hn", j)], W=[pk])
                add("vector", lambda e, ps=ps, kt=kt, c=c: e.tensor_copy(out=vm[:, kt, c * 128:(c + 1) * 128], in_=ps[:, 0:128]),
                    R=[pk], W=["vm"])
        if prev == "even":
            glub = C.sb("glub", [128, 8], F32)
            dma("sync", glub[:], glu_b[:, :], W=["glub"])
    if nxt is not None:
        load_g(3, g_mix)
    if final:
        load_g(4, g_fin)

    if prev is not None:
        xin = C.sb("xin", [128, 16, TG], BF16)
        et = [C.sb("et%d" % i, [128, TG], BF16) for i in range(2)]
        rl = C.sb("rl", [128, TG], F32)
        gT = C.sb("gT", [128, 44, TG], BF16)
        qx = gT[:, 24:40, :]
        sa = [C.sb("sa%d" % i, [128, TG], F32) for i in range(2)]
        if prev == "even":
            ysf = [C.sb("ysf%d" % i, [128, TG], F32) for i in range(2)]
            gg = gT[:, 16:24, :]
            sg = C.sb("sg", [128, TG], F32)
            g2 = C.sb("g2", [128, TG], F32)
    if nxt is not None:
        rc = C.sb("rc", [32, TG], F32)
        rs = C.sb("rs", [32, TG], F32)
        t1 = [C.sb("t1_%d" % i, [32, TG], F32) for i in range(1)]
        t2 = [C.sb("t2_%d" % i, [32, TG], F32) for i in range(1)]
        ob = [C.sb("ob%d" % i, [128, TG], BF16) for i in range(3)]
        nvc = 1024 if nxt == "even" else 2048
        if prev is not None:
            vtok = gT[:, 0:16, :].rearrange("p (a b) t -> p a (b t)", a=4)[:, :, 0:nvc]
        else:
            vtok = C.sb("vtok", [128, 4, nvc], BF16)[:, :, :]
    if final:
        of = [C.sb("of%d" % i, [128, TG], F32) for i in range(2)]

    for g in range(ngrp):
        t0 = g * TG
        dma("gpsimd", hT[:], hT_in[:, t0:t0 + TG].rearrange("(c p) t -> p c t", p=128), W=["hT"])
        if prev is not None:
            if prev == "even":
                dma("gpsimd", xin[:, 8:16, :], ydiffT[:, t0:t0 + TG].rearrange("(c p) t -> p c t", p=128), W=["xin_hi"])
                for c in range(8):
                    y = C.nxt("ysf", 2)
                    dma("gpsimd", ysf[y][:], ys5T[c * 128:(c + 1) * 128, t0:t0 + TG], W=[("ysf", y)])
                    add("scalar", lambda e, y=y: e.activation(out=g2[:], in_=ysf[y][:], func=AF.Square), R=[("ysf", y)], W=["g2"])
                    add("vector", lambda e: e.tensor_scalar(out=g2[:], in0=g2[:], scalar1=0.044715, scalar2=1.0, op0=ALU.mult, op1=ALU.add),
                        R=["g2"], W=["g2"])
                    add("vector", lambda e, y=y: e.tensor_tensor(out=g2[:], in0=g2[:], in1=ysf[y][:], op=ALU.mult), R=["g2", ("ysf", y)], W=["g2"])
                    add("scalar", lambda e: e.activation(out=g2[:], in_=g2[:], func=AF.Sigmoid, scale=1.5957691216057308), R=["g2"], W=["g2"])
                    add("vector", lambda e, y=y, c=c: e.tensor_tensor(out=gg[:, c, :], in0=g2[:], in1=ysf[y][:], op=ALU.mult),
                        R=["g2", ("ysf", y)], W=["gT"])

                def glu_evac(c0, ps, pk):
                    c = c0 // 128
                    add("scalar", lambda e: e.activation(out=sg[:], in_=ps[:, :], func=AF.Sigmoid, bias=glub[:, c:c + 1]),
                        R=[pk, "glub"], W=["sg"])
                    add("vector", lambda e: e.tensor_tensor(out=xin[:, c, :], in0=sg[:], in1=gg[:, c, :], op=ALU.mult),
                        R=["sg", "gT"], W=["xin_lo"])
                linear(glu_w, 8, [c * 128 for c in range(8)], gg, ["gT"], glu_evac)
                xkeys = ["xin_lo", "xin_hi"]
            else:
                dma("gpsimd", xin[:], oT_in[:, t0:t0 + TG].rearrange("(c p) t -> p c t", p=128), W=["xin_lo", "xin_hi"])
                xkeys = ["xin_lo", "xin_hi"]
            if "mix" not in SKIP:
                linear(w_out, 16, [c * 128 for c in range(16)], xin, xkeys, resid_evac)

            if "xa" in SKIP:
                continue
            rmsnorm(hT, ["hT"], 0, hn, "hn")

            def q_evac(c0, ps, pk):
                c = c0 // 128
                add("vector", lambda e: e.tensor_copy(out=qx[:, c, :], in_=ps[:, :]), R=[pk], W=["gT"])
            linear(wq, 16, [c * 128 for c in range(16)], hn, ["hn"], q_evac)
            for h in range(4):
                eks = []
                for kt in range(2):
                    ps, pk = C.next_ps("main")
                    for c in range(4):
                        add("tensor", lambda e, ps=ps, c=c, kt=kt, h=h: e.matmul(ps[:, :], lhsT=kmT[:, 4 * h + c, kt * 128:(kt + 1) * 128],
                                                                                   rhs=qx[:, 4 * h + c, :], start=(c == 0), stop=(c == 3)),
                            R=["kmT", "gT"], W=[pk])
                    add("scalar", lambda e, ps=ps, kt=kt: e.activation(out=et[kt][:], in_=ps[:, :], func=AF.Exp, scale=512 ** -0.5),
                        R=[pk], W=[("et", kt)])
                psl, pkl = C.next_ps("main")
                for kt in range(2):
                    add("tensor", lambda e, psl=psl, kt=kt: e.matmul(psl[:, :], lhsT=ones[:], rhs=et[kt][:], start=(kt == 0), stop=(kt == 1)),
                        R=[("et", kt), "ones"], W=[pkl])
                add("vector", lambda e, psl=psl: e.reciprocal(out=rl[:], in_=psl[:, :]), R=[pkl], W=["rl"])
                for dc in range(4):
                    ps, pk = C.next_ps("main")
                    for kt in range(2):
                        add("tensor", lambda e, ps=ps, kt=kt, dc=dc, h=h: e.matmul(ps[:, :], lhsT=vm[:, kt, h * 512 + dc * 128:h * 512 + (dc + 1) * 128],
                                                                                     rhs=et[kt][:], start=(kt == 0), stop=(kt == 1)),
                            R=["vm", ("et", kt)], W=[pk])
                    add("vector", lambda e, ps=ps, dc=dc, h=h: e.tensor_tensor(out=xin[:, 4 * h + dc, :], in0=ps[:, :], in1=rl[:], op=ALU.mult),
                        R=[pk, "rl"], W=["xin_lo", "xin_hi"])
            linear(wo, 16, [c * 128 for c in range(16)], xin, ["xin_lo", "xin_hi"], resid_evac)

            if "ffn" in SKIP:
                continue
            rmsnorm(hT, ["hT"], 2, hn, "hn")
            for j in range(44):
                holder = {}

                def a_evac(c0, ps, pk, holder=holder):
                    s = C.nxt("sa", 2)
                    holder["s"] = s
                    add("scalar", lambda e: e.activation(out=sa[s][:], in_=ps[:, :], func=AF.Silu), R=[pk], W=[("sa", s)])

                def b_evac(c0, ps, pk, holder=holder, j=j):
                    s = holder["s"]
                    add("vector", lambda e: e.tensor_tensor(out=gT[:, j, :], in0=ps[:, :], in1=sa[s][:], op=ALU.mult),
                        R=[pk, ("sa", s)], W=["gT"])
                linear(w13, 16, [j * 128], hn, ["hn"], a_evac)
                linear(w13, 16, [FFN + j * 128], hn, ["hn"], b_evac)
            linear(w2, 44, [c * 128 for c in range(16)], gT, ["gT"], resid_evac)

        if nxt is not None:
            dma("gpsimd", hT_out[:, t0:t0 + TG].rearrange("(c p) t -> p c t", p=128), hT[:], R=["hT"])
            rmsnorm(hT, ["hT"], 3, hn, "hn")
            dma("sync", rc[:], ropeC[:, t0:t0 + TG], W=["rc"])
            dma("sync", rs[:], ropeS[:, t0:t0 + TG], W=["rs"])

            def plain_out(dst):
                def ev(c0, ps, pk):
                    s = C.nxt("ob", 3)
                    add("vector", lambda e: e.tensor_copy(out=ob[s][:], in_=ps[:, :]), R=[pk], W=[("ob", s)])
                    dma("gpsimd", dst, ob[s][:], R=[("ob", s)])
                return ev

            def rope_proj(col0, swi, dst):
                hold = {}

                def ev_main(c0, ps, pk):
                    hold["ps"], hold["pk"] = ps, pk
                linear(w_in, 16, [col0], hn, ["hn"], ev_main)
                ps, pk = hold["ps"], hold["pk"]
                ps2, pk2 = C.next_ps("main")
                wt, wk = load_w(w_sw, 0, 16, swi * 32, ncol=32)
                for j in range(16):
                    add("tensor", lambda e, wt=wt, j=j, ps2=ps2: e.matmul(ps2[0:32, :], lhsT=wt[:, j, 0:32], rhs=hn[:, j, :],
                                                                           start=(j == 0), stop=(j == 15)),
                        R=[wk, ("hn", j)], W=[pk2])
                s = C.nxt("ob", 3)
                r = 0
                add("vector", lambda e: e.tensor_tensor(out=t1[r][:], in0=ps[0:32, :], in1=rc[:], op=ALU.mult), R=[pk, "rc"], W=[("t1", r)])
                add("vector", lambda e: e.tensor_tensor(out=t2[r][:], in0=ps2[0:32, :], in1=rs[:], op=ALU.mult), R=[pk2, "rs"], W=[("t2", r)])
                add("vector", lambda e: e.tensor_tensor(out=ob[s][0:32, :], in0=t1[r][:], in1=t2[r][:], op=ALU.add),
                    R=[("t1", r), ("t2", r)], W=[("ob", s)])
                add("vector", lambda e: e.tensor_copy(out=ob[s][32:64, :], in_=ps[32:64, :]), R=[pk], W=[("ob", s)])
                add("vector", lambda e: e.tensor_copy(out=ob[s][64:128, :], in_=ps[64:128, :]), R=[pk], W=[("ob", s)])
                dma("gpsimd", dst, ob[s][:], R=[("ob", s)])

            if nxt == "even":
                for c in range(8):
                    linear(w_in, 16, [c * 128], hn, ["hn"], plain_out(uT_out[c * 128:(c + 1) * 128, t0:t0 + TG]))
                for hc in range(8):
                    rope_proj(1024 + hc * 128, hc, qT_out[hc, :, t0:t0 + TG])
                for hc in range(8):
                    rope_proj(2048 + hc * 128, 8 + hc, kT_out[hc, :, t0:t0 + TG])
                vcol0 = 3072
            else:
                for hc in range(16):
                    rope_proj(hc * 128, hc, qT_out[hc, :, t0:t0 + TG])
                for hc in range(16):
                    rope_proj(2048 + hc * 128, 16 + hc, kT_out[hc, :, t0:t0 + TG])
                vcol0 = 4096
            for c in range(nvc // 128):
                wt, wk = load_w(w_in, 0, 16, vcol0 + c * 128)
                for tt in range(4):
                    ps, pk = C.next_ps("main")
                    for j in range(16):
                        add("tensor", lambda e, wt=wt, j=j, ps=ps, tt=tt: e.matmul(ps[:, 0:128], lhsT=hn[:, j, tt * 128:(tt + 1) * 128],
                                                                                    rhs=wt[:, j, :], start=(j == 0), stop=(j == 15)),
                            R=[wk, ("hn", j)], W=[pk])
                    add("vector", lambda e, ps=ps, tt=tt, c=c: e.tensor_copy(out=vtok[:, tt, c * 128:(c + 1) * 128], in_=ps[:, 0:128]),
                        R=[pk], W=["gT"])
            dma("gpsimd", v_out[t0:t0 + TG, :].rearrange("(t p) c -> p t c", p=128), vtok, R=["gT"])
        if final:
            rmsnorm(hT, ["hT"], 4, hn, "hn")
            for c in range(16):
                s = C.nxt("of", 2)
                add("vector", lambda e, c=c, s=s: e.scalar_tensor_tensor(out=of[s][:], in0=hT[:, c, :], scalar=gvec[:, 4, c:c + 1],
                                                                          in1=rstd[:], op0=ALU.mult, op1=ALU.mult),
                    R=["hT", "rstd", ("gvec", 4)], W=[("of", s)])
                dma("gpsimd", outT[c * 128:(c + 1) * 128, t0:t0 + TG], of[s][:], R=[("of", s)])
    return C.finish()


def build_H_odd(nc=None, io=None, bg=None, bg_total=0):
    C = Ctx(nc, io)
    nc, P = C.nc, C.P
    add, dma = P.add, P.dma
    qT_in = C.din("qT_in", [4, 128, SEQ], BF16)
    kT_in = C.din("kT_in", [4, 128, SEQ], BF16)
    v_in = C.din("v_in", [SEQ, 4, 128], BF16)
    mask_in = C.din("mask_in", [128, 20, 512], BF16)
    oT_out = C.dout("oT_out", [512, SEQ], BF16)
    C.alloc_psum(8)
    qT = C.sb("qT", [128, SEQ], BF16)
    kT = C.sb("kT", [128, SEQ], BF16)
    vv = C.sb("vv", [128, 64, 128], BF16)
    mk = C.sb("mk", [128, 20, 512], BF16)
    ones = C.sb("ones", [128, 128], BF16)
    et = [C.sb("et%d" % i, [128, 512], BF16) for i in range(3)]
    em = [C.sb("em%d" % i, [128, 512], BF16) for i in range(4)]
    rl = C.sb("rl", [128, 512], F32)
    ob = [C.sb("ob%d" % i, [128, 512], BF16) for i in range(2)]
    esum = [C.sb("esum%d" % i, [128, 512], F32) for i in range(2)]
    esb = [C.sb("esb%d" % i, [128, 512], BF16) for i in range(2)]
    add("gpsimd", lambda e: e.memset(ones[:], 1.0), W=["ones"])
    dma("sync", mk[:], mask_in[:, :, :], W=["mk"])
    scale = 128 ** -0.5
    LOOK = 2
    for hh in range(4):
        dma("sync", qT[:], qT_in[hh], W=["qT"])
        dma("sync", kT[:], kT_in[hh], W=["kT"])
        dma("gpsimd", vv[:], v_in[:, hh, :].rearrange("(t p) c -> p t c", p=128), W=["vv"])
        steps = []
        for qg in range(16):
            kts = [kt for kt in range(4 * qg - 8, 4 * qg + 12) if 0 <= kt < 64]
            for n, kt in enumerate(kts):
                steps.append((qg, kt, n, len(kts)))
        acc = {}

        def front(st, gi):
            qg, kt, n, nk = st
            idx = kt - (4 * qg - 8)
            ps, pk = C.next_ps("S", [0, 1, 2, 3])
            add("tensor", lambda e, ps=ps, kt=kt, qg=qg: e.matmul(ps[:, :], lhsT=kT[:, kt * 128:(kt + 1) * 128],
                                                                   rhs=qT[:, qg * 512:(qg + 1) * 512], start=True, stop=True),
                R=["kT", "qT"], W=[pk])
            s_ = C.nxt("et", 3)
            add("scalar", lambda e, ps=ps, s_=s_: e.activation(out=et[s_][:], in_=ps[:, :], func=AF.Exp, scale=scale),
                R=[pk], W=[("et", s_)])
            m = C.nxt("em", 4)
            me = ["vector", "gpsimd"][gi % 2]
            add(me, lambda e, s_=s_, m=m, idx=idx: e.tensor_tensor(out=em[m][:], in0=et[s_][:], in1=mk[:, idx, :], op=ALU.mult),
                R=[("et", s_), "mk"], W=[("em", m)])
            return m

        def back(st, m):
            qg, kt, n, nk = st
            if n == 0:
                acc["o"] = C.next_ps("O", [4, 5])
                acc["l"] = C.next_ps("L", [6, 7])
            (po, pko), (pl, pkl) = acc["o"], acc["l"]
            last = (n == nk - 1)
            add("tensor", lambda e, po=po, kt=kt, m=m, n=n, last=last: e.matmul(
                po[:, :], lhsT=vv[:, kt, :], rhs=em[m][:], start=(n == 0), stop=last), R=["vv", ("em", m)], W=[pko])
            add("tensor", lambda e, pl=pl, m=m, n=n, last=last: e.matmul(
                pl[:, :], lhsT=ones[:], rhs=em[m][:], start=(n == 0), stop=last), R=["ones", ("em", m)], W=[pkl])
            if last:
                add("vector", lambda e, pl=pl: e.reciprocal(out=rl[:], in_=pl[:, :]), R=[pkl], W=["rl"])
                o = C.nxt("ob", 2)
                add("vector", lambda e, po=po, o=o: e.tensor_tensor(out=ob[o][:], in0=po[:, :], in1=rl[:], op=ALU.mult),
                    R=[pko, "rl"], W=[("ob", o)])
                dma("gpsimd", oT_out[hh * 128:(hh + 1) * 128, qg * 512:(qg + 1) * 512], ob[o][:], R=[("ob", o)])

        queue = []
        bg_every = max(1, (4 * len(steps)) // max(1, bg_total)) if bg is not None else 0
        for gi, st in enumerate(steps):
            queue.append((st, front(st, gi)))
            if len(queue) > LOOK:
                back(*queue.pop(0))
            if bg is not None and gi % bg_every == 0:
                bg.emit(C, 1, cast_engines=("scalar", "vector"))
        while queue:
            back(*queue.pop(0))
    return C.finish()


def dilated_mask_table():
    kl = np.arange(128)[:, None, None]
    idx = np.arange(20)[None, :, None]
    ql = np.arange(512)[None, None, :]
    off = (idx - 8) * 128 + kl - ql
    m = np.zeros(off.shape, np.float32)
    for window, dil in ((128, 1), (512, 4), (2048, 16)):
        half = window // (2 * dil)
        m += ((off % dil == 0) & (np.abs(off) <= half * dil)).astype(np.float32)
    return m.astype(NPBF)


def build_H_even(lambda_init, do_s5=True, do_attn=True, nc=None, io=None, bg=None, bg_total=0):
    C = Ctx(nc, io)
    nc, P = C.nc, C.P
    add, dma = P.add, P.dma
    qT_in = C.din("qT_in", [2, 128, SEQ], BF16)
    kT_in = C.din("kT_in", [2, 128, SEQ], BF16)
    v_in = C.din("v_in", [SEQ, 256], BF16)
    lam_in = C.din("lam_in", [128, 4, 128])
    sg_in = C.din("sg_in", [128, 2])
    ydT_out = C.dout("ydT_out", [256, SEQ], BF16)
    uT_in = C.din("uT_in", [256, SEQ], BF16)
    s5v = C.din("s5v", [8, 2, 128, 3])
    s5B = C.din("s5B", [8, 2, 2, 32, 128], BF16)
    s5C = C.din("s5C", [8, 2, 2, 128, 32], BF16)
    s5d = C.din("s5d", [8, 32, 1])
    ysT_out = C.dout("ysT_out", [256, SEQ])
    C.alloc_psum(8)
    ones = C.sb("ones", [128, 128], BF16)
    epsc = C.sb("epsc", [128, 1], F32)
    add("gpsimd", lambda e: e.memset(ones[:], 1.0), W=["ones"])
    add("gpsimd", lambda e: e.memset(epsc[:], EPS), W=["epsc"])

    if do_attn:
        qTg = [C.sb("qTg%d" % i, [128, 2, 512], BF16) for i in range(2)]
        kT = C.sb("kT", [128, 2, SEQ], BF16)
        vv = C.sb("vv", [128, 64, 256], BF16)
        et = [C.sb("et%d" % i, [128, 512], BF16) for i in range(3)]
        rl = C.sb("rl", [128, 512], F32)
        oc = C.sb("oc", [128, 2, 2, 512], F32)
        od = C.sb("od", [128, 2, 512], F32)
        sq = C.sb("sq", [128, 512], BF16)
        rstd = C.sb("rstd", [128, 512], F32)
        ob = [C.sb("ob%d" % i, [128, 512], BF16) for i in range(2)]
        lamt = C.sb("lamt", [128, 4, 128], F32)
        lprod = C.sb("lprod", [128, 128], F32)
        lsc = C.sb("lsc", [128, 4], F32)
        sgt = C.sb("sgt", [128, 2], F32)
        for c in range(2):
            dma("sync", kT[:, c, :], kT_in[c], W=["kT"])
        dma("gpsimd", vv[:], v_in.rearrange("(t p) c -> p t c", p=128), W=["vv"])
        dma("sync", lamt[:], lam_in[:, :, :], W=["lamt"])
        dma("sync", sgt[:], sg_in[:, :], W=["sgt"])
        for i in range(2):
            add("vector", lambda e, i=i: e.tensor_tensor(out=lprod[:], in0=lamt[:, 2 * i, :], in1=lamt[:, 2 * i + 1, :], op=ALU.mult),
                R=["lamt"], W=["lprod"])
            add("vector", lambda e, i=i: e.reduce_sum(out=lsc[:, i:i + 1], in_=lprod[:], axis=mybir.AxisListType.X), R=["lprod"], W=["lsc"])
        add("scalar", lambda e: e.activation(out=lsc[:, 0:2], in_=lsc[:, 0:2], func=AF.Exp), R=["lsc"], W=["lsc"])
        add("vector", lambda e: e.tensor_tensor(out=lsc[:, 2:3], in0=lsc[:, 1:2], in1=lsc[:, 0:1], op=ALU.subtract), R=["lsc"], W=["lsc"])
        add("vector", lambda e: e.tensor_scalar(out=lsc[:, 2:3], in0=lsc[:, 2:3], scalar1=-float(lambda_init), scalar2=None, op0=ALU.add),
            R=["lsc"], W=["lsc"])
        add("vector", lambda e: e.tensor_scalar(out=sgt[:], in0=sgt[:], scalar1=float(1.0 - lambda_init), scalar2=None, op0=ALU.mult),
            R=["sgt"], W=["sgt"])
        scale = 128 ** -0.5

    def attn_gen():
        LOOK = 1
        SB, OB, LB = [0, 1], [2, 3], [4]
        steps = [(qg, comp, kt) for qg in range(16) for comp in range(2) for kt in range(64)]
        st8 = {}

        def front(stp):
            qg, comp, kt = stp
            if comp == 0 and kt == 0:
                qs = C.nxt("qTg", 2)
                st8["qs"] = qs
                dma("sync", qTg[qs][:], qT_in[:, :, qg * 512:(qg + 1) * 512].rearrange("c p t -> p c t"), W=[("qTg", qs)])
            qs = st8["qs"]
            ps, pk = C.next_ps("S", SB)
            add("tensor", lambda e, ps=ps, kt=kt, qs=qs, comp=comp: e.matmul(
                ps[:, :], lhsT=kT[:, comp, kt * 128:(kt + 1) * 128], rhs=qTg[qs][:, comp, :],
                start=True, stop=True), R=["kT", ("qTg", qs)], W=[pk])
            s_ = C.nxt("et", 3)
            add("scalar", lambda e, ps=ps, s_=s_: e.activation(out=et[s_][:], in_=ps[:, :], func=AF.Exp, scale=scale),
                R=[pk], W=[("et", s_)])
            return s_

        def back(stp, s_):
            qg, comp, kt = stp
            if kt == 0:
                st8["po"] = [C.next_ps("O", OB), C.next_ps("O", OB)]
                st8["pl"] = C.next_ps("L", LB)
            po = st8["po"]
            pl, pkl = st8["pl"]
            for dc in range(2):
                add("tensor", lambda e, dc=dc, kt=kt, s_=s_, pp=po[dc][0]: e.matmul(
                    pp[:, :], lhsT=vv[:, kt, dc * 128:(dc + 1) * 128], rhs=et[s_][:], start=(kt == 0), stop=(kt == 63)),
                    R=["vv", ("et", s_)], W=[po[dc][1]])
            add("tensor", lambda e, pl=pl, s_=s_, kt=kt: e.matmul(pl[:, :], lhsT=ones[:], rhs=et[s_][:], start=(kt == 0), stop=(kt == 63)),
                R=["ones", ("et", s_)], W=[pkl])
            if kt != 63:
                return
            add("vector", lambda e, pl=pl: e.reciprocal(out=rl[:], in_=pl[:, :]), R=[pkl], W=["rl"])
            for dc in range(2):
                add("vector", lambda e, dc=dc, comp=comp, pp=po[dc][0]: e.tensor_tensor(out=oc[:, comp, dc, :], in0=pp[:, :], in1=rl[:], op=ALU.mult),
                    R=[po[dc][1], "rl"], W=["oc"])
            if comp != 1:
                return
            psn, pkn = C.next_ps("N", [7])
            for dc in range(2):
                add("vector", lambda e, dc=dc: e.scalar_tensor_tensor(out=od[:, dc, :], in0=oc[:, 1, dc, :], scalar=lsc[:, 2:3], in1=oc[:, 0, dc, :],
                                                                       op0=ALU.mult, op1=ALU.add), R=["oc", "lsc"], W=["od"])
                add("scalar", lambda e, dc=dc: e.activation(out=sq[:], in_=od[:, dc, :], func=AF.Square), R=["od"], W=["sq"])
                add("tensor", lambda e, dc=dc, psn=psn: e.matmul(psn[:, :], lhsT=ones[:], rhs=sq[:], start=(dc == 0), stop=(dc == 1)),
                    R=["sq", "ones"], W=[pkn])
            add("scalar", lambda e, psn=psn: e.activation(out=rstd[:], in_=psn[:, :], func=AF.Sqrt, scale=1.0 / 256, bias=epsc[:, 0:1]),
                R=[pkn, "epsc"], W=["rstd"])
            add("vector", lambda e: e.reciprocal(out=rstd[:], in_=rstd[:]), R=["rstd"], W=["rstd"])
            for dc in range(2):
                o = C.nxt("ob", 2)
                add("vector", lambda e, dc=dc, o=o: e.scalar_tensor_tensor(out=ob[o][:], in0=od[:, dc, :], scalar=sgt[:, dc:dc + 1], in1=rstd[:],
                                                                            op0=ALU.mult, op1=ALU.mult), R=["od", "sgt", "rstd"], W=[("ob", o)])
                dma("gpsimd", ydT_out[dc * 128:(dc + 1) * 128, qg * 512:(qg + 1) * 512], ob[o][:], R=[("ob", o)])

        queue = []
        for stp in steps:
            queue.append((stp, front(stp)))
            if len(queue) > LOOK:
                back(*queue.pop(0))
                yield
        while queue:
            back(*queue.pop(0))
            yield

    def s5_gen():
        BL = 512
        NBLK = SEQ // BL
        ug = C.sb("ug", [32, SEQ], BF16)
        yf = C.sb("yf", [32, SEQ], F32)
        pv = C.sb("pv", [128, 3], F32)
        sc = C.sb("sc", [128, 24], F32)
        sci = C.sb("sci", [128, 2], I32)
        Tr = C.sb("Tr", [128, BL + 1], F32)
        Ti = C.sb("Ti", [128, BL + 1], F32)
        tmp = C.sb("tmp", [128, BL], F32)
        Ar = C.sb("Ar", [128, BL], F32)
        Ai = C.sb("Ai", [128, BL], F32)
        Bb = C.sb("Bb", [32, 2, 128], BF16)
        Cb = C.sb("Cb", [128, 2, 32], BF16)
        dsk = C.sb("dsk", [32, 1], F32)
        m = [C.sb("m%d" % i, [128, BL], F32) for i in range(4)]
        wr = C.sb("wr", [128, BL], F32)
        wi = C.sb("wi", [128, BL], F32)
        sr = C.sb("sr", [128, BL], F32)
        si = C.sb("si", [128, BL], F32)
        xr = [C.sb("xr%d" % i, [128, BL], BF16) for i in range(2)]
        xi = [C.sb("xi%d" % i, [128, BL], BF16) for i in range(2)]
        init = C.sb("init", [128, 2], F32)
        yo = [C.sb("yo%d" % i, [32, BL], F32) for i in range(2)]
        TWO_PI = 2.0 * math.pi
        V = "vector"
        G = "gpsimd"

        def col(i):
            return sc[:, i:i + 1]

        def cmul_scalar(out_r, out_i, in_r, in_i, s_r, s_i, keys_in, keys_out, n):
            add(V, lambda e: e.tensor_scalar(out=tmp[:, 0:n], in0=in_i, scalar1=s_i, scalar2=None, op0=ALU.mult), R=keys_in, W=["tmp"])
            add(V, lambda e: e.scalar_tensor_tensor(out=out_r, in0=in_r, scalar=s_r, in1=tmp[:, 0:n], op0=ALU.mult, op1=ALU.subtract),
                R=keys_in + ["tmp"], W=keys_out)
            add(V, lambda e: e.tensor_scalar(out=tmp[:, 0:n], in0=in_i, scalar1=s_r, scalar2=None, op0=ALU.mult), R=keys_in, W=["tmp"])
            add(V, lambda e: e.scalar_tensor_tensor(out=out_i, in0=in_r, scalar=s_i, in1=tmp[:, 0:n], op0=ALU.mult, op1=ALU.add),
                R=keys_in + ["tmp"], W=keys_out)

        for gp in range(8):
            dma("sync", ug[:], uT_in[gp * 32:(gp + 1) * 32, :], W=["ug"])
            dma("sync", dsk[:], s5d[gp], W=["dsk"])
            for dr in range(2):
                dma("sync", pv[:], s5v[gp, dr], W=["pv"])
                dma("sync", Bb[:, 0, :], s5B[gp, dr, 0], W=["Bb"])
                dma("sync", Bb[:, 1, :], s5B[gp, dr, 1], W=["Bb"])
                dma("sync", Cb[:, 0, :], s5C[gp, dr, 0], W=["Cb"])
                dma("sync", Cb[:, 1, :], s5C[gp, dr, 1], W=["Cb"])
                K = ["sc"]
                add("scalar", lambda e: e.activation(out=col(0), in_=pv[:, 2:3], func=AF.Exp), R=["pv"], W=K)
                add("scalar", lambda e: e.activation(out=col(1), in_=pv[:, 0:1], func=AF.Exp, scale=col(0)), R=["pv"] + K, W=K)
                add(V, lambda e: e.tensor_tensor(out=col(2), in0=pv[:, 1:2], in1=col(0), op=ALU.mult), R=["pv"] + K, W=K)
                add(V, lambda e: e.tensor_scalar(out=col(3), in0=col(2), scalar1=0.5 * math.pi, scalar2=None, op0=ALU.add), R=K, W=K)
                add(V, lambda e: e.tensor_scalar(out=sc[:, 4:6], in0=sc[:, 2:4], scalar1=1.0 / TWO_PI, scalar2=None, op0=ALU.mult), R=K, W=K)
                add(V, lambda e: e.tensor_copy(out=sci[:, 0:2], in_=sc[:, 4:6]), R=K, W=["sci"])
                add(V, lambda e: e.tensor_copy(out=sc[:, 4:6], in_=sci[:, 0:2]), R=["sci"], W=K)
                add(V, lambda e: e.scalar_tensor_tensor(out=sc[:, 6:8], in0=sc[:, 4:6], scalar=-TWO_PI, in1=sc[:, 2:4], op0=ALU.mult, op1=ALU.add),
                    R=K, W=K)
                add(V, lambda e: e.tensor_scalar(out=sc[:, 6:8], in0=sc[:, 6:8], scalar1=-math.pi, scalar2=math.pi, op0=ALU.max, op1=ALU.min), R=K, W=K)
                add("scalar", lambda e: e.activation(out=sc[:, 8:10], in_=sc[:, 6:8], func=AF.Sin), R=K, W=K)
                add(G, lambda e: e.memset(Tr[:, 0:1], 1.0), W=["T"])
                add(G, lambda e: e.memset(Ti[:, 0:1], 0.0), W=["T"])
                add(V, lambda e: e.tensor_copy(out=Tr[:, 1:2], in_=col(9)), R=K, W=["T"])
                add(V, lambda e: e.tensor_copy(out=Ti[:, 1:2], in_=col(8)), R=K, W=["T"])
                n = 1
                while n < BL:
                    cmul_scalar(Tr[:, n + 1:2 * n + 1], Ti[:, n + 1:2 * n + 1], Tr[:, 1:n + 1], Ti[:, 1:n + 1],
                                Tr[:, n:n + 1], Ti[:, n:n + 1], ["T"], ["T"], n)
                    n *= 2
                add(V, lambda e: e.tensor_tensor(out=col(10), in0=col(1), in1=col(9), op=ALU.mult), R=K, W=K)
                add(V, lambda e: e.tensor_tensor(out=col(11), in0=col(1), in1=col(8), op=ALU.mult), R=K, W=K)
                add(V, lambda e: e.tensor_scalar(out=col(10), in0=col(10), scalar1=-1.0, scalar2=None, op0=ALU.add), R=K, W=K)
                add(V, lambda e: e.tensor_tensor(out=col(12), in0=col(11), in1=pv[:, 1:2], op=ALU.mult), R=K + ["pv"], W=K)
                add(V, lambda e: e.scalar_tensor_tensor(out=col(13), in0=col(10), scalar=pv[:, 0:1], in1=col(12), op0=ALU.mult, op1=ALU.add),
                    R=K + ["pv"], W=K)
                add(V, lambda e: e.tensor_tensor(out=col(12), in0=col(10), in1=pv[:, 1:2], op=ALU.mult), R=K + ["pv"], W=K)
                add(V, lambda e: e.scalar_tensor_tensor(out=col(14), in0=col(11), scalar=pv[:, 0:1], in1=col(12), op0=ALU.mult, op1=ALU.subtract),
                    R=K + ["pv"], W=K)
                add(V, lambda e: e.tensor_tensor(out=col(15), in0=pv[:, 0:1], in1=pv[:, 0:1], op=ALU.mult), R=["pv"], W=K)
                add(V, lambda e: e.scalar_tensor_tensor(out=col(15), in0=pv[:, 1:2], scalar=pv[:, 1:2], in1=col(15), op0=ALU.mult, op1=ALU.add),
                    R=K + ["pv"], W=K)
                add(V, lambda e: e.reciprocal(out=col(15), in_=col(15)), R=K, W=K)
                add(V, lambda e: e.tensor_tensor(out=col(16), in0=col(13), in1=col(15), op=ALU.mult), R=K, W=K)
                add(V, lambda e: e.tensor_tensor(out=col(17), in0=col(14), in1=col(15), op=ALU.mult), R=K, W=K)
                add(V, lambda e: e.tensor_scalar(out=col(18), in0=col(17), scalar1=-1.0, scalar2=None, op0=ALU.mult), R=K, W=K)
                add(V, lambda e: e.tensor_scalar(out=tmp[:], in0=Ti[:, 0:BL], scalar1=col(17), scalar2=None, op0=ALU.mult), R=["T"] + K, W=["tmp"])
                add(V, lambda e: e.scalar_tensor_tensor(out=Ar[:], in0=Tr[:, 0:BL], scalar=col(16), in1=tmp[:], op0=ALU.mult, op1=ALU.add),
                    R=["T", "tmp"] + K, W=["A"])
                add(V, lambda e: e.tensor_scalar(out=tmp[:], in0=Ti[:, 0:BL], scalar1=col(16), scalar2=None, op0=ALU.mult), R=["T"] + K, W=["tmp"])
                add(V, lambda e: e.scalar_tensor_tensor(out=Ai[:], in0=Tr[:, 0:BL], scalar=col(17), in1=tmp[:], op0=ALU.mult, op1=ALU.subtract),
                    R=["T", "tmp"] + K, W=["A"])
                add(G, lambda e: e.memset(init[:], 0.0), W=["init"])
                def emit_bu(bi):
                    blk = bi if dr == 0 else NBLK - 1 - bi
                    t0 = blk * BL
                    pa, pka = C.next_ps("S5a", [5])
                    pb, pkb = C.next_ps("S5b", [6])
                    add("tensor", lambda e, pa=pa, t0=t0: e.matmul(pa[:, :], lhsT=Bb[:, 0, :], rhs=ug[:, t0:t0 + BL], start=True, stop=True),
                        R=["Bb", "ug"], W=[pka])
                    add("tensor", lambda e, pb=pb, t0=t0: e.matmul(pb[:, :], lhsT=Bb[:, 1, :], rhs=ug[:, t0:t0 + BL], start=True, stop=True),
                        R=["Bb", "ug"], W=[pkb])
                    return pa, pka, pb, pkb
                nxt_bu = emit_bu(0)
                for bi in range(NBLK):
                    blk = bi if dr == 0 else NBLK - 1 - bi
                    t0 = blk * BL
                    pa, pka, pb, pkb = nxt_bu
                    if dr == 0:
                        bur, bui = pa[:, :], pb[:, :]
                    else:
                        bur, bui = pa[:, ::-1], pb[:, ::-1]
                    add(V, lambda e, bur=bur: e.tensor_tensor(out=m[0][:], in0=bur, in1=Ar[:], op=ALU.mult), R=[pka, "A"], W=["m0"])
                    add(V, lambda e, bui=bui: e.tensor_tensor(out=m[1][:], in0=bui, in1=Ai[:], op=ALU.mult), R=[pkb, "A"], W=["m1"])
                    add(V, lambda e, bui=bui: e.tensor_tensor(out=m[2][:], in0=bui, in1=Ar[:], op=ALU.mult), R=[pkb, "A"], W=["m2"])
                    add(V, lambda e, bur=bur: e.tensor_tensor(out=m[3][:], in0=bur, in1=Ai[:], op=ALU.mult), R=[pka, "A"], W=["m3"])
                    if bi + 1 < NBLK:
                        nxt_bu = emit_bu(bi + 1)
                    add(G, lambda e: e.tensor_tensor(out=wr[:], in0=m[0][:], in1=m[1][:], op=ALU.subtract), R=["m0", "m1"], W=["wr"])
                    add(G, lambda e: e.tensor_tensor(out=wi[:], in0=m[2][:], in1=m[3][:], op=ALU.add), R=["m2", "m3"], W=["wi"])
                    add(V, lambda e: e.tensor_tensor_scan(out=sr[:], data0=col(1).to_broadcast([128, BL]), data1=wr[:], initial=init[:, 0:1],
                                                          op0=ALU.mult, op1=ALU.add), R=["wr", "init"] + K, W=["sr"])
                    add(V, lambda e: e.tensor_tensor_scan(out=si[:], data0=col(1).to_broadcast([128, BL]), data1=wi[:], initial=init[:, 1:2],
                                                          op0=ALU.mult, op1=ALU.add), R=["wi", "init"] + K, W=["si"])
                    cmul_scalar(init[:, 0:1], init[:, 1:2], sr[:, BL - 1:BL], si[:, BL - 1:BL], Tr[:, BL:BL + 1], Ti[:, BL:BL + 1],
                                ["sr", "si", "T"], ["init"], 1)
                    add(V, lambda e: e.tensor_tensor(out=m[0][:], in0=sr[:], in1=Tr[:, 0:BL], op=ALU.mult), R=["sr", "T"], W=["m0"])
                    add(V, lambda e: e.tensor_tensor(out=m[1][:], in0=si[:], in1=Ti[:, 0:BL], op=ALU.mult), R=["si", "T"], W=["m1"])
                    add(V, lambda e: e.tensor_tensor(out=m[2][:], in0=si[:], in1=Tr[:, 0:BL], op=ALU.mult), R=["si", "T"], W=["m2"])
                    add(V, lambda e: e.tensor_tensor(out=m[3][:], in0=sr[:], in1=Ti[:, 0:BL], op=ALU.mult), R=["sr", "T"], W=["m3"])
                    xs = C.nxt("xs", 2)
                    if dr == 0:
                        a0, a1, a2, a3 = m[0][:], m[1][:], m[2][:], m[3][:]
                    else:
                        a0, a1, a2, a3 = m[0][:, ::-1], m[1][:, ::-1], m[2][:, ::-1], m[3][:, ::-1]
                    add(V, lambda e, xs=xs, a0=a0, a1=a1: e.tensor_tensor(out=xr[xs][:], in0=a0, in1=a1, op=ALU.subtract),
                        R=["m0", "m1"], W=[("xr", xs)])
                    add(V, lambda e, xs=xs, a2=a2, a3=a3: e.scalar_tensor_tensor(out=xi[xs][:], in0=a2, scalar=-1.0, in1=a3, op0=ALU.mult, op1=ALU.subtract),
                        R=["m2", "m3"], W=[("xi", xs)])
                    py, pky = C.next_ps("Y", [7])
                    add("tensor", lambda e, py=py, xs=xs: e.matmul(py[0:32, :], lhsT=Cb[:, 0, :], rhs=xr[xs][:], start=True, stop=False),
                        R=["Cb", ("xr", xs)], W=[pky])
                    add("tensor", lambda e, py=py, xs=xs: e.matmul(py[0:32, :], lhsT=Cb[:, 1, :], rhs=xi[xs][:], start=False, stop=True),
                        R=["Cb", ("xi", xs)], W=[pky])
                    if dr == 0:
                        add(V, lambda e, py=py, t0=t0: e.tensor_copy(out=yf[:, t0:t0 + BL], in_=py[0:32, :]), R=[pky], W=["yf"])
                    else:
                        y = C.nxt("yo", 2)
                        add(V, lambda e, py=py, t0=t0: e.tensor_tensor(out=yf[:, t0:t0 + BL], in0=py[0:32, :], in1=yf[:, t0:t0 + BL], op=ALU.add),
                            R=[pky, "yf"], W=["yf"])
                        add(V, lambda e, y=y, t0=t0: e.scalar_tensor_tensor(out=yo[y][:], in0=ug[:, t0:t0 + BL], scalar=dsk[:, 0:1], in1=yf[:, t0:t0 + BL],
                                                                             op0=ALU.mult, op1=ALU.add), R=["ug", "dsk", "yf"], W=[("yo", y)])
                        dma("gpsimd", ysT_out[gp * 32:(gp + 1) * 32, t0:t0 + BL], yo[y][:], R=[("yo", y)])
                    yield
    gens = []
    if do_attn:
        gens.append((attn_gen(), 8))
    if do_s5:
        gens.append((s5_gen(), 1))
    alive = [True] * len(gens)
    it = 0
    bg_every = max(1, 256 // max(1, bg_total)) if bg is not None else 0
    bg_per = max(1, -(-bg_total // 256)) if bg is not None else 0
    while any(alive):
        it += 1
        if bg is not None and it % bg_every == 0:
            bg.emit(C, bg_per, cast_engines=("scalar",))
        for gi, (g, reps) in enumerate(gens):
            if not alive[gi]:
                continue
            for _ in range(reps):
                try:
                    next(g)
                except StopIteration:
                    alive[gi] = False
                    break
    return C.finish()


def pack_s5(lam_re, lam_im, log_step, b_re, b_im, c_re, c_im, d_skip, g0, ng):
    ngp = ng // 2
    s5v = np.zeros((ngp, 2, 128, 3), np.float32)
    s5B = np.zeros((ngp, 2, 2, 32, 128), NPBF)
    s5C = np.zeros((ngp, 2, 2, 128, 32), NPBF)
    s5d = np.zeros((ngp, 32, 1), np.float32)
    for gp in range(ngp):
        for j in range(2):
            g = g0 + 2 * gp + j
            s5d[gp, j * 16:(j + 1) * 16, 0] = d_skip[g * 16:(g + 1) * 16]
            for dr in range(2):
                s5v[gp, dr, j * 64:(j + 1) * 64, 0] = lam_re[dr, g]
                s5v[gp, dr, j * 64:(j + 1) * 64, 1] = lam_im[dr, g]
                s5v[gp, dr, j * 64:(j + 1) * 64, 2] = log_step[dr, g]
                s5B[gp, dr, 0, j * 16:(j + 1) * 16, j * 64:(j + 1) * 64] = b_re[dr, g].T.astype(NPBF)
                s5B[gp, dr, 1, j * 16:(j + 1) * 16, j * 64:(j + 1) * 64] = b_im[dr, g].T.astype(NPBF)
                s5C[gp, dr, 0, j * 64:(j + 1) * 64, j * 16:(j + 1) * 16] = c_re[dr, g].T.astype(NPBF)
                s5C[gp, dr, 1, j * 64:(j + 1) * 64, j * 16:(j + 1) * 16] = c_im[dr, g].T.astype(NPBF)
    return {"s5v": s5v, "s5B": s5B, "s5C": s5C, "s5d": s5d}


def _g16(g):
    return np.ascontiguousarray(np.asarray(g, np.float32).reshape(16, 128).T)


def rope_tables_host(p0, n):
    f32 = np.float32
    pos = np.arange(p0, p0 + n, dtype=f32)
    inv = (f32(500000.0) ** (-(np.arange(0, 32, 2, dtype=f32)) / f32(32))).astype(f32)
    ang = (pos[:, None] * inv[None, :]).astype(f32)
    cos, sin = np.cos(ang).astype(f32), np.sin(ang).astype(f32)
    C32 = np.concatenate([cos.T, cos.T], axis=0)
    S32 = np.concatenate([-sin.T, sin.T], axis=0)
    return np.ascontiguousarray(C32), np.ascontiguousarray(S32)


def swap_cols(W, col0s):
    parts = []
    for c0 in col0s:
        parts.append(W[:, c0 + 16:c0 + 32])
        parts.append(W[:, c0:c0 + 16])
    return np.ascontiguousarray(np.concatenate(parts, axis=1))


_NC_CACHE = {}


def _get_nc(key, fn):
    if key not in _NC_CACHE:
        _NC_CACHE[key] = fn()
    return _NC_CACHE[key]


def _run(nc, in_maps):
    res = run_bass_kernel_spmd(nc, in_maps, core_ids=list(range(NCORE)))
    return res.results


def kernel_unfused(x, mem, norm_mix_g, norm_xa_g, norm_mem_g, xa_wq, xa_wkv, xa_wo, norm_ffn_g,
           ffn_w13, ffn_w2, ab_w_in, ab_w_out, s5_lambda_re, s5_lambda_im, s5_log_step,
           s5_b_re, s5_b_im, s5_c_re, s5_c_im, s5_d, s5_glu_w, s5_glu_b, diff_lambda,
           diff_subln_g, c_w_qkv, c_w_out, final_norm_g):
    A = lambda a: np.asarray(a)
    x = A(x).astype(np.float32, copy=False)
    mem = A(mem).astype(np.float32, copy=False)
    depth = 4
    cores = [(c // 4, (c % 4) * TOK) for c in range(NCORE)]
    hT = [np.ascontiguousarray(x[b, t0:t0 + TOK, :].T) for (b, t0) in cores]
    memT = [np.ascontiguousarray(mem[b].T) for b in range(NB)]
    ropes = [rope_tables_host(t0, TOK) for (b, t0) in cores]
    mask_tab = dilated_mask_table()

    def nxt_inputs(l):
        i = l // 2
        d = {"g_mix": _g16(A(norm_mix_g)[l])}
        if l % 2 == 0:
            W = A(ab_w_in)[i]
            d["w_in"] = W
            d["w_sw"] = swap_cols(W, [1024 + hc * 128 for hc in range(8)] + [2048 + hc * 128 for hc in range(8)])
        else:
            W = A(c_w_qkv)[i]
            d["w_in"] = W
            d["w_sw"] = swap_cols(W, [hc * 128 for hc in range(16)] + [2048 + hc * 128 for hc in range(16)])
        return d

    def prev_inputs(l):
        i = l // 2
        d = {"g_xa": _g16(A(norm_xa_g)[l]), "g_mem": _g16(A(norm_mem_g)[l]), "g_ffn": _g16(A(norm_ffn_g)[l]),
             "wq": A(xa_wq)[l], "wkv": A(xa_wkv)[l], "wo": A(xa_wo)[l], "w13": A(ffn_w13)[l], "w2": A(ffn_w2)[l]}
        if l % 2 == 0:
            d["w_out"] = A(ab_w_out)[i]
            d["glu_w"] = A(s5_glu_w)[i]
            d["glu_b"] = np.ascontiguousarray(A(s5_glu_b)[i].reshape(8, 128).T)
        else:
            d["w_out"] = A(c_w_out)[i]
        return d

    typ = lambda l: "even" if l % 2 == 0 else "odd"
    mix = None
    out = None
    for l in range(depth + 1):
        prev = typ(l - 1) if l > 0 else None
        nxt = typ(l) if l < depth else None
        final = (l == depth)
        nc = _get_nc(("T", prev, nxt, final), lambda: build_T(prev, nxt, final))
        shared = {}
        if prev is not None:
            shared.update(prev_inputs(l - 1))
        if nxt is not None:
            shared.update(nxt_inputs(l))
        if final:
            shared["g_fin"] = _g16(A(final_norm_g))
        in_maps = []
        for c, (b, t0) in enumerate(cores):
            m = dict(shared)
            m["hT_in"] = hT[c]
            if prev is not None:
                m["memT"] = memT[b]
                m.update(mix[c])
            if nxt is not None:
                m["ropeC"], m["ropeS"] = ropes[c]
            in_maps.append(m)
        res = _run(nc, in_maps)
        if final:
            out = np.empty((NB, SEQ, D), np.float32)
            for c, (b, t0) in enumerate(cores):
                out[b, t0:t0 + TOK, :] = res[c]["outT"].T
            break
        hT = [res[c]["hT_out"] for c in range(NCORE)]
        i = l // 2
        cat = lambda name, b, sl: np.ascontiguousarray(np.concatenate([res[4 * b + r][name][sl] for r in range(4)], axis=-1))
        if nxt == "even":
            lambda_init = 0.8 - 0.6 * math.exp(-0.3 * l)
            nch = _get_nc(("He", l), lambda: build_H_even(lambda_init))
            in_maps = []
            lam_rep = np.ascontiguousarray(np.broadcast_to(A(diff_lambda)[i].astype(np.float32), (128, 4, 128)))
            sgi = np.ascontiguousarray(A(diff_subln_g)[i].astype(np.float32).reshape(2, 128).T)
            for c in range(NCORE):
                b, hd = c // 4, c % 4
                m = {"qT_in": cat("qT_out", b, slice(2 * hd, 2 * hd + 2)),
                     "kT_in": cat("kT_out", b, slice(2 * hd, 2 * hd + 2)),
                     "v_in": np.ascontiguousarray(np.concatenate([res[4 * b + r]["v_out"][:, hd * 256:(hd + 1) * 256] for r in range(4)], axis=0)),
                     "uT_in": cat("uT_out", b, slice(hd * 256, (hd + 1) * 256)),
                     "lam_in": lam_rep, "sg_in": sgi}
                m.update(pack_s5(A(s5_lambda_re)[i], A(s5_lambda_im)[i], A(s5_log_step)[i], A(s5_b_re)[i], A(s5_b_im)[i],
                                 A(s5_c_re)[i], A(s5_c_im)[i], A(s5_d)[i], 16 * hd, 16))
                in_maps.append(m)
            rh = _run(nch, in_maps)
            mix = []
            for c, (b, t0) in enumerate(cores):
                mix.append({"ys5T": np.ascontiguousarray(np.concatenate([rh[4 * b + hd]["ysT_out"][:, t0:t0 + TOK] for hd in range(4)], axis=0)),
                            "ydiffT": np.ascontiguousarray(np.concatenate([rh[4 * b + hd]["ydT_out"][:, t0:t0 + TOK] for hd in range(4)], axis=0))})
        else:
            nch = _get_nc(("Ho",), build_H_odd)
            in_maps = []
            for c in range(NCORE):
                b, hq = c // 4, c % 4
                vv = np.concatenate([res[4 * b + r]["v_out"][:, hq * 512:(hq + 1) * 512] for r in range(4)], axis=0)
                in_maps.append({"qT_in": cat("qT_out", b, slice(4 * hq, 4 * hq + 4)),
                                "kT_in": cat("kT_out", b, slice(4 * hq, 4 * hq + 4)),
                                "v_in": np.ascontiguousarray(vv.reshape(SEQ, 4, 128)),
                                "mask_in": mask_tab})
            rh = _run(nch, in_maps)
            mix = []
            for c, (b, t0) in enumerate(cores):
                mix.append({"oT_in": np.ascontiguousarray(np.concatenate([rh[4 * b + hq]["oT_out"][:, t0:t0 + TOK] for hq in range(4)], axis=0))})
    return out


def build_fused(depth=4):
    nc = bass.Bass("TRN2", target_bir_lowering=False)

    def ext(name, shape, dt=F32):
        return nc.dram_tensor(name, list(shape), dt, kind="ExternalInput").ap()

    def internal(name, shape, dt=F32):
        return nc.dram_tensor(name, list(shape), dt).ap()

    x_hT = ext("x_hT", [D, SEQ])
    memT = ext("memT", [D, 256])
    ropeC = ext("ropeC", [32, SEQ])
    ropeS = ext("ropeS", [32, SEQ])
    mask_in = ext("mask_in", [128, 20, 512], BF16)
    g_fin = ext("g_fin", [128, 16])
    outT = nc.dram_tensor("outT", [D, SEQ], F32, kind="ExternalOutput").ap()
    L = []
    for l in range(depth):
        d = {}
        for nm in ("g_mix", "g_xa", "g_mem", "g_ffn"):
            d[nm] = ext("%s_%d" % (nm, l), [128, 16])
        d["wq"] = ext("wq_%d" % l, [D, D])
        d["wkv"] = ext("wkv_%d" % l, [D, 2 * D])
        d["wo"] = ext("wo_%d" % l, [D, D])
        d["w13"] = ext("w13_%d" % l, [D, 2 * FFN])
        d["w2"] = ext("w2_%d" % l, [FFN, D])
        d["w_out"] = ext("w_out_%d" % l, [D, D])
        if l % 2 == 0:
            d["w_in"] = ext("w_in_%d" % l, [D, 4096])
            d["w_sw"] = ext("w_sw_%d" % l, [D, 512])
            d["glu_w"] = ext("glu_w_%d" % l, [1024, 1024])
            d["glu_b"] = ext("glu_b_%d" % l, [128, 8])
            d["lam_in"] = ext("lam_in_%d" % l, [128, 4, 128])
            d["sg_in"] = ext("sg_in_%d" % l, [128, 2])
            d["s5v"] = ext("s5v_%d" % l, [32, 2, 128, 3])
            d["s5B"] = ext("s5B_%d" % l, [32, 2, 2, 32, 128], BF16)
            d["s5C"] = ext("s5C_%d" % l, [32, 2, 2, 128, 32], BF16)
            d["s5d"] = ext("s5d_%d" % l, [32, 32, 1])
        else:
            d["w_in"] = ext("w_in_%d" % l, [D, 6144])
            d["w_sw"] = ext("w_sw_%d" % l, [D, 1024])
        L.append(d)
    hT = internal("hT", [D, SEQ])
    uT = internal("uT", [1024, SEQ], BF16)
    qTe = internal("qTe", [8, 128, SEQ], BF16)
    kTe = internal("kTe", [8, 128, SEQ], BF16)
    ve = internal("ve", [SEQ, 1024], BF16)
    qTo = internal("qTo", [16, 128, SEQ], BF16)
    kTo = internal("kTo", [16, 128, SEQ], BF16)
    vo = internal("vo", [SEQ, 2048], BF16)
    ys5T = internal("ys5T", [1024, SEQ])
    ydiffT = internal("ydiffT", [1024, SEQ], BF16)
    oT = internal("oT", [2048, SEQ], BF16)
    typ = lambda l: "even" if l % 2 == 0 else "odd"
    bgs = {}

    def launch_wio(l):
        prev = typ(l - 1) if l > 0 else None
        nxt = typ(l) if l < depth else None
        wio = {}
        if prev is not None:
            wio.update({nm: L[l - 1][nm] for nm in ("w_out", "wq", "wkv", "wo", "w13", "w2")})
            if prev == "even":
                wio["glu_w"] = L[l - 1]["glu_w"]
        if nxt is not None:
            wio.update({"w_in": L[l]["w_in"], "w_sw": L[l]["w_sw"]})
        return t_weight_specs(prev, nxt, wio)

    for l in range(depth + 1):
        prev = typ(l - 1) if l > 0 else None
        nxt = typ(l) if l < depth else None
        final = (l == depth)
        wio = {}
        if prev is not None:
            wio.update({nm: L[l - 1][nm] for nm in ("w_out", "wq", "wkv", "wo", "w13", "w2")})
            if prev == "even":
                wio["glu_w"] = L[l - 1]["glu_w"]
        if nxt is not None:
            wio.update({"w_in": L[l]["w_in"], "w_sw": L[l]["w_sw"]})
        if bgs.get(l) is None:
            wb = emit_W(nc, t_weight_specs(prev, nxt, wio))
        else:
            bgs[l].flush()
            wb = bgs[l].out
        for r in range(SEQ // TOK):
            ts = slice(r * TOK, (r + 1) * TOK)
            io = {"hT_in": (x_hT if l == 0 else hT)[:, ts]}
            if prev is not None:
                pl = L[l - 1]
                for nm in ("w_out", "g_xa", "g_mem", "wq", "wkv", "wo", "g_ffn", "w13", "w2"):
                    io[nm] = pl[nm]
                io["memT"] = memT
                if prev == "even":
                    io["ys5T"] = ys5T[:, ts]
                    io["ydiffT"] = ydiffT[:, ts]
                    io["glu_w"] = pl["glu_w"]
                    io["glu_b"] = pl["glu_b"]
                else:
                    io["oT_in"] = oT[:, ts]
            if nxt is not None:
                nl = L[l]
                io["g_mix"] = nl["g_mix"]
                io["ropeC"] = ropeC[:, ts]
                io["ropeS"] = ropeS[:, ts]
                io["w_in"] = nl["w_in"]
                io["w_sw"] = nl["w_sw"]
                io["hT_out"] = hT[:, ts]
                if nxt == "even":
                    io["uT_out"] = uT[:, ts]
                    io["qT_out"] = qTe[:, :, ts]
                    io["kT_out"] = kTe[:, :, ts]
                    io["v_out"] = ve[ts, :]
                else:
                    io["qT_out"] = qTo[:, :, ts]
                    io["kT_out"] = kTo[:, :, ts]
                    io["v_out"] = vo[ts, :]
            if final:
                io["g_fin"] = g_fin
                io["outT"] = outT[:, ts]
            build_T(prev, nxt, final, nc=nc, io=io, wb=wb)
        if final:
            break
        bg = BgW(nc, launch_wio(l + 1))
        bgs[l + 1] = bg
        bg_total = -(-len(bg.jobs) // 4)
        if nxt == "even":
            lambda_init = 0.8 - 0.6 * math.exp(-0.3 * l)
            nl = L[l]
            for hd in range(4):
                cs = slice(hd * 256, (hd + 1) * 256)
                io = {"qT_in": qTe[2 * hd:2 * hd + 2], "kT_in": kTe[2 * hd:2 * hd + 2], "v_in": ve[:, cs],
                      "lam_in": nl["lam_in"], "sg_in": nl["sg_in"], "ydT_out": ydiffT[cs, :], "uT_in": uT[cs, :],
                      "s5v": nl["s5v"][8 * hd:8 * hd + 8], "s5B": nl["s5B"][8 * hd:8 * hd + 8], "s5C": nl["s5C"][8 * hd:8 * hd + 8],
                      "s5d": nl["s5d"][8 * hd:8 * hd + 8], "ysT_out": ys5T[cs, :]}
                build_H_even(lambda_init, nc=nc, io=io, bg=bg, bg_total=bg_total)
        else:
            for hq in range(4):
                io = {"qT_in": qTo[4 * hq:4 * hq + 4], "kT_in": kTo[4 * hq:4 * hq + 4],
                      "v_in": vo[:, hq * 512:(hq + 1) * 512].rearrange("s (h e) -> s h e", h=4),
                      "mask_in": mask_in, "oT_out": oT[hq * 512:(hq + 1) * 512, :]}
                build_H_odd(nc=nc, io=io, bg=bg, bg_total=bg_total)
    return nc


def kernel_fused(x, mem, norm_mix_g, norm_xa_g, norm_mem_g, xa_wq, xa_wkv, xa_wo, norm_ffn_g,
                 ffn_w13, ffn_w2, ab_w_in, ab_w_out, s5_lambda_re, s5_lambda_im, s5_log_step,
                 s5_b_re, s5_b_im, s5_c_re, s5_c_im, s5_d, s5_glu_w, s5_glu_b, diff_lambda,
                 diff_subln_g, c_w_qkv, c_w_out, final_norm_g):
    A = lambda a: np.asarray(a)
    x = A(x).astype(np.float32, copy=False)
    mem = A(mem).astype(np.float32, copy=False)
    depth = 4
    nc = _get_nc(("fused",), build_fused)
    C32, S32 = rope_tables_host(0, SEQ)
    shared = {"ropeC": C32, "ropeS": S32, "mask_in": dilated_mask_table(), "g_fin": _g16(A(final_norm_g))}
    for l in range(depth):
        i = l // 2
        shared["g_mix_%d" % l] = _g16(A(norm_mix_g)[l])
        shared["g_xa_%d" % l] = _g16(A(norm_xa_g)[l])
        shared["g_mem_%d" % l] = _g16(A(norm_mem_g)[l])
        shared["g_ffn_%d" % l] = _g16(A(norm_ffn_g)[l])
        shared["wq_%d" % l] = A(xa_wq)[l]
        shared["wkv_%d" % l] = A(xa_wkv)[l]
        shared["wo_%d" % l] = A(xa_wo)[l]
        shared["w13_%d" % l] = A(ffn_w13)[l]
        shared["w2_%d" % l] = A(ffn_w2)[l]
        if l % 2 == 0:
            W = A(ab_w_in)[i]
            shared["w_in_%d" % l] = W
            shared["w_sw_%d" % l] = swap_cols(W, [1024 + hc * 128 for hc in range(8)] + [2048 + hc * 128 for hc in range(8)])
            shared["w_out_%d" % l] = A(ab_w_out)[i]
            shared["glu_w_%d" % l] = A(s5_glu_w)[i]
            shared["glu_b_%d" % l] = np.ascontiguousarray(A(s5_glu_b)[i].reshape(8, 128).T)
            shared["lam_in_%d" % l] = np.ascontiguousarray(np.broadcast_to(A(diff_lambda)[i].astype(np.float32), (128, 4, 128)))
            shared["sg_in_%d" % l] = np.ascontiguousarray(A(diff_subln_g)[i].astype(np.float32).reshape(2, 128).T)
            pk = pack_s5(A(s5_lambda_re)[i], A(s5_lambda_im)[i], A(s5_log_step)[i], A(s5_b_re)[i], A(s5_b_im)[i],
                         A(s5_c_re)[i], A(s5_c_im)[i], A(s5_d)[i], 0, 64)
            for k, v in pk.items():
                shared["%s_%d" % (k, l)] = v
        else:
            W = A(c_w_qkv)[i]
            shared["w_in_%d" % l] = W
            shared["w_sw_%d" % l] = swap_cols(W, [hc * 128 for hc in range(16)] + [2048 + hc * 128 for hc in range(16)])
            shared["w_out_%d" % l] = A(c_w_out)[i]
    in_maps = []
    for b in range(NB):
        m = dict(shared)
        m["x_hT"] = np.ascontiguousarray(x[b].T)
        m["memT"] = np.ascontiguousarray(mem[b].T)
        in_maps.append(m)
    res = run_bass_kernel_spmd(nc, in_maps, core_ids=list(range(NB))).results
    out = np.empty((NB, SEQ, D), np.float32)
    for b in range(NB):
        out[b] = res[b]["outT"].T
    return out


def kernel(**inputs):
    return kernel_fused(**inputs)
```

```python
import contextlib
import math
import numpy as np
import ml_dtypes
import concourse.bass as bass
import concourse.mybir as mybir
from concourse.bass_utils import run_bass_kernel_spmd

F32 = mybir.dt.float32
BF16 = mybir.dt.bfloat16
I32 = mybir.dt.int32
ALU = mybir.AluOpType
AF = mybir.ActivationFunctionType
NPBF = ml_dtypes.bfloat16

D = 2048
SEQ = 8192
NB = 2
NCORE = 8
TOK = 2048
TG = 512
FFN = 5632
EPS = 1e-6
SKIP = set()
NGRP_OVERRIDE = 0


class Prog:
    DMA_K = 6

    def __init__(self, nc):
        self.nc = nc
        self.ops = []
        self.last_w = {}
        self.readers = {}

    def add(self, eng, fn, R=(), W=(), dma=False):
        i = len(self.ops)
        deps = set()
        for k in list(R) + list(W):
            if k in self.last_w:
                deps.add(self.last_w[k])
        for k in W:
            for r in self.readers.get(k, ()):
                deps.add(r)
        for k in R:
            lst = self.readers.setdefault(k, [])
            if not dma:
                lst[:] = [r for r in lst if self.ops[r]["dma"] or self.ops[r]["eng"] != eng]
            lst.append(i)
        for k in W:
            self.last_w[k] = i
            self.readers[k] = []
        self.ops.append(dict(eng=eng, fn=fn, deps=deps, dma=dma, sig=dma, signal=None))
        return i

    def dma(self, eng, out, in_, R=(), W=()):
        return self.add(eng, lambda e: e.dma_start(out=out, in_=in_), R, W, dma=True)

    def emit(self):
        nc = self.nc
        ops = self.ops
        for i, o in enumerate(ops):
            nd = set()
            for d in o["deps"]:
                if ops[d]["eng"] == "tensor" and o["eng"] == "tensor" and not ops[d]["dma"] and not o["dma"]:
                    continue
                nd.add(d)
            o["deps"] = nd
            for d in nd:
                ops[d]["sig"] = True
        engs = ["tensor", "vector", "scalar", "gpsimd", "sync"]
        pfx = getattr(self, "pfx", "")
        self.sems = []

        def mk(name):
            h = nc.alloc_semaphore(name=pfx + name)
            self.sems.append(h)
            return h
        csem = {e: mk("c_" + e) for e in engs}
        dsem = {e: [mk("d_%s%d" % (e, k)) for k in range(self.DMA_K)] for e in ["sync", "gpsimd"]}
        ccount = {e: 0 for e in engs}
        dcount = {e: 0 for e in dsem}
        streams = {e: [] for e in engs}
        waited = {e: {} for e in engs}
        final = {}
        for i, o in enumerate(ops):
            E = o["eng"]
            st = streams[E]
            w = waited[E]
            for d in sorted(o["deps"]):
                sem, val = ops[d]["signal"]
                if w.get(id(sem), 0) < val:
                    w[id(sem)] = val
                    st.append((lambda e, sem=sem, val=val: e.wait_ge(sem, val)))
            if o["dma"]:
                n = dcount[E]
                dcount[E] += 1
                sem = dsem[E][n % self.DMA_K]
                val = 16 * (n // self.DMA_K + 1)
                if n >= self.DMA_K and w.get(id(sem), 0) < val - 16:
                    w[id(sem)] = val - 16
                    st.append((lambda e, sem=sem, val=val: e.wait_ge(sem, val - 16)))
                o["signal"] = (sem, val)
                final[id(sem)] = (sem, val)
                st.append((lambda e, fn=o["fn"], sem=sem: fn(e).then_inc(sem, 16)))
            elif o["sig"]:
                ccount[E] += 1
                sem = csem[E]
                val = ccount[E]
                o["signal"] = (sem, val)
                st.append((lambda e, fn=o["fn"], sem=sem: fn(e).then_inc(sem, 1)))
            else:
                st.append((lambda e, fn=o["fn"]: fn(e)))
        for sem, val in final.values():
            if waited["sync"].get(id(sem), 0) < val:
                streams["sync"].append((lambda e, sem=sem, val=val: e.wait_ge(sem, val)))
        with nc.Block() as block:
            @block.tensor
            def _(e):
                for f in streams["tensor"]:
                    f(e)

            @block.vector
            def _(e):
                for f in streams["vector"]:
                    f(e)

            @block.scalar
            def _(e):
                for f in streams["scalar"]:
                    f(e)

            @block.gpsimd
            def _(e):
                for f in streams["gpsimd"]:
                    f(e)

            @block.sync
            def _(e):
                for f in streams["sync"]:
                    f(e)
        self.stats = {e: len(streams[e]) for e in engs}
        self.counts = (dict(ccount), dict(dcount))


class Ctx:
    count = 0

    def __init__(self, nc=None, io=None):
        self.io = io
        Ctx.count += 1
        self.pfx = "p%d_" % Ctx.count
        self.nc = nc if nc is not None else bass.Bass("TRN2", target_bir_lowering=False)
        self.P = Prog(self.nc)
        self.S = contextlib.ExitStack()
        self.P.stack = self.S
        self.P.pfx = self.pfx
        self.nps = 0
        self.ps = []
        self.rot = {}
        self.cast_i = 0

    def din(self, name, shape, dt=F32):
        if self.io is not None:
            ap = self.io[name]
            assert list(ap.shape) == list(shape), (name, ap.shape, shape)
            return ap
        return self.nc.dram_tensor(name, list(shape), dt, kind="ExternalInput").ap()

    def dout(self, name, shape, dt=F32):
        if self.io is not None:
            ap = self.io[name]
            assert list(ap.shape) == list(shape), (name, ap.shape, shape)
            return ap
        return self.nc.dram_tensor(name, list(shape), dt, kind="ExternalOutput").ap()

    def finish(self):
        self.P.emit()
        if self.io is not None:
            self.nc.all_engine_barrier()
            self.nc.clear_and_free_semaphores(self.P.sems)
            self.nc.all_engine_barrier()
        self.S.close()
        return self.nc

    def sb(self, name, shape, dt):
        return self.S.enter_context(self.nc.sbuf_tensor(self.pfx + name, list(shape), dt))

    def alloc_psum(self, n=8):
        self.ps = [self.S.enter_context(self.nc.psum_tensor(self.pfx + "ps%d" % i, [128, 512], F32)) for i in range(n)]

    def next_ps(self, group="main", banks=None):
        banks = banks if banks is not None else list(range(len(self.ps)))
        i = self.rot.get(group, 0)
        self.rot[group] = i + 1
        b = banks[i % len(banks)]
        return self.ps[b], ("ps", b)

    def nxt(self, group, n):
        i = self.rot.get(group, 0)
        self.rot[group] = i + 1
        return i % n


class BgW:
    count = 0

    def __init__(self, nc, specs):
        BgW.count += 1
        self.nc = nc
        self.jobs = []
        self.out = {}
        self.pos = 0
        for key, W, blk in specs:
            K_, N_ = W.shape
            nci, nb = K_ // 128, N_ // blk
            Wb = nc.dram_tensor("bgw%d_%s" % (BgW.count, key), [nb, 128, nci, blk], BF16).ap()
            for j in range(nb):
                for ci0 in range(0, nci, 16):
                    n = min(16, nci - ci0)
                    src = W[ci0 * 128:(ci0 + n) * 128, j * blk:(j + 1) * blk].rearrange("(c p) o -> p c o", p=128)
                    self.jobs.append((src, Wb[j, :, ci0:ci0 + n, :], n, blk))
            self.out[key] = (Wb, blk)

    def remaining(self):
        return len(self.jobs) - self.pos

    def emit(self, C, count=1, cast_engines=("scalar",)):
        if not hasattr(C, "bg_st"):
            C.bg_st = [C.sb("bgst%d" % i, [128, 16, 128], F32) for i in range(2)]
            C.bg_bf = [C.sb("bgbf%d" % i, [128, 16, 128], BF16) for i in range(2)]
            C.bg_k = 0
        P = C.P
        for _ in range(count):
            if self.pos >= len(self.jobs):
                return
            src, dst, n, blk = self.jobs[self.pos]
            self.pos += 1
            k = C.bg_k
            C.bg_k += 1
            s = k % 2
            st, bf = C.bg_st[s], C.bg_bf[s]
            P.dma("sync", st[:, 0:n, 0:blk], src, W=[("bgst", s)])
            ce = cast_engines[k % len(cast_engines)]
            if ce == "scalar":
                P.add("scalar", lambda e, st=st, bf=bf, n=n, blk=blk: e.activation(out=bf[:, 0:n, 0:blk], in_=st[:, 0:n, 0:blk], func=AF.Copy),
                      R=[("bgst", s)], W=[("bgbf", s)])
            else:
                P.add(ce, lambda e, st=st, bf=bf, n=n, blk=blk: e.tensor_copy(out=bf[:, 0:n, 0:blk], in_=st[:, 0:n, 0:blk]),
                      R=[("bgst", s)], W=[("bgbf", s)])
            P.dma("gpsimd", dst, bf[:, 0:n, 0:blk], R=[("bgbf", s)])

    def flush(self):
        if self.remaining() == 0:
            return
        C = Ctx(self.nc, io={})
        self.emit(C, self.remaining(), cast_engines=("vector", "scalar"))
        C.finish()


def emit_W(nc, specs):
    C = Ctx(nc, io={})
    P = C.P
    NS = 4
    wst = [C.sb("wst%d" % i, [128, 16, 128], F32) for i in range(NS)]
    wbf = [C.sb("wbf%d" % i, [128, 16, 128], BF16) for i in range(NS)]
    out = {}
    k = 0
    for key, W, blk in specs:
        K_, N_ = W.shape
        nci, nb = K_ // 128, N_ // blk
        Wb = nc.dram_tensor(C.pfx + "wb_" + key, [nb, 128, nci, blk], BF16).ap()
        for j in range(nb):
            for ci0 in range(0, nci, 16):
                n = min(16, nci - ci0)
                s = k % NS
                src = W[ci0 * 128:(ci0 + n) * 128, j * blk:(j + 1) * blk].rearrange("(c p) o -> p c o", p=128)
                P.dma("sync", wst[s][:, 0:n, 0:blk], src, W=[("wst", s)])
                ce = ["vector", "scalar"][k % 2]
                if ce == "scalar":
                    P.add("scalar", lambda e, s=s, n=n, blk=blk: e.activation(out=wbf[s][:, 0:n, 0:blk], in_=wst[s][:, 0:n, 0:blk], func=AF.Copy),
                          R=[("wst", s)], W=[("wbf", s)])
                else:
                    P.add(ce, lambda e, s=s, n=n, blk=blk: e.tensor_copy(out=wbf[s][:, 0:n, 0:blk], in_=wst[s][:, 0:n, 0:blk]),
                          R=[("wst", s)], W=[("wbf", s)])
                P.dma("sync" if k % 2 else "gpsimd", Wb[j, :, ci0:ci0 + n, :], wbf[s][:, 0:n, 0:blk], R=[("wbf", s)])
                k += 1
        out[key] = (Wb, blk)
    C.finish()
    return out


def t_weight_specs(prev, nxt, io):
    specs = []
    if prev is not None:
        if prev == "even":
            specs.append(("glu_w", io["glu_w"], 128))
        for nm in ("w_out", "wq", "wkv", "wo", "w13", "w2"):
            specs.append((nm, io[nm], 128))
    if nxt is not None:
        specs.append(("w_in", io["w_in"], 128))
    return specs


def build_T(prev, nxt, final, nc=None, io=None, wb=None):
    C = Ctx(nc, io)
    nc, P = C.nc, C.P
    add, dma = P.add, P.dma
    ngrp = NGRP_OVERRIDE or (TOK // TG)

    hT_in = C.din("hT_in", [D, TOK])
    if prev is not None:
        if prev == "even":
            ys5T = C.din("ys5T", [1024, TOK])
            ydiffT = C.din("ydiffT", [1024, TOK], BF16)
            glu_w = C.din("glu_w", [1024, 1024])
            glu_b = C.din("glu_b", [128, 8])
        else:
            oT_in = C.din("oT_in", [D, TOK], BF16)
        w_out = C.din("w_out", [D, D])
        memT = C.din("memT", [D, 256])
        g_xa = C.din("g_xa", [128, 16])
        g_mem = C.din("g_mem", [128, 16])
        wq = C.din("wq", [D, D])
        wkv = C.din("wkv", [D, 2 * D])
        wo = C.din("wo", [D, D])
        g_ffn = C.din("g_ffn", [128, 16])
        w13 = C.din("w13", [D, 2 * FFN])
        w2 = C.din("w2", [FFN, D])
    if nxt is not None:
        g_mix = C.din("g_mix", [128, 16])
        ropeC = C.din("ropeC", [32, TOK])
        ropeS = C.din("ropeS", [32, TOK])
        perm_in = C.din("perm_in", [32, 32], BF16)
        if nxt == "even":
            w_in = C.din("w_in", [D, 4096])
            w_sw = C.din("w_sw", [D, 16 * 32])
            uT_out = C.dout("uT_out", [1024, TOK], BF16)
            qT_out = C.dout("qT_out", [8, 128, TOK], BF16)
            kT_out = C.dout("kT_out", [8, 128, TOK], BF16)
            v_out = C.dout("v_out", [TOK, 1024], BF16)
        else:
            w_in = C.din("w_in", [D, 6144])
            w_sw = C.din("w_sw", [D, 32 * 32])
            qT_out = C.dout("qT_out", [16, 128, TOK], BF16)
            kT_out = C.dout("kT_out", [16, 128, TOK], BF16)
            v_out = C.dout("v_out", [TOK, 2048], BF16)
        hT_out = C.dout("hT_out", [D, TOK])
    if final:
        g_fin = C.din("g_fin", [128, 16])
        outT = C.dout("outT", [D, TOK])

    if wb is None:
        wio = {}
        if prev is not None:
            wio.update({"w_out": w_out, "wq": wq, "wkv": wkv, "wo": wo, "w13": w13, "w2": w2})
            if prev == "even":
                wio["glu_w"] = glu_w
        if nxt is not None:
            wio.update({"w_in": w_in, "w_sw": w_sw})
        wb = emit_W(nc, t_weight_specs(prev, nxt, wio))
    wbmap = {}
    if prev is not None:
        for nm, ap in (("w_out", w_out), ("wq", wq), ("wkv", wkv), ("wo", wo), ("w13", w13), ("w2", w2)):
            wbmap[id(ap)] = wb[nm]
        if prev == "even":
            wbmap[id(glu_w)] = wb["glu_w"]
    if nxt is not None:
        wbmap[id(w_in)] = wb["w_in"]

    C.alloc_psum(8)
    hT = C.sb("hT", [128, 16, TG], F32)
    hn = C.sb("hn", [128, 16, TG], BF16)
    sq = [C.sb("sq%d" % i, [128, TG], BF16) for i in range(2)]
    rstd = C.sb("rstd", [128, TG], F32)
    ones = C.sb("ones", [128, 128], BF16)
    epsc = C.sb("epsc", [128, 1], F32)
    NW = 6
    wbf = [C.sb("wbf%d" % i, [128, 16, 128], BF16) for i in range(NW)]
    gvec = C.sb("gvec", [128, 5, 16], F32)
    add("gpsimd", lambda e: e.memset(ones[:], 1.0), W=["ones"])
    add("gpsimd", lambda e: e.memset(epsc[:], EPS), W=["epsc"])

    def load_g(slot, src):
        dma("sync", gvec[:, slot, :], src[:, :], W=[("gvec", slot)])

    def load_w(Wd, ci0, n, c0, ncol=128):
        Wb, blk = wbmap[id(Wd)]
        assert ncol == blk and c0 % blk == 0
        s = C.nxt("w", NW)
        dma("sync", wbf[s][:, 0:n, 0:ncol], Wb[c0 // blk, :, ci0:ci0 + n, :], W=[("wbf", s)])
        return wbf[s], ("wbf", s)

    def linear(Wd, n_ci, cols, x_tile, x_keys, evac, ntok=TG, banks=None):
        for c0 in cols:
            ps, pk = C.next_ps("main", banks)
            blocks = []
            ci = 0
            while ci < n_ci:
                n = min(16, n_ci - ci)
                blocks.append((ci, n))
                ci += n
            first = True
            for (ci0, n) in blocks:
                wt, wk = load_w(Wd, ci0, n, c0)
                for j in range(n):
                    last = (ci0 + j == n_ci - 1)
                    xk = [("hn", ci0 + j)] if list(x_keys) == ["hn"] else list(x_keys)
                    add("tensor", lambda e, wt=wt, j=j, cj=ci0 + j, ps=ps, first=first, last=last:
                        e.matmul(ps[:, 0:ntok], lhsT=wt[:, j, :], rhs=x_tile[:, cj, 0:ntok], start=first, stop=last),
                        R=[wk] + xk, W=[pk])
                    first = False
            evac(c0, ps, pk)

    def rmsnorm(src_tile, src_keys, gslot, dst_tile, dst_key, ntok=TG, nchunk=16):
        ps, pk = C.next_ps("main")
        for c in range(nchunk):
            s = C.nxt("sq", 2)
            add("scalar", lambda e, s=s, c=c: e.activation(out=sq[s][:, 0:ntok], in_=src_tile[:, c, 0:ntok], func=AF.Square),
                R=list(src_keys), W=[("sq", s)])
            add("tensor", lambda e, s=s, c=c, ps=ps: e.matmul(ps[:, 0:ntok], lhsT=ones[:], rhs=sq[s][:, 0:ntok],
                                                               start=(c == 0), stop=(c == nchunk - 1)),
                R=[("sq", s), "ones"], W=[pk])
        add("scalar", lambda e, ps=ps: e.activation(out=rstd[:, 0:ntok], in_=ps[:, 0:ntok], func=AF.Sqrt,
                                                     scale=1.0 / D, bias=epsc[:, 0:1]), R=[pk, "epsc"], W=["rstd"])
        add("vector", lambda e: e.reciprocal(out=rstd[:, 0:ntok], in_=rstd[:, 0:ntok]), R=["rstd"], W=["rstd"])
        for c in range(nchunk):
            add("vector", lambda e, c=c: e.scalar_tensor_tensor(out=dst_tile[:, c, 0:ntok], in0=src_tile[:, c, 0:ntok],
                                                                 scalar=gvec[:, gslot, c:c + 1], in1=rstd[:, 0:ntok],
                                                                 op0=ALU.mult, op1=ALU.mult),
                R=list(src_keys) + ["rstd", ("gvec", gslot)], W=[(dst_key, c) if dst_key == "hn" else dst_key])

    def resid_evac(c0, ps, pk):
        c = c0 // 128
        add("vector", lambda e: e.tensor_tensor(out=hT[:, c, :], in0=hT[:, c, :], in1=ps[:, :], op=ALU.add),
            R=[pk, "hT"], W=["hT"])

    if prev is not None:
        load_g(0, g_xa)
        load_g(1, g_mem)
        load_g(2, g_ffn)
        memt = hT
        memn = hn
        kmT = C.sb("kmT", [128, 16, 256], BF16)
        vm = C.sb("vm", [128, 2, 2048], BF16)
        dma("sync", memt[:, :, 0:256], memT.rearrange("(c p) t -> p c t", p=128), W=["hT"])
        rmsnorm(memt, ["hT"], 1, memn, "hn", ntok=256)

        def k_evac(c0, ps, pk):
            c = c0 // 128
            add("vector", lambda e: e.tensor_copy(out=kmT[:, c, :], in_=ps[:, 0:256]), R=[pk], W=["kmT"])
        linear(wkv, 16, [c * 128 for c in range(16)], memn, ["hn"], k_evac, ntok=256)
        for c in range(16):
            wt, wk = load_w(wkv, 0, 16, D + c * 128)
            for kt in range(2):
                ps, pk = C.next_ps("main")
                for j in range(16):
                    add("tensor", lambda e, wt=wt, j=j, ps=ps, kt=kt: e.matmul(ps[:, 0:128], lhsT=memn[:, j, kt * 128:(kt + 1) * 128],
                                                                                rhs=wt[:, j, :], start=(j == 0), stop=(j == 15)),
                        R=[wk, ("hn", j)], W=[pk])
                add("vector", lambda e, ps=ps, kt=kt, c=c: e.tensor_copy(out=vm[:, kt, c * 128:(c + 1) * 128], in_=ps[:, 0:128]),
                    R=[pk], W=["vm"])
        if prev == "even":
            glub = C.sb("glub", [128, 8], F32)
            dma("sync", glub[:], glu_b[:, :], W=["glub"])
    if nxt is not None:
        load_g(3, g_mix)
    if final:
        load_g(4, g_fin)

    if prev is not None:
        xin = C.sb("xin", [128, 16, TG], BF16)
        et = [C.sb("et%d" % i, [128, TG], BF16) for i in range(2)]
        rl = C.sb("rl", [128, TG], F32)
        gT = C.sb("gT", [128, 44, TG], BF16)
        qx = gT[:, 24:40, :]
        sa = [C.sb("sa%d" % i, [128, TG], F32) for i in range(2)]
        if prev == "even":
            ysf = [C.sb("ysf%d" % i, [128, TG], F32) for i in range(2)]
            gg = gT[:, 16:24, :]
            sg = C.sb("sg", [128, TG], F32)
            g2 = C.sb("g2", [128, TG], F32)
    if nxt is not None:
        rc = C.sb("rc", [32, TG], F32)
        rs = C.sb("rs", [32, TG], F32)
        qhi = [C.sb("qhi%d" % i, [32, TG], BF16) for i in range(2)]
        qlo = [C.sb("qlo%d" % i, [32, TG], BF16) for i in range(2)]
        permt = C.sb("permt", [32, 32], BF16)
        dma("sync", permt[:], perm_in[:, :], W=["permt"])
        t1 = [C.sb("t1_%d" % i, [32, TG], F32) for i in range(1)]
        t2 = [C.sb("t2_%d" % i, [32, TG], F32) for i in range(1)]
        ob = [C.sb("ob%d" % i, [128, TG], BF16) for i in range(3)]
        nvc = 1024 if nxt == "even" else 2048
        if prev is not None:
            vtok = gT[:, 0:16, :].rearrange("p (a b) t -> p a (b t)", a=4)[:, :, 0:nvc]
        else:
            vtok = C.sb("vtok", [128, 4, nvc], BF16)[:, :, :]
    if final:
        of = [C.sb("of%d" % i, [128, TG], F32) for i in range(2)]

    for g in range(ngrp):
        t0 = g * TG
        dma("gpsimd", hT[:], hT_in[:, t0:t0 + TG].rearrange("(c p) t -> p c t", p=128), W=["hT"])
        if prev is not None:
            if prev == "even":
                dma("gpsimd", xin[:, 8:16, :], ydiffT[:, t0:t0 + TG].rearrange("(c p) t -> p c t", p=128), W=["xin_hi"])
                for c in range(8):
                    y = C.nxt("ysf", 2)
                    dma("gpsimd", ysf[y][:], ys5T[c * 128:(c + 1) * 128, t0:t0 + TG], W=[("ysf", y)])
                    add("scalar", lambda e, y=y: e.activation(out=g2[:], in_=ysf[y][:], func=AF.Square), R=[("ysf", y)], W=["g2"])
                    add("vector", lambda e: e.tensor_scalar(out=g2[:], in0=g2[:], scalar1=0.044715, scalar2=1.0, op0=ALU.mult, op1=ALU.add),
                        R=["g2"], W=["g2"])
                    add("vector", lambda e, y=y: e.tensor_tensor(out=g2[:], in0=g2[:], in1=ysf[y][:], op=ALU.mult), R=["g2", ("ysf", y)], W=["g2"])
                    add("scalar", lambda e: e.activation(out=g2[:], in_=g2[:], func=AF.Sigmoid, scale=1.5957691216057308), R=["g2"], W=["g2"])
                    add("vector", lambda e, y=y, c=c: e.tensor_tensor(out=gg[:, c, :], in0=g2[:], in1=ysf[y][:], op=ALU.mult),
                        R=["g2", ("ysf", y)], W=["gT"])

                def glu_evac(c0, ps, pk):
                    c = c0 // 128
                    add("scalar", lambda e: e.activation(out=sg[:], in_=ps[:, :], func=AF.Sigmoid, bias=glub[:, c:c + 1]),
                        R=[pk, "glub"], W=["sg"])
                    add("vector", lambda e: e.tensor_tensor(out=xin[:, c, :], in0=sg[:], in1=gg[:, c, :], op=ALU.mult),
                        R=["sg", "gT"], W=["xin_lo"])
                linear(glu_w, 8, [c * 128 for c in range(8)], gg, ["gT"], glu_evac)
                xkeys = ["xin_lo", "xin_hi"]
            else:
                dma("gpsimd", xin[:], oT_in[:, t0:t0 + TG].rearrange("(c p) t -> p c t", p=128), W=["xin_lo", "xin_hi"])
                xkeys = ["xin_lo", "xin_hi"]
            if "mix" not in SKIP:
                linear(w_out, 16, [c * 128 for c in range(16)], xin, xkeys, resid_evac)

            if "xa" in SKIP:
                continue
            rmsnorm(hT, ["hT"], 0, hn, "hn")

            def q_evac(c0, ps, pk):
                c = c0 // 128
                add("vector", lambda e: e.tensor_copy(out=qx[:, c, :], in_=ps[:, :]), R=[pk], W=["gT"])
            linear(wq, 16, [c * 128 for c in range(16)], hn, ["hn"], q_evac)
            for h in range(4):
                eks = []
                for kt in range(2):
                    ps, pk = C.next_ps("main")
                    for c in range(4):
                        add("tensor", lambda e, ps=ps, c=c, kt=kt, h=h: e.matmul(ps[:, :], lhsT=kmT[:, 4 * h + c, kt * 128:(kt + 1) * 128],
                                                                                   rhs=qx[:, 4 * h + c, :], start=(c == 0), stop=(c == 3)),
                            R=["kmT", "gT"], W=[pk])
                    add("scalar", lambda e, ps=ps, kt=kt: e.activation(out=et[kt][:], in_=ps[:, :], func=AF.Exp, scale=512 ** -0.5),
                        R=[pk], W=[("et", kt)])
                psl, pkl = C.next_ps("main")
                for kt in range(2):
                    add("tensor", lambda e, psl=psl, kt=kt: e.matmul(psl[:, :], lhsT=ones[:], rhs=et[kt][:], start=(kt == 0), stop=(kt == 1)),
                        R=[("et", kt), "ones"], W=[pkl])
                add("vector", lambda e, psl=psl: e.reciprocal(out=rl[:], in_=psl[:, :]), R=[pkl], W=["rl"])
                for dc in range(4):
                    ps, pk = C.next_ps("main")
                    for kt in range(2):
                        add("tensor", lambda e, ps=ps, kt=kt, dc=dc, h=h: e.matmul(ps[:, :], lhsT=vm[:, kt, h * 512 + dc * 128:h * 512 + (dc + 1) * 128],
                                                                                     rhs=et[kt][:], start=(kt == 0), stop=(kt == 1)),
                            R=["vm", ("et", kt)], W=[pk])
                    add("vector", lambda e, ps=ps, dc=dc, h=h: e.tensor_tensor(out=xin[:, 4 * h + dc, :], in0=ps[:, :], in1=rl[:], op=ALU.mult),
                        R=[pk, "rl"], W=["xin_lo", "xin_hi"])
            linear(wo, 16, [c * 128 for c in range(16)], xin, ["xin_lo", "xin_hi"], resid_evac)

            if "ffn" in SKIP:
                continue
            rmsnorm(hT, ["hT"], 2, hn, "hn")
            for j in range(44):
                holder = {}

                def a_evac(c0, ps, pk, holder=holder):
                    s = C.nxt("sa", 2)
                    holder["s"] = s
                    add("scalar", lambda e: e.activation(out=sa[s][:], in_=ps[:, :], func=AF.Silu), R=[pk], W=[("sa", s)])

                def b_evac(c0, ps, pk, holder=holder, j=j):
                    s = holder["s"]
                    add("vector", lambda e: e.tensor_tensor(out=gT[:, j, :], in0=ps[:, :], in1=sa[s][:], op=ALU.mult),
                        R=[pk, ("sa", s)], W=["gT"])
                linear(w13, 16, [j * 128], hn, ["hn"], a_evac)
                linear(w13, 16, [FFN + j * 128], hn, ["hn"], b_evac)
            linear(w2, 44, [c * 128 for c in range(16)], gT, ["gT"], resid_evac)

        if nxt is not None:
            dma("gpsimd", hT_out[:, t0:t0 + TG].rearrange("(c p) t -> p c t", p=128), hT[:], R=["hT"])
            rmsnorm(hT, ["hT"], 3, hn, "hn")
            dma("sync", rc[:], ropeC[:, t0:t0 + TG], W=["rc"])
            dma("sync", rs[:], ropeS[:, t0:t0 + TG], W=["rs"])

            def plain_out(dst):
                def ev(c0, ps, pk):
                    s = C.nxt("ob", 3)
                    add("vector", lambda e: e.tensor_copy(out=ob[s][:], in_=ps[:, :]), R=[pk], W=[("ob", s)])
                    dma("gpsimd", dst, ob[s][:], R=[("ob", s)])
                return ev

            def rope_A(col0, dst):
                hold = {}

                def ev_main(c0, ps, pk):
                    hold["ps"], hold["pk"] = ps, pk
                linear(w_in, 16, [col0], hn, ["hn"], ev_main)
                ps, pk = hold["ps"], hold["pk"]
                h = C.nxt("qh", 2)
                add("scalar", lambda e: e.activation(out=qhi[h][:], in_=ps[0:32, :], func=AF.Copy), R=[pk], W=[("qhi", h)])
                add("vector", lambda e: e.tensor_tensor(out=qlo[h][:], in0=ps[0:32, :], in1=qhi[h][:], op=ALU.subtract),
                    R=[pk, ("qhi", h)], W=[("qlo", h)])
                return ps, pk, h, dst

            def rope_B(ps, pk, h, dst):
                ps2, pk2 = C.next_ps("main")
                add("tensor", lambda e: e.matmul(ps2[0:32, :], lhsT=permt[:, :], rhs=qhi[h][:], start=True, stop=False),
                    R=["permt", ("qhi", h)], W=[pk2])
                add("tensor", lambda e: e.matmul(ps2[0:32, :], lhsT=permt[:, :], rhs=qlo[h][:], start=False, stop=True),
                    R=["permt", ("qlo", h)], W=[pk2])
                s = C.nxt("ob", 3)
                r = 0
                add("vector", lambda e: e.tensor_tensor(out=t1[r][:], in0=ps[0:32, :], in1=rc[:], op=ALU.mult), R=[pk, "rc"], W=[("t1", r)])
                add("vector", lambda e: e.tensor_tensor(out=t2[r][:], in0=ps2[0:32, :], in1=rs[:], op=ALU.mult), R=[pk2, "rs"], W=[("t2", r)])
                add("vector", lambda e: e.tensor_tensor(out=ob[s][0:32, :], in0=t1[r][:], in1=t2[r][:], op=ALU.add),
                    R=[("t1", r), ("t2", r)], W=[("ob", s)])
                add("vector", lambda e: e.tensor_copy(out=ob[s][32:64, :], in_=ps[32:64, :]), R=[pk], W=[("ob", s)])
                add("vector", lambda e: e.tensor_copy(out=ob[s][64:128, :], in_=ps[64:128, :]), R=[pk], W=[("ob", s)])
                dma("gpsimd", dst, ob[s][:], R=[("ob", s)])

            def rope_chain(items):
                pend = None
                for col0, dst in items:
                    cur = rope_A(col0, dst)
                    if pend is not None:
                        rope_B(*pend)
                    pend = cur
                rope_B(*pend)

            if nxt == "even":
                for c in range(8):
                    linear(w_in, 16, [c * 128], hn, ["hn"], plain_out(uT_out[c * 128:(c + 1) * 128, t0:t0 + TG]))
                rope_chain([(1024 + hc * 128, qT_out[hc, :, t0:t0 + TG]) for hc in range(8)] +
                           [(2048 + hc * 128, kT_out[hc, :, t0:t0 + TG]) for hc in range(8)])
                vcol0 = 3072
            else:
                rope_chain([(hc * 128, qT_out[hc, :, t0:t0 + TG]) for hc in range(16)] +
                           [(2048 + hc * 128, kT_out[hc, :, t0:t0 + TG]) for hc in range(16)])
                vcol0 = 4096
            for c in range(nvc // 128):
                wt, wk = load_w(w_in, 0, 16, vcol0 + c * 128)
                for tt in range(4):
                    ps, pk = C.next_ps("main")
                    for j in range(16):
                        add("tensor", lambda e, wt=wt, j=j, ps=ps, tt=tt: e.matmul(ps[:, 0:128], lhsT=hn[:, j, tt * 128:(tt + 1) * 128],
                                                                                    rhs=wt[:, j, :], start=(j == 0), stop=(j == 15)),
                            R=[wk, ("hn", j)], W=[pk])
                    add("vector", lambda e, ps=ps, tt=tt, c=c: e.tensor_copy(out=vtok[:, tt, c * 128:(c + 1) * 128], in_=ps[:, 0:128]),
                        R=[pk], W=["gT"])
            dma("gpsimd", v_out[t0:t0 + TG, :].rearrange("(t p) c -> p t c", p=128), vtok, R=["gT"])
        if final:
            rmsnorm(hT, ["hT"], 4, hn, "hn")
            for c in range(16):
                s = C.nxt("of", 2)
                add("vector", lambda e, c=c, s=s: e.scalar_tensor_tensor(out=of[s][:], in0=hT[:, c, :], scalar=gvec[:, 4, c:c + 1],
                                                                          in1=rstd[:], op0=ALU.mult, op1=ALU.mult),
                    R=["hT", "rstd", ("gvec", 4)], W=[("of", s)])
                dma("gpsimd", outT[c * 128:(c + 1) * 128, t0:t0 + TG], of[s][:], R=[("of", s)])
    return C.finish()


def build_H_odd(nc=None, io=None, bg=None, bg_total=0):
    C = Ctx(nc, io)
    nc, P = C.nc, C.P
    add, dma = P.add, P.dma
    qT_in = C.din("qT_in", [4, 128, SEQ], BF16)
    kT_in = C.din("kT_in", [4, 128, SEQ], BF16)
    v_in = C.din("v_in", [SEQ, 4, 128], BF16)
    mask_in = C.din("mask_in", [128, 20, 512], BF16)
    oT_out = C.dout("oT_out", [512, SEQ], BF16)
    C.alloc_psum(8)
    qT = C.sb("qT", [128, SEQ], BF16)
    kT = C.sb("kT", [128, SEQ], BF16)
    vv = C.sb("vv", [128, 64, 128], BF16)
    mk = C.sb("mk", [128, 20, 512], BF16)
    ones = C.sb("ones", [128, 128], BF16)
    et = [C.sb("et%d" % i, [128, 512], BF16) for i in range(3)]
    em = [C.sb("em%d" % i, [128, 512], BF16) for i in range(5)]
    rl = C.sb("rl", [128, 512], F32)
    ob = [C.sb("ob%d" % i, [128, 512], BF16) for i in range(2)]
    esum = [C.sb("esum%d" % i, [128, 512], F32) for i in range(2)]
    esb = [C.sb("esb%d" % i, [128, 512], BF16) for i in range(2)]
    add("gpsimd", lambda e: e.memset(ones[:], 1.0), W=["ones"])
    dma("sync", mk[:], mask_in[:, :, :], W=["mk"])
    scale = 128 ** -0.5
    LOOK = 3
    for hh in range(4):
        dma("sync", qT[:], qT_in[hh], W=["qT"])
        dma("sync", kT[:], kT_in[hh], W=["kT"])
        dma("gpsimd", vv[:], v_in[:, hh, :].rearrange("(t p) c -> p t c", p=128), W=["vv"])
        steps = []
        for qg in range(16):
            kts = [kt for kt in range(4 * qg - 8, 4 * qg + 12) if 0 <= kt < 64]
            for n, kt in enumerate(kts):
                steps.append((qg, kt, n, len(kts)))
        acc = {}

        def front(st, gi):
            qg, kt, n, nk = st
            idx = kt - (4 * qg - 8)
            ps, pk = C.next_ps("S", [0, 1, 2, 3])
            add("tensor", lambda e, ps=ps, kt=kt, qg=qg: e.matmul(ps[:, :], lhsT=kT[:, kt * 128:(kt + 1) * 128],
                                                                   rhs=qT[:, qg * 512:(qg + 1) * 512], start=True, stop=True),
                R=["kT", "qT"], W=[pk])
            s_ = C.nxt("et", 3)
            add("scalar", lambda e, ps=ps, s_=s_: e.activation(out=et[s_][:], in_=ps[:, :], func=AF.Exp, scale=scale),
                R=[pk], W=[("et", s_)])
            m = C.nxt("em", 5)
            me = ["vector", "gpsimd"][gi % 2]
            add(me, lambda e, s_=s_, m=m, idx=idx: e.tensor_tensor(out=em[m][:], in0=et[s_][:], in1=mk[:, idx, :], op=ALU.mult),
                R=[("et", s_), "mk"], W=[("em", m)])
            return m

        def back(st, m):
            qg, kt, n, nk = st
            if n == 0:
                acc["o"] = C.next_ps("O", [4, 5])
                acc["l"] = C.next_ps("L", [6, 7])
            (po, pko), (pl, pkl) = acc["o"], acc["l"]
            last = (n == nk - 1)
            add("tensor", lambda e, po=po, kt=kt, m=m, n=n, last=last: e.matmul(
                po[:, :], lhsT=vv[:, kt, :], rhs=em[m][:], start=(n == 0), stop=last), R=["vv", ("em", m)], W=[pko])
            add("tensor", lambda e, pl=pl, m=m, n=n, last=last: e.matmul(
                pl[:, :], lhsT=ones[:], rhs=em[m][:], start=(n == 0), stop=last), R=["ones", ("em", m)], W=[pkl])
            if last:
                add("vector", lambda e, pl=pl: e.reciprocal(out=rl[:], in_=pl[:, :]), R=[pkl], W=["rl"])
                o = C.nxt("ob", 2)
                add("vector", lambda e, po=po, o=o: e.tensor_tensor(out=ob[o][:], in0=po[:, :], in1=rl[:], op=ALU.mult),
                    R=[pko, "rl"], W=[("ob", o)])
                dma("gpsimd", oT_out[hh * 128:(hh + 1) * 128, qg * 512:(qg + 1) * 512], ob[o][:], R=[("ob", o)])

        queue = []
        bg_every = max(1, (4 * len(steps)) // max(1, bg_total)) if bg is not None else 0
        for gi, st in enumerate(steps):
            queue.append((st, front(st, gi)))
            if len(queue) > LOOK:
                back(*queue.pop(0))
            if bg is not None and gi % bg_every == 0:
                bg.emit(C, 1, cast_engines=("scalar", "vector"))
        while queue:
            back(*queue.pop(0))
    return C.finish()


def dilated_mask_table():
    kl = np.arange(128)[:, None, None]
    idx = np.arange(20)[None, :, None]
    ql = np.arange(512)[None, None, :]
    off = (idx - 8) * 128 + kl - ql
    m = np.zeros(off.shape, np.float32)
    for window, dil in ((128, 1), (512, 4), (2048, 16)):
        half = window // (2 * dil)
        m += ((off % dil == 0) & (np.abs(off) <= half * dil)).astype(np.float32)
    return m.astype(NPBF)


def build_H_even(lambda_init, do_s5=True, do_attn=True, nc=None, io=None, bg=None, bg_total=0):
    C = Ctx(nc, io)
    nc, P = C.nc, C.P
    add, dma = P.add, P.dma
    qT_in = C.din("qT_in", [2, 128, SEQ], BF16)
    kT_in = C.din("kT_in", [2, 128, SEQ], BF16)
    v_in = C.din("v_in", [SEQ, 256], BF16)
    lam_in = C.din("lam_in", [128, 4, 128])
    sg_in = C.din("sg_in", [128, 2])
    ydT_out = C.dout("ydT_out", [256, SEQ], BF16)
    uT_in = C.din("uT_in", [256, SEQ], BF16)
    s5v = C.din("s5v", [8, 2, 128, 3])
    s5B = C.din("s5B", [8, 2, 2, 32, 128], BF16)
    s5C = C.din("s5C", [8, 2, 2, 128, 32], BF16)
    s5d = C.din("s5d", [8, 32, 1])
    ysT_out = C.dout("ysT_out", [256, SEQ])
    C.alloc_psum(8)
    ones = C.sb("ones", [128, 128], BF16)
    epsc = C.sb("epsc", [128, 1], F32)
    add("gpsimd", lambda e: e.memset(ones[:], 1.0), W=["ones"])
    add("gpsimd", lambda e: e.memset(epsc[:], EPS), W=["epsc"])

    if do_attn:
        qTg = [C.sb("qTg%d" % i, [128, 2, 512], BF16) for i in range(2)]
        kT = C.sb("kT", [128, 2, SEQ], BF16)
        vv = C.sb("vv", [128, 64, 256], BF16)
        et = [C.sb("et%d" % i, [128, 512], BF16) for i in range(3)]
        rl = C.sb("rl", [128, 512], F32)
        oc = C.sb("oc", [128, 2, 2, 512], F32)
        od = C.sb("od", [128, 2, 512], F32)
        sq = C.sb("sq", [128, 512], BF16)
        rstd = C.sb("rstd", [128, 512], F32)
        ob = [C.sb("ob%d" % i, [128, 512], BF16) for i in range(2)]
        lamt = C.sb("lamt", [128, 4, 128], F32)
        lprod = C.sb("lprod", [128, 128], F32)
        lsc = C.sb("lsc", [128, 4], F32)
        sgt = C.sb("sgt", [128, 2], F32)
        for c in range(2):
            dma("sync", kT[:, c, :], kT_in[c], W=["kT"])
        dma("gpsimd", vv[:], v_in.rearrange("(t p) c -> p t c", p=128), W=["vv"])
        dma("sync", lamt[:], lam_in[:, :, :], W=["lamt"])
        dma("sync", sgt[:], sg_in[:, :], W=["sgt"])
        for i in range(2):
            add("vector", lambda e, i=i: e.tensor_tensor(out=lprod[:], in0=lamt[:, 2 * i, :], in1=lamt[:, 2 * i + 1, :], op=ALU.mult),
                R=["lamt"], W=["lprod"])
            add("vector", lambda e, i=i: e.reduce_sum(out=lsc[:, i:i + 1], in_=lprod[:], axis=mybir.AxisListType.X), R=["lprod"], W=["lsc"])
        add("scalar", lambda e: e.activation(out=lsc[:, 0:2], in_=lsc[:, 0:2], func=AF.Exp), R=["lsc"], W=["lsc"])
        add("vector", lambda e: e.tensor_tensor(out=lsc[:, 2:3], in0=lsc[:, 1:2], in1=lsc[:, 0:1], op=ALU.subtract), R=["lsc"], W=["lsc"])
        add("vector", lambda e: e.tensor_scalar(out=lsc[:, 2:3], in0=lsc[:, 2:3], scalar1=-float(lambda_init), scalar2=None, op0=ALU.add),
            R=["lsc"], W=["lsc"])
        add("vector", lambda e: e.tensor_scalar(out=sgt[:], in0=sgt[:], scalar1=float(1.0 - lambda_init), scalar2=None, op0=ALU.mult),
            R=["sgt"], W=["sgt"])
        scale = 128 ** -0.5

    def attn_gen():
        LOOK = 1
        SB, OB, LB = [0, 1], [2, 3], [4]
        steps = [(qg, comp, kt) for qg in range(16) for comp in range(2) for kt in range(64)]
        st8 = {}

        def front(stp):
            qg, comp, kt = stp
            if comp == 0 and kt == 0:
                qs = C.nxt("qTg", 2)
                st8["qs"] = qs
                dma("sync", qTg[qs][:], qT_in[:, :, qg * 512:(qg + 1) * 512].rearrange("c p t -> p c t"), W=[("qTg", qs)])
            qs = st8["qs"]
            ps, pk = C.next_ps("S", SB)
            add("tensor", lambda e, ps=ps, kt=kt, qs=qs, comp=comp: e.matmul(
                ps[:, :], lhsT=kT[:, comp, kt * 128:(kt + 1) * 128], rhs=qTg[qs][:, comp, :],
                start=True, stop=True), R=["kT", ("qTg", qs)], W=[pk])
            s_ = C.nxt("et", 3)
            add("scalar", lambda e, ps=ps, s_=s_: e.activation(out=et[s_][:], in_=ps[:, :], func=AF.Exp, scale=scale),
                R=[pk], W=[("et", s_)])
            return s_

        def back(stp, s_):
            qg, comp, kt = stp
            if kt == 0:
                st8["po"] = [C.next_ps("O", OB), C.next_ps("O", OB)]
                st8["pl"] = C.next_ps("L", LB)
            po = st8["po"]
            pl, pkl = st8["pl"]
            for dc in range(2):
                add("tensor", lambda e, dc=dc, kt=kt, s_=s_, pp=po[dc][0]: e.matmul(
                    pp[:, :], lhsT=vv[:, kt, dc * 128:(dc + 1) * 128], rhs=et[s_][:], start=(kt == 0), stop=(kt == 63)),
                    R=["vv", ("et", s_)], W=[po[dc][1]])
            add("tensor", lambda e, pl=pl, s_=s_, kt=kt: e.matmul(pl[:, :], lhsT=ones[:], rhs=et[s_][:], start=(kt == 0), stop=(kt == 63)),
                R=["ones", ("et", s_)], W=[pkl])
            if kt != 63:
                return
            add("vector", lambda e, pl=pl: e.reciprocal(out=rl[:], in_=pl[:, :]), R=[pkl], W=["rl"])
            for dc in range(2):
                add("vector", lambda e, dc=dc, comp=comp, pp=po[dc][0]: e.tensor_tensor(out=oc[:, comp, dc, :], in0=pp[:, :], in1=rl[:], op=ALU.mult),
                    R=[po[dc][1], "rl"], W=["oc"])
            if comp != 1:
                return
            psn, pkn = C.next_ps("N", [7])
            for dc in range(2):
                add("vector", lambda e, dc=dc: e.scalar_tensor_tensor(out=od[:, dc, :], in0=oc[:, 1, dc, :], scalar=lsc[:, 2:3], in1=oc[:, 0, dc, :],
                                                                       op0=ALU.mult, op1=ALU.add), R=["oc", "lsc"], W=["od"])
                add("scalar", lambda e, dc=dc: e.activation(out=sq[:], in_=od[:, dc, :], func=AF.Square), R=["od"], W=["sq"])
                add("tensor", lambda e, dc=dc, psn=psn: e.matmul(psn[:, :], lhsT=ones[:], rhs=sq[:], start=(dc == 0), stop=(dc == 1)),
                    R=["sq", "ones"], W=[pkn])
            add("scalar", lambda e, psn=psn: e.activation(out=rstd[:], in_=psn[:, :], func=AF.Sqrt, scale=1.0 / 256, bias=epsc[:, 0:1]),
                R=[pkn, "epsc"], W=["rstd"])
            add("vector", lambda e: e.reciprocal(out=rstd[:], in_=rstd[:]), R=["rstd"], W=["rstd"])
            for dc in range(2):
                o = C.nxt("ob", 2)
                add("vector", lambda e, dc=dc, o=o: e.scalar_tensor_tensor(out=ob[o][:], in0=od[:, dc, :], scalar=sgt[:, dc:dc + 1], in1=rstd[:],
                                                                            op0=ALU.mult, op1=ALU.mult), R=["od", "sgt", "rstd"], W=[("ob", o)])
                dma("gpsimd", ydT_out[dc * 128:(dc + 1) * 128, qg * 512:(qg + 1) * 512], ob[o][:], R=[("ob", o)])

        queue = []
        for stp in steps:
            queue.append((stp, front(stp)))
            if len(queue) > LOOK:
                back(*queue.pop(0))
                yield
        while queue:
            back(*queue.pop(0))
            yield

    def s5_gen():
        BL = 512
        NBLK = SEQ // BL
        ug = C.sb("ug", [32, SEQ], BF16)
        yf = C.sb("yf", [32, SEQ], F32)
        pv = C.sb("pv", [128, 3], F32)
        sc = C.sb("sc", [128, 24], F32)
        sci = C.sb("sci", [128, 2], I32)
        Tr = C.sb("Tr", [128, BL + 1], F32)
        Ti = C.sb("Ti", [128, BL + 1], F32)
        tmp = C.sb("tmp", [128, BL], F32)
        Ar = C.sb("Ar", [128, BL], F32)
        Ai = C.sb("Ai", [128, BL], F32)
        Bb = C.sb("Bb", [32, 2, 128], BF16)
        Cb = C.sb("Cb", [128, 2, 32], BF16)
        dsk = C.sb("dsk", [32, 1], F32)
        m = [C.sb("m%d" % i, [128, BL], F32) for i in range(4)]
        wr = C.sb("wr", [128, BL], F32)
        wi = C.sb("wi", [128, BL], F32)
        sr = C.sb("sr", [128, BL], F32)
        si = C.sb("si", [128, BL], F32)
        xr = [C.sb("xr%d" % i, [128, BL], BF16) for i in range(2)]
        xi = [C.sb("xi%d" % i, [128, BL], BF16) for i in range(2)]
        init = C.sb("init", [128, 2], F32)
        yo = [C.sb("yo%d" % i, [32, BL], F32) for i in range(2)]
        TWO_PI = 2.0 * math.pi
        V = "vector"
        G = "gpsimd"

        def col(i):
            return sc[:, i:i + 1]

        def cmul_scalar(out_r, out_i, in_r, in_i, s_r, s_i, keys_in, keys_out, n):
            add(V, lambda e: e.tensor_scalar(out=tmp[:, 0:n], in0=in_i, scalar1=s_i, scalar2=None, op0=ALU.mult), R=keys_in, W=["tmp"])
            add(V, lambda e: e.scalar_tensor_tensor(out=out_r, in0=in_r, scalar=s_r, in1=tmp[:, 0:n], op0=ALU.mult, op1=ALU.subtract),
                R=keys_in + ["tmp"], W=keys_out)
            add(V, lambda e: e.tensor_scalar(out=tmp[:, 0:n], in0=in_i, scalar1=s_r, scalar2=None, op0=ALU.mult), R=keys_in, W=["tmp"])
            add(V, lambda e: e.scalar_tensor_tensor(out=out_i, in0=in_r, scalar=s_i, in1=tmp[:, 0:n], op0=ALU.mult, op1=ALU.add),
                R=keys_in + ["tmp"], W=keys_out)

        for gp in range(8):
            dma("sync", ug[:], uT_in[gp * 32:(gp + 1) * 32, :], W=["ug"])
            dma("sync", dsk[:], s5d[gp], W=["dsk"])
            for dr in range(2):
                dma("sync", pv[:], s5v[gp, dr], W=["pv"])
                dma("sync", Bb[:, 0, :], s5B[gp, dr, 0], W=["Bb"])
                dma("sync", Bb[:, 1, :], s5B[gp, dr, 1], W=["Bb"])
                dma("sync", Cb[:, 0, :], s5C[gp, dr, 0], W=["Cb"])
                dma("sync", Cb[:, 1, :], s5C[gp, dr, 1], W=["Cb"])
                K = ["sc"]
                add("scalar", lambda e: e.activation(out=col(0), in_=pv[:, 2:3], func=AF.Exp), R=["pv"], W=K)
                add("scalar", lambda e: e.activation(out=col(1), in_=pv[:, 0:1], func=AF.Exp, scale=col(0)), R=["pv"] + K, W=K)
                add(V, lambda e: e.tensor_tensor(out=col(2), in0=pv[:, 1:2], in1=col(0), op=ALU.mult), R=["pv"] + K, W=K)
                add(V, lambda e: e.tensor_scalar(out=col(3), in0=col(2), scalar1=0.5 * math.pi, scalar2=None, op0=ALU.add), R=K, W=K)
                add(V, lambda e: e.tensor_scalar(out=sc[:, 4:6], in0=sc[:, 2:4], scalar1=1.0 / TWO_PI, scalar2=None, op0=ALU.mult), R=K, W=K)
                add(V, lambda e: e.tensor_copy(out=sci[:, 0:2], in_=sc[:, 4:6]), R=K, W=["sci"])
                add(V, lambda e: e.tensor_copy(out=sc[:, 4:6], in_=sci[:, 0:2]), R=["sci"], W=K)
                add(V, lambda e: e.scalar_tensor_tensor(out=sc[:, 6:8], in0=sc[:, 4:6], scalar=-TWO_PI, in1=sc[:, 2:4], op0=ALU.mult, op1=ALU.add),
                    R=K, W=K)
                add(V, lambda e: e.tensor_scalar(out=sc[:, 6:8], in0=sc[:, 6:8], scalar1=-math.pi, scalar2=math.pi, op0=ALU.max, op1=ALU.min), R=K, W=K)
                add("scalar", lambda e: e.activation(out=sc[:, 8:10], in_=sc[:, 6:8], func=AF.Sin), R=K, W=K)
                add(G, lambda e: e.memset(Tr[:, 0:1], 1.0), W=["T"])
                add(G, lambda e: e.memset(Ti[:, 0:1], 0.0), W=["T"])
                add(V, lambda e: e.tensor_copy(out=Tr[:, 1:2], in_=col(9)), R=K, W=["T"])
                add(V, lambda e: e.tensor_copy(out=Ti[:, 1:2], in_=col(8)), R=K, W=["T"])
                n = 1
                while n < BL:
                    cmul_scalar(Tr[:, n + 1:2 * n + 1], Ti[:, n + 1:2 * n + 1], Tr[:, 1:n + 1], Ti[:, 1:n + 1],
                                Tr[:, n:n + 1], Ti[:, n:n + 1], ["T"], ["T"], n)
                    n *= 2
                add(V, lambda e: e.tensor_tensor(out=col(10), in0=col(1), in1=col(9), op=ALU.mult), R=K, W=K)
                add(V, lambda e: e.tensor_tensor(out=col(11), in0=col(1), in1=col(8), op=ALU.mult), R=K, W=K)
                add(V, lambda e: e.tensor_scalar(out=col(10), in0=col(10), scalar1=-1.0, scalar2=None, op0=ALU.add), R=K, W=K)
                add(V, lambda e: e.tensor_tensor(out=col(12), in0=col(11), in1=pv[:, 1:2], op=ALU.mult), R=K + ["pv"], W=K)
                add(V, lambda e: e.scalar_tensor_tensor(out=col(13), in0=col(10), scalar=pv[:, 0:1], in1=col(12), op0=ALU.mult, op1=ALU.add),
                    R=K + ["pv"], W=K)
                add(V, lambda e: e.tensor_tensor(out=col(12), in0=col(10), in1=pv[:, 1:2], op=ALU.mult), R=K + ["pv"], W=K)
                add(V, lambda e: e.scalar_tensor_tensor(out=col(14), in0=col(11), scalar=pv[:, 0:1], in1=col(12), op0=ALU.mult, op1=ALU.subtract),
                    R=K + ["pv"], W=K)
                add(V, lambda e: e.tensor_tensor(out=col(15), in0=pv[:, 0:1], in1=pv[:, 0:1], op=ALU.mult), R=["pv"], W=K)
                add(V, lambda e: e.scalar_tensor_tensor(out=col(15), in0=pv[:, 1:2], scalar=pv[:, 1:2], in1=col(15), op0=ALU.mult, op1=ALU.add),
                    R=K + ["pv"], W=K)
                add(V, lambda e: e.reciprocal(out=col(15), in_=col(15)), R=K, W=K)
                add(V, lambda e: e.tensor_tensor(out=col(16), in0=col(13), in1=col(15), op=ALU.mult), R=K, W=K)
                add(V, lambda e: e.tensor_tensor(out=col(17), in0=col(14), in1=col(15), op=ALU.mult), R=K, W=K)
                add(V, lambda e: e.tensor_scalar(out=col(18), in0=col(17), scalar1=-1.0, scalar2=None, op0=ALU.mult), R=K, W=K)
                add(V, lambda e: e.tensor_scalar(out=tmp[:], in0=Ti[:, 0:BL], scalar1=col(17), scalar2=None, op0=ALU.mult), R=["T"] + K, W=["tmp"])
                add(V, lambda e: e.scalar_tensor_tensor(out=Ar[:], in0=Tr[:, 0:BL], scalar=col(16), in1=tmp[:], op0=ALU.mult, op1=ALU.add),
                    R=["T", "tmp"] + K, W=["A"])
                add(V, lambda e: e.tensor_scalar(out=tmp[:], in0=Ti[:, 0:BL], scalar1=col(16), scalar2=None, op0=ALU.mult), R=["T"] + K, W=["tmp"])
                add(V, lambda e: e.scalar_tensor_tensor(out=Ai[:], in0=Tr[:, 0:BL], scalar=col(17), in1=tmp[:], op0=ALU.mult, op1=ALU.subtract),
                    R=["T", "tmp"] + K, W=["A"])
                add(G, lambda e: e.memset(init[:], 0.0), W=["init"])
                def emit_bu(bi):
                    blk = bi if dr == 0 else NBLK - 1 - bi
                    t0 = blk * BL
                    pa, pka = C.next_ps("S5a", [5])
                    pb, pkb = C.next_ps("S5b", [6])
                    add("tensor", lambda e, pa=pa, t0=t0: e.matmul(pa[:, :], lhsT=Bb[:, 0, :], rhs=ug[:, t0:t0 + BL], start=True, stop=True),
                        R=["Bb", "ug"], W=[pka])
                    add("tensor", lambda e, pb=pb, t0=t0: e.matmul(pb[:, :], lhsT=Bb[:, 1, :], rhs=ug[:, t0:t0 + BL], start=True, stop=True),
                        R=["Bb", "ug"], W=[pkb])
                    return pa, pka, pb, pkb
                nxt_bu = emit_bu(0)
                for bi in range(NBLK):
                    blk = bi if dr == 0 else NBLK - 1 - bi
                    t0 = blk * BL
                    pa, pka, pb, pkb = nxt_bu
                    if dr == 0:
                        bur, bui = pa[:, :], pb[:, :]
                    else:
                        bur, bui = pa[:, ::-1], pb[:, ::-1]
                    add(V, lambda e, bur=bur: e.tensor_tensor(out=m[0][:], in0=bur, in1=Ar[:], op=ALU.mult), R=[pka, "A"], W=["m0"])
                    add(V, lambda e, bui=bui: e.tensor_tensor(out=m[1][:], in0=bui, in1=Ai[:], op=ALU.mult), R=[pkb, "A"], W=["m1"])
                    add(V, lambda e, bui=bui: e.tensor_tensor(out=m[2][:], in0=bui, in1=Ar[:], op=ALU.mult), R=[pkb, "A"], W=["m2"])
                    add(V, lambda e, bur=bur: e.tensor_tensor(out=m[3][:], in0=bur, in1=Ai[:], op=ALU.mult), R=[pka, "A"], W=["m3"])
                    if bi + 1 < NBLK:
                        nxt_bu = emit_bu(bi + 1)
                    add(G, lambda e: e.tensor_tensor(out=wr[:], in0=m[0][:], in1=m[1][:], op=ALU.subtract), R=["m0", "m1"], W=["wr"])
                    add(G, lambda e: e.tensor_tensor(out=wi[:], in0=m[2][:], in1=m[3][:], op=ALU.add), R=["m2", "m3"], W=["wi"])
                    add(V, lambda e: e.tensor_tensor_scan(out=sr[:], data0=col(1).to_broadcast([128, BL]), data1=wr[:], initial=init[:, 0:1],
                                                          op0=ALU.mult, op1=ALU.add), R=["wr", "init"] + K, W=["sr"])
                    add(V, lambda e: e.tensor_tensor_scan(out=si[:], data0=col(1).to_broadcast([128, BL]), data1=wi[:], initial=init[:, 1:2],
                                                          op0=ALU.mult, op1=ALU.add), R=["wi", "init"] + K, W=["si"])
                    cmul_scalar(init[:, 0:1], init[:, 1:2], sr[:, BL - 1:BL], si[:, BL - 1:BL], Tr[:, BL:BL + 1], Ti[:, BL:BL + 1],
                                ["sr", "si", "T"], ["init"], 1)
                    add(V, lambda e: e.tensor_tensor(out=m[0][:], in0=sr[:], in1=Tr[:, 0:BL], op=ALU.mult), R=["sr", "T"], W=["m0"])
                    add(V, lambda e: e.tensor_tensor(out=m[1][:], in0=si[:], in1=Ti[:, 0:BL], op=ALU.mult), R=["si", "T"], W=["m1"])
                    add(V, lambda e: e.tensor_tensor(out=m[2][:], in0=si[:], in1=Tr[:, 0:BL], op=ALU.mult), R=["si", "T"], W=["m2"])
                    add(V, lambda e: e.tensor_tensor(out=m[3][:], in0=sr[:], in1=Ti[:, 0:BL], op=ALU.mult), R=["sr", "T"], W=["m3"])
                    xs = C.nxt("xs", 2)
                    if dr == 0:
                        a0, a1, a2, a3 = m[0][:], m[1][:], m[2][:], m[3][:]
                    else:
                        a0, a1, a2, a3 = m[0][:, ::-1], m[1][:, ::-1], m[2][:, ::-1], m[3][:, ::-1]
                    add(V, lambda e, xs=xs, a0=a0, a1=a1: e.tensor_tensor(out=xr[xs][:], in0=a0, in1=a1, op=ALU.subtract),
                        R=["m0", "m1"], W=[("xr", xs)])
                    add(V, lambda e, xs=xs, a2=a2, a3=a3: e.scalar_tensor_tensor(out=xi[xs][:], in0=a2, scalar=-1.0, in1=a3, op0=ALU.mult, op1=ALU.subtract),
                        R=["m2", "m3"], W=[("xi", xs)])
                    py, pky = C.next_ps("Y", [7])
                    add("tensor", lambda e, py=py, xs=xs: e.matmul(py[0:32, :], lhsT=Cb[:, 0, :], rhs=xr[xs][:], start=True, stop=False),
                        R=["Cb", ("xr", xs)], W=[pky])
                    add("tensor", lambda e, py=py, xs=xs: e.matmul(py[0:32, :], lhsT=Cb[:, 1, :], rhs=xi[xs][:], start=False, stop=True),
                        R=["Cb", ("xi", xs)], W=[pky])
                    if dr == 0:
                        add(V, lambda e, py=py, t0=t0: e.tensor_copy(out=yf[:, t0:t0 + BL], in_=py[0:32, :]), R=[pky], W=["yf"])
                    else:
                        y = C.nxt("yo", 2)
                        add(V, lambda e, py=py, t0=t0: e.tensor_tensor(out=yf[:, t0:t0 + BL], in0=py[0:32, :], in1=yf[:, t0:t0 + BL], op=ALU.add),
                            R=[pky, "yf"], W=["yf"])
                        add(V, lambda e, y=y, t0=t0: e.scalar_tensor_tensor(out=yo[y][:], in0=ug[:, t0:t0 + BL], scalar=dsk[:, 0:1], in1=yf[:, t0:t0 + BL],
                                                                             op0=ALU.mult, op1=ALU.add), R=["ug", "dsk", "yf"], W=[("yo", y)])
                        dma("gpsimd", ysT_out[gp * 32:(gp + 1) * 32, t0:t0 + BL], yo[y][:], R=[("yo", y)])
                    yield
    gens = []
    if do_attn:
        gens.append((attn_gen(), 8))
    if do_s5:
        gens.append((s5_gen(), 1))
    alive = [True] * len(gens)
    it = 0
    bg_every = max(1, 256 // max(1, bg_total)) if bg is not None else 0
    bg_per = max(1, -(-bg_total // 256)) if bg is not None else 0
    while any(alive):
        it += 1
        if bg is not None and it % bg_every == 0:
            bg.emit(C, bg_per, cast_engines=("scalar",))
        for gi, (g, reps) in enumerate(gens):
            if not alive[gi]:
                continue
            for _ in range(reps):
                try:
                    next(g)
                except StopIteration:
                    alive[gi] = False
                    break
    return C.finish()


def pack_s5(lam_re, lam_im, log_step, b_re, b_im, c_re, c_im, d_skip, g0, ng):
    ngp = ng // 2
    s5v = np.zeros((ngp, 2, 128, 3), np.float32)
    s5B = np.zeros((ngp, 2, 2, 32, 128), NPBF)
    s5C = np.zeros((ngp, 2, 2, 128, 32), NPBF)
    s5d = np.zeros((ngp, 32, 1), np.float32)
    for gp in range(ngp):
        for j in range(2):
            g = g0 + 2 * gp + j
            s5d[gp, j * 16:(j + 1) * 16, 0] = d_skip[g * 16:(g + 1) * 16]
            for dr in range(2):
                s5v[gp, dr, j * 64:(j + 1) * 64, 0] = lam_re[dr, g]
                s5v[gp, dr, j * 64:(j + 1) * 64, 1] = lam_im[dr, g]
                s5v[gp, dr, j * 64:(j + 1) * 64, 2] = log_step[dr, g]
                s5B[gp, dr, 0, j * 16:(j + 1) * 16, j * 64:(j + 1) * 64] = b_re[dr, g].T.astype(NPBF)
                s5B[gp, dr, 1, j * 16:(j + 1) * 16, j * 64:(j + 1) * 64] = b_im[dr, g].T.astype(NPBF)
                s5C[gp, dr, 0, j * 64:(j + 1) * 64, j * 16:(j + 1) * 16] = c_re[dr, g].T.astype(NPBF)
                s5C[gp, dr, 1, j * 64:(j + 1) * 64, j * 16:(j + 1) * 16] = c_im[dr, g].T.astype(NPBF)
    return {"s5v": s5v, "s5B": s5B, "s5C": s5C, "s5d": s5d}


def _g16(g):
    return np.ascontiguousarray(np.asarray(g, np.float32).reshape(16, 128).T)


def rope_tables_host(p0, n):
    f32 = np.float32
    pos = np.arange(p0, p0 + n, dtype=f32)
    inv = (f32(500000.0) ** (-(np.arange(0, 32, 2, dtype=f32)) / f32(32))).astype(f32)
    ang = (pos[:, None] * inv[None, :]).astype(f32)
    cos, sin = np.cos(ang).astype(f32), np.sin(ang).astype(f32)
    C32 = np.concatenate([cos.T, cos.T], axis=0)
    S32 = np.concatenate([-sin.T, sin.T], axis=0)
    return np.ascontiguousarray(C32), np.ascontiguousarray(S32)


def perm_table():
    p = np.zeros((32, 32), np.float32)
    for m in range(32):
        p[(m + 16) % 32, m] = 1.0
    return p.astype(NPBF)


def swap_cols(W, col0s):
    parts = []
    for c0 in col0s:
        parts.append(W[:, c0 + 16:c0 + 32])
        parts.append(W[:, c0:c0 + 16])
    return np.ascontiguousarray(np.concatenate(parts, axis=1))


_NC_CACHE = {}


def _get_nc(key, fn):
    if key not in _NC_CACHE:
        _NC_CACHE[key] = fn()
    return _NC_CACHE[key]


def _run(nc, in_maps):
    res = run_bass_kernel_spmd(nc, in_maps, core_ids=list(range(NCORE)))
    return res.results


def kernel_unfused(x, mem, norm_mix_g, norm_xa_g, norm_mem_g, xa_wq, xa_wkv, xa_wo, norm_ffn_g,
           ffn_w13, ffn_w2, ab_w_in, ab_w_out, s5_lambda_re, s5_lambda_im, s5_log_step,
           s5_b_re, s5_b_im, s5_c_re, s5_c_im, s5_d, s5_glu_w, s5_glu_b, diff_lambda,
           diff_subln_g, c_w_qkv, c_w_out, final_norm_g):
    A = lambda a: np.asarray(a)
    x = A(x).astype(np.float32, copy=False)
    mem = A(mem).astype(np.float32, copy=False)
    depth = 4
    cores = [(c // 4, (c % 4) * TOK) for c in range(NCORE)]
    hT = [np.ascontiguousarray(x[b, t0:t0 + TOK, :].T) for (b, t0) in cores]
    memT = [np.ascontiguousarray(mem[b].T) for b in range(NB)]
    ropes = [rope_tables_host(t0, TOK) for (b, t0) in cores]
    mask_tab = dilated_mask_table()

    def nxt_inputs(l):
        i = l // 2
        d = {"g_mix": _g16(A(norm_mix_g)[l])}
        if l % 2 == 0:
            W = A(ab_w_in)[i]
            d["w_in"] = W
            d["w_sw"] = swap_cols(W, [1024 + hc * 128 for hc in range(8)] + [2048 + hc * 128 for hc in range(8)])
        else:
            W = A(c_w_qkv)[i]
            d["w_in"] = W
            d["w_sw"] = swap_cols(W, [hc * 128 for hc in range(16)] + [2048 + hc * 128 for hc in range(16)])
        return d

    def prev_inputs(l):
        i = l // 2
        d = {"g_xa": _g16(A(norm_xa_g)[l]), "g_mem": _g16(A(norm_mem_g)[l]), "g_ffn": _g16(A(norm_ffn_g)[l]),
             "wq": A(xa_wq)[l], "wkv": A(xa_wkv)[l], "wo": A(xa_wo)[l], "w13": A(ffn_w13)[l], "w2": A(ffn_w2)[l]}
        if l % 2 == 0:
            d["w_out"] = A(ab_w_out)[i]
            d["glu_w"] = A(s5_glu_w)[i]
            d["glu_b"] = np.ascontiguousarray(A(s5_glu_b)[i].reshape(8, 128).T)
        else:
            d["w_out"] = A(c_w_out)[i]
        return d

    typ = lambda l: "even" if l % 2 == 0 else "odd"
    mix = None
    out = None
    for l in range(depth + 1):
        prev = typ(l - 1) if l > 0 else None
        nxt = typ(l) if l < depth else None
        final = (l == depth)
        nc = _get_nc(("T", prev, nxt, final), lambda: build_T(prev, nxt, final))
        shared = {}
        if prev is not None:
            shared.update(prev_inputs(l - 1))
        if nxt is not None:
            shared.update(nxt_inputs(l))
        if final:
            shared["g_fin"] = _g16(A(final_norm_g))
        in_maps = []
        for c, (b, t0) in enumerate(cores):
            m = dict(shared)
            m["hT_in"] = hT[c]
            if prev is not None:
                m["memT"] = memT[b]
                m.update(mix[c])
            if nxt is not None:
                m["ropeC"], m["ropeS"] = ropes[c]
                m["perm_in"] = perm_table()
            in_maps.append(m)
        res = _run(nc, in_maps)
        if final:
            out = np.empty((NB, SEQ, D), np.float32)
            for c, (b, t0) in enumerate(cores):
                out[b, t0:t0 + TOK, :] = res[c]["outT"].T
            break
        hT = [res[c]["hT_out"] for c in range(NCORE)]
        i = l // 2
        cat = lambda name, b, sl: np.ascontiguousarray(np.concatenate([res[4 * b + r][name][sl] for r in range(4)], axis=-1))
        if nxt == "even":
            lambda_init = 0.8 - 0.6 * math.exp(-0.3 * l)
            nch = _get_nc(("He", l), lambda: build_H_even(lambda_init))
            in_maps = []
            lam_rep = np.ascontiguousarray(np.broadcast_to(A(diff_lambda)[i].astype(np.float32), (128, 4, 128)))
            sgi = np.ascontiguousarray(A(diff_subln_g)[i].astype(np.float32).reshape(2, 128).T)
            for c in range(NCORE):
                b, hd = c // 4, c % 4
                m = {"qT_in": cat("qT_out", b, slice(2 * hd, 2 * hd + 2)),
                     "kT_in": cat("kT_out", b, slice(2 * hd, 2 * hd + 2)),
                     "v_in": np.ascontiguousarray(np.concatenate([res[4 * b + r]["v_out"][:, hd * 256:(hd + 1) * 256] for r in range(4)], axis=0)),
                     "uT_in": cat("uT_out", b, slice(hd * 256, (hd + 1) * 256)),
                     "lam_in": lam_rep, "sg_in": sgi}
                m.update(pack_s5(A(s5_lambda_re)[i], A(s5_lambda_im)[i], A(s5_log_step)[i], A(s5_b_re)[i], A(s5_b_im)[i],
                                 A(s5_c_re)[i], A(s5_c_im)[i], A(s5_d)[i], 16 * hd, 16))
                in_maps.append(m)
            rh = _run(nch, in_maps)
            mix = []
            for c, (b, t0) in enumerate(cores):
                mix.append({"ys5T": np.ascontiguousarray(np.concatenate([rh[4 * b + hd]["ysT_out"][:, t0:t0 + TOK] for hd in range(4)], axis=0)),
                            "ydiffT": np.ascontiguousarray(np.concatenate([rh[4 * b + hd]["ydT_out"][:, t0:t0 + TOK] for hd in range(4)], axis=0))})
        else:
            nch = _get_nc(("Ho",), build_H_odd)
            in_maps = []
            for c in range(NCORE):
                b, hq = c // 4, c % 4
                vv = np.concatenate([res[4 * b + r]["v_out"][:, hq * 512:(hq + 1) * 512] for r in range(4)], axis=0)
                in_maps.append({"qT_in": cat("qT_out", b, slice(4 * hq, 4 * hq + 4)),
                                "kT_in": cat("kT_out", b, slice(4 * hq, 4 * hq + 4)),
                                "v_in": np.ascontiguousarray(vv.reshape(SEQ, 4, 128)),
                                "mask_in": mask_tab})
            rh = _run(nch, in_maps)
            mix = []
            for c, (b, t0) in enumerate(cores):
                mix.append({"oT_in": np.ascontiguousarray(np.concatenate([rh[4 * b + hq]["oT_out"][:, t0:t0 + TOK] for hq in range(4)], axis=0))})
    return out


def build_fused(depth=4):
    nc = bass.Bass("TRN2", target_bir_lowering=False)

    def ext(name, shape, dt=F32):
        return nc.dram_tensor(name, list(shape), dt, kind="ExternalInput").ap()

    def internal(name, shape, dt=F32):
        return nc.dram_tensor(name, list(shape), dt).ap()

    x_hT = ext("x_hT", [D, SEQ])
    memT = ext("memT", [D, 256])
    ropeC = ext("ropeC", [32, SEQ])
    ropeS = ext("ropeS", [32, SEQ])
    mask_in = ext("mask_in", [128, 20, 512], BF16)
    perm_in = ext("perm_in", [32, 32], BF16)
    g_fin = ext("g_fin", [128, 16])
    outT = nc.dram_tensor("outT", [D, SEQ], F32, kind="ExternalOutput").ap()
    L = []
    for l in range(depth):
        d = {}
        for nm in ("g_mix", "g_xa", "g_mem", "g_ffn"):
            d[nm] = ext("%s_%d" % (nm, l), [128, 16])
        d["wq"] = ext("wq_%d" % l, [D, D])
        d["wkv"] = ext("wkv_%d" % l, [D, 2 * D])
        d["wo"] = ext("wo_%d" % l, [D, D])
        d["w13"] = ext("w13_%d" % l, [D, 2 * FFN])
        d["w2"] = ext("w2_%d" % l, [FFN, D])
        d["w_out"] = ext("w_out_%d" % l, [D, D])
        if l % 2 == 0:
            d["w_in"] = ext("w_in_%d" % l, [D, 4096])
            d["w_sw"] = ext("w_sw_%d" % l, [D, 512])
            d["glu_w"] = ext("glu_w_%d" % l, [1024, 1024])
            d["glu_b"] = ext("glu_b_%d" % l, [128, 8])
            d["lam_in"] = ext("lam_in_%d" % l, [128, 4, 128])
            d["sg_in"] = ext("sg_in_%d" % l, [128, 2])
            d["s5v"] = ext("s5v_%d" % l, [32, 2, 128, 3])
            d["s5B"] = ext("s5B_%d" % l, [32, 2, 2, 32, 128], BF16)
            d["s5C"] = ext("s5C_%d" % l, [32, 2, 2, 128, 32], BF16)
            d["s5d"] = ext("s5d_%d" % l, [32, 32, 1])
        else:
            d["w_in"] = ext("w_in_%d" % l, [D, 6144])
            d["w_sw"] = ext("w_sw_%d" % l, [D, 1024])
        L.append(d)
    hT = internal("hT", [D, SEQ])
    uT = internal("uT", [1024, SEQ], BF16)
    qTe = internal("qTe", [8, 128, SEQ], BF16)
    kTe = internal("kTe", [8, 128, SEQ], BF16)
    ve = internal("ve", [SEQ, 1024], BF16)
    qTo = internal("qTo", [16, 128, SEQ], BF16)
    kTo = internal("kTo", [16, 128, SEQ], BF16)
    vo = internal("vo", [SEQ, 2048], BF16)
    ys5T = internal("ys5T", [1024, SEQ])
    ydiffT = internal("ydiffT", [1024, SEQ], BF16)
    oT = internal("oT", [2048, SEQ], BF16)
    typ = lambda l: "even" if l % 2 == 0 else "odd"
    bgs = {}

    def launch_wio(l):
        prev = typ(l - 1) if l > 0 else None
        nxt = typ(l) if l < depth else None
        wio = {}
        if prev is not None:
            wio.update({nm: L[l - 1][nm] for nm in ("w_out", "wq", "wkv", "wo", "w13", "w2")})
            if prev == "even":
                wio["glu_w"] = L[l - 1]["glu_w"]
        if nxt is not None:
            wio.update({"w_in": L[l]["w_in"], "w_sw": L[l]["w_sw"]})
        return t_weight_specs(prev, nxt, wio)

    for l in range(depth + 1):
        prev = typ(l - 1) if l > 0 else None
        nxt = typ(l) if l < depth else None
        final = (l == depth)
        wio = {}
        if prev is not None:
            wio.update({nm: L[l - 1][nm] for nm in ("w_out", "wq", "wkv", "wo", "w13", "w2")})
            if prev == "even":
                wio["glu_w"] = L[l - 1]["glu_w"]
        if nxt is not None:
            wio.update({"w_in": L[l]["w_in"], "w_sw": L[l]["w_sw"]})
        if bgs.get(l) is None:
            wb = emit_W(nc, t_weight_specs(prev, nxt, wio))
        else:
            bgs[l].flush()
            wb = bgs[l].out
        for r in range(SEQ // TOK):
            ts = slice(r * TOK, (r + 1) * TOK)
            io = {"hT_in": (x_hT if l == 0 else hT)[:, ts]}
            if prev is not None:
                pl = L[l - 1]
                for nm in ("w_out", "g_xa", "g_mem", "wq", "wkv", "wo", "g_ffn", "w13", "w2"):
                    io[nm] = pl[nm]
                io["memT"] = memT
                if prev == "even":
                    io["ys5T"] = ys5T[:, ts]
                    io["ydiffT"] = ydiffT[:, ts]
                    io["glu_w"] = pl["glu_w"]
                    io["glu_b"] = pl["glu_b"]
                else:
                    io["oT_in"] = oT[:, ts]
            if nxt is not None:
                nl = L[l]
                io["g_mix"] = nl["g_mix"]
                io["ropeC"] = ropeC[:, ts]
                io["ropeS"] = ropeS[:, ts]
                io["perm_in"] = perm_in
                io["w_in"] = nl["w_in"]
                io["w_sw"] = nl["w_sw"]
                io["hT_out"] = hT[:, ts]
                if nxt == "even":
                    io["uT_out"] = uT[:, ts]
                    io["qT_out"] = qTe[:, :, ts]
                    io["kT_out"] = kTe[:, :, ts]
                    io["v_out"] = ve[ts, :]
                else:
                    io["qT_out"] = qTo[:, :, ts]
                    io["kT_out"] = kTo[:, :, ts]
                    io["v_out"] = vo[ts, :]
            if final:
                io["g_fin"] = g_fin
                io["outT"] = outT[:, ts]
            build_T(prev, nxt, final, nc=nc, io=io, wb=wb)
        if final:
            break
        bg = BgW(nc, launch_wio(l + 1))
        bgs[l + 1] = bg
        bg_total = -(-len(bg.jobs) // 4)
        if nxt == "even":
            lambda_init = 0.8 - 0.6 * math.exp(-0.3 * l)
            nl = L[l]
            for hd in range(4):
                cs = slice(hd * 256, (hd + 1) * 256)
                io = {"qT_in": qTe[2 * hd:2 * hd + 2], "kT_in": kTe[2 * hd:2 * hd + 2], "v_in": ve[:, cs],
                      "lam_in": nl["lam_in"], "sg_in": nl["sg_in"], "ydT_out": ydiffT[cs, :], "uT_in": uT[cs, :],
                      "s5v": nl["s5v"][8 * hd:8 * hd + 8], "s5B": nl["s5B"][8 * hd:8 * hd + 8], "s5C": nl["s5C"][8 * hd:8 * hd + 8],
                      "s5d": nl["s5d"][8 * hd:8 * hd + 8], "ysT_out": ys5T[cs, :]}
                build_H_even(lambda_init, nc=nc, io=io, bg=bg, bg_total=bg_total)
        else:
            for hq in range(4):
                io = {"qT_in": qTo[4 * hq:4 * hq + 4], "kT_in": kTo[4 * hq:4 * hq + 4],
                      "v_in": vo[:, hq * 512:(hq + 1) * 512].rearrange("s (h e) -> s h e", h=4),
                      "mask_in": mask_in, "oT_out": oT[hq * 512:(hq + 1) * 512, :]}
                build_H_odd(nc=nc, io=io, bg=bg, bg_total=bg_total)
    return nc


def kernel_fused(x, mem, norm_mix_g, norm_xa_g, norm_mem_g, xa_wq, xa_wkv, xa_wo, norm_ffn_g,
                 ffn_w13, ffn_w2, ab_w_in, ab_w_out, s5_lambda_re, s5_lambda_im, s5_log_step,
                 s5_b_re, s5_b_im, s5_c_re, s5_c_im, s5_d, s5_glu_w, s5_glu_b, diff_lambda,
                 diff_subln_g, c_w_qkv, c_w_out, final_norm_g):
    A = lambda a: np.asarray(a)
    x = A(x).astype(np.float32, copy=False)
    mem = A(mem).astype(np.float32, copy=False)
    depth = 4
    nc = _get_nc(("fused",), build_fused)
    C32, S32 = rope_tables_host(0, SEQ)
    shared = {"ropeC": C32, "ropeS": S32, "mask_in": dilated_mask_table(), "g_fin": _g16(A(final_norm_g)), "perm_in": perm_table()}
    for l in range(depth):
        i = l // 2
        shared["g_mix_%d" % l] = _g16(A(norm_mix_g)[l])
        shared["g_xa_%d" % l] = _g16(A(norm_xa_g)[l])
        shared["g_mem_%d" % l] = _g16(A(norm_mem_g)[l])
        shared["g_ffn_%d" % l] = _g16(A(norm_ffn_g)[l])
        shared["wq_%d" % l] = A(xa_wq)[l]
        shared["wkv_%d" % l] = A(xa_wkv)[l]
        shared["wo_%d" % l] = A(xa_wo)[l]
        shared["w13_%d" % l] = A(ffn_w13)[l]
        shared["w2_%d" % l] = A(ffn_w2)[l]
        if l % 2 == 0:
            W = A(ab_w_in)[i]
            shared["w_in_%d" % l] = W
            shared["w_sw_%d" % l] = swap_cols(W, [1024 + hc * 128 for hc in range(8)] + [2048 + hc * 128 for hc in range(8)])
            shared["w_out_%d" % l] = A(ab_w_out)[i]
            shared["glu_w_%d" % l] = A(s5_glu_w)[i]
            shared["glu_b_%d" % l] = np.ascontiguousarray(A(s5_glu_b)[i].reshape(8, 128).T)
            shared["lam_in_%d" % l] = np.ascontiguousarray(np.broadcast_to(A(diff_lambda)[i].astype(np.float32), (128, 4, 128)))
            shared["sg_in_%d" % l] = np.ascontiguousarray(A(diff_subln_g)[i].astype(np.float32).reshape(2, 128).T)
            pk = pack_s5(A(s5_lambda_re)[i], A(s5_lambda_im)[i], A(s5_log_step)[i], A(s5_b_re)[i], A(s5_b_im)[i],
                         A(s5_c_re)[i], A(s5_c_im)[i], A(s5_d)[i], 0, 64)
            for k, v in pk.items():
                shared["%s_%d" % (k, l)] = v
        else:
            W = A(c_w_qkv)[i]
            shared["w_in_%d" % l] = W
            shared["w_sw_%d" % l] = swap_cols(W, [hc * 128 for hc in range(16)] + [2048 + hc * 128 for hc in range(16)])
            shared["w_out_%d" % l] = A(c_w_out)[i]
    in_maps = []
    for b in range(NB):
        m = dict(shared)
        m["x_hT"] = np.ascontiguousarray(x[b].T)
        m["memT"] = np.ascontiguousarray(mem[b].T)
        in_maps.append(m)
    res = run_bass_kernel_spmd(nc, in_maps, core_ids=list(range(NB))).results
    out = np.empty((NB, SEQ, D), np.float32)
    for b in range(NB):
        out[b] = res[b]["outT"].T
    return out


def kernel(**inputs):
    return kernel_fused(**inputs)
```

```python
import contextlib
import math
import numpy as np
import ml_dtypes
import concourse.bass as bass
import concourse.mybir as mybir
from concourse.bass_utils import run_bass_kernel_spmd

F32 = mybir.dt.float32
BF16 = mybir.dt.bfloat16
I32 = mybir.dt.int32
ALU = mybir.AluOpType
AF = mybir.ActivationFunctionType
NPBF = ml_dtypes.bfloat16

D = 2048
SEQ = 8192
NB = 2
NCORE = 8
TOK = 2048
TG = 512
FFN = 5632
EPS = 1e-6
SKIP = set()
NGRP_OVERRIDE = 0


class Prog:
    DMA_K = 6

    def __init__(self, nc):
        self.nc = nc
        self.ops = []
        self.last_w = {}
        self.readers = {}

    def add(self, eng, fn, R=(), W=(), dma=False):
        i = len(self.ops)
        deps = set()
        for k in list(R) + list(W):
            if k in self.last_w:
                deps.add(self.last_w[k])
        for k in W:
            for r in self.readers.get(k, ()):
                deps.add(r)
        for k in R:
            lst = self.readers.setdefault(k, [])
            if not dma:
                lst[:] = [r for r in lst if self.ops[r]["dma"] or self.ops[r]["eng"] != eng]
            lst.append(i)
        for k in W:
            self.last_w[k] = i
            self.readers[k] = []
        self.ops.append(dict(eng=eng, fn=fn, deps=deps, dma=dma, sig=dma, signal=None))
        return i

    def dma(self, eng, out, in_, R=(), W=()):
        return self.add(eng, lambda e: e.dma_start(out=out, in_=in_), R, W, dma=True)

    def emit(self):
        nc = self.nc
        ops = self.ops
        for i, o in enumerate(ops):
            nd = set()
            for d in o["deps"]:
                if ops[d]["eng"] == "tensor" and o["eng"] == "tensor" and not ops[d]["dma"] and not o["dma"]:
                    continue
                nd.add(d)
            o["deps"] = nd
            for d in nd:
                ops[d]["sig"] = True
        engs = ["tensor", "vector", "scalar", "gpsimd", "sync"]
        pfx = getattr(self, "pfx", "")
        self.sems = []

        def mk(name):
            h = nc.alloc_semaphore(name=pfx + name)
            self.sems.append(h)
            return h
        csem = {e: mk("c_" + e) for e in engs}
        dsem = {e: [mk("d_%s%d" % (e, k)) for k in range(self.DMA_K)] for e in ["sync", "gpsimd"]}
        ccount = {e: 0 for e in engs}
        dcount = {e: 0 for e in dsem}
        streams = {e: [] for e in engs}
        waited = {e: {} for e in engs}
        final = {}
        for i, o in enumerate(ops):
            E = o["eng"]
            st = streams[E]
            w = waited[E]
            for d in sorted(o["deps"]):
                sem, val = ops[d]["signal"]
                if w.get(id(sem), 0) < val:
                    w[id(sem)] = val
                    st.append((lambda e, sem=sem, val=val: e.wait_ge(sem, val)))
            if o["dma"]:
                n = dcount[E]
                dcount[E] += 1
                sem = dsem[E][n % self.DMA_K]
                val = 16 * (n // self.DMA_K + 1)
                if n >= self.DMA_K and w.get(id(sem), 0) < val - 16:
                    w[id(sem)] = val - 16
                    st.append((lambda e, sem=sem, val=val: e.wait_ge(sem, val - 16)))
                o["signal"] = (sem, val)
                final[id(sem)] = (sem, val)
                st.append((lambda e, fn=o["fn"], sem=sem: fn(e).then_inc(sem, 16)))
            elif o["sig"]:
                ccount[E] += 1
                sem = csem[E]
                val = ccount[E]
                o["signal"] = (sem, val)
                st.append((lambda e, fn=o["fn"], sem=sem: fn(e).then_inc(sem, 1)))
            else:
                st.append((lambda e, fn=o["fn"]: fn(e)))
        for sem, val in final.values():
            if waited["sync"].get(id(sem), 0) < val:
                streams["sync"].append((lambda e, sem=sem, val=val: e.wait_ge(sem, val)))
        with nc.Block() as block:
            @block.tensor
            def _(e):
                for f in streams["tensor"]:
                    f(e)

            @block.vector
            def _(e):
                for f in streams["vector"]:
                    f(e)

            @block.scalar
            def _(e):
                for f in streams["scalar"]:
                    f(e)

            @block.gpsimd
            def _(e):
                for f in streams["gpsimd"]:
                    f(e)

            @block.sync
            def _(e):
                for f in streams["sync"]:
                    f(e)
        self.stats = {e: len(streams[e]) for e in engs}
        self.counts = (dict(ccount), dict(dcount))


class Ctx:
    count = 0

    def __init__(self, nc=None, io=None):
        self.io = io
        Ctx.count += 1
        self.pfx = "p%d_" % Ctx.count
        self.nc = nc if nc is not None else bass.Bass("TRN2", target_bir_lowering=False)
        self.P = Prog(self.nc)
        self.S = contextlib.ExitStack()
        self.P.stack = self.S
        self.P.pfx = self.pfx
        self.nps = 0
        self.ps = []
        self.rot = {}
        self.cast_i = 0

    def din(self, name, shape, dt=F32):
        if self.io is not None:
            ap = self.io[name]
            assert list(ap.shape) == list(shape), (name, ap.shape, shape)
            return ap
        return self.nc.dram_tensor(name, list(shape), dt, kind="ExternalInput").ap()

    def dout(self, name, shape, dt=F32):
        if self.io is not None:
            ap = self.io[name]
            assert list(ap.shape) == list(shape), (name, ap.shape, shape)
            return ap
        return self.nc.dram_tensor(name, list(shape), dt, kind="ExternalOutput").ap()

    def finish(self):
        self.P.emit()
        if self.io is not None:
            self.nc.all_engine_barrier()
            self.nc.clear_and_free_semaphores(self.P.sems)
            self.nc.all_engine_barrier()
        self.S.close()
        return self.nc

    def sb(self, name, shape, dt):
        return self.S.enter_context(self.nc.sbuf_tensor(self.pfx + name, list(shape), dt))

    def alloc_psum(self, n=8):
        self.ps = [self.S.enter_context(self.nc.psum_tensor(self.pfx + "ps%d" % i, [128, 512], F32)) for i in range(n)]

    def next_ps(self, group="main", banks=None):
        banks = banks if banks is not None else list(range(len(self.ps)))
        i = self.rot.get(group, 0)
        self.rot[group] = i + 1
        b = banks[i % len(banks)]
        return self.ps[b], ("ps", b)

    def nxt(self, group, n):
        i = self.rot.get(group, 0)
        self.rot[group] = i + 1
        return i % n


class BgW:
    count = 0

    def __init__(self, nc, specs):
        BgW.count += 1
        self.nc = nc
        self.jobs = []
        self.out = {}
        self.pos = 0
        for key, W, blk in specs:
            K_, N_ = W.shape
            nci, nb = K_ // 128, N_ // blk
            Wb = nc.dram_tensor("bgw%d_%s" % (BgW.count, key), [nb, 128, nci, blk], BF16).ap()
            for j in range(nb):
                for ci0 in range(0, nci, 16):
                    n = min(16, nci - ci0)
                    src = W[ci0 * 128:(ci0 + n) * 128, j * blk:(j + 1) * blk].rearrange("(c p) o -> p c o", p=128)
                    self.jobs.append((src, Wb[j, :, ci0:ci0 + n, :], n, blk))
            self.out[key] = (Wb, blk)

    def remaining(self):
        return len(self.jobs) - self.pos

    def emit(self, C, count=1, cast_engines=("scalar",)):
        if not hasattr(C, "bg_st"):
            C.bg_st = [C.sb("bgst%d" % i, [128, 16, 128], F32) for i in range(2)]
            C.bg_bf = [C.sb("bgbf%d" % i, [128, 16, 128], BF16) for i in range(2)]
            C.bg_k = 0
        P = C.P
        for _ in range(count):
            if self.pos >= len(self.jobs):
                return
            src, dst, n, blk = self.jobs[self.pos]
            self.pos += 1
            k = C.bg_k
            C.bg_k += 1
            s = k % 2
            st, bf = C.bg_st[s], C.bg_bf[s]
            P.dma("sync", st[:, 0:n, 0:blk], src, W=[("bgst", s)])
            ce = cast_engines[k % len(cast_engines)]
            if ce == "scalar":
                P.add("scalar", lambda e, st=st, bf=bf, n=n, blk=blk: e.activation(out=bf[:, 0:n, 0:blk], in_=st[:, 0:n, 0:blk], func=AF.Copy),
                      R=[("bgst", s)], W=[("bgbf", s)])
            else:
                P.add(ce, lambda e, st=st, bf=bf, n=n, blk=blk: e.tensor_copy(out=bf[:, 0:n, 0:blk], in_=st[:, 0:n, 0:blk]),
                      R=[("bgst", s)], W=[("bgbf", s)])
            P.dma("gpsimd", dst, bf[:, 0:n, 0:blk], R=[("bgbf", s)])

    def flush(self):
        if self.remaining() == 0:
            return
        C = Ctx(self.nc, io={})
        self.emit(C, self.remaining(), cast_engines=("vector", "scalar"))
        C.finish()


def emit_W(nc, specs):
    C = Ctx(nc, io={})
    P = C.P
    NS = 4
    wst = [C.sb("wst%d" % i, [128, 16, 128], F32) for i in range(NS)]
    wbf = [C.sb("wbf%d" % i, [128, 16, 128], BF16) for i in range(NS)]
    out = {}
    k = 0
    for key, W, blk in specs:
        K_, N_ = W.shape
        nci, nb = K_ // 128, N_ // blk
        Wb = nc.dram_tensor(C.pfx + "wb_" + key, [nb, 128, nci, blk], BF16).ap()
        for j in range(nb):
            for ci0 in range(0, nci, 16):
                n = min(16, nci - ci0)
                s = k % NS
                src = W[ci0 * 128:(ci0 + n) * 128, j * blk:(j + 1) * blk].rearrange("(c p) o -> p c o", p=128)
                P.dma("sync", wst[s][:, 0:n, 0:blk], src, W=[("wst", s)])
                ce = ["vector", "scalar"][k % 2]
                if ce == "scalar":
                    P.add("scalar", lambda e, s=s, n=n, blk=blk: e.activation(out=wbf[s][:, 0:n, 0:blk], in_=wst[s][:, 0:n, 0:blk], func=AF.Copy),
                          R=[("wst", s)], W=[("wbf", s)])
                else:
                    P.add(ce, lambda e, s=s, n=n, blk=blk: e.tensor_copy(out=wbf[s][:, 0:n, 0:blk], in_=wst[s][:, 0:n, 0:blk]),
                          R=[("wst", s)], W=[("wbf", s)])
                P.dma("sync" if k % 2 else "gpsimd", Wb[j, :, ci0:ci0 + n, :], wbf[s][:, 0:n, 0:blk], R=[("wbf", s)])
                k += 1
        out[key] = (Wb, blk)
    C.finish()
    return out


def t_weight_specs(prev, nxt, io):
    specs = []
    if prev is not None:
        if prev == "even":
            specs.append(("glu_w", io["glu_w"], 128))
        for nm in ("w_out", "wq", "wkv", "wo", "w13", "w2"):
            specs.append((nm, io[nm], 128))
    if nxt is not None:
        specs.append(("w_in", io["w_in"], 128))
    return specs


def build_T(prev, nxt, final, nc=None, io=None, wb=None, ntok=None):
    C = Ctx(nc, io)
    nc, P = C.nc, C.P
    add, dma = P.add, P.dma
    ntok = ntok or TOK
    ngrp = NGRP_OVERRIDE or (ntok // TG)

    hT_in = C.din("hT_in", [D, ntok])
    if prev is not None:
        if prev == "even":
            ys5T = C.din("ys5T", [1024, ntok])
            ydiffT = C.din("ydiffT", [1024, ntok], BF16)
            glu_w = C.din("glu_w", [1024, 1024])
            glu_b = C.din("glu_b", [128, 8])
        else:
            oT_in = C.din("oT_in", [D, ntok], BF16)
        w_out = C.din("w_out", [D, D])
        memT = C.din("memT", [D, 256])
        g_xa = C.din("g_xa", [128, 16])
        g_mem = C.din("g_mem", [128, 16])
        wq = C.din("wq", [D, D])
        wkv = C.din("wkv", [D, 2 * D])
        wo = C.din("wo", [D, D])
        g_ffn = C.din("g_ffn", [128, 16])
        w13 = C.din("w13", [D, 2 * FFN])
        w2 = C.din("w2", [FFN, D])
    if nxt is not None:
        g_mix = C.din("g_mix", [128, 16])
        ropeC = C.din("ropeC", [32, ntok])
        ropeS = C.din("ropeS", [32, ntok])
        perm_in = C.din("perm_in", [32, 32], BF16)
        if nxt == "even":
            w_in = C.din("w_in", [D, 4096])
            w_sw = C.din("w_sw", [D, 16 * 32])
            uT_out = C.dout("uT_out", [1024, ntok], BF16)
            qT_out = C.dout("qT_out", [8, 128, ntok], BF16)
            kT_out = C.dout("kT_out", [8, 128, ntok], BF16)
            v_out = C.dout("v_out", [ntok, 1024], BF16)
        else:
            w_in = C.din("w_in", [D, 6144])
            w_sw = C.din("w_sw", [D, 32 * 32])
            qT_out = C.dout("qT_out", [16, 128, ntok], BF16)
            kT_out = C.dout("kT_out", [16, 128, ntok], BF16)
            v_out = C.dout("v_out", [ntok, 2048], BF16)
        hT_out = C.dout("hT_out", [D, ntok])
    if final:
        g_fin = C.din("g_fin", [128, 16])
        outT = C.dout("outT", [D, ntok])

    if wb is None:
        wio = {}
        if prev is not None:
            wio.update({"w_out": w_out, "wq": wq, "wkv": wkv, "wo": wo, "w13": w13, "w2": w2})
            if prev == "even":
                wio["glu_w"] = glu_w
        if nxt is not None:
            wio.update({"w_in": w_in, "w_sw": w_sw})
        wb = emit_W(nc, t_weight_specs(prev, nxt, wio))
    wbmap = {}
    if prev is not None:
        for nm, ap in (("w_out", w_out), ("wq", wq), ("wkv", wkv), ("wo", wo), ("w13", w13), ("w2", w2)):
            wbmap[id(ap)] = wb[nm]
        if prev == "even":
            wbmap[id(glu_w)] = wb["glu_w"]
    if nxt is not None:
        wbmap[id(w_in)] = wb["w_in"]

    C.alloc_psum(8)
    hT = C.sb("hT", [128, 16, TG], F32)
    hn = C.sb("hn", [128, 16, TG], BF16)
    sq = [C.sb("sq%d" % i, [128, TG], BF16) for i in range(2)]
    rstd = C.sb("rstd", [128, TG], F32)
    ones = C.sb("ones", [128, 128], BF16)
    epsc = C.sb("epsc", [128, 1], F32)
    NW = 6
    wbf = [C.sb("wbf%d" % i, [128, 16, 128], BF16) for i in range(NW)]
    gvec = C.sb("gvec", [128, 5, 16], F32)
    add("gpsimd", lambda e: e.memset(ones[:], 1.0), W=["ones"])
    add("gpsimd", lambda e: e.memset(epsc[:], EPS), W=["epsc"])

    def load_g(slot, src):
        dma("sync", gvec[:, slot, :], src[:, :], W=[("gvec", slot)])

    def load_w(Wd, ci0, n, c0, ncol=128):
        Wb, blk = wbmap[id(Wd)]
        assert ncol == blk and c0 % blk == 0
        s = C.nxt("w", NW)
        dma("sync", wbf[s][:, 0:n, 0:ncol], Wb[c0 // blk, :, ci0:ci0 + n, :], W=[("wbf", s)])
        return wbf[s], ("wbf", s)

    def linear(Wd, n_ci, cols, x_tile, x_keys, evac, ntok=TG, banks=None):
        for c0 in cols:
            ps, pk = C.next_ps("main", banks)
            blocks = []
            ci = 0
            while ci < n_ci:
                n = min(16, n_ci - ci)
                blocks.append((ci, n))
                ci += n
            first = True
            for (ci0, n) in blocks:
                wt, wk = load_w(Wd, ci0, n, c0)
                for j in range(n):
                    last = (ci0 + j == n_ci - 1)
                    xk = [("hn", ci0 + j)] if list(x_keys) == ["hn"] else list(x_keys)
                    add("tensor", lambda e, wt=wt, j=j, cj=ci0 + j, ps=ps, first=first, last=last:
                        e.matmul(ps[:, 0:ntok], lhsT=wt[:, j, :], rhs=x_tile[:, cj, 0:ntok], start=first, stop=last),
                        R=[wk] + xk, W=[pk])
                    first = False
            evac(c0, ps, pk)

    def rmsnorm(src_tile, src_keys, gslot, dst_tile, dst_key, ntok=TG, nchunk=16):
        ps, pk = C.next_ps("main")
        for c in range(nchunk):
            s = C.nxt("sq", 2)
            add("scalar", lambda e, s=s, c=c: e.activation(out=sq[s][:, 0:ntok], in_=src_tile[:, c, 0:ntok], func=AF.Square),
                R=list(src_keys), W=[("sq", s)])
            add("tensor", lambda e, s=s, c=c, ps=ps: e.matmul(ps[:, 0:ntok], lhsT=ones[:], rhs=sq[s][:, 0:ntok],
                                                               start=(c == 0), stop=(c == nchunk - 1)),
                R=[("sq", s), "ones"], W=[pk])
        add("scalar", lambda e, ps=ps: e.activation(out=rstd[:, 0:ntok], in_=ps[:, 0:ntok], func=AF.Sqrt,
                                                     scale=1.0 / D, bias=epsc[:, 0:1]), R=[pk, "epsc"], W=["rstd"])
        add("vector", lambda e: e.reciprocal(out=rstd[:, 0:ntok], in_=rstd[:, 0:ntok]), R=["rstd"], W=["rstd"])
        for c in range(nchunk):
            add("vector", lambda e, c=c: e.scalar_tensor_tensor(out=dst_tile[:, c, 0:ntok], in0=src_tile[:, c, 0:ntok],
                                                                 scalar=gvec[:, gslot, c:c + 1], in1=rstd[:, 0:ntok],
                                                                 op0=ALU.mult, op1=ALU.mult),
                R=list(src_keys) + ["rstd", ("gvec", gslot)], W=[(dst_key, c) if dst_key == "hn" else dst_key])

    def resid_evac(c0, ps, pk):
        c = c0 // 128
        add("vector", lambda e: e.tensor_tensor(out=hT[:, c, :], in0=hT[:, c, :], in1=ps[:, :], op=ALU.add),
            R=[pk, "hT"], W=["hT"])

    if prev is not None:
        load_g(0, g_xa)
        load_g(1, g_mem)
        load_g(2, g_ffn)
        memt = hT
        memn = hn
        kmT = C.sb("kmT", [128, 16, 256], BF16)
        vm = C.sb("vm", [128, 2, 2048], BF16)
        dma("sync", memt[:, :, 0:256], memT.rearrange("(c p) t -> p c t", p=128), W=["hT"])
        rmsnorm(memt, ["hT"], 1, memn, "hn", ntok=256)

        def k_evac(c0, ps, pk):
            c = c0 // 128
            add("vector", lambda e: e.tensor_copy(out=kmT[:, c, :], in_=ps[:, 0:256]), R=[pk], W=["kmT"])
        linear(wkv, 16, [c * 128 for c in range(16)], memn, ["hn"], k_evac, ntok=256)
        for c in range(16):
            wt, wk = load_w(wkv, 0, 16, D + c * 128)
            for kt in range(2):
                ps, pk = C.next_ps("main")
                for j in range(16):
                    add("tensor", lambda e, wt=wt, j=j, ps=ps, kt=kt: e.matmul(ps[:, 0:128], lhsT=memn[:, j, kt * 128:(kt + 1) * 128],
                                                                                rhs=wt[:, j, :], start=(j == 0), stop=(j == 15)),
                        R=[wk, ("hn", j)], W=[pk])
                add("vector", lambda e, ps=ps, kt=kt, c=c: e.tensor_copy(out=vm[:, kt, c * 128:(c + 1) * 128], in_=ps[:, 0:128]),
                    R=[pk], W=["vm"])
        if prev == "even":
            glub = C.sb("glub", [128, 8], F32)
            dma("sync", glub[:], glu_b[:, :], W=["glub"])
    if nxt is not None:
        load_g(3, g_mix)
    if final:
        load_g(4, g_fin)

    if prev is not None:
        xin = C.sb("xin", [128, 16, TG], BF16)
        et = [C.sb("et%d" % i, [128, TG], BF16) for i in range(2)]
        rl = C.sb("rl", [128, TG], F32)
        gT = C.sb("gT", [128, 44, TG], BF16)
        qx = gT[:, 24:40, :]
        sa = [C.sb("sa%d" % i, [128, TG], F32) for i in range(2)]
        if prev == "even":
            ysf = [C.sb("ysf%d" % i, [128, TG], F32) for i in range(2)]
            gg = gT[:, 16:24, :]
            sg = C.sb("sg", [128, TG], F32)
            g2 = C.sb("g2", [128, TG], F32)
    if nxt is not None:
        rc = C.sb("rc", [32, TG], F32)
        rs = C.sb("rs", [32, TG], F32)
        qhi = [C.sb("qhi%d" % i, [32, TG], BF16) for i in range(2)]
        qlo = [C.sb("qlo%d" % i, [32, TG], BF16) for i in range(2)]
        permt = C.sb("permt", [32, 32], BF16)
        dma("sync", permt[:], perm_in[:, :], W=["permt"])
        t1 = [C.sb("t1_%d" % i, [32, TG], F32) for i in range(1)]
        t2 = [C.sb("t2_%d" % i, [32, TG], F32) for i in range(1)]
        ob = [C.sb("ob%d" % i, [128, TG], BF16) for i in range(3)]
        nvc = 1024 if nxt == "even" else 2048
        if prev is not None:
            vtok = gT[:, 0:16, :].rearrange("p (a b) t -> p a (b t)", a=4)[:, :, 0:nvc]
        else:
            vtok = C.sb("vtok", [128, 4, nvc], BF16)[:, :, :]
    if final:
        of = [C.sb("of%d" % i, [128, TG], F32) for i in range(2)]

    for g in range(ngrp):
        t0 = g * TG
        dma("gpsimd", hT[:], hT_in[:, t0:t0 + TG].rearrange("(c p) t -> p c t", p=128), W=["hT"])
        if prev is not None:
            if prev == "even":
                dma("gpsimd", xin[:, 8:16, :], ydiffT[:, t0:t0 + TG].rearrange("(c p) t -> p c t", p=128), W=["xin_hi"])
                for c in range(8):
                    y = C.nxt("ysf", 2)
                    dma("gpsimd", ysf[y][:], ys5T[c * 128:(c + 1) * 128, t0:t0 + TG], W=[("ysf", y)])
                    add("scalar", lambda e, y=y: e.activation(out=g2[:], in_=ysf[y][:], func=AF.Square), R=[("ysf", y)], W=["g2"])
                    add("vector", lambda e: e.tensor_scalar(out=g2[:], in0=g2[:], scalar1=0.044715, scalar2=1.0, op0=ALU.mult, op1=ALU.add),
                        R=["g2"], W=["g2"])
                    add("vector", lambda e, y=y: e.tensor_tensor(out=g2[:], in0=g2[:], in1=ysf[y][:], op=ALU.mult), R=["g2", ("ysf", y)], W=["g2"])
                    add("scalar", lambda e: e.activation(out=g2[:], in_=g2[:], func=AF.Sigmoid, scale=1.5957691216057308), R=["g2"], W=["g2"])
                    add("vector", lambda e, y=y, c=c: e.tensor_tensor(out=gg[:, c, :], in0=g2[:], in1=ysf[y][:], op=ALU.mult),
                        R=["g2", ("ysf", y)], W=["gT"])

                def glu_evac(c0, ps, pk):
                    c = c0 // 128
                    add("scalar", lambda e: e.activation(out=sg[:], in_=ps[:, :], func=AF.Sigmoid, bias=glub[:, c:c + 1]),
                        R=[pk, "glub"], W=["sg"])
                    add("vector", lambda e: e.tensor_tensor(out=xin[:, c, :], in0=sg[:], in1=gg[:, c, :], op=ALU.mult),
                        R=["sg", "gT"], W=["xin_lo"])
                linear(glu_w, 8, [c * 128 for c in range(8)], gg, ["gT"], glu_evac)
                xkeys = ["xin_lo", "xin_hi"]
            else:
                dma("gpsimd", xin[:], oT_in[:, t0:t0 + TG].rearrange("(c p) t -> p c t", p=128), W=["xin_lo", "xin_hi"])
                xkeys = ["xin_lo", "xin_hi"]
            if "mix" not in SKIP:
                linear(w_out, 16, [c * 128 for c in range(16)], xin, xkeys, resid_evac)

            if "xa" in SKIP:
                continue
            rmsnorm(hT, ["hT"], 0, hn, "hn")

            def q_evac(c0, ps, pk):
                c = c0 // 128
                add("vector", lambda e: e.tensor_copy(out=qx[:, c, :], in_=ps[:, :]), R=[pk], W=["gT"])
            linear(wq, 16, [c * 128 for c in range(16)], hn, ["hn"], q_evac)
            for h in range(4):
                eks = []
                for kt in range(2):
                    ps, pk = C.next_ps("main")
                    for c in range(4):
                        add("tensor", lambda e, ps=ps, c=c, kt=kt, h=h: e.matmul(ps[:, :], lhsT=kmT[:, 4 * h + c, kt * 128:(kt + 1) * 128],
                                                                                   rhs=qx[:, 4 * h + c, :], start=(c == 0), stop=(c == 3)),
                            R=["kmT", "gT"], W=[pk])
                    add("scalar", lambda e, ps=ps, kt=kt: e.activation(out=et[kt][:], in_=ps[:, :], func=AF.Exp, scale=512 ** -0.5),
                        R=[pk], W=[("et", kt)])
                psl, pkl = C.next_ps("main")
                for kt in range(2):
                    add("tensor", lambda e, psl=psl, kt=kt: e.matmul(psl[:, :], lhsT=ones[:], rhs=et[kt][:], start=(kt == 0), stop=(kt == 1)),
                        R=[("et", kt), "ones"], W=[pkl])
                add("vector", lambda e, psl=psl: e.reciprocal(out=rl[:], in_=psl[:, :]), R=[pkl], W=["rl"])
                for dc in range(4):
                    ps, pk = C.next_ps("main")
                    for kt in range(2):
                        add("tensor", lambda e, ps=ps, kt=kt, dc=dc, h=h: e.matmul(ps[:, :], lhsT=vm[:, kt, h * 512 + dc * 128:h * 512 + (dc + 1) * 128],
                                                                                     rhs=et[kt][:], start=(kt == 0), stop=(kt == 1)),
                            R=["vm", ("et", kt)], W=[pk])
                    add("vector", lambda e, ps=ps, dc=dc, h=h: e.tensor_tensor(out=xin[:, 4 * h + dc, :], in0=ps[:, :], in1=rl[:], op=ALU.mult),
                        R=[pk, "rl"], W=["xin_lo", "xin_hi"])
            linear(wo, 16, [c * 128 for c in range(16)], xin, ["xin_lo", "xin_hi"], resid_evac)

            if "ffn" in SKIP:
                continue
            rmsnorm(hT, ["hT"], 2, hn, "hn")
            for j in range(44):
                holder = {}

                def a_evac(c0, ps, pk, holder=holder):
                    s = C.nxt("sa", 2)
                    holder["s"] = s
                    add("scalar", lambda e: e.activation(out=sa[s][:], in_=ps[:, :], func=AF.Silu), R=[pk], W=[("sa", s)])

                def b_evac(c0, ps, pk, holder=holder, j=j):
                    s = holder["s"]
                    add("vector", lambda e: e.tensor_tensor(out=gT[:, j, :], in0=ps[:, :], in1=sa[s][:], op=ALU.mult),
                        R=[pk, ("sa", s)], W=["gT"])
                linear(w13, 16, [j * 128], hn, ["hn"], a_evac)
                linear(w13, 16, [FFN + j * 128], hn, ["hn"], b_evac)
            linear(w2, 44, [c * 128 for c in range(16)], gT, ["gT"], resid_evac)

        if nxt is not None:
            dma("gpsimd", hT_out[:, t0:t0 + TG].rearrange("(c p) t -> p c t", p=128), hT[:], R=["hT"])
            rmsnorm(hT, ["hT"], 3, hn, "hn")
            dma("sync", rc[:], ropeC[:, t0:t0 + TG], W=["rc"])
            dma("sync", rs[:], ropeS[:, t0:t0 + TG], W=["rs"])

            def plain_out(dst):
                def ev(c0, ps, pk):
                    s = C.nxt("ob", 3)
                    add("vector", lambda e: e.tensor_copy(out=ob[s][:], in_=ps[:, :]), R=[pk], W=[("ob", s)])
                    dma("gpsimd", dst, ob[s][:], R=[("ob", s)])
                return ev

            def rope_A(col0, dst):
                hold = {}

                def ev_main(c0, ps, pk):
                    hold["ps"], hold["pk"] = ps, pk
                linear(w_in, 16, [col0], hn, ["hn"], ev_main)
                ps, pk = hold["ps"], hold["pk"]
                h = C.nxt("qh", 2)
                add("scalar", lambda e: e.activation(out=qhi[h][:], in_=ps[0:32, :], func=AF.Copy), R=[pk], W=[("qhi", h)])
                add("vector", lambda e: e.tensor_tensor(out=qlo[h][:], in0=ps[0:32, :], in1=qhi[h][:], op=ALU.subtract),
                    R=[pk, ("qhi", h)], W=[("qlo", h)])
                return ps, pk, h, dst

            def rope_B(ps, pk, h, dst):
                ps2, pk2 = C.next_ps("main")
                add("tensor", lambda e: e.matmul(ps2[0:32, :], lhsT=permt[:, :], rhs=qhi[h][:], start=True, stop=False),
                    R=["permt", ("qhi", h)], W=[pk2])
                add("tensor", lambda e: e.matmul(ps2[0:32, :], lhsT=permt[:, :], rhs=qlo[h][:], start=False, stop=True),
                    R=["permt", ("qlo", h)], W=[pk2])
                s = C.nxt("ob", 3)
                r = 0
                add("vector", lambda e: e.tensor_tensor(out=t1[r][:], in0=ps[0:32, :], in1=rc[:], op=ALU.mult), R=[pk, "rc"], W=[("t1", r)])
                add("vector", lambda e: e.tensor_tensor(out=t2[r][:], in0=ps2[0:32, :], in1=rs[:], op=ALU.mult), R=[pk2, "rs"], W=[("t2", r)])
                add("vector", lambda e: e.tensor_tensor(out=ob[s][0:32, :], in0=t1[r][:], in1=t2[r][:], op=ALU.add),
                    R=[("t1", r), ("t2", r)], W=[("ob", s)])
                add("vector", lambda e: e.tensor_copy(out=ob[s][32:64, :], in_=ps[32:64, :]), R=[pk], W=[("ob", s)])
                add("vector", lambda e: e.tensor_copy(out=ob[s][64:128, :], in_=ps[64:128, :]), R=[pk], W=[("ob", s)])
                dma("gpsimd", dst, ob[s][:], R=[("ob", s)])

            def rope_chain(items):
                pend = None
                for col0, dst in items:
                    cur = rope_A(col0, dst)
                    if pend is not None:
                        rope_B(*pend)
                    pend = cur
                rope_B(*pend)

            if nxt == "even":
                for c in range(8):
                    linear(w_in, 16, [c * 128], hn, ["hn"], plain_out(uT_out[c * 128:(c + 1) * 128, t0:t0 + TG]))
                rope_chain([(1024 + hc * 128, qT_out[hc, :, t0:t0 + TG]) for hc in range(8)] +
                           [(2048 + hc * 128, kT_out[hc, :, t0:t0 + TG]) for hc in range(8)])
                vcol0 = 3072
            else:
                rope_chain([(hc * 128, qT_out[hc, :, t0:t0 + TG]) for hc in range(16)] +
                           [(2048 + hc * 128, kT_out[hc, :, t0:t0 + TG]) for hc in range(16)])
                vcol0 = 4096
            for c in range(nvc // 128):
                wt, wk = load_w(w_in, 0, 16, vcol0 + c * 128)
                for tt in range(4):
                    ps, pk = C.next_ps("main")
                    for j in range(16):
                        add("tensor", lambda e, wt=wt, j=j, ps=ps, tt=tt: e.matmul(ps[:, 0:128], lhsT=hn[:, j, tt * 128:(tt + 1) * 128],
                                                                                    rhs=wt[:, j, :], start=(j == 0), stop=(j == 15)),
                            R=[wk, ("hn", j)], W=[pk])
                    add("vector", lambda e, ps=ps, tt=tt, c=c: e.tensor_copy(out=vtok[:, tt, c * 128:(c + 1) * 128], in_=ps[:, 0:128]),
                        R=[pk], W=["gT"])
            dma("gpsimd", v_out[t0:t0 + TG, :].rearrange("(t p) c -> p t c", p=128), vtok, R=["gT"])
        if final:
            rmsnorm(hT, ["hT"], 4, hn, "hn")
            for c in range(16):
                s = C.nxt("of", 2)
                add("vector", lambda e, c=c, s=s: e.scalar_tensor_tensor(out=of[s][:], in0=hT[:, c, :], scalar=gvec[:, 4, c:c + 1],
                                                                          in1=rstd[:], op0=ALU.mult, op1=ALU.mult),
                    R=["hT", "rstd", ("gvec", 4)], W=[("of", s)])
                dma("gpsimd", outT[c * 128:(c + 1) * 128, t0:t0 + TG], of[s][:], R=[("of", s)])
    return C.finish()


def build_H_odd(nc=None, io=None, bg=None, bg_total=0):
    C = Ctx(nc, io)
    nc, P = C.nc, C.P
    add, dma = P.add, P.dma
    qT_in = C.din("qT_in", [4, 128, SEQ], BF16)
    kT_in = C.din("kT_in", [4, 128, SEQ], BF16)
    v_in = C.din("v_in", [SEQ, 4, 128], BF16)
    mask_in = C.din("mask_in", [128, 20, 512], BF16)
    oT_out = C.dout("oT_out", [512, SEQ], BF16)
    C.alloc_psum(8)
    qT = C.sb("qT", [128, SEQ], BF16)
    kT = C.sb("kT", [128, SEQ], BF16)
    vv = C.sb("vv", [128, 64, 128], BF16)
    mk = C.sb("mk", [128, 20, 512], BF16)
    ones = C.sb("ones", [128, 128], BF16)
    et = [C.sb("et%d" % i, [128, 512], BF16) for i in range(3)]
    em = [C.sb("em%d" % i, [128, 512], BF16) for i in range(5)]
    rl = C.sb("rl", [128, 512], F32)
    ob = [C.sb("ob%d" % i, [128, 512], BF16) for i in range(2)]
    esum = [C.sb("esum%d" % i, [128, 512], F32) for i in range(2)]
    esb = [C.sb("esb%d" % i, [128, 512], BF16) for i in range(2)]
    add("gpsimd", lambda e: e.memset(ones[:], 1.0), W=["ones"])
    dma("sync", mk[:], mask_in[:, :, :], W=["mk"])
    scale = 128 ** -0.5
    LOOK = 3
    for hh in range(4):
        dma("sync", qT[:], qT_in[hh], W=["qT"])
        dma("sync", kT[:], kT_in[hh], W=["kT"])
        dma("gpsimd", vv[:], v_in[:, hh, :].rearrange("(t p) c -> p t c", p=128), W=["vv"])
        steps = []
        for qg in range(16):
            kts = [kt for kt in range(4 * qg - 8, 4 * qg + 12) if 0 <= kt < 64]
            for n, kt in enumerate(kts):
                steps.append((qg, kt, n, len(kts)))
        acc = {}

        def front(st, gi):
            qg, kt, n, nk = st
            idx = kt - (4 * qg - 8)
            ps, pk = C.next_ps("S", [0, 1, 2, 3])
            add("tensor", lambda e, ps=ps, kt=kt, qg=qg: e.matmul(ps[:, :], lhsT=kT[:, kt * 128:(kt + 1) * 128],
                                                                   rhs=qT[:, qg * 512:(qg + 1) * 512], start=True, stop=True),
                R=["kT", "qT"], W=[pk])
            s_ = C.nxt("et", 3)
            add("scalar", lambda e, ps=ps, s_=s_: e.activation(out=et[s_][:], in_=ps[:, :], func=AF.Exp, scale=scale),
                R=[pk], W=[("et", s_)])
            m = C.nxt("em", 5)
            me = ["vector", "gpsimd"][gi % 2]
            add(me, lambda e, s_=s_, m=m, idx=idx: e.tensor_tensor(out=em[m][:], in0=et[s_][:], in1=mk[:, idx, :], op=ALU.mult),
                R=[("et", s_), "mk"], W=[("em", m)])
            return m

        def back(st, m):
            qg, kt, n, nk = st
            if n == 0:
                acc["o"] = C.next_ps("O", [4, 5])
                acc["l"] = C.next_ps("L", [6, 7])
            (po, pko), (pl, pkl) = acc["o"], acc["l"]
            last = (n == nk - 1)
            add("tensor", lambda e, po=po, kt=kt, m=m, n=n, last=last: e.matmul(
                po[:, :], lhsT=vv[:, kt, :], rhs=em[m][:], start=(n == 0), stop=last), R=["vv", ("em", m)], W=[pko])
            add("tensor", lambda e, pl=pl, m=m, n=n, last=last: e.matmul(
                pl[:, :], lhsT=ones[:], rhs=em[m][:], start=(n == 0), stop=last), R=["ones", ("em", m)], W=[pkl])
            if last:
                add("vector", lambda e, pl=pl: e.reciprocal(out=rl[:], in_=pl[:, :]), R=[pkl], W=["rl"])
                o = C.nxt("ob", 2)
                add("vector", lambda e, po=po, o=o: e.tensor_tensor(out=ob[o][:], in0=po[:, :], in1=rl[:], op=ALU.mult),
                    R=[pko, "rl"], W=[("ob", o)])
                dma("gpsimd", oT_out[hh * 128:(hh + 1) * 128, qg * 512:(qg + 1) * 512], ob[o][:], R=[("ob", o)])

        queue = []
        bg_every = max(1, (4 * len(steps)) // max(1, bg_total)) if bg is not None else 0
        for gi, st in enumerate(steps):
            queue.append((st, front(st, gi)))
            if len(queue) > LOOK:
                back(*queue.pop(0))
            if bg is not None and gi % bg_every == 0:
                bg.emit(C, 1, cast_engines=("scalar", "vector"))
        while queue:
            back(*queue.pop(0))
    return C.finish()


def dilated_mask_table():
    kl = np.arange(128)[:, None, None]
    idx = np.arange(20)[None, :, None]
    ql = np.arange(512)[None, None, :]
    off = (idx - 8) * 128 + kl - ql
    m = np.zeros(off.shape, np.float32)
    for window, dil in ((128, 1), (512, 4), (2048, 16)):
        half = window // (2 * dil)
        m += ((off % dil == 0) & (np.abs(off) <= half * dil)).astype(np.float32)
    return m.astype(NPBF)


def build_H_even(lambda_init, do_s5=True, do_attn=True, nc=None, io=None, bg=None, bg_total=0):
    C = Ctx(nc, io)
    nc, P = C.nc, C.P
    add, dma = P.add, P.dma
    qT_in = C.din("qT_in", [2, 128, SEQ], BF16)
    kT_in = C.din("kT_in", [2, 128, SEQ], BF16)
    v_in = C.din("v_in", [SEQ, 256], BF16)
    lam_in = C.din("lam_in", [128, 4, 128])
    sg_in = C.din("sg_in", [128, 2])
    ydT_out = C.dout("ydT_out", [256, SEQ], BF16)
    uT_in = C.din("uT_in", [256, SEQ], BF16)
    s5v = C.din("s5v", [8, 2, 128, 3])
    s5B = C.din("s5B", [8, 2, 2, 32, 128], BF16)
    s5C = C.din("s5C", [8, 2, 2, 128, 32], BF16)
    s5d = C.din("s5d", [8, 32, 1])
    ysT_out = C.dout("ysT_out", [256, SEQ])
    C.alloc_psum(8)
    ones = C.sb("ones", [128, 128], BF16)
    epsc = C.sb("epsc", [128, 1], F32)
    add("gpsimd", lambda e: e.memset(ones[:], 1.0), W=["ones"])
    add("gpsimd", lambda e: e.memset(epsc[:], EPS), W=["epsc"])

    if do_attn:
        qTg = [C.sb("qTg%d" % i, [128, 2, 512], BF16) for i in range(2)]
        kT = C.sb("kT", [128, 2, SEQ], BF16)
        vv = C.sb("vv", [128, 64, 256], BF16)
        et = [C.sb("et%d" % i, [128, 512], BF16) for i in range(3)]
        rl = C.sb("rl", [128, 512], F32)
        oc = C.sb("oc", [128, 2, 2, 512], F32)
        od = C.sb("od", [128, 2, 512], F32)
        sq = C.sb("sq", [128, 512], BF16)
        rstd = C.sb("rstd", [128, 512], F32)
        ob = [C.sb("ob%d" % i, [128, 512], BF16) for i in range(2)]
        lamt = C.sb("lamt", [128, 4, 128], F32)
        lprod = C.sb("lprod", [128, 128], F32)
        lsc = C.sb("lsc", [128, 4], F32)
        sgt = C.sb("sgt", [128, 2], F32)
        for c in range(2):
            dma("sync", kT[:, c, :], kT_in[c], W=["kT"])
        dma("gpsimd", vv[:], v_in.rearrange("(t p) c -> p t c", p=128), W=["vv"])
        dma("sync", lamt[:], lam_in[:, :, :], W=["lamt"])
        dma("sync", sgt[:], sg_in[:, :], W=["sgt"])
        for i in range(2):
            add("vector", lambda e, i=i: e.tensor_tensor(out=lprod[:], in0=lamt[:, 2 * i, :], in1=lamt[:, 2 * i + 1, :], op=ALU.mult),
                R=["lamt"], W=["lprod"])
            add("vector", lambda e, i=i: e.reduce_sum(out=lsc[:, i:i + 1], in_=lprod[:], axis=mybir.AxisListType.X), R=["lprod"], W=["lsc"])
        add("scalar", lambda e: e.activation(out=lsc[:, 0:2], in_=lsc[:, 0:2], func=AF.Exp), R=["lsc"], W=["lsc"])
        add("vector", lambda e: e.tensor_tensor(out=lsc[:, 2:3], in0=lsc[:, 1:2], in1=lsc[:, 0:1], op=ALU.subtract), R=["lsc"], W=["lsc"])
        add("vector", lambda e: e.tensor_scalar(out=lsc[:, 2:3], in0=lsc[:, 2:3], scalar1=-float(lambda_init), scalar2=None, op0=ALU.add),
            R=["lsc"], W=["lsc"])
        add("vector", lambda e: e.tensor_scalar(out=sgt[:], in0=sgt[:], scalar1=float(1.0 - lambda_init), scalar2=None, op0=ALU.mult),
            R=["sgt"], W=["sgt"])
        scale = 128 ** -0.5

    def attn_gen():
        LOOK = 1
        SB, OB, LB = [0, 1], [2, 3], [4]
        steps = [(qg, comp, kt) for qg in range(16) for comp in range(2) for kt in range(64)]
        st8 = {}

        def front(stp):
            qg, comp, kt = stp
            if comp == 0 and kt == 0:
                qs = C.nxt("qTg", 2)
                st8["qs"] = qs
                dma("sync", qTg[qs][:], qT_in[:, :, qg * 512:(qg + 1) * 512].rearrange("c p t -> p c t"), W=[("qTg", qs)])
            qs = st8["qs"]
            ps, pk = C.next_ps("S", SB)
            add("tensor", lambda e, ps=ps, kt=kt, qs=qs, comp=comp: e.matmul(
                ps[:, :], lhsT=kT[:, comp, kt * 128:(kt + 1) * 128], rhs=qTg[qs][:, comp, :],
                start=True, stop=True), R=["kT", ("qTg", qs)], W=[pk])
            s_ = C.nxt("et", 3)
            add("scalar", lambda e, ps=ps, s_=s_: e.activation(out=et[s_][:], in_=ps[:, :], func=AF.Exp, scale=scale),
                R=[pk], W=[("et", s_)])
            return s_

        def back(stp, s_):
            qg, comp, kt = stp
            if kt == 0:
                st8["po"] = [C.next_ps("O", OB), C.next_ps("O", OB)]
                st8["pl"] = C.next_ps("L", LB)
            po = st8["po"]
            pl, pkl = st8["pl"]
            for dc in range(2):
                add("tensor", lambda e, dc=dc, kt=kt, s_=s_, pp=po[dc][0]: e.matmul(
                    pp[:, :], lhsT=vv[:, kt, dc * 128:(dc + 1) * 128], rhs=et[s_][:], start=(kt == 0), stop=(kt == 63)),
                    R=["vv", ("et", s_)], W=[po[dc][1]])
            add("tensor", lambda e, pl=pl, s_=s_, kt=kt: e.matmul(pl[:, :], lhsT=ones[:], rhs=et[s_][:], start=(kt == 0), stop=(kt == 63)),
                R=["ones", ("et", s_)], W=[pkl])
            if kt != 63:
                return
            add("vector", lambda e, pl=pl: e.reciprocal(out=rl[:], in_=pl[:, :]), R=[pkl], W=["rl"])
            for dc in range(2):
                add("vector", lambda e, dc=dc, comp=comp, pp=po[dc][0]: e.tensor_tensor(out=oc[:, comp, dc, :], in0=pp[:, :], in1=rl[:], op=ALU.mult),
                    R=[po[dc][1], "rl"], W=["oc"])
            if comp != 1:
                return
            psn, pkn = C.next_ps("N", [7])
            for dc in range(2):
                add("vector", lambda e, dc=dc: e.scalar_tensor_tensor(out=od[:, dc, :], in0=oc[:, 1, dc, :], scalar=lsc[:, 2:3], in1=oc[:, 0, dc, :],
                                                                       op0=ALU.mult, op1=ALU.add), R=["oc", "lsc"], W=["od"])
                add("scalar", lambda e, dc=dc: e.activation(out=sq[:], in_=od[:, dc, :], func=AF.Square), R=["od"], W=["sq"])
                add("tensor", lambda e, dc=dc, psn=psn: e.matmul(psn[:, :], lhsT=ones[:], rhs=sq[:], start=(dc == 0), stop=(dc == 1)),
                    R=["sq", "ones"], W=[pkn])
            add("scalar", lambda e, psn=psn: e.activation(out=rstd[:], in_=psn[:, :], func=AF.Sqrt, scale=1.0 / 256, bias=epsc[:, 0:1]),
                R=[pkn, "epsc"], W=["rstd"])
            add("vector", lambda e: e.reciprocal(out=rstd[:], in_=rstd[:]), R=["rstd"], W=["rstd"])
            for dc in range(2):
                o = C.nxt("ob", 2)
                add("vector", lambda e, dc=dc, o=o: e.scalar_tensor_tensor(out=ob[o][:], in0=od[:, dc, :], scalar=sgt[:, dc:dc + 1], in1=rstd[:],
                                                                            op0=ALU.mult, op1=ALU.mult), R=["od", "sgt", "rstd"], W=[("ob", o)])
                dma("gpsimd", ydT_out[dc * 128:(dc + 1) * 128, qg * 512:(qg + 1) * 512], ob[o][:], R=[("ob", o)])

        queue = []
        for stp in steps:
            queue.append((stp, front(stp)))
            if len(queue) > LOOK:
                back(*queue.pop(0))
                yield
        while queue:
            back(*queue.pop(0))
            yield

    def s5_gen():
        BL = 512
        NBLK = SEQ // BL
        ug = C.sb("ug", [32, SEQ], BF16)
        yf = C.sb("yf", [32, SEQ], F32)
        pv = C.sb("pv", [128, 3], F32)
        sc = C.sb("sc", [128, 24], F32)
        sci = C.sb("sci", [128, 2], I32)
        Tr = C.sb("Tr", [128, BL + 1], F32)
        Ti = C.sb("Ti", [128, BL + 1], F32)
        tmp = C.sb("tmp", [128, BL], F32)
        Ar = C.sb("Ar", [128, BL], F32)
        Ai = C.sb("Ai", [128, BL], F32)
        Bb = C.sb("Bb", [32, 2, 128], BF16)
        Cb = C.sb("Cb", [128, 2, 32], BF16)
        dsk = C.sb("dsk", [32, 1], F32)
        m = [C.sb("m%d" % i, [128, BL], F32) for i in range(4)]
        wr = C.sb("wr", [128, BL], F32)
        wi = C.sb("wi", [128, BL], F32)
        sr = C.sb("sr", [128, BL], F32)
        si = C.sb("si", [128, BL], F32)
        xr = [C.sb("xr%d" % i, [128, BL], BF16) for i in range(2)]
        xi = [C.sb("xi%d" % i, [128, BL], BF16) for i in range(2)]
        init = C.sb("init", [128, 2], F32)
        yo = [C.sb("yo%d" % i, [32, BL], F32) for i in range(2)]
        TWO_PI = 2.0 * math.pi
        V = "vector"
        G = "gpsimd"

        def col(i):
            return sc[:, i:i + 1]

        def cmul_scalar(out_r, out_i, in_r, in_i, s_r, s_i, keys_in, keys_out, n):
            add(V, lambda e: e.tensor_scalar(out=tmp[:, 0:n], in0=in_i, scalar1=s_i, scalar2=None, op0=ALU.mult), R=keys_in, W=["tmp"])
            add(V, lambda e: e.scalar_tensor_tensor(out=out_r, in0=in_r, scalar=s_r, in1=tmp[:, 0:n], op0=ALU.mult, op1=ALU.subtract),
                R=keys_in + ["tmp"], W=keys_out)
            add(V, lambda e: e.tensor_scalar(out=tmp[:, 0:n], in0=in_i, scalar1=s_r, scalar2=None, op0=ALU.mult), R=keys_in, W=["tmp"])
            add(V, lambda e: e.scalar_tensor_tensor(out=out_i, in0=in_r, scalar=s_i, in1=tmp[:, 0:n], op0=ALU.mult, op1=ALU.add),
                R=keys_in + ["tmp"], W=keys_out)

        for gp in range(8):
            dma("sync", ug[:], uT_in[gp * 32:(gp + 1) * 32, :], W=["ug"])
            dma("sync", dsk[:], s5d[gp], W=["dsk"])
            for dr in range(2):
                dma("sync", pv[:], s5v[gp, dr], W=["pv"])
                dma("sync", Bb[:, 0, :], s5B[gp, dr, 0], W=["Bb"])
                dma("sync", Bb[:, 1, :], s5B[gp, dr, 1], W=["Bb"])
                dma("sync", Cb[:, 0, :], s5C[gp, dr, 0], W=["Cb"])
                dma("sync", Cb[:, 1, :], s5C[gp, dr, 1], W=["Cb"])
                K = ["sc"]
                add("scalar", lambda e: e.activation(out=col(0), in_=pv[:, 2:3], func=AF.Exp), R=["pv"], W=K)
                add("scalar", lambda e: e.activation(out=col(1), in_=pv[:, 0:1], func=AF.Exp, scale=col(0)), R=["pv"] + K, W=K)
                add(V, lambda e: e.tensor_tensor(out=col(2), in0=pv[:, 1:2], in1=col(0), op=ALU.mult), R=["pv"] + K, W=K)
                add(V, lambda e: e.tensor_scalar(out=col(3), in0=col(2), scalar1=0.5 * math.pi, scalar2=None, op0=ALU.add), R=K, W=K)
                add(V, lambda e: e.tensor_scalar(out=sc[:, 4:6], in0=sc[:, 2:4], scalar1=1.0 / TWO_PI, scalar2=None, op0=ALU.mult), R=K, W=K)
                add(V, lambda e: e.tensor_copy(out=sci[:, 0:2], in_=sc[:, 4:6]), R=K, W=["sci"])
                add(V, lambda e: e.tensor_copy(out=sc[:, 4:6], in_=sci[:, 0:2]), R=["sci"], W=K)
                add(V, lambda e: e.scalar_tensor_tensor(out=sc[:, 6:8], in0=sc[:, 4:6], scalar=-TWO_PI, in1=sc[:, 2:4], op0=ALU.mult, op1=ALU.add),
                    R=K, W=K)
                add(V, lambda e: e.tensor_scalar(out=sc[:, 6:8], in0=sc[:, 6:8], scalar1=-math.pi, scalar2=math.pi, op0=ALU.max, op1=ALU.min), R=K, W=K)
                add("scalar", lambda e: e.activation(out=sc[:, 8:10], in_=sc[:, 6:8], func=AF.Sin), R=K, W=K)
                add(G, lambda e: e.memset(Tr[:, 0:1], 1.0), W=["T"])
                add(G, lambda e: e.memset(Ti[:, 0:1], 0.0), W=["T"])
                add(V, lambda e: e.tensor_copy(out=Tr[:, 1:2], in_=col(9)), R=K, W=["T"])
                add(V, lambda e: e.tensor_copy(out=Ti[:, 1:2], in_=col(8)), R=K, W=["T"])
                n = 1
                while n < BL:
                    cmul_scalar(Tr[:, n + 1:2 * n + 1], Ti[:, n + 1:2 * n + 1], Tr[:, 1:n + 1], Ti[:, 1:n + 1],
                                Tr[:, n:n + 1], Ti[:, n:n + 1], ["T"], ["T"], n)
                    n *= 2
                add(V, lambda e: e.tensor_tensor(out=col(10), in0=col(1), in1=col(9), op=ALU.mult), R=K, W=K)
                add(V, lambda e: e.tensor_tensor(out=col(11), in0=col(1), in1=col(8), op=ALU.mult), R=K, W=K)
                add(V, lambda e: e.tensor_scalar(out=col(10), in0=col(10), scalar1=-1.0, scalar2=None, op0=ALU.add), R=K, W=K)
                add(V, lambda e: e.tensor_tensor(out=col(12), in0=col(11), in1=pv[:, 1:2], op=ALU.mult), R=K + ["pv"], W=K)
                add(V, lambda e: e.scalar_tensor_tensor(out=col(13), in0=col(10), scalar=pv[:, 0:1], in1=col(12), op0=ALU.mult, op1=ALU.add),
                    R=K + ["pv"], W=K)
                add(V, lambda e: e.tensor_tensor(out=col(12), in0=col(10), in1=pv[:, 1:2], op=ALU.mult), R=K + ["pv"], W=K)
                add(V, lambda e: e.scalar_tensor_tensor(out=col(14), in0=col(11), scalar=pv[:, 0:1], in1=col(12), op0=ALU.mult, op1=ALU.subtract),
                    R=K + ["pv"], W=K)
                add(V, lambda e: e.tensor_tensor(out=col(15), in0=pv[:, 0:1], in1=pv[:, 0:1], op=ALU.mult), R=["pv"], W=K)
                add(V, lambda e: e.scalar_tensor_tensor(out=col(15), in0=pv[:, 1:2], scalar=pv[:, 1:2], in1=col(15), op0=ALU.mult, op1=ALU.add),
                    R=K + ["pv"], W=K)
                add(V, lambda e: e.reciprocal(out=col(15), in_=col(15)), R=K, W=K)
                add(V, lambda e: e.tensor_tensor(out=col(16), in0=col(13), in1=col(15), op=ALU.mult), R=K, W=K)
                add(V, lambda e: e.tensor_tensor(out=col(17), in0=col(14), in1=col(15), op=ALU.mult), R=K, W=K)
                add(V, lambda e: e.tensor_scalar(out=col(18), in0=col(17), scalar1=-1.0, scalar2=None, op0=ALU.mult), R=K, W=K)
                add(V, lambda e: e.tensor_scalar(out=tmp[:], in0=Ti[:, 0:BL], scalar1=col(17), scalar2=None, op0=ALU.mult), R=["T"] + K, W=["tmp"])
                add(V, lambda e: e.scalar_tensor_tensor(out=Ar[:], in0=Tr[:, 0:BL], scalar=col(16), in1=tmp[:], op0=ALU.mult, op1=ALU.add),
                    R=["T", "tmp"] + K, W=["A"])
                add(V, lambda e: e.tensor_scalar(out=tmp[:], in0=Ti[:, 0:BL], scalar1=col(16), scalar2=None, op0=ALU.mult), R=["T"] + K, W=["tmp"])
                add(V, lambda e: e.scalar_tensor_tensor(out=Ai[:], in0=Tr[:, 0:BL], scalar=col(17), in1=tmp[:], op0=ALU.mult, op1=ALU.subtract),
                    R=["T", "tmp"] + K, W=["A"])
                add(G, lambda e: e.memset(init[:], 0.0), W=["init"])
                def emit_bu(bi):
                    blk = bi if dr == 0 else NBLK - 1 - bi
                    t0 = blk * BL
                    pa, pka = C.next_ps("S5a", [5])
                    pb, pkb = C.next_ps("S5b", [6])
                    add("tensor", lambda e, pa=pa, t0=t0: e.matmul(pa[:, :], lhsT=Bb[:, 0, :], rhs=ug[:, t0:t0 + BL], start=True, stop=True),
                        R=["Bb", "ug"], W=[pka])
                    add("tensor", lambda e, pb=pb, t0=t0: e.matmul(pb[:, :], lhsT=Bb[:, 1, :], rhs=ug[:, t0:t0 + BL], start=True, stop=True),
                        R=["Bb", "ug"], W=[pkb])
                    return pa, pka, pb, pkb
                nxt_bu = emit_bu(0)
                for bi in range(NBLK):
                    blk = bi if dr == 0 else NBLK - 1 - bi
                    t0 = blk * BL
                    pa, pka, pb, pkb = nxt_bu
                    if dr == 0:
                        bur, bui = pa[:, :], pb[:, :]
                    else:
                        bur, bui = pa[:, ::-1], pb[:, ::-1]
                    add(V, lambda e, bur=bur: e.tensor_tensor(out=m[0][:], in0=bur, in1=Ar[:], op=ALU.mult), R=[pka, "A"], W=["m0"])
                    add(V, lambda e, bui=bui: e.tensor_tensor(out=m[1][:], in0=bui, in1=Ai[:], op=ALU.mult), R=[pkb, "A"], W=["m1"])
                    add(V, lambda e, bui=bui: e.tensor_tensor(out=m[2][:], in0=bui, in1=Ar[:], op=ALU.mult), R=[pkb, "A"], W=["m2"])
                    add(V, lambda e, bur=bur: e.tensor_tensor(out=m[3][:], in0=bur, in1=Ai[:], op=ALU.mult), R=[pka, "A"], W=["m3"])
                    if bi + 1 < NBLK:
                        nxt_bu = emit_bu(bi + 1)
                    add(G, lambda e: e.tensor_tensor(out=wr[:], in0=m[0][:], in1=m[1][:], op=ALU.subtract), R=["m0", "m1"], W=["wr"])
                    add(G, lambda e: e.tensor_tensor(out=wi[:], in0=m[2][:], in1=m[3][:], op=ALU.add), R=["m2", "m3"], W=["wi"])
                    add(V, lambda e: e.tensor_tensor_scan(out=sr[:], data0=col(1).to_broadcast([128, BL]), data1=wr[:], initial=init[:, 0:1],
                                                          op0=ALU.mult, op1=ALU.add), R=["wr", "init"] + K, W=["sr"])
                    add(V, lambda e: e.tensor_tensor_scan(out=si[:], data0=col(1).to_broadcast([128, BL]), data1=wi[:], initial=init[:, 1:2],
                                                          op0=ALU.mult, op1=ALU.add), R=["wi", "init"] + K, W=["si"])
                    cmul_scalar(init[:, 0:1], init[:, 1:2], sr[:, BL - 1:BL], si[:, BL - 1:BL], Tr[:, BL:BL + 1], Ti[:, BL:BL + 1],
                                ["sr", "si", "T"], ["init"], 1)
                    add(V, lambda e: e.tensor_tensor(out=m[0][:], in0=sr[:], in1=Tr[:, 0:BL], op=ALU.mult), R=["sr", "T"], W=["m0"])
                    add(V, lambda e: e.tensor_tensor(out=m[1][:], in0=si[:], in1=Ti[:, 0:BL], op=ALU.mult), R=["si", "T"], W=["m1"])
                    add(V, lambda e: e.tensor_tensor(out=m[2][:], in0=si[:], in1=Tr[:, 0:BL], op=ALU.mult), R=["si", "T"], W=["m2"])
                    add(V, lambda e: e.tensor_tensor(out=m[3][:], in0=sr[:], in1=Ti[:, 0:BL], op=ALU.mult), R=["sr", "T"], W=["m3"])
                    xs = C.nxt("xs", 2)
                    if dr == 0:
                        a0, a1, a2, a3 = m[0][:], m[1][:], m[2][:], m[3][:]
                    else:
                        a0, a1, a2, a3 = m[0][:, ::-1], m[1][:, ::-1], m[2][:, ::-1], m[3][:, ::-1]
                    add(V, lambda e, xs=xs, a0=a0, a1=a1: e.tensor_tensor(out=xr[xs][:], in0=a0, in1=a1, op=ALU.subtract),
                        R=["m0", "m1"], W=[("xr", xs)])
                    add(V, lambda e, xs=xs, a2=a2, a3=a3: e.scalar_tensor_tensor(out=xi[xs][:], in0=a2, scalar=-1.0, in1=a3, op0=ALU.mult, op1=ALU.subtract),
                        R=["m2", "m3"], W=[("xi", xs)])
                    py, pky = C.next_ps("Y", [7])
                    add("tensor", lambda e, py=py, xs=xs: e.matmul(py[0:32, :], lhsT=Cb[:, 0, :], rhs=xr[xs][:], start=True, stop=False),
                        R=["Cb", ("xr", xs)], W=[pky])
                    add("tensor", lambda e, py=py, xs=xs: e.matmul(py[0:32, :], lhsT=Cb[:, 1, :], rhs=xi[xs][:], start=False, stop=True),
                        R=["Cb", ("xi", xs)], W=[pky])
                    if dr == 0:
                        add(V, lambda e, py=py, t0=t0: e.tensor_copy(out=yf[:, t0:t0 + BL], in_=py[0:32, :]), R=[pky], W=["yf"])
                    else:
                        y = C.nxt("yo", 2)
                        add(V, lambda e, py=py, t0=t0: e.tensor_tensor(out=yf[:, t0:t0 + BL], in0=py[0:32, :], in1=yf[:, t0:t0 + BL], op=ALU.add),
                            R=[pky, "yf"], W=["yf"])
                        add(V, lambda e, y=y, t0=t0: e.scalar_tensor_tensor(out=yo[y][:], in0=ug[:, t0:t0 + BL], scalar=dsk[:, 0:1], in1=yf[:, t0:t0 + BL],
                                                                             op0=ALU.mult, op1=ALU.add), R=["ug", "dsk", "yf"], W=[("yo", y)])
                        dma("gpsimd", ysT_out[gp * 32:(gp + 1) * 32, t0:t0 + BL], yo[y][:], R=[("yo", y)])
                    yield
    gens = []
    if do_attn:
        gens.append((attn_gen(), 8))
    if do_s5:
        gens.append((s5_gen(), 1))
    alive = [True] * len(gens)
    it = 0
    bg_every = max(1, 256 // max(1, bg_total)) if bg is not None else 0
    bg_per = max(1, -(-bg_total // 256)) if bg is not None else 0
    while any(alive):
        it += 1
        if bg is not None and it % bg_every == 0:
            bg.emit(C, bg_per, cast_engines=("scalar",))
        for gi, (g, reps) in enumerate(gens):
            if not alive[gi]:
                continue
            for _ in range(reps):
                try:
                    next(g)
                except StopIteration:
                    alive[gi] = False
                    break
    return C.finish()


def pack_s5(lam_re, lam_im, log_step, b_re, b_im, c_re, c_im, d_skip, g0, ng):
    ngp = ng // 2
    s5v = np.zeros((ngp, 2, 128, 3), np.float32)
    s5B = np.zeros((ngp, 2, 2, 32, 128), NPBF)
    s5C = np.zeros((ngp, 2, 2, 128, 32), NPBF)
    s5d = np.zeros((ngp, 32, 1), np.float32)
    for gp in range(ngp):
        for j in range(2):
            g = g0 + 2 * gp + j
            s5d[gp, j * 16:(j + 1) * 16, 0] = d_skip[g * 16:(g + 1) * 16]
            for dr in range(2):
                s5v[gp, dr, j * 64:(j + 1) * 64, 0] = lam_re[dr, g]
                s5v[gp, dr, j * 64:(j + 1) * 64, 1] = lam_im[dr, g]
                s5v[gp, dr, j * 64:(j + 1) * 64, 2] = log_step[dr, g]
                s5B[gp, dr, 0, j * 16:(j + 1) * 16, j * 64:(j + 1) * 64] = b_re[dr, g].T.astype(NPBF)
                s5B[gp, dr, 1, j * 16:(j + 1) * 16, j * 64:(j + 1) * 64] = b_im[dr, g].T.astype(NPBF)
                s5C[gp, dr, 0, j * 64:(j + 1) * 64, j * 16:(j + 1) * 16] = c_re[dr, g].T.astype(NPBF)
                s5C[gp, dr, 1, j * 64:(j + 1) * 64, j * 16:(j + 1) * 16] = c_im[dr, g].T.astype(NPBF)
    return {"s5v": s5v, "s5B": s5B, "s5C": s5C, "s5d": s5d}


def _g16(g):
    return np.ascontiguousarray(np.asarray(g, np.float32).reshape(16, 128).T)


def rope_tables_host(p0, n):
    f32 = np.float32
    pos = np.arange(p0, p0 + n, dtype=f32)
    inv = (f32(500000.0) ** (-(np.arange(0, 32, 2, dtype=f32)) / f32(32))).astype(f32)
    ang = (pos[:, None] * inv[None, :]).astype(f32)
    cos, sin = np.cos(ang).astype(f32), np.sin(ang).astype(f32)
    C32 = np.concatenate([cos.T, cos.T], axis=0)
    S32 = np.concatenate([-sin.T, sin.T], axis=0)
    return np.ascontiguousarray(C32), np.ascontiguousarray(S32)


def perm_table():
    p = np.zeros((32, 32), np.float32)
    for m in range(32):
        p[(m + 16) % 32, m] = 1.0
    return p.astype(NPBF)


def swap_cols(W, col0s):
    parts = []
    for c0 in col0s:
        parts.append(W[:, c0 + 16:c0 + 32])
        parts.append(W[:, c0:c0 + 16])
    return np.ascontiguousarray(np.concatenate(parts, axis=1))


_NC_CACHE = {}


def _get_nc(key, fn):
    if key not in _NC_CACHE:
        _NC_CACHE[key] = fn()
    return _NC_CACHE[key]


def _run(nc, in_maps):
    res = run_bass_kernel_spmd(nc, in_maps, core_ids=list(range(NCORE)))
    return res.results


def kernel_unfused(x, mem, norm_mix_g, norm_xa_g, norm_mem_g, xa_wq, xa_wkv, xa_wo, norm_ffn_g,
           ffn_w13, ffn_w2, ab_w_in, ab_w_out, s5_lambda_re, s5_lambda_im, s5_log_step,
           s5_b_re, s5_b_im, s5_c_re, s5_c_im, s5_d, s5_glu_w, s5_glu_b, diff_lambda,
           diff_subln_g, c_w_qkv, c_w_out, final_norm_g):
    A = lambda a: np.asarray(a)
    x = A(x).astype(np.float32, copy=False)
    mem = A(mem).astype(np.float32, copy=False)
    depth = 4
    cores = [(c // 4, (c % 4) * TOK) for c in range(NCORE)]
    hT = [np.ascontiguousarray(x[b, t0:t0 + TOK, :].T) for (b, t0) in cores]
    memT = [np.ascontiguousarray(mem[b].T) for b in range(NB)]
    ropes = [rope_tables_host(t0, TOK) for (b, t0) in cores]
    mask_tab = dilated_mask_table()

    def nxt_inputs(l):
        i = l // 2
        d = {"g_mix": _g16(A(norm_mix_g)[l])}
        if l % 2 == 0:
            W = A(ab_w_in)[i]
            d["w_in"] = W
            d["w_sw"] = swap_cols(W, [1024 + hc * 128 for hc in range(8)] + [2048 + hc * 128 for hc in range(8)])
        else:
            W = A(c_w_qkv)[i]
            d["w_in"] = W
            d["w_sw"] = swap_cols(W, [hc * 128 for hc in range(16)] + [2048 + hc * 128 for hc in range(16)])
        return d

    def prev_inputs(l):
        i = l // 2
        d = {"g_xa": _g16(A(norm_xa_g)[l]), "g_mem": _g16(A(norm_mem_g)[l]), "g_ffn": _g16(A(norm_ffn_g)[l]),
             "wq": A(xa_wq)[l], "wkv": A(xa_wkv)[l], "wo": A(xa_wo)[l], "w13": A(ffn_w13)[l], "w2": A(ffn_w2)[l]}
        if l % 2 == 0:
            d["w_out"] = A(ab_w_out)[i]
            d["glu_w"] = A(s5_glu_w)[i]
            d["glu_b"] = np.ascontiguousarray(A(s5_glu_b)[i].reshape(8, 128).T)
        else:
            d["w_out"] = A(c_w_out)[i]
        return d

    typ = lambda l: "even" if l % 2 == 0 else "odd"
    mix = None
    out = None
    for l in range(depth + 1):
        prev = typ(l - 1) if l > 0 else None
        nxt = typ(l) if l < depth else None
        final = (l == depth)
        nc = _get_nc(("T", prev, nxt, final), lambda: build_T(prev, nxt, final))
        shared = {}
        if prev is not None:
            shared.update(prev_inputs(l - 1))
        if nxt is not None:
            shared.update(nxt_inputs(l))
        if final:
            shared["g_fin"] = _g16(A(final_norm_g))
        in_maps = []
        for c, (b, t0) in enumerate(cores):
            m = dict(shared)
            m["hT_in"] = hT[c]
            if prev is not None:
                m["memT"] = memT[b]
                m.update(mix[c])
            if nxt is not None:
                m["ropeC"], m["ropeS"] = ropes[c]
                m["perm_in"] = perm_table()
            in_maps.append(m)
        res = _run(nc, in_maps)
        if final:
            out = np.empty((NB, SEQ, D), np.float32)
            for c, (b, t0) in enumerate(cores):
                out[b, t0:t0 + TOK, :] = res[c]["outT"].T
            break
        hT = [res[c]["hT_out"] for c in range(NCORE)]
        i = l // 2
        cat = lambda name, b, sl: np.ascontiguousarray(np.concatenate([res[4 * b + r][name][sl] for r in range(4)], axis=-1))
        if nxt == "even":
            lambda_init = 0.8 - 0.6 * math.exp(-0.3 * l)
            nch = _get_nc(("He", l), lambda: build_H_even(lambda_init))
            in_maps = []
            lam_rep = np.ascontiguousarray(np.broadcast_to(A(diff_lambda)[i].astype(np.float32), (128, 4, 128)))
            sgi = np.ascontiguousarray(A(diff_subln_g)[i].astype(np.float32).reshape(2, 128).T)
            for c in range(NCORE):
                b, hd = c // 4, c % 4
                m = {"qT_in": cat("qT_out", b, slice(2 * hd, 2 * hd + 2)),
                     "kT_in": cat("kT_out", b, slice(2 * hd, 2 * hd + 2)),
                     "v_in": np.ascontiguousarray(np.concatenate([res[4 * b + r]["v_out"][:, hd * 256:(hd + 1) * 256] for r in range(4)], axis=0)),
                     "uT_in": cat("uT_out", b, slice(hd * 256, (hd + 1) * 256)),
                     "lam_in": lam_rep, "sg_in": sgi}
                m.update(pack_s5(A(s5_lambda_re)[i], A(s5_lambda_im)[i], A(s5_log_step)[i], A(s5_b_re)[i], A(s5_b_im)[i],
                                 A(s5_c_re)[i], A(s5_c_im)[i], A(s5_d)[i], 16 * hd, 16))
                in_maps.append(m)
            rh = _run(nch, in_maps)
            mix = []
            for c, (b, t0) in enumerate(cores):
                mix.append({"ys5T": np.ascontiguousarray(np.concatenate([rh[4 * b + hd]["ysT_out"][:, t0:t0 + TOK] for hd in range(4)], axis=0)),
                            "ydiffT": np.ascontiguousarray(np.concatenate([rh[4 * b + hd]["ydT_out"][:, t0:t0 + TOK] for hd in range(4)], axis=0))})
        else:
            nch = _get_nc(("Ho",), build_H_odd)
            in_maps = []
            for c in range(NCORE):
                b, hq = c // 4, c % 4
                vv = np.concatenate([res[4 * b + r]["v_out"][:, hq * 512:(hq + 1) * 512] for r in range(4)], axis=0)
                in_maps.append({"qT_in": cat("qT_out", b, slice(4 * hq, 4 * hq + 4)),
                                "kT_in": cat("kT_out", b, slice(4 * hq, 4 * hq + 4)),
                                "v_in": np.ascontiguousarray(vv.reshape(SEQ, 4, 128)),
                                "mask_in": mask_tab})
            rh = _run(nch, in_maps)
            mix = []
            for c, (b, t0) in enumerate(cores):
                mix.append({"oT_in": np.ascontiguousarray(np.concatenate([rh[4 * b + hq]["oT_out"][:, t0:t0 + TOK] for hq in range(4)], axis=0))})
    return out


def build_fused(depth=4):
    nc = bass.Bass("TRN2", target_bir_lowering=False)

    def ext(name, shape, dt=F32):
        return nc.dram_tensor(name, list(shape), dt, kind="ExternalInput").ap()

    def internal(name, shape, dt=F32):
        return nc.dram_tensor(name, list(shape), dt).ap()

    x_hT = ext("x_hT", [D, SEQ])
    memT = ext("memT", [D, 256])
    ropeC = ext("ropeC", [32, SEQ])
    ropeS = ext("ropeS", [32, SEQ])
    mask_in = ext("mask_in", [128, 20, 512], BF16)
    perm_in = ext("perm_in", [32, 32], BF16)
    g_fin = ext("g_fin", [128, 16])
    outT = nc.dram_tensor("outT", [D, SEQ], F32, kind="ExternalOutput").ap()
    L = []
    for l in range(depth):
        d = {}
        for nm in ("g_mix", "g_xa", "g_mem", "g_ffn"):
            d[nm] = ext("%s_%d" % (nm, l), [128, 16])
        d["wq"] = ext("wq_%d" % l, [D, D])
        d["wkv"] = ext("wkv_%d" % l, [D, 2 * D])
        d["wo"] = ext("wo_%d" % l, [D, D])
        d["w13"] = ext("w13_%d" % l, [D, 2 * FFN])
        d["w2"] = ext("w2_%d" % l, [FFN, D])
        d["w_out"] = ext("w_out_%d" % l, [D, D])
        if l % 2 == 0:
            d["w_in"] = ext("w_in_%d" % l, [D, 4096])
            d["w_sw"] = ext("w_sw_%d" % l, [D, 512])
            d["glu_w"] = ext("glu_w_%d" % l, [1024, 1024])
            d["glu_b"] = ext("glu_b_%d" % l, [128, 8])
            d["lam_in"] = ext("lam_in_%d" % l, [128, 4, 128])
            d["sg_in"] = ext("sg_in_%d" % l, [128, 2])
            d["s5v"] = ext("s5v_%d" % l, [32, 2, 128, 3])
            d["s5B"] = ext("s5B_%d" % l, [32, 2, 2, 32, 128], BF16)
            d["s5C"] = ext("s5C_%d" % l, [32, 2, 2, 128, 32], BF16)
            d["s5d"] = ext("s5d_%d" % l, [32, 32, 1])
        else:
            d["w_in"] = ext("w_in_%d" % l, [D, 6144])
            d["w_sw"] = ext("w_sw_%d" % l, [D, 1024])
        L.append(d)
    hT = internal("hT", [D, SEQ])
    uT = internal("uT", [1024, SEQ], BF16)
    qTe = internal("qTe", [8, 128, SEQ], BF16)
    kTe = internal("kTe", [8, 128, SEQ], BF16)
    ve = internal("ve", [SEQ, 1024], BF16)
    qTo = internal("qTo", [16, 128, SEQ], BF16)
    kTo = internal("kTo", [16, 128, SEQ], BF16)
    vo = internal("vo", [SEQ, 2048], BF16)
    ys5T = internal("ys5T", [1024, SEQ])
    ydiffT = internal("ydiffT", [1024, SEQ], BF16)
    oT = internal("oT", [2048, SEQ], BF16)
    typ = lambda l: "even" if l % 2 == 0 else "odd"
    bgs = {}

    def launch_wio(l):
        prev = typ(l - 1) if l > 0 else None
        nxt = typ(l) if l < depth else None
        wio = {}
        if prev is not None:
            wio.update({nm: L[l - 1][nm] for nm in ("w_out", "wq", "wkv", "wo", "w13", "w2")})
            if prev == "even":
                wio["glu_w"] = L[l - 1]["glu_w"]
        if nxt is not None:
            wio.update({"w_in": L[l]["w_in"], "w_sw": L[l]["w_sw"]})
        return t_weight_specs(prev, nxt, wio)

    for l in range(depth + 1):
        prev = typ(l - 1) if l > 0 else None
        nxt = typ(l) if l < depth else None
        final = (l == depth)
        wio = {}
        if prev is not None:
            wio.update({nm: L[l - 1][nm] for nm in ("w_out", "wq", "wkv", "wo", "w13", "w2")})
            if prev == "even":
                wio["glu_w"] = L[l - 1]["glu_w"]
        if nxt is not None:
            wio.update({"w_in": L[l]["w_in"], "w_sw": L[l]["w_sw"]})
        if bgs.get(l) is None:
            wb = emit_W(nc, t_weight_specs(prev, nxt, wio))
        else:
            bgs[l].flush()
            wb = bgs[l].out
        TSUB = 4096
        for r in range(SEQ // TSUB):
            ts = slice(r * TSUB, (r + 1) * TSUB)
            io = {"hT_in": (x_hT if l == 0 else hT)[:, ts]}
            if prev is not None:
                pl = L[l - 1]
                for nm in ("w_out", "g_xa", "g_mem", "wq", "wkv", "wo", "g_ffn", "w13", "w2"):
                    io[nm] = pl[nm]
                io["memT"] = memT
                if prev == "even":
                    io["ys5T"] = ys5T[:, ts]
                    io["ydiffT"] = ydiffT[:, ts]
                    io["glu_w"] = pl["glu_w"]
                    io["glu_b"] = pl["glu_b"]
                else:
                    io["oT_in"] = oT[:, ts]
            if nxt is not None:
                nl = L[l]
                io["g_mix"] = nl["g_mix"]
                io["ropeC"] = ropeC[:, ts]
                io["ropeS"] = ropeS[:, ts]
                io["perm_in"] = perm_in
                io["w_in"] = nl["w_in"]
                io["w_sw"] = nl["w_sw"]
                io["hT_out"] = hT[:, ts]
                if nxt == "even":
                    io["uT_out"] = uT[:, ts]
                    io["qT_out"] = qTe[:, :, ts]
                    io["kT_out"] = kTe[:, :, ts]
                    io["v_out"] = ve[ts, :]
                else:
                    io["qT_out"] = qTo[:, :, ts]
                    io["kT_out"] = kTo[:, :, ts]
                    io["v_out"] = vo[ts, :]
            if final:
                io["g_fin"] = g_fin
                io["outT"] = outT[:, ts]
            build_T(prev, nxt, final, nc=nc, io=io, wb=wb, ntok=TSUB)
        if final:
            break
        bg = BgW(nc, launch_wio(l + 1))
        bgs[l + 1] = bg
        bg_total = -(-len(bg.jobs) // 4)
        if nxt == "even":
            lambda_init = 0.8 - 0.6 * math.exp(-0.3 * l)
            nl = L[l]
            for hd in range(4):
                cs = slice(hd * 256, (hd + 1) * 256)
                io = {"qT_in": qTe[2 * hd:2 * hd + 2], "kT_in": kTe[2 * hd:2 * hd + 2], "v_in": ve[:, cs],
                      "lam_in": nl["lam_in"], "sg_in": nl["sg_in"], "ydT_out": ydiffT[cs, :], "uT_in": uT[cs, :],
                      "s5v": nl["s5v"][8 * hd:8 * hd + 8], "s5B": nl["s5B"][8 * hd:8 * hd + 8], "s5C": nl["s5C"][8 * hd:8 * hd + 8],
                      "s5d": nl["s5d"][8 * hd:8 * hd + 8], "ysT_out": ys5T[cs, :]}
                build_H_even(lambda_init, nc=nc, io=io, bg=bg, bg_total=bg_total)
        else:
            for hq in range(4):
                io = {"qT_in": qTo[4 * hq:4 * hq + 4], "kT_in": kTo[4 * hq:4 * hq + 4],
                      "v_in": vo[:, hq * 512:(hq + 1) * 512].rearrange("s (h e) -> s h e", h=4),
                      "mask_in": mask_in, "oT_out": oT[hq * 512:(hq + 1) * 512, :]}
                build_H_odd(nc=nc, io=io, bg=bg, bg_total=bg_total)
    return nc


def kernel_fused(x, mem, norm_mix_g, norm_xa_g, norm_mem_g, xa_wq, xa_wkv, xa_wo, norm_ffn_g,
                 ffn_w13, ffn_w2, ab_w_in, ab_w_out, s5_lambda_re, s5_lambda_im, s5_log_step,
                 s5_b_re, s5_b_im, s5_c_re, s5_c_im, s5_d, s5_glu_w, s5_glu_b, diff_lambda,
                 diff_subln_g, c_w_qkv, c_w_out, final_norm_g):
    A = lambda a: np.asarray(a)
    x = A(x).astype(np.float32, copy=False)
    mem = A(mem).astype(np.float32, copy=False)
    depth = 4
    nc = _get_nc(("fused",), build_fused)
    C32, S32 = rope_tables_host(0, SEQ)
    shared = {"ropeC": C32, "ropeS": S32, "mask_in": dilated_mask_table(), "g_fin": _g16(A(final_norm_g)), "perm_in": perm_table()}
    for l in range(depth):
        i = l // 2
        shared["g_mix_%d" % l] = _g16(A(norm_mix_g)[l])
        shared["g_xa_%d" % l] = _g16(A(norm_xa_g)[l])
        shared["g_mem_%d" % l] = _g16(A(norm_mem_g)[l])
        shared["g_ffn_%d" % l] = _g16(A(norm_ffn_g)[l])
        shared["wq_%d" % l] = A(xa_wq)[l]
        shared["wkv_%d" % l] = A(xa_wkv)[l]
        shared["wo_%d" % l] = A(xa_wo)[l]
        shared["w13_%d" % l] = A(ffn_w13)[l]
        shared["w2_%d" % l] = A(ffn_w2)[l]
        if l % 2 == 0:
            W = A(ab_w_in)[i]
            shared["w_in_%d" % l] = W
            shared["w_sw_%d" % l] = swap_cols(W, [1024 + hc * 128 for hc in range(8)] + [2048 + hc * 128 for hc in range(8)])
            shared["w_out_%d" % l] = A(ab_w_out)[i]
            shared["glu_w_%d" % l] = A(s5_glu_w)[i]
            shared["glu_b_%d" % l] = np.ascontiguousarray(A(s5_glu_b)[i].reshape(8, 128).T)
            shared["lam_in_%d" % l] = np.ascontiguousarray(np.broadcast_to(A(diff_lambda)[i].astype(np.float32), (128, 4, 128)))
            shared["sg_in_%d" % l] = np.ascontiguousarray(A(diff_subln_g)[i].astype(np.float32).reshape(2, 128).T)
            pk = pack_s5(A(s5_lambda_re)[i], A(s5_lambda_im)[i], A(s5_log_step)[i], A(s5_b_re)[i], A(s5_b_im)[i],
                         A(s5_c_re)[i], A(s5_c_im)[i], A(s5_d)[i], 0, 64)
            for k, v in pk.items():
                shared["%s_%d" % (k, l)] = v
        else:
            W = A(c_w_qkv)[i]
            shared["w_in_%d" % l] = W
            shared["w_sw_%d" % l] = swap_cols(W, [hc * 128 for hc in range(16)] + [2048 + hc * 128 for hc in range(16)])
            shared["w_out_%d" % l] = A(c_w_out)[i]
    in_maps = []
    for b in range(NB):
        m = dict(shared)
        m["x_hT"] = np.ascontiguousarray(x[b].T)
        m["memT"] = np.ascontiguousarray(mem[b].T)
        in_maps.append(m)
    res = run_bass_kernel_spmd(nc, in_maps, core_ids=list(range(NB))).results
    out = np.empty((NB, SEQ, D), np.float32)
    for b in range(NB):
        out[b] = res[b]["outT"].T
    return out


def kernel(**inputs):
    return kernel_fused(**inputs)
```

```python
import contextlib
import math
import numpy as np
import ml_dtypes
import concourse.bass as bass
import concourse.mybir as mybir
from concourse.bass_utils import run_bass_kernel_spmd

F32 = mybir.dt.float32
BF16 = mybir.dt.bfloat16
I32 = mybir.dt.int32
ALU = mybir.AluOpType
AF = mybir.ActivationFunctionType
NPBF = ml_dtypes.bfloat16

D = 2048
SEQ = 8192
NB = 2
NCORE = 8
TOK = 2048
TG = 512
FFN = 5632
EPS = 1e-6
SKIP = set()
NGRP_OVERRIDE = 0


class Prog:
    DMA_K = 6

    def __init__(self, nc):
        self.nc = nc
        self.ops = []
        self.last_w = {}
        self.readers = {}

    def add(self, eng, fn, R=(), W=(), dma=False):
        i = len(self.ops)
        deps = set()
        for k in list(R) + list(W):
            if k in self.last_w:
                deps.add(self.last_w[k])
        for k in W:
            for r in self.readers.get(k, ()):
                deps.add(r)
        for k in R:
            lst = self.readers.setdefault(k, [])
            if not dma:
                lst[:] = [r for r in lst if self.ops[r]["dma"] or self.ops[r]["eng"] != eng]
            lst.append(i)
        for k in W:
            self.last_w[k] = i
            self.readers[k] = []
        self.ops.append(dict(eng=eng, fn=fn, deps=deps, dma=dma, sig=dma, signal=None))
        return i

    def dma(self, eng, out, in_, R=(), W=()):
        return self.add(eng, lambda e: e.dma_start(out=out, in_=in_), R, W, dma=True)

    def emit(self):
        nc = self.nc
        ops = self.ops
        for i, o in enumerate(ops):
            nd = set()
            for d in o["deps"]:
                if ops[d]["eng"] == "tensor" and o["eng"] == "tensor" and not ops[d]["dma"] and not o["dma"]:
                    continue
                nd.add(d)
            o["deps"] = nd
            for d in nd:
                ops[d]["sig"] = True
        engs = ["tensor", "vector", "scalar", "gpsimd", "sync"]
        pfx = getattr(self, "pfx", "")
        self.sems = []

        def mk(name):
            h = nc.alloc_semaphore(name=pfx + name)
            self.sems.append(h)
            return h
        csem = {e: mk("c_" + e) for e in engs}
        dsem = {e: [mk("d_%s%d" % (e, k)) for k in range(self.DMA_K)] for e in ["sync", "gpsimd"]}
        ccount = {e: 0 for e in engs}
        dcount = {e: 0 for e in dsem}
        streams = {e: [] for e in engs}
        waited = {e: {} for e in engs}
        final = {}
        for i, o in enumerate(ops):
            E = o["eng"]
            st = streams[E]
            w = waited[E]
            for d in sorted(o["deps"]):
                sem, val = ops[d]["signal"]
                if w.get(id(sem), 0) < val:
                    w[id(sem)] = val
                    st.append((lambda e, sem=sem, val=val: e.wait_ge(sem, val)))
            if o["dma"]:
                n = dcount[E]
                dcount[E] += 1
                sem = dsem[E][n % self.DMA_K]
                val = 16 * (n // self.DMA_K + 1)
                if n >= self.DMA_K and w.get(id(sem), 0) < val - 16:
                    w[id(sem)] = val - 16
                    st.append((lambda e, sem=sem, val=val: e.wait_ge(sem, val - 16)))
                o["signal"] = (sem, val)
                final[id(sem)] = (sem, val)
                st.append((lambda e, fn=o["fn"], sem=sem: fn(e).then_inc(sem, 16)))
            elif o["sig"]:
                ccount[E] += 1
                sem = csem[E]
                val = ccount[E]
                o["signal"] = (sem, val)
                st.append((lambda e, fn=o["fn"], sem=sem: fn(e).then_inc(sem, 1)))
            else:
                st.append((lambda e, fn=o["fn"]: fn(e)))
        for sem, val in final.values():
            if waited["sync"].get(id(sem), 0) < val:
                streams["sync"].append((lambda e, sem=sem, val=val: e.wait_ge(sem, val)))
        with nc.Block() as block:
            @block.tensor
            def _(e):
                for f in streams["tensor"]:
                    f(e)

            @block.vector
            def _(e):
                for f in streams["vector"]:
                    f(e)

            @block.scalar
            def _(e):
                for f in streams["scalar"]:
                    f(e)

            @block.gpsimd
            def _(e):
                for f in streams["gpsimd"]:
                    f(e)

            @block.sync
            def _(e):
                for f in streams["sync"]:
                    f(e)
        self.stats = {e: len(streams[e]) for e in engs}
        self.counts = (dict(ccount), dict(dcount))


class Ctx:
    count = 0

    def __init__(self, nc=None, io=None):
        self.io = io
        Ctx.count += 1
        self.pfx = "p%d_" % Ctx.count
        self.nc = nc if nc is not None else bass.Bass("TRN2", target_bir_lowering=False)
        self.P = Prog(self.nc)
        self.S = contextlib.ExitStack()
        self.P.stack = self.S
        self.P.pfx = self.pfx
        self.nps = 0
        self.ps = []
        self.rot = {}
        self.cast_i = 0

    def din(self, name, shape, dt=F32):
        if self.io is not None:
            ap = self.io[name]
            assert list(ap.shape) == list(shape), (name, ap.shape, shape)
            return ap
        return self.nc.dram_tensor(name, list(shape), dt, kind="ExternalInput").ap()

    def dout(self, name, shape, dt=F32):
        if self.io is not None:
            ap = self.io[name]
            assert list(ap.shape) == list(shape), (name, ap.shape, shape)
            return ap
        return self.nc.dram_tensor(name, list(shape), dt, kind="ExternalOutput").ap()

    def finish(self):
        self.P.emit()
        if self.io is not None:
            self.nc.all_engine_barrier()
            self.nc.clear_and_free_semaphores(self.P.sems)
            self.nc.all_engine_barrier()
        self.S.close()
        return self.nc

    def sb(self, name, shape, dt):
        return self.S.enter_context(self.nc.sbuf_tensor(self.pfx + name, list(shape), dt))

    def alloc_psum(self, n=8):
        self.ps = [self.S.enter_context(self.nc.psum_tensor(self.pfx + "ps%d" % i, [128, 512], F32)) for i in range(n)]

    def next_ps(self, group="main", banks=None):
        banks = banks if banks is not None else list(range(len(self.ps)))
        i = self.rot.get(group, 0)
        self.rot[group] = i + 1
        b = banks[i % len(banks)]
        return self.ps[b], ("ps", b)

    def nxt(self, group, n):
        i = self.rot.get(group, 0)
        self.rot[group] = i + 1
        return i % n


class BgW:
    count = 0

    def __init__(self, nc, specs):
        BgW.count += 1
        self.nc = nc
        self.jobs = []
        self.out = {}
        self.pos = 0
        for key, W, blk in specs:
            K_, N_ = W.shape
            nci, nb = K_ // 128, N_ // blk
            Wb = nc.dram_tensor("bgw%d_%s" % (BgW.count, key), [nb, 128, nci, blk], BF16).ap()
            for j in range(nb):
                for ci0 in range(0, nci, 16):
                    n = min(16, nci - ci0)
                    src = W[ci0 * 128:(ci0 + n) * 128, j * blk:(j + 1) * blk].rearrange("(c p) o -> p c o", p=128)
                    self.jobs.append((src, Wb[j, :, ci0:ci0 + n, :], n, blk))
            self.out[key] = (Wb, blk)

    def remaining(self):
        return len(self.jobs) - self.pos

    def emit(self, C, count=1, cast_engines=("scalar",)):
        if not hasattr(C, "bg_st"):
            C.bg_st = [C.sb("bgst%d" % i, [128, 16, 128], F32) for i in range(2)]
            C.bg_bf = [C.sb("bgbf%d" % i, [128, 16, 128], BF16) for i in range(2)]
            C.bg_k = 0
        P = C.P
        for _ in range(count):
            if self.pos >= len(self.jobs):
                return
            src, dst, n, blk = self.jobs[self.pos]
            self.pos += 1
            k = C.bg_k
            C.bg_k += 1
            s = k % 2
            st, bf = C.bg_st[s], C.bg_bf[s]
            P.dma("sync", st[:, 0:n, 0:blk], src, W=[("bgst", s)])
            ce = cast_engines[k % len(cast_engines)]
            if ce == "scalar":
                P.add("scalar", lambda e, st=st, bf=bf, n=n, blk=blk: e.activation(out=bf[:, 0:n, 0:blk], in_=st[:, 0:n, 0:blk], func=AF.Copy),
                      R=[("bgst", s)], W=[("bgbf", s)])
            else:
                P.add(ce, lambda e, st=st, bf=bf, n=n, blk=blk: e.tensor_copy(out=bf[:, 0:n, 0:blk], in_=st[:, 0:n, 0:blk]),
                      R=[("bgst", s)], W=[("bgbf", s)])
            P.dma("gpsimd", dst, bf[:, 0:n, 0:blk], R=[("bgbf", s)])

    def flush(self):
        if self.remaining() == 0:
            return
        C = Ctx(self.nc, io={})
        self.emit(C, self.remaining(), cast_engines=("vector", "scalar"))
        C.finish()


def emit_W(nc, specs):
    C = Ctx(nc, io={})
    P = C.P
    NS = 4
    wst = [C.sb("wst%d" % i, [128, 16, 128], F32) for i in range(NS)]
    wbf = [C.sb("wbf%d" % i, [128, 16, 128], BF16) for i in range(NS)]
    out = {}
    k = 0
    for key, W, blk in specs:
        K_, N_ = W.shape
        nci, nb = K_ // 128, N_ // blk
        Wb = nc.dram_tensor(C.pfx + "wb_" + key, [nb, 128, nci, blk], BF16).ap()
        for j in range(nb):
            for ci0 in range(0, nci, 16):
                n = min(16, nci - ci0)
                s = k % NS
                src = W[ci0 * 128:(ci0 + n) * 128, j * blk:(j + 1) * blk].rearrange("(c p) o -> p c o", p=128)
                P.dma("sync", wst[s][:, 0:n, 0:blk], src, W=[("wst", s)])
                ce = ["vector", "scalar"][k % 2]
                if ce == "scalar":
                    P.add("scalar", lambda e, s=s, n=n, blk=blk: e.activation(out=wbf[s][:, 0:n, 0:blk], in_=wst[s][:, 0:n, 0:blk], func=AF.Copy),
                          R=[("wst", s)], W=[("wbf", s)])
                else:
                    P.add(ce, lambda e, s=s, n=n, blk=blk: e.tensor_copy(out=wbf[s][:, 0:n, 0:blk], in_=wst[s][:, 0:n, 0:blk]),
                          R=[("wst", s)], W=[("wbf", s)])
                P.dma("sync" if k % 2 else "gpsimd", Wb[j, :, ci0:ci0 + n, :], wbf[s][:, 0:n, 0:blk], R=[("wbf", s)])
                k += 1
        out[key] = (Wb, blk)
    C.finish()
    return out


def t_weight_specs(prev, nxt, io):
    specs = []
    if prev is not None:
        if prev == "even":
            specs.append(("glu_w", io["glu_w"], 128))
        for nm in ("w_out", "wq", "wkv", "wo", "w13", "w2"):
            specs.append((nm, io[nm], 128))
    if nxt is not None:
        specs.append(("w_in", io["w_in"], 128))
    return specs


def build_T(prev, nxt, final, nc=None, io=None, wb=None, ntok=None):
    C = Ctx(nc, io)
    nc, P = C.nc, C.P
    add, dma = P.add, P.dma
    ntok = ntok or TOK
    ngrp = NGRP_OVERRIDE or (ntok // TG)

    hT_in = C.din("hT_in", [D, ntok])
    if prev is not None:
        if prev == "even":
            ys5T = C.din("ys5T", [1024, ntok])
            ydiffT = C.din("ydiffT", [1024, ntok], BF16)
            glu_w = C.din("glu_w", [1024, 1024])
            glu_b = C.din("glu_b", [128, 8])
        else:
            oT_in = C.din("oT_in", [D, ntok], BF16)
        w_out = C.din("w_out", [D, D])
        memT = C.din("memT", [D, 256])
        g_xa = C.din("g_xa", [128, 16])
        g_mem = C.din("g_mem", [128, 16])
        wq = C.din("wq", [D, D])
        wkv = C.din("wkv", [D, 2 * D])
        wo = C.din("wo", [D, D])
        g_ffn = C.din("g_ffn", [128, 16])
        w13 = C.din("w13", [D, 2 * FFN])
        w2 = C.din("w2", [FFN, D])
    if nxt is not None:
        g_mix = C.din("g_mix", [128, 16])
        ropeC = C.din("ropeC", [32, ntok])
        ropeS = C.din("ropeS", [32, ntok])
        perm_in = C.din("perm_in", [32, 32], BF16)
        if nxt == "even":
            w_in = C.din("w_in", [D, 4096])
            w_sw = C.din("w_sw", [D, 16 * 32])
            uT_out = C.dout("uT_out", [1024, ntok], BF16)
            qT_out = C.dout("qT_out", [8, 128, ntok], BF16)
            kT_out = C.dout("kT_out", [8, 128, ntok], BF16)
            v_out = C.dout("v_out", [ntok, 1024], BF16)
        else:
            w_in = C.din("w_in", [D, 6144])
            w_sw = C.din("w_sw", [D, 32 * 32])
            qT_out = C.dout("qT_out", [16, 128, ntok], BF16)
            kT_out = C.dout("kT_out", [16, 128, ntok], BF16)
            v_out = C.dout("v_out", [ntok, 2048], BF16)
        hT_out = C.dout("hT_out", [D, ntok])
    if final:
        g_fin = C.din("g_fin", [128, 16])
        outT = C.dout("outT", [D, ntok])

    if wb is None:
        wio = {}
        if prev is not None:
            wio.update({"w_out": w_out, "wq": wq, "wkv": wkv, "wo": wo, "w13": w13, "w2": w2})
            if prev == "even":
                wio["glu_w"] = glu_w
        if nxt is not None:
            wio.update({"w_in": w_in, "w_sw": w_sw})
        wb = emit_W(nc, t_weight_specs(prev, nxt, wio))
    wbmap = {}
    if prev is not None:
        for nm, ap in (("w_out", w_out), ("wq", wq), ("wkv", wkv), ("wo", wo), ("w13", w13), ("w2", w2)):
            wbmap[id(ap)] = wb[nm]
        if prev == "even":
            wbmap[id(glu_w)] = wb["glu_w"]
    if nxt is not None:
        wbmap[id(w_in)] = wb["w_in"]

    C.alloc_psum(8)
    hT = C.sb("hT", [128, 16, TG], F32)
    hn = C.sb("hn", [128, 16, TG], BF16)
    sq = [C.sb("sq%d" % i, [128, TG], BF16) for i in range(2)]
    rstd = C.sb("rstd", [128, TG], F32)
    ones = C.sb("ones", [128, 128], BF16)
    epsc = C.sb("epsc", [128, 1], F32)
    NW = 6
    wbf = [C.sb("wbf%d" % i, [128, 16, 128], BF16) for i in range(NW)]
    gvec = C.sb("gvec", [128, 5, 16], F32)
    add("gpsimd", lambda e: e.memset(ones[:], 1.0), W=["ones"])
    add("gpsimd", lambda e: e.memset(epsc[:], EPS), W=["epsc"])

    def load_g(slot, src):
        dma("sync", gvec[:, slot, :], src[:, :], W=[("gvec", slot)])

    def load_w(Wd, ci0, n, c0, ncol=128):
        Wb, blk = wbmap[id(Wd)]
        assert ncol == blk and c0 % blk == 0
        s = C.nxt("w", NW)
        dma("sync", wbf[s][:, 0:n, 0:ncol], Wb[c0 // blk, :, ci0:ci0 + n, :], W=[("wbf", s)])
        return wbf[s], ("wbf", s)

    def linear(Wd, n_ci, cols, x_tile, x_keys, evac, ntok=TG, banks=None):
        for c0 in cols:
            ps, pk = C.next_ps("main", banks)
            blocks = []
            ci = 0
            while ci < n_ci:
                n = min(16, n_ci - ci)
                blocks.append((ci, n))
                ci += n
            first = True
            for (ci0, n) in blocks:
                wt, wk = load_w(Wd, ci0, n, c0)
                for j in range(n):
                    last = (ci0 + j == n_ci - 1)
                    xk = [("hn", ci0 + j)] if list(x_keys) == ["hn"] else list(x_keys)
                    add("tensor", lambda e, wt=wt, j=j, cj=ci0 + j, ps=ps, first=first, last=last:
                        e.matmul(ps[:, 0:ntok], lhsT=wt[:, j, :], rhs=x_tile[:, cj, 0:ntok], start=first, stop=last),
                        R=[wk] + xk, W=[pk])
                    first = False
            evac(c0, ps, pk)

    def rmsnorm(src_tile, src_keys, gslot, dst_tile, dst_key, ntok=TG, nchunk=16):
        ps, pk = C.next_ps("main")
        for c in range(nchunk):
            s = C.nxt("sq", 2)
            add("scalar", lambda e, s=s, c=c: e.activation(out=sq[s][:, 0:ntok], in_=src_tile[:, c, 0:ntok], func=AF.Square),
                R=list(src_keys), W=[("sq", s)])
            add("tensor", lambda e, s=s, c=c, ps=ps: e.matmul(ps[:, 0:ntok], lhsT=ones[:], rhs=sq[s][:, 0:ntok],
                                                               start=(c == 0), stop=(c == nchunk - 1)),
                R=[("sq", s), "ones"], W=[pk])
        add("scalar", lambda e, ps=ps: e.activation(out=rstd[:, 0:ntok], in_=ps[:, 0:ntok], func=AF.Sqrt,
                                                     scale=1.0 / D, bias=epsc[:, 0:1]), R=[pk, "epsc"], W=["rstd"])
        add("vector", lambda e: e.reciprocal(out=rstd[:, 0:ntok], in_=rstd[:, 0:ntok]), R=["rstd"], W=["rstd"])
        for c in range(nchunk):
            add("vector", lambda e, c=c: e.scalar_tensor_tensor(out=dst_tile[:, c, 0:ntok], in0=src_tile[:, c, 0:ntok],
                                                                 scalar=gvec[:, gslot, c:c + 1], in1=rstd[:, 0:ntok],
                                                                 op0=ALU.mult, op1=ALU.mult),
                R=list(src_keys) + ["rstd", ("gvec", gslot)], W=[(dst_key, c) if dst_key == "hn" else dst_key])

    def resid_evac(c0, ps, pk):
        c = c0 // 128
        add("vector", lambda e: e.tensor_tensor(out=hT[:, c, :], in0=hT[:, c, :], in1=ps[:, :], op=ALU.add),
            R=[pk, "hT"], W=["hT"])

    if prev is not None:
        load_g(0, g_xa)
        load_g(1, g_mem)
        load_g(2, g_ffn)
        memt = hT
        memn = hn
        kmT = C.sb("kmT", [128, 16, 256], BF16)
        vm = C.sb("vm", [128, 2, 2048], BF16)
        dma("sync", memt[:, :, 0:256], memT.rearrange("(c p) t -> p c t", p=128), W=["hT"])
        rmsnorm(memt, ["hT"], 1, memn, "hn", ntok=256)

        def k_evac(c0, ps, pk):
            c = c0 // 128
            add("vector", lambda e: e.tensor_copy(out=kmT[:, c, :], in_=ps[:, 0:256]), R=[pk], W=["kmT"])
        linear(wkv, 16, [c * 128 for c in range(16)], memn, ["hn"], k_evac, ntok=256)
        for c in range(16):
            wt, wk = load_w(wkv, 0, 16, D + c * 128)
            for kt in range(2):
                ps, pk = C.next_ps("main")
                for j in range(16):
                    add("tensor", lambda e, wt=wt, j=j, ps=ps, kt=kt: e.matmul(ps[:, 0:128], lhsT=memn[:, j, kt * 128:(kt + 1) * 128],
                                                                                rhs=wt[:, j, :], start=(j == 0), stop=(j == 15)),
                        R=[wk, ("hn", j)], W=[pk])
                add("vector", lambda e, ps=ps, kt=kt, c=c: e.tensor_copy(out=vm[:, kt, c * 128:(c + 1) * 128], in_=ps[:, 0:128]),
                    R=[pk], W=["vm"])
        if prev == "even":
            glub = C.sb("glub", [128, 8], F32)
            dma("sync", glub[:], glu_b[:, :], W=["glub"])
    if nxt is not None:
        load_g(3, g_mix)
    if final:
        load_g(4, g_fin)

    if prev is not None:
        xin = C.sb("xin", [128, 16, TG], BF16)
        et = [C.sb("et%d" % i, [128, TG], BF16) for i in range(2)]
        rl = C.sb("rl", [128, TG], F32)
        gT = C.sb("gT", [128, 44, TG], BF16)
        qx = gT[:, 24:40, :]
        sa = [C.sb("sa%d" % i, [128, TG], F32) for i in range(2)]
        if prev == "even":
            ysf = [C.sb("ysf%d" % i, [128, TG], F32) for i in range(2)]
            gg = gT[:, 16:24, :]
            sg = C.sb("sg", [128, TG], F32)
            g2 = C.sb("g2", [128, TG], F32)
    if nxt is not None:
        rc = C.sb("rc", [32, TG], F32)
        rs = C.sb("rs", [32, TG], F32)
        qhi = [C.sb("qhi%d" % i, [32, TG], BF16) for i in range(2)]
        qlo = [C.sb("qlo%d" % i, [32, TG], BF16) for i in range(2)]
        permt = C.sb("permt", [32, 32], BF16)
        dma("sync", permt[:], perm_in[:, :], W=["permt"])
        t1 = [C.sb("t1_%d" % i, [32, TG], F32) for i in range(1)]
        t2 = [C.sb("t2_%d" % i, [32, TG], F32) for i in range(1)]
        ob = [C.sb("ob%d" % i, [128, TG], BF16) for i in range(3)]
        nvc = 1024 if nxt == "even" else 2048
        if prev is not None:
            vtok = gT[:, 0:16, :].rearrange("p (a b) t -> p a (b t)", a=4)[:, :, 0:nvc]
        else:
            vtok = C.sb("vtok", [128, 4, nvc], BF16)[:, :, :]
    if final:
        of = [C.sb("of%d" % i, [128, TG], F32) for i in range(2)]

    def emit_gg(gi):
        tg0 = gi * TG
        for c in range(8):
            y = C.nxt("ysf", 2)
            dma("gpsimd", ysf[y][:], ys5T[c * 128:(c + 1) * 128, tg0:tg0 + TG], W=[("ysf", y)])
            add("scalar", lambda e, y=y: e.activation(out=g2[:], in_=ysf[y][:], func=AF.Square), R=[("ysf", y)], W=["g2"])
            add("vector", lambda e: e.tensor_scalar(out=g2[:], in0=g2[:], scalar1=0.044715, scalar2=1.0, op0=ALU.mult, op1=ALU.add),
                R=["g2"], W=["g2"])
            add("vector", lambda e, y=y: e.tensor_tensor(out=g2[:], in0=g2[:], in1=ysf[y][:], op=ALU.mult), R=["g2", ("ysf", y)], W=["g2"])
            add("scalar", lambda e: e.activation(out=g2[:], in_=g2[:], func=AF.Sigmoid, scale=1.5957691216057308), R=["g2"], W=["g2"])
            add("vector", lambda e, y=y, c=c: e.tensor_tensor(out=gg[:, c, :], in0=g2[:], in1=ysf[y][:], op=ALU.mult),
                R=["g2", ("ysf", y)], W=["gT"])

    for g in range(ngrp):
        t0 = g * TG
        dma("gpsimd", hT[:], hT_in[:, t0:t0 + TG].rearrange("(c p) t -> p c t", p=128), W=["hT"])
        if prev is not None:
            if prev == "even":
                dma("gpsimd", xin[:, 8:16, :], ydiffT[:, t0:t0 + TG].rearrange("(c p) t -> p c t", p=128), W=["xin_hi"])
                if g == 0:
                    emit_gg(0)

                def glu_evac(c0, ps, pk):
                    c = c0 // 128
                    add("scalar", lambda e: e.activation(out=sg[:], in_=ps[:, :], func=AF.Sigmoid, bias=glub[:, c:c + 1]),
                        R=[pk, "glub"], W=["sg"])
                    add("vector", lambda e: e.tensor_tensor(out=xin[:, c, :], in0=sg[:], in1=gg[:, c, :], op=ALU.mult),
                        R=["sg", "gT"], W=["xin_lo"])
                linear(glu_w, 8, [c * 128 for c in range(8)], gg, ["gT"], glu_evac)
                xkeys = ["xin_lo", "xin_hi"]
            else:
                dma("gpsimd", xin[:], oT_in[:, t0:t0 + TG].rearrange("(c p) t -> p c t", p=128), W=["xin_lo", "xin_hi"])
                xkeys = ["xin_lo", "xin_hi"]
            if "mix" not in SKIP:
                linear(w_out, 16, [c * 128 for c in range(16)], xin, xkeys, resid_evac)

            if "xa" in SKIP:
                continue
            rmsnorm(hT, ["hT"], 0, hn, "hn")

            def q_evac(c0, ps, pk):
                c = c0 // 128
                add("vector", lambda e: e.tensor_copy(out=qx[:, c, :], in_=ps[:, :]), R=[pk], W=["gT"])
            linear(wq, 16, [c * 128 for c in range(16)], hn, ["hn"], q_evac)
            for h in range(4):
                eks = []
                for kt in range(2):
                    ps, pk = C.next_ps("main")
                    for c in range(4):
                        add("tensor", lambda e, ps=ps, c=c, kt=kt, h=h: e.matmul(ps[:, :], lhsT=kmT[:, 4 * h + c, kt * 128:(kt + 1) * 128],
                                                                                   rhs=qx[:, 4 * h + c, :], start=(c == 0), stop=(c == 3)),
                            R=["kmT", "gT"], W=[pk])
                    add("scalar", lambda e, ps=ps, kt=kt: e.activation(out=et[kt][:], in_=ps[:, :], func=AF.Exp, scale=512 ** -0.5),
                        R=[pk], W=[("et", kt)])
                psl, pkl = C.next_ps("main")
                for kt in range(2):
                    add("tensor", lambda e, psl=psl, kt=kt: e.matmul(psl[:, :], lhsT=ones[:], rhs=et[kt][:], start=(kt == 0), stop=(kt == 1)),
                        R=[("et", kt), "ones"], W=[pkl])
                add("vector", lambda e, psl=psl: e.reciprocal(out=rl[:], in_=psl[:, :]), R=[pkl], W=["rl"])
                for dc in range(4):
                    ps, pk = C.next_ps("main")
                    for kt in range(2):
                        add("tensor", lambda e, ps=ps, kt=kt, dc=dc, h=h: e.matmul(ps[:, :], lhsT=vm[:, kt, h * 512 + dc * 128:h * 512 + (dc + 1) * 128],
                                                                                     rhs=et[kt][:], start=(kt == 0), stop=(kt == 1)),
                            R=["vm", ("et", kt)], W=[pk])
                    add("vector", lambda e, ps=ps, dc=dc, h=h: e.tensor_tensor(out=xin[:, 4 * h + dc, :], in0=ps[:, :], in1=rl[:], op=ALU.mult),
                        R=[pk, "rl"], W=["xin_lo", "xin_hi"])
            linear(wo, 16, [c * 128 for c in range(16)], xin, ["xin_lo", "xin_hi"], resid_evac)

            if "ffn" in SKIP:
                continue
            rmsnorm(hT, ["hT"], 2, hn, "hn")
            for j in range(44):
                holder = {}

                def a_evac(c0, ps, pk, holder=holder):
                    s = C.nxt("sa", 2)
                    holder["s"] = s
                    add("scalar", lambda e: e.activation(out=sa[s][:], in_=ps[:, :], func=AF.Silu), R=[pk], W=[("sa", s)])

                def b_evac(c0, ps, pk, holder=holder, j=j):
                    s = holder["s"]
                    add("vector", lambda e: e.tensor_tensor(out=gT[:, j, :], in0=ps[:, :], in1=sa[s][:], op=ALU.mult),
                        R=[pk, ("sa", s)], W=["gT"])
                linear(w13, 16, [j * 128], hn, ["hn"], a_evac)
                linear(w13, 16, [FFN + j * 128], hn, ["hn"], b_evac)
            linear(w2, 44, [c * 128 for c in range(16)], gT, ["gT"], resid_evac)
            if prev == "even" and g + 1 < ngrp:
                emit_gg(g + 1)

        if nxt is not None:
            dma("gpsimd", hT_out[:, t0:t0 + TG].rearrange("(c p) t -> p c t", p=128), hT[:], R=["hT"])
            rmsnorm(hT, ["hT"], 3, hn, "hn")
            dma("sync", rc[:], ropeC[:, t0:t0 + TG], W=["rc"])
            dma("sync", rs[:], ropeS[:, t0:t0 + TG], W=["rs"])

            def plain_out(dst):
                def ev(c0, ps, pk):
                    s = C.nxt("ob", 3)
                    add("vector", lambda e: e.tensor_copy(out=ob[s][:], in_=ps[:, :]), R=[pk], W=[("ob", s)])
                    dma("gpsimd", dst, ob[s][:], R=[("ob", s)])
                return ev

            def rope_A(col0, dst):
                hold = {}

                def ev_main(c0, ps, pk):
                    hold["ps"], hold["pk"] = ps, pk
                linear(w_in, 16, [col0], hn, ["hn"], ev_main)
                ps, pk = hold["ps"], hold["pk"]
                h = C.nxt("qh", 2)
                add("scalar", lambda e: e.activation(out=qhi[h][:], in_=ps[0:32, :], func=AF.Copy), R=[pk], W=[("qhi", h)])
                add("vector", lambda e: e.tensor_tensor(out=qlo[h][:], in0=ps[0:32, :], in1=qhi[h][:], op=ALU.subtract),
                    R=[pk, ("qhi", h)], W=[("qlo", h)])
                return ps, pk, h, dst

            def rope_B(ps, pk, h, dst):
                ps2, pk2 = C.next_ps("main")
                add("tensor", lambda e: e.matmul(ps2[0:32, :], lhsT=permt[:, :], rhs=qhi[h][:], start=True, stop=False),
                    R=["permt", ("qhi", h)], W=[pk2])
                add("tensor", lambda e: e.matmul(ps2[0:32, :], lhsT=permt[:, :], rhs=qlo[h][:], start=False, stop=True),
                    R=["permt", ("qlo", h)], W=[pk2])
                s = C.nxt("ob", 3)
                r = 0
                add("vector", lambda e: e.tensor_tensor(out=t1[r][:], in0=ps[0:32, :], in1=rc[:], op=ALU.mult), R=[pk, "rc"], W=[("t1", r)])
                add("vector", lambda e: e.tensor_tensor(out=t2[r][:], in0=ps2[0:32, :], in1=rs[:], op=ALU.mult), R=[pk2, "rs"], W=[("t2", r)])
                add("vector", lambda e: e.tensor_tensor(out=ob[s][0:32, :], in0=t1[r][:], in1=t2[r][:], op=ALU.add),
                    R=[("t1", r), ("t2", r)], W=[("ob", s)])
                add("vector", lambda e: e.tensor_copy(out=ob[s][32:64, :], in_=ps[32:64, :]), R=[pk], W=[("ob", s)])
                add("vector", lambda e: e.tensor_copy(out=ob[s][64:128, :], in_=ps[64:128, :]), R=[pk], W=[("ob", s)])
                dma("gpsimd", dst, ob[s][:], R=[("ob", s)])

            def rope_chain(items):
                pend = None
                for col0, dst in items:
                    cur = rope_A(col0, dst)
                    if pend is not None:
                        rope_B(*pend)
                    pend = cur
                rope_B(*pend)

            if nxt == "even":
                for c in range(8):
                    linear(w_in, 16, [c * 128], hn, ["hn"], plain_out(uT_out[c * 128:(c + 1) * 128, t0:t0 + TG]))
                rope_chain([(1024 + hc * 128, qT_out[hc, :, t0:t0 + TG]) for hc in range(8)] +
                           [(2048 + hc * 128, kT_out[hc, :, t0:t0 + TG]) for hc in range(8)])
                vcol0 = 3072
            else:
                rope_chain([(hc * 128, qT_out[hc, :, t0:t0 + TG]) for hc in range(16)] +
                           [(2048 + hc * 128, kT_out[hc, :, t0:t0 + TG]) for hc in range(16)])
                vcol0 = 4096
            for c in range(nvc // 128):
                wt, wk = load_w(w_in, 0, 16, vcol0 + c * 128)
                for tt in range(4):
                    ps, pk = C.next_ps("main")
                    for j in range(16):
                        add("tensor", lambda e, wt=wt, j=j, ps=ps, tt=tt: e.matmul(ps[:, 0:128], lhsT=hn[:, j, tt * 128:(tt + 1) * 128],
                                                                                    rhs=wt[:, j, :], start=(j == 0), stop=(j == 15)),
                            R=[wk, ("hn", j)], W=[pk])
                    add("vector", lambda e, ps=ps, tt=tt, c=c: e.tensor_copy(out=vtok[:, tt, c * 128:(c + 1) * 128], in_=ps[:, 0:128]),
                        R=[pk], W=["gT"])
            dma("gpsimd", v_out[t0:t0 + TG, :].rearrange("(t p) c -> p t c", p=128), vtok, R=["gT"])
        if final:
            rmsnorm(hT, ["hT"], 4, hn, "hn")
            for c in range(16):
                s = C.nxt("of", 2)
                add("vector", lambda e, c=c, s=s: e.scalar_tensor_tensor(out=of[s][:], in0=hT[:, c, :], scalar=gvec[:, 4, c:c + 1],
                                                                          in1=rstd[:], op0=ALU.mult, op1=ALU.mult),
                    R=["hT", "rstd", ("gvec", 4)], W=[("of", s)])
                dma("gpsimd", outT[c * 128:(c + 1) * 128, t0:t0 + TG], of[s][:], R=[("of", s)])
    return C.finish()


def build_H_odd(nc=None, io=None, bg=None, bg_total=0):
    C = Ctx(nc, io)
    nc, P = C.nc, C.P
    add, dma = P.add, P.dma
    qT_in = C.din("qT_in", [4, 128, SEQ], BF16)
    kT_in = C.din("kT_in", [4, 128, SEQ], BF16)
    v_in = C.din("v_in", [SEQ, 4, 128], BF16)
    mask_in = C.din("mask_in", [128, 20, 512], BF16)
    oT_out = C.dout("oT_out", [512, SEQ], BF16)
    C.alloc_psum(8)
    qT = C.sb("qT", [128, SEQ], BF16)
    kT = C.sb("kT", [128, SEQ], BF16)
    vv = C.sb("vv", [128, 64, 128], BF16)
    mk = C.sb("mk", [128, 20, 512], BF16)
    ones = C.sb("ones", [128, 128], BF16)
    et = [C.sb("et%d" % i, [128, 512], BF16) for i in range(3)]
    em = [C.sb("em%d" % i, [128, 512], BF16) for i in range(5)]
    rl = C.sb("rl", [128, 512], F32)
    ob = [C.sb("ob%d" % i, [128, 512], BF16) for i in range(2)]
    esum = [C.sb("esum%d" % i, [128, 512], F32) for i in range(2)]
    esb = [C.sb("esb%d" % i, [128, 512], BF16) for i in range(2)]
    add("gpsimd", lambda e: e.memset(ones[:], 1.0), W=["ones"])
    dma("sync", mk[:], mask_in[:, :, :], W=["mk"])
    scale = 128 ** -0.5
    LOOK = 3
    for hh in range(4):
        dma("sync", qT[:], qT_in[hh], W=["qT"])
        dma("sync", kT[:], kT_in[hh], W=["kT"])
        dma("gpsimd", vv[:], v_in[:, hh, :].rearrange("(t p) c -> p t c", p=128), W=["vv"])
        steps = []
        for qg in range(16):
            kts = [kt for kt in range(4 * qg - 8, 4 * qg + 12) if 0 <= kt < 64]
            for n, kt in enumerate(kts):
                steps.append((qg, kt, n, len(kts)))
        acc = {}

        def front(st, gi):
            qg, kt, n, nk = st
            idx = kt - (4 * qg - 8)
            ps, pk = C.next_ps("S", [0, 1, 2, 3])
            add("tensor", lambda e, ps=ps, kt=kt, qg=qg: e.matmul(ps[:, :], lhsT=kT[:, kt * 128:(kt + 1) * 128],
                                                                   rhs=qT[:, qg * 512:(qg + 1) * 512], start=True, stop=True),
                R=["kT", "qT"], W=[pk])
            s_ = C.nxt("et", 3)
            add("scalar", lambda e, ps=ps, s_=s_: e.activation(out=et[s_][:], in_=ps[:, :], func=AF.Exp, scale=scale),
                R=[pk], W=[("et", s_)])
            m = C.nxt("em", 5)
            me = ["vector", "gpsimd"][gi % 2]
            add(me, lambda e, s_=s_, m=m, idx=idx: e.tensor_tensor(out=em[m][:], in0=et[s_][:], in1=mk[:, idx, :], op=ALU.mult),
                R=[("et", s_), "mk"], W=[("em", m)])
            return m

        def back(st, m):
            qg, kt, n, nk = st
            if n == 0:
                acc["o"] = C.next_ps("O", [4, 5])
                acc["l"] = C.next_ps("L", [6, 7])
            (po, pko), (pl, pkl) = acc["o"], acc["l"]
            last = (n == nk - 1)
            add("tensor", lambda e, po=po, kt=kt, m=m, n=n, last=last: e.matmul(
                po[:, :], lhsT=vv[:, kt, :], rhs=em[m][:], start=(n == 0), stop=last), R=["vv", ("em", m)], W=[pko])
            add("tensor", lambda e, pl=pl, m=m, n=n, last=last: e.matmul(
                pl[:, :], lhsT=ones[:], rhs=em[m][:], start=(n == 0), stop=last), R=["ones", ("em", m)], W=[pkl])
            if last:
                add("vector", lambda e, pl=pl: e.reciprocal(out=rl[:], in_=pl[:, :]), R=[pkl], W=["rl"])
                o = C.nxt("ob", 2)
                add("vector", lambda e, po=po, o=o: e.tensor_tensor(out=ob[o][:], in0=po[:, :], in1=rl[:], op=ALU.mult),
                    R=[pko, "rl"], W=[("ob", o)])
                dma("gpsimd", oT_out[hh * 128:(hh + 1) * 128, qg * 512:(qg + 1) * 512], ob[o][:], R=[("ob", o)])

        queue = []
        bg_every = max(1, (4 * len(steps)) // max(1, bg_total)) if bg is not None else 0
        for gi, st in enumerate(steps):
            queue.append((st, front(st, gi)))
            if len(queue) > LOOK:
                back(*queue.pop(0))
            if bg is not None and gi % bg_every == 0:
                bg.emit(C, 1, cast_engines=("scalar", "vector"))
        while queue:
            back(*queue.pop(0))
    return C.finish()


def dilated_mask_table():
    kl = np.arange(128)[:, None, None]
    idx = np.arange(20)[None, :, None]
    ql = np.arange(512)[None, None, :]
    off = (idx - 8) * 128 + kl - ql
    m = np.zeros(off.shape, np.float32)
    for window, dil in ((128, 1), (512, 4), (2048, 16)):
        half = window // (2 * dil)
        m += ((off % dil == 0) & (np.abs(off) <= half * dil)).astype(np.float32)
    return m.astype(NPBF)


def build_H_even(lambda_init, do_s5=True, do_attn=True, nc=None, io=None, bg=None, bg_total=0):
    C = Ctx(nc, io)
    nc, P = C.nc, C.P
    add, dma = P.add, P.dma
    qT_in = C.din("qT_in", [2, 128, SEQ], BF16)
    kT_in = C.din("kT_in", [2, 128, SEQ], BF16)
    v_in = C.din("v_in", [SEQ, 256], BF16)
    lam_in = C.din("lam_in", [128, 4, 128])
    sg_in = C.din("sg_in", [128, 2])
    ydT_out = C.dout("ydT_out", [256, SEQ], BF16)
    uT_in = C.din("uT_in", [256, SEQ], BF16)
    s5v = C.din("s5v", [8, 2, 128, 3])
    s5B = C.din("s5B", [8, 2, 2, 32, 128], BF16)
    s5C = C.din("s5C", [8, 2, 2, 128, 32], BF16)
    s5d = C.din("s5d", [8, 32, 1])
    ysT_out = C.dout("ysT_out", [256, SEQ])
    C.alloc_psum(8)
    ones = C.sb("ones", [128, 128], BF16)
    epsc = C.sb("epsc", [128, 1], F32)
    add("gpsimd", lambda e: e.memset(ones[:], 1.0), W=["ones"])
    add("gpsimd", lambda e: e.memset(epsc[:], EPS), W=["epsc"])

    if do_attn:
        qTg = [C.sb("qTg%d" % i, [128, 2, 512], BF16) for i in range(2)]
        kT = C.sb("kT", [128, 2, SEQ], BF16)
        vv = C.sb("vv", [128, 64, 256], BF16)
        et = [C.sb("et%d" % i, [128, 512], BF16) for i in range(3)]
        rl = C.sb("rl", [128, 512], F32)
        oc = C.sb("oc", [128, 2, 2, 512], F32)
        od = C.sb("od", [128, 2, 512], F32)
        sq = C.sb("sq", [128, 512], BF16)
        rstd = C.sb("rstd", [128, 512], F32)
        ob = [C.sb("ob%d" % i, [128, 512], BF16) for i in range(2)]
        lamt = C.sb("lamt", [128, 4, 128], F32)
        lprod = C.sb("lprod", [128, 128], F32)
        lsc = C.sb("lsc", [128, 4], F32)
        sgt = C.sb("sgt", [128, 2], F32)
        for c in range(2):
            dma("sync", kT[:, c, :], kT_in[c], W=["kT"])
        dma("gpsimd", vv[:], v_in.rearrange("(t p) c -> p t c", p=128), W=["vv"])
        dma("sync", lamt[:], lam_in[:, :, :], W=["lamt"])
        dma("sync", sgt[:], sg_in[:, :], W=["sgt"])
        for i in range(2):
            add("vector", lambda e, i=i: e.tensor_tensor(out=lprod[:], in0=lamt[:, 2 * i, :], in1=lamt[:, 2 * i + 1, :], op=ALU.mult),
                R=["lamt"], W=["lprod"])
            add("vector", lambda e, i=i: e.reduce_sum(out=lsc[:, i:i + 1], in_=lprod[:], axis=mybir.AxisListType.X), R=["lprod"], W=["lsc"])
        add("scalar", lambda e: e.activation(out=lsc[:, 0:2], in_=lsc[:, 0:2], func=AF.Exp), R=["lsc"], W=["lsc"])
        add("vector", lambda e: e.tensor_tensor(out=lsc[:, 2:3], in0=lsc[:, 1:2], in1=lsc[:, 0:1], op=ALU.subtract), R=["lsc"], W=["lsc"])
        add("vector", lambda e: e.tensor_scalar(out=lsc[:, 2:3], in0=lsc[:, 2:3], scalar1=-float(lambda_init), scalar2=None, op0=ALU.add),
            R=["lsc"], W=["lsc"])
        add("vector", lambda e: e.tensor_scalar(out=sgt[:], in0=sgt[:], scalar1=float(1.0 - lambda_init), scalar2=None, op0=ALU.mult),
            R=["sgt"], W=["sgt"])
        scale = 128 ** -0.5

    def attn_gen():
        LOOK = 1
        SB, OB, LB = [0, 1], [2, 3], [4]
        steps = [(qg, comp, kt) for qg in range(16) for comp in range(2) for kt in range(64)]
        st8 = {}

        def front(stp):
            qg, comp, kt = stp
            if comp == 0 and kt == 0:
                qs = C.nxt("qTg", 2)
                st8["qs"] = qs
                dma("sync", qTg[qs][:], qT_in[:, :, qg * 512:(qg + 1) * 512].rearrange("c p t -> p c t"), W=[("qTg", qs)])
            qs = st8["qs"]
            ps, pk = C.next_ps("S", SB)
            add("tensor", lambda e, ps=ps, kt=kt, qs=qs, comp=comp: e.matmul(
                ps[:, :], lhsT=kT[:, comp, kt * 128:(kt + 1) * 128], rhs=qTg[qs][:, comp, :],
                start=True, stop=True), R=["kT", ("qTg", qs)], W=[pk])
            s_ = C.nxt("et", 3)
            add("scalar", lambda e, ps=ps, s_=s_: e.activation(out=et[s_][:], in_=ps[:, :], func=AF.Exp, scale=scale),
                R=[pk], W=[("et", s_)])
            return s_

        def back(stp, s_):
            qg, comp, kt = stp
            if kt == 0:
                st8["po"] = [C.next_ps("O", OB), C.next_ps("O", OB)]
                st8["pl"] = C.next_ps("L", LB)
            po = st8["po"]
            pl, pkl = st8["pl"]
            for dc in range(2):
                add("tensor", lambda e, dc=dc, kt=kt, s_=s_, pp=po[dc][0]: e.matmul(
                    pp[:, :], lhsT=vv[:, kt, dc * 128:(dc + 1) * 128], rhs=et[s_][:], start=(kt == 0), stop=(kt == 63)),
                    R=["vv", ("et", s_)], W=[po[dc][1]])
            add("tensor", lambda e, pl=pl, s_=s_, kt=kt: e.matmul(pl[:, :], lhsT=ones[:], rhs=et[s_][:], start=(kt == 0), stop=(kt == 63)),
                R=["ones", ("et", s_)], W=[pkl])
            if kt != 63:
                return
            add("vector", lambda e, pl=pl: e.reciprocal(out=rl[:], in_=pl[:, :]), R=[pkl], W=["rl"])
            for dc in range(2):
                add("vector", lambda e, dc=dc, comp=comp, pp=po[dc][0]: e.tensor_tensor(out=oc[:, comp, dc, :], in0=pp[:, :], in1=rl[:], op=ALU.mult),
                    R=[po[dc][1], "rl"], W=["oc"])
            if comp != 1:
                return
            psn, pkn = C.next_ps("N", [7])
            for dc in range(2):
                add("vector", lambda e, dc=dc: e.scalar_tensor_tensor(out=od[:, dc, :], in0=oc[:, 1, dc, :], scalar=lsc[:, 2:3], in1=oc[:, 0, dc, :],
                                                                       op0=ALU.mult, op1=ALU.add), R=["oc", "lsc"], W=["od"])
                add("scalar", lambda e, dc=dc: e.activation(out=sq[:], in_=od[:, dc, :], func=AF.Square), R=["od"], W=["sq"])
                add("tensor", lambda e, dc=dc, psn=psn: e.matmul(psn[:, :], lhsT=ones[:], rhs=sq[:], start=(dc == 0), stop=(dc == 1)),
                    R=["sq", "ones"], W=[pkn])
            add("scalar", lambda e, psn=psn: e.activation(out=rstd[:], in_=psn[:, :], func=AF.Sqrt, scale=1.0 / 256, bias=epsc[:, 0:1]),
                R=[pkn, "epsc"], W=["rstd"])
            add("vector", lambda e: e.reciprocal(out=rstd[:], in_=rstd[:]), R=["rstd"], W=["rstd"])
            for dc in range(2):
                o = C.nxt("ob", 2)
                add("vector", lambda e, dc=dc, o=o: e.scalar_tensor_tensor(out=ob[o][:], in0=od[:, dc, :], scalar=sgt[:, dc:dc + 1], in1=rstd[:],
                                                                            op0=ALU.mult, op1=ALU.mult), R=["od", "sgt", "rstd"], W=[("ob", o)])
                dma("gpsimd", ydT_out[dc * 128:(dc + 1) * 128, qg * 512:(qg + 1) * 512], ob[o][:], R=[("ob", o)])

        queue = []
        for stp in steps:
            queue.append((stp, front(stp)))
            if len(queue) > LOOK:
                back(*queue.pop(0))
                yield
        while queue:
            back(*queue.pop(0))
            yield

    def s5_gen():
        BL = 512
        NBLK = SEQ // BL
        ug = C.sb("ug", [32, SEQ], BF16)
        yf = C.sb("yf", [32, SEQ], F32)
        pv = C.sb("pv", [128, 3], F32)
        sc = C.sb("sc", [128, 24], F32)
        sci = C.sb("sci", [128, 2], I32)
        Tr = C.sb("Tr", [128, BL + 1], F32)
        Ti = C.sb("Ti", [128, BL + 1], F32)
        tmp = C.sb("tmp", [128, BL], F32)
        Ar = C.sb("Ar", [128, BL], F32)
        Ai = C.sb("Ai", [128, BL], F32)
        Bb = C.sb("Bb", [32, 2, 128], BF16)
        Cb = C.sb("Cb", [128, 2, 32], BF16)
        dsk = C.sb("dsk", [32, 1], F32)
        m = [C.sb("m%d" % i, [128, BL], F32) for i in range(4)]
        wr = C.sb("wr", [128, BL], F32)
        wi = C.sb("wi", [128, BL], F32)
        sr = C.sb("sr", [128, BL], F32)
        si = C.sb("si", [128, BL], F32)
        xr = [C.sb("xr%d" % i, [128, BL], BF16) for i in range(2)]
        xi = [C.sb("xi%d" % i, [128, BL], BF16) for i in range(2)]
        init = C.sb("init", [128, 2], F32)
        yo = [C.sb("yo%d" % i, [32, BL], F32) for i in range(2)]
        TWO_PI = 2.0 * math.pi
        V = "vector"
        G = "gpsimd"

        def col(i):
            return sc[:, i:i + 1]

        def cmul_scalar(out_r, out_i, in_r, in_i, s_r, s_i, keys_in, keys_out, n):
            add(V, lambda e: e.tensor_scalar(out=tmp[:, 0:n], in0=in_i, scalar1=s_i, scalar2=None, op0=ALU.mult), R=keys_in, W=["tmp"])
            add(V, lambda e: e.scalar_tensor_tensor(out=out_r, in0=in_r, scalar=s_r, in1=tmp[:, 0:n], op0=ALU.mult, op1=ALU.subtract),
                R=keys_in + ["tmp"], W=keys_out)
            add(V, lambda e: e.tensor_scalar(out=tmp[:, 0:n], in0=in_i, scalar1=s_r, scalar2=None, op0=ALU.mult), R=keys_in, W=["tmp"])
            add(V, lambda e: e.scalar_tensor_tensor(out=out_i, in0=in_r, scalar=s_i, in1=tmp[:, 0:n], op0=ALU.mult, op1=ALU.add),
                R=keys_in + ["tmp"], W=keys_out)

        for gp in range(8):
            dma("sync", ug[:], uT_in[gp * 32:(gp + 1) * 32, :], W=["ug"])
            dma("sync", dsk[:], s5d[gp], W=["dsk"])
            for dr in range(2):
                dma("sync", pv[:], s5v[gp, dr], W=["pv"])
                dma("sync", Bb[:, 0, :], s5B[gp, dr, 0], W=["Bb"])
                dma("sync", Bb[:, 1, :], s5B[gp, dr, 1], W=["Bb"])
                dma("sync", Cb[:, 0, :], s5C[gp, dr, 0], W=["Cb"])
                dma("sync", Cb[:, 1, :], s5C[gp, dr, 1], W=["Cb"])
                K = ["sc"]
                add("scalar", lambda e: e.activation(out=col(0), in_=pv[:, 2:3], func=AF.Exp), R=["pv"], W=K)
                add("scalar", lambda e: e.activation(out=col(1), in_=pv[:, 0:1], func=AF.Exp, scale=col(0)), R=["pv"] + K, W=K)
                add(V, lambda e: e.tensor_tensor(out=col(2), in0=pv[:, 1:2], in1=col(0), op=ALU.mult), R=["pv"] + K, W=K)
                add(V, lambda e: e.tensor_scalar(out=col(3), in0=col(2), scalar1=0.5 * math.pi, scalar2=None, op0=ALU.add), R=K, W=K)
                add(V, lambda e: e.tensor_scalar(out=sc[:, 4:6], in0=sc[:, 2:4], scalar1=1.0 / TWO_PI, scalar2=None, op0=ALU.mult), R=K, W=K)
                add(V, lambda e: e.tensor_copy(out=sci[:, 0:2], in_=sc[:, 4:6]), R=K, W=["sci"])
                add(V, lambda e: e.tensor_copy(out=sc[:, 4:6], in_=sci[:, 0:2]), R=["sci"], W=K)
                add(V, lambda e: e.scalar_tensor_tensor(out=sc[:, 6:8], in0=sc[:, 4:6], scalar=-TWO_PI, in1=sc[:, 2:4], op0=ALU.mult, op1=ALU.add),
                    R=K, W=K)
                add(V, lambda e: e.tensor_scalar(out=sc[:, 6:8], in0=sc[:, 6:8], scalar1=-math.pi, scalar2=math.pi, op0=ALU.max, op1=ALU.min), R=K, W=K)
                add("scalar", lambda e: e.activation(out=sc[:, 8:10], in_=sc[:, 6:8], func=AF.Sin), R=K, W=K)
                add(G, lambda e: e.memset(Tr[:, 0:1], 1.0), W=["T"])
                add(G, lambda e: e.memset(Ti[:, 0:1], 0.0), W=["T"])
                add(V, lambda e: e.tensor_copy(out=Tr[:, 1:2], in_=col(9)), R=K, W=["T"])
                add(V, lambda e: e.tensor_copy(out=Ti[:, 1:2], in_=col(8)), R=K, W=["T"])
                n = 1
                while n < BL:
                    cmul_scalar(Tr[:, n + 1:2 * n + 1], Ti[:, n + 1:2 * n + 1], Tr[:, 1:n + 1], Ti[:, 1:n + 1],
                                Tr[:, n:n + 1], Ti[:, n:n + 1], ["T"], ["T"], n)
                    n *= 2
                add(V, lambda e: e.tensor_tensor(out=col(10), in0=col(1), in1=col(9), op=ALU.mult), R=K, W=K)
                add(V, lambda e: e.tensor_tensor(out=col(11), in0=col(1), in1=col(8), op=ALU.mult), R=K, W=K)
                add(V, lambda e: e.tensor_scalar(out=col(10), in0=col(10), scalar1=-1.0, scalar2=None, op0=ALU.add), R=K, W=K)
                add(V, lambda e: e.tensor_tensor(out=col(12), in0=col(11), in1=pv[:, 1:2], op=ALU.mult), R=K + ["pv"], W=K)
                add(V, lambda e: e.scalar_tensor_tensor(out=col(13), in0=col(10), scalar=pv[:, 0:1], in1=col(12), op0=ALU.mult, op1=ALU.add),
                    R=K + ["pv"], W=K)
                add(V, lambda e: e.tensor_tensor(out=col(12), in0=col(10), in1=pv[:, 1:2], op=ALU.mult), R=K + ["pv"], W=K)
                add(V, lambda e: e.scalar_tensor_tensor(out=col(14), in0=col(11), scalar=pv[:, 0:1], in1=col(12), op0=ALU.mult, op1=ALU.subtract),
                    R=K + ["pv"], W=K)
                add(V, lambda e: e.tensor_tensor(out=col(15), in0=pv[:, 0:1], in1=pv[:, 0:1], op=ALU.mult), R=["pv"], W=K)
                add(V, lambda e: e.scalar_tensor_tensor(out=col(15), in0=pv[:, 1:2], scalar=pv[:, 1:2], in1=col(15), op0=ALU.mult, op1=ALU.add),
                    R=K + ["pv"], W=K)
                add(V, lambda e: e.reciprocal(out=col(15), in_=col(15)), R=K, W=K)
                add(V, lambda e: e.tensor_tensor(out=col(16), in0=col(13), in1=col(15), op=ALU.mult), R=K, W=K)
                add(V, lambda e: e.tensor_tensor(out=col(17), in0=col(14), in1=col(15), op=ALU.mult), R=K, W=K)
                add(V, lambda e: e.tensor_scalar(out=col(18), in0=col(17), scalar1=-1.0, scalar2=None, op0=ALU.mult), R=K, W=K)
                add(V, lambda e: e.tensor_scalar(out=tmp[:], in0=Ti[:, 0:BL], scalar1=col(17), scalar2=None, op0=ALU.mult), R=["T"] + K, W=["tmp"])
                add(V, lambda e: e.scalar_tensor_tensor(out=Ar[:], in0=Tr[:, 0:BL], scalar=col(16), in1=tmp[:], op0=ALU.mult, op1=ALU.add),
                    R=["T", "tmp"] + K, W=["A"])
                add(V, lambda e: e.tensor_scalar(out=tmp[:], in0=Ti[:, 0:BL], scalar1=col(16), scalar2=None, op0=ALU.mult), R=["T"] + K, W=["tmp"])
                add(V, lambda e: e.scalar_tensor_tensor(out=Ai[:], in0=Tr[:, 0:BL], scalar=col(17), in1=tmp[:], op0=ALU.mult, op1=ALU.subtract),
                    R=["T", "tmp"] + K, W=["A"])
                add(G, lambda e: e.memset(init[:], 0.0), W=["init"])
                def emit_bu(bi):
                    blk = bi if dr == 0 else NBLK - 1 - bi
                    t0 = blk * BL
                    pa, pka = C.next_ps("S5a", [5])
                    pb, pkb = C.next_ps("S5b", [6])
                    add("tensor", lambda e, pa=pa, t0=t0: e.matmul(pa[:, :], lhsT=Bb[:, 0, :], rhs=ug[:, t0:t0 + BL], start=True, stop=True),
                        R=["Bb", "ug"], W=[pka])
                    add("tensor", lambda e, pb=pb, t0=t0: e.matmul(pb[:, :], lhsT=Bb[:, 1, :], rhs=ug[:, t0:t0 + BL], start=True, stop=True),
                        R=["Bb", "ug"], W=[pkb])
                    return pa, pka, pb, pkb
                nxt_bu = emit_bu(0)
                for bi in range(NBLK):
                    blk = bi if dr == 0 else NBLK - 1 - bi
                    t0 = blk * BL
                    pa, pka, pb, pkb = nxt_bu
                    if dr == 0:
                        bur, bui = pa[:, :], pb[:, :]
                    else:
                        bur, bui = pa[:, ::-1], pb[:, ::-1]
                    add(V, lambda e, bur=bur: e.tensor_tensor(out=m[0][:], in0=bur, in1=Ar[:], op=ALU.mult), R=[pka, "A"], W=["m0"])
                    add(V, lambda e, bui=bui: e.tensor_tensor(out=m[1][:], in0=bui, in1=Ai[:], op=ALU.mult), R=[pkb, "A"], W=["m1"])
                    add(V, lambda e, bui=bui: e.tensor_tensor(out=m[2][:], in0=bui, in1=Ar[:], op=ALU.mult), R=[pkb, "A"], W=["m2"])
                    add(V, lambda e, bur=bur: e.tensor_tensor(out=m[3][:], in0=bur, in1=Ai[:], op=ALU.mult), R=[pka, "A"], W=["m3"])
                    if bi + 1 < NBLK:
                        nxt_bu = emit_bu(bi + 1)
                    add(G, lambda e: e.tensor_tensor(out=wr[:], in0=m[0][:], in1=m[1][:], op=ALU.subtract), R=["m0", "m1"], W=["wr"])
                    add(G, lambda e: e.tensor_tensor(out=wi[:], in0=m[2][:], in1=m[3][:], op=ALU.add), R=["m2", "m3"], W=["wi"])
                    add(V, lambda e: e.tensor_tensor_scan(out=sr[:], data0=col(1).to_broadcast([128, BL]), data1=wr[:], initial=init[:, 0:1],
                                                          op0=ALU.mult, op1=ALU.add), R=["wr", "init"] + K, W=["sr"])
                    add(V, lambda e: e.tensor_tensor_scan(out=si[:], data0=col(1).to_broadcast([128, BL]), data1=wi[:], initial=init[:, 1:2],
                                                          op0=ALU.mult, op1=ALU.add), R=["wi", "init"] + K, W=["si"])
                    cmul_scalar(init[:, 0:1], init[:, 1:2], sr[:, BL - 1:BL], si[:, BL - 1:BL], Tr[:, BL:BL + 1], Ti[:, BL:BL + 1],
                                ["sr", "si", "T"], ["init"], 1)
                    add(V, lambda e: e.tensor_tensor(out=m[0][:], in0=sr[:], in1=Tr[:, 0:BL], op=ALU.mult), R=["sr", "T"], W=["m0"])
                    add(V, lambda e: e.tensor_tensor(out=m[1][:], in0=si[:], in1=Ti[:, 0:BL], op=ALU.mult), R=["si", "T"], W=["m1"])
                    add(V, lambda e: e.tensor_tensor(out=m[2][:], in0=si[:], in1=Tr[:, 0:BL], op=ALU.mult), R=["si", "T"], W=["m2"])
                    add(V, lambda e: e.tensor_tensor(out=m[3][:], in0=sr[:], in1=Ti[:, 0:BL], op=ALU.mult), R=["sr", "T"], W=["m3"])
                    xs = C.nxt("xs", 2)
                    if dr == 0:
                        a0, a1, a2, a3 = m[0][:], m[1][:], m[2][:], m[3][:]
                    else:
                        a0, a1, a2, a3 = m[0][:, ::-1], m[1][:, ::-1], m[2][:, ::-1], m[3][:, ::-1]
                    add(V, lambda e, xs=xs, a0=a0, a1=a1: e.tensor_tensor(out=xr[xs][:], in0=a0, in1=a1, op=ALU.subtract),
                        R=["m0", "m1"], W=[("xr", xs)])
                    add(V, lambda e, xs=xs, a2=a2, a3=a3: e.scalar_tensor_tensor(out=xi[xs][:], in0=a2, scalar=-1.0, in1=a3, op0=ALU.mult, op1=ALU.subtract),
                        R=["m2", "m3"], W=[("xi", xs)])
                    py, pky = C.next_ps("Y", [7])
                    add("tensor", lambda e, py=py, xs=xs: e.matmul(py[0:32, :], lhsT=Cb[:, 0, :], rhs=xr[xs][:], start=True, stop=False),
                        R=["Cb", ("xr", xs)], W=[pky])
                    add("tensor", lambda e, py=py, xs=xs: e.matmul(py[0:32, :], lhsT=Cb[:, 1, :], rhs=xi[xs][:], start=False, stop=True),
                        R=["Cb", ("xi", xs)], W=[pky])
                    if dr == 0:
                        add(V, lambda e, py=py, t0=t0: e.tensor_copy(out=yf[:, t0:t0 + BL], in_=py[0:32, :]), R=[pky], W=["yf"])
                    else:
                        y = C.nxt("yo", 2)
                        add(V, lambda e, py=py, t0=t0: e.tensor_tensor(out=yf[:, t0:t0 + BL], in0=py[0:32, :], in1=yf[:, t0:t0 + BL], op=ALU.add),
                            R=[pky, "yf"], W=["yf"])
                        add(V, lambda e, y=y, t0=t0: e.scalar_tensor_tensor(out=yo[y][:], in0=ug[:, t0:t0 + BL], scalar=dsk[:, 0:1], in1=yf[:, t0:t0 + BL],
                                                                             op0=ALU.mult, op1=ALU.add), R=["ug", "dsk", "yf"], W=[("yo", y)])
                        dma("gpsimd", ysT_out[gp * 32:(gp + 1) * 32, t0:t0 + BL], yo[y][:], R=[("yo", y)])
                    yield
    gens = []
    if do_attn:
        gens.append((attn_gen(), 8))
    if do_s5:
        gens.append((s5_gen(), 1))
    alive = [True] * len(gens)
    it = 0
    bg_every = max(1, 256 // max(1, bg_total)) if bg is not None else 0
    bg_per = max(1, -(-bg_total // 256)) if bg is not None else 0
    while any(alive):
        it += 1
        if bg is not None and it % bg_every == 0:
            bg.emit(C, bg_per, cast_engines=("scalar",))
        for gi, (g, reps) in enumerate(gens):
            if not alive[gi]:
                continue
            for _ in range(reps):
                try:
                    next(g)
                except StopIteration:
                    alive[gi] = False
                    break
    return C.finish()


def pack_s5(lam_re, lam_im, log_step, b_re, b_im, c_re, c_im, d_skip, g0, ng):
    ngp = ng // 2
    s5v = np.zeros((ngp, 2, 128, 3), np.float32)
    s5B = np.zeros((ngp, 2, 2, 32, 128), NPBF)
    s5C = np.zeros((ngp, 2, 2, 128, 32), NPBF)
    s5d = np.zeros((ngp, 32, 1), np.float32)
    for gp in range(ngp):
        for j in range(2):
            g = g0 + 2 * gp + j
            s5d[gp, j * 16:(j + 1) * 16, 0] = d_skip[g * 16:(g + 1) * 16]
            for dr in range(2):
                s5v[gp, dr, j * 64:(j + 1) * 64, 0] = lam_re[dr, g]
                s5v[gp, dr, j * 64:(j + 1) * 64, 1] = lam_im[dr, g]
                s5v[gp, dr, j * 64:(j + 1) * 64, 2] = log_step[dr, g]
                s5B[gp, dr, 0, j * 16:(j + 1) * 16, j * 64:(j + 1) * 64] = b_re[dr, g].T.astype(NPBF)
                s5B[gp, dr, 1, j * 16:(j + 1) * 16, j * 64:(j + 1) * 64] = b_im[dr, g].T.astype(NPBF)
                s5C[gp, dr, 0, j * 64:(j + 1) * 64, j * 16:(j + 1) * 16] = c_re[dr, g].T.astype(NPBF)
                s5C[gp, dr, 1, j * 64:(j + 1) * 64, j * 16:(j + 1) * 16] = c_im[dr, g].T.astype(NPBF)
    return {"s5v": s5v, "s5B": s5B, "s5C": s5C, "s5d": s5d}


def _g16(g):
    return np.ascontiguousarray(np.asarray(g, np.float32).reshape(16, 128).T)


def rope_tables_host(p0, n):
    f32 = np.float32
    pos = np.arange(p0, p0 + n, dtype=f32)
    inv = (f32(500000.0) ** (-(np.arange(0, 32, 2, dtype=f32)) / f32(32))).astype(f32)
    ang = (pos[:, None] * inv[None, :]).astype(f32)
    cos, sin = np.cos(ang).astype(f32), np.sin(ang).astype(f32)
    C32 = np.concatenate([cos.T, cos.T], axis=0)
    S32 = np.concatenate([-sin.T, sin.T], axis=0)
    return np.ascontiguousarray(C32), np.ascontiguousarray(S32)


def perm_table():
    p = np.zeros((32, 32), np.float32)
    for m in range(32):
        p[(m + 16) % 32, m] = 1.0
    return p.astype(NPBF)


def swap_cols(W, col0s):
    parts = []
    for c0 in col0s:
        parts.append(W[:, c0 + 16:c0 + 32])
        parts.append(W[:, c0:c0 + 16])
    return np.ascontiguousarray(np.concatenate(parts, axis=1))


_NC_CACHE = {}


def _get_nc(key, fn):
    if key not in _NC_CACHE:
        _NC_CACHE[key] = fn()
    return _NC_CACHE[key]


def _run(nc, in_maps):
    res = run_bass_kernel_spmd(nc, in_maps, core_ids=list(range(NCORE)))
    return res.results


def kernel_unfused(x, mem, norm_mix_g, norm_xa_g, norm_mem_g, xa_wq, xa_wkv, xa_wo, norm_ffn_g,
           ffn_w13, ffn_w2, ab_w_in, ab_w_out, s5_lambda_re, s5_lambda_im, s5_log_step,
           s5_b_re, s5_b_im, s5_c_re, s5_c_im, s5_d, s5_glu_w, s5_glu_b, diff_lambda,
           diff_subln_g, c_w_qkv, c_w_out, final_norm_g):
    A = lambda a: np.asarray(a)
    x = A(x).astype(np.float32, copy=False)
    mem = A(mem).astype(np.float32, copy=False)
    depth = 4
    cores = [(c // 4, (c % 4) * TOK) for c in range(NCORE)]
    hT = [np.ascontiguousarray(x[b, t0:t0 + TOK, :].T) for (b, t0) in cores]
    memT = [np.ascontiguousarray(mem[b].T) for b in range(NB)]
    ropes = [rope_tables_host(t0, TOK) for (b, t0) in cores]
    mask_tab = dilated_mask_table()

    def nxt_inputs(l):
        i = l // 2
        d = {"g_mix": _g16(A(norm_mix_g)[l])}
        if l % 2 == 0:
            W = A(ab_w_in)[i]
            d["w_in"] = W
            d["w_sw"] = swap_cols(W, [1024 + hc * 128 for hc in range(8)] + [2048 + hc * 128 for hc in range(8)])
        else:
            W = A(c_w_qkv)[i]
            d["w_in"] = W
            d["w_sw"] = swap_cols(W, [hc * 128 for hc in range(16)] + [2048 + hc * 128 for hc in range(16)])
        return d

    def prev_inputs(l):
        i = l // 2
        d = {"g_xa": _g16(A(norm_xa_g)[l]), "g_mem": _g16(A(norm_mem_g)[l]), "g_ffn": _g16(A(norm_ffn_g)[l]),
             "wq": A(xa_wq)[l], "wkv": A(xa_wkv)[l], "wo": A(xa_wo)[l], "w13": A(ffn_w13)[l], "w2": A(ffn_w2)[l]}
        if l % 2 == 0:
            d["w_out"] = A(ab_w_out)[i]
            d["glu_w"] = A(s5_glu_w)[i]
            d["glu_b"] = np.ascontiguousarray(A(s5_glu_b)[i].reshape(8, 128).T)
        else:
            d["w_out"] = A(c_w_out)[i]
        return d

    typ = lambda l: "even" if l % 2 == 0 else "odd"
    mix = None
    out = None
    for l in range(depth + 1):
        prev = typ(l - 1) if l > 0 else None
        nxt = typ(l) if l < depth else None
        final = (l == depth)
        nc = _get_nc(("T", prev, nxt, final), lambda: build_T(prev, nxt, final))
        shared = {}
        if prev is not None:
            shared.update(prev_inputs(l - 1))
        if nxt is not None:
            shared.update(nxt_inputs(l))
        if final:
            shared["g_fin"] = _g16(A(final_norm_g))
        in_maps = []
        for c, (b, t0) in enumerate(cores):
            m = dict(shared)
            m["hT_in"] = hT[c]
            if prev is not None:
                m["memT"] = memT[b]
                m.update(mix[c])
            if nxt is not None:
                m["ropeC"], m["ropeS"] = ropes[c]
                m["perm_in"] = perm_table()
            in_maps.append(m)
        res = _run(nc, in_maps)
        if final:
            out = np.empty((NB, SEQ, D), np.float32)
            for c, (b, t0) in enumerate(cores):
                out[b, t0:t0 + TOK, :] = res[c]["outT"].T
            break
        hT = [res[c]["hT_out"] for c in range(NCORE)]
        i = l // 2
        cat = lambda name, b, sl: np.ascontiguousarray(np.concatenate([res[4 * b + r][name][sl] for r in range(4)], axis=-1))
        if nxt == "even":
            lambda_init = 0.8 - 0.6 * math.exp(-0.3 * l)
            nch = _get_nc(("He", l), lambda: build_H_even(lambda_init))
            in_maps = []
            lam_rep = np.ascontiguousarray(np.broadcast_to(A(diff_lambda)[i].astype(np.float32), (128, 4, 128)))
            sgi = np.ascontiguousarray(A(diff_subln_g)[i].astype(np.float32).reshape(2, 128).T)
            for c in range(NCORE):
                b, hd = c // 4, c % 4
                m = {"qT_in": cat("qT_out", b, slice(2 * hd, 2 * hd + 2)),
                     "kT_in": cat("kT_out", b, slice(2 * hd, 2 * hd + 2)),
                     "v_in": np.ascontiguousarray(np.concatenate([res[4 * b + r]["v_out"][:, hd * 256:(hd + 1) * 256] for r in range(4)], axis=0)),
                     "uT_in": cat("uT_out", b, slice(hd * 256, (hd + 1) * 256)),
                     "lam_in": lam_rep, "sg_in": sgi}
                m.update(pack_s5(A(s5_lambda_re)[i], A(s5_lambda_im)[i], A(s5_log_step)[i], A(s5_b_re)[i], A(s5_b_im)[i],
                                 A(s5_c_re)[i], A(s5_c_im)[i], A(s5_d)[i], 16 * hd, 16))
                in_maps.append(m)
            rh = _run(nch, in_maps)
            mix = []
            for c, (b, t0) in enumerate(cores):
                mix.append({"ys5T": np.ascontiguousarray(np.concatenate([rh[4 * b + hd]["ysT_out"][:, t0:t0 + TOK] for hd in range(4)], axis=0)),
                            "ydiffT": np.ascontiguousarray(np.concatenate([rh[4 * b + hd]["ydT_out"][:, t0:t0 + TOK] for hd in range(4)], axis=0))})
        else:
            nch = _get_nc(("Ho",), build_H_odd)
            in_maps = []
            for c in range(NCORE):
                b, hq = c // 4, c % 4
                vv = np.concatenate([res[4 * b + r]["v_out"][:, hq * 512:(hq + 1) * 512] for r in range(4)], axis=0)
                in_maps.append({"qT_in": cat("qT_out", b, slice(4 * hq, 4 * hq + 4)),
                                "kT_in": cat("kT_out", b, slice(4 * hq, 4 * hq + 4)),
                                "v_in": np.ascontiguousarray(vv.reshape(SEQ, 4, 128)),
                                "mask_in": mask_tab})
            rh = _run(nch, in_maps)
            mix = []
            for c, (b, t0) in enumerate(cores):
                mix.append({"oT_in": np.ascontiguousarray(np.concatenate([rh[4 * b + hq]["oT_out"][:, t0:t0 + TOK] for hq in range(4)], axis=0))})
    return out


def build_fused(depth=4):
    nc = bass.Bass("TRN2", target_bir_lowering=False)

    def ext(name, shape, dt=F32):
        return nc.dram_tensor(name, list(shape), dt, kind="ExternalInput").ap()

    def internal(name, shape, dt=F32):
        return nc.dram_tensor(name, list(shape), dt).ap()

    x_hT = ext("x_hT", [D, SEQ])
    memT = ext("memT", [D, 256])
    ropeC = ext("ropeC", [32, SEQ])
    ropeS = ext("ropeS", [32, SEQ])
    mask_in = ext("mask_in", [128, 20, 512], BF16)
    perm_in = ext("perm_in", [32, 32], BF16)
    g_fin = ext("g_fin", [128, 16])
    outT = nc.dram_tensor("outT", [D, SEQ], F32, kind="ExternalOutput").ap()
    L = []
    for l in range(depth):
        d = {}
        for nm in ("g_mix", "g_xa", "g_mem", "g_ffn"):
            d[nm] = ext("%s_%d" % (nm, l), [128, 16])
        d["wq"] = ext("wq_%d" % l, [D, D])
        d["wkv"] = ext("wkv_%d" % l, [D, 2 * D])
        d["wo"] = ext("wo_%d" % l, [D, D])
        d["w13"] = ext("w13_%d" % l, [D, 2 * FFN])
        d["w2"] = ext("w2_%d" % l, [FFN, D])
        d["w_out"] = ext("w_out_%d" % l, [D, D])
        if l % 2 == 0:
            d["w_in"] = ext("w_in_%d" % l, [D, 4096])
            d["w_sw"] = ext("w_sw_%d" % l, [D, 512])
            d["glu_w"] = ext("glu_w_%d" % l, [1024, 1024])
            d["glu_b"] = ext("glu_b_%d" % l, [128, 8])
            d["lam_in"] = ext("lam_in_%d" % l, [128, 4, 128])
            d["sg_in"] = ext("sg_in_%d" % l, [128, 2])
            d["s5v"] = ext("s5v_%d" % l, [32, 2, 128, 3])
            d["s5B"] = ext("s5B_%d" % l, [32, 2, 2, 32, 128], BF16)
            d["s5C"] = ext("s5C_%d" % l, [32, 2, 2, 128, 32], BF16)
            d["s5d"] = ext("s5d_%d" % l, [32, 32, 1])
        else:
            d["w_in"] = ext("w_in_%d" % l, [D, 6144])
            d["w_sw"] = ext("w_sw_%d" % l, [D, 1024])
        L.append(d)
    hT = internal("hT", [D, SEQ])
    uT = internal("uT", [1024, SEQ], BF16)
    qTe = internal("qTe", [8, 128, SEQ], BF16)
    kTe = internal("kTe", [8, 128, SEQ], BF16)
    ve = internal("ve", [SEQ, 1024], BF16)
    qTo = internal("qTo", [16, 128, SEQ], BF16)
    kTo = internal("kTo", [16, 128, SEQ], BF16)
    vo = internal("vo", [SEQ, 2048], BF16)
    ys5T = internal("ys5T", [1024, SEQ])
    ydiffT = internal("ydiffT", [1024, SEQ], BF16)
    oT = internal("oT", [2048, SEQ], BF16)
    typ = lambda l: "even" if l % 2 == 0 else "odd"
    bgs = {}

    def launch_wio(l):
        prev = typ(l - 1) if l > 0 else None
        nxt = typ(l) if l < depth else None
        wio = {}
        if prev is not None:
            wio.update({nm: L[l - 1][nm] for nm in ("w_out", "wq", "wkv", "wo", "w13", "w2")})
            if prev == "even":
                wio["glu_w"] = L[l - 1]["glu_w"]
        if nxt is not None:
            wio.update({"w_in": L[l]["w_in"], "w_sw": L[l]["w_sw"]})
        return t_weight_specs(prev, nxt, wio)

    for l in range(depth + 1):
        prev = typ(l - 1) if l > 0 else None
        nxt = typ(l) if l < depth else None
        final = (l == depth)
        wio = {}
        if prev is not None:
            wio.update({nm: L[l - 1][nm] for nm in ("w_out", "wq", "wkv", "wo", "w13", "w2")})
            if prev == "even":
                wio["glu_w"] = L[l - 1]["glu_w"]
        if nxt is not None:
            wio.update({"w_in": L[l]["w_in"], "w_sw": L[l]["w_sw"]})
        if bgs.get(l) is None:
            wb = emit_W(nc, t_weight_specs(prev, nxt, wio))
        else:
            bgs[l].flush()
            wb = bgs[l].out
        TSUB = 4096
        for r in range(SEQ // TSUB):
            ts = slice(r * TSUB, (r + 1) * TSUB)
            io = {"hT_in": (x_hT if l == 0 else hT)[:, ts]}
            if prev is not None:
                pl = L[l - 1]
                for nm in ("w_out", "g_xa", "g_mem", "wq", "wkv", "wo", "g_ffn", "w13", "w2"):
                    io[nm] = pl[nm]
                io["memT"] = memT
                if prev == "even":
                    io["ys5T"] = ys5T[:, ts]
                    io["ydiffT"] = ydiffT[:, ts]
                    io["glu_w"] = pl["glu_w"]
                    io["glu_b"] = pl["glu_b"]
                else:
                    io["oT_in"] = oT[:, ts]
            if nxt is not None:
                nl = L[l]
                io["g_mix"] = nl["g_mix"]
                io["ropeC"] = ropeC[:, ts]
                io["ropeS"] = ropeS[:, ts]
                io["perm_in"] = perm_in
                io["w_in"] = nl["w_in"]
                io["w_sw"] = nl["w_sw"]
                io["hT_out"] = hT[:, ts]
                if nxt == "even":
                    io["uT_out"] = uT[:, ts]
                    io["qT_out"] = qTe[:, :, ts]
                    io["kT_out"] = kTe[:, :, ts]
                    io["v_out"] = ve[ts, :]
                else:
                    io["qT_out"] = qTo[:, :, ts]
                    io["kT_out"] = kTo[:, :, ts]
                    io["v_out"] = vo[ts, :]
            if final:
                io["g_fin"] = g_fin
                io["outT"] = outT[:, ts]
            build_T(prev, nxt, final, nc=nc, io=io, wb=wb, ntok=TSUB)
        if final:
            break
        bg = BgW(nc, launch_wio(l + 1))
        bgs[l + 1] = bg
        bg_total = -(-len(bg.jobs) // 4)
        if nxt == "even":
            lambda_init = 0.8 - 0.6 * math.exp(-0.3 * l)
            nl = L[l]
            for hd in range(4):
                cs = slice(hd * 256, (hd + 1) * 256)
                io = {"qT_in": qTe[2 * hd:2 * hd + 2], "kT_in": kTe[2 * hd:2 * hd + 2], "v_in": ve[:, cs],
                      "lam_in": nl["lam_in"], "sg_in": nl["sg_in"], "ydT_out": ydiffT[cs, :], "uT_in": uT[cs, :],
                      "s5v": nl["s5v"][8 * hd:8 * hd + 8], "s5B": nl["s5B"][8 * hd:8 * hd + 8], "s5C": nl["s5C"][8 * hd:8 * hd + 8],
                      "s5d": nl["s5d"][8 * hd:8 * hd + 8], "ysT_out": ys5T[cs, :]}
                build_H_even(lambda_init, nc=nc, io=io, bg=bg, bg_total=bg_total)
        else:
            for hq in range(4):
                io = {"qT_in": qTo[4 * hq:4 * hq + 4], "kT_in": kTo[4 * hq:4 * hq + 4],
                      "v_in": vo[:, hq * 512:(hq + 1) * 512].rearrange("s (h e) -> s h e", h=4),
                      "mask_in": mask_in, "oT_out": oT[hq * 512:(hq + 1) * 512, :]}
                build_H_odd(nc=nc, io=io, bg=bg, bg_total=bg_total)
    return nc


def kernel_fused(x, mem, norm_mix_g, norm_xa_g, norm_mem_g, xa_wq, xa_wkv, xa_wo, norm_ffn_g,
                 ffn_w13, ffn_w2, ab_w_in, ab_w_out, s5_lambda_re, s5_lambda_im, s5_log_step,
                 s5_b_re, s5_b_im, s5_c_re, s5_c_im, s5_d, s5_glu_w, s5_glu_b, diff_lambda,
                 diff_subln_g, c_w_qkv, c_w_out, final_norm_g):
    A = lambda a: np.asarray(a)
    x = A(x).astype(np.float32, copy=False)
    mem = A(mem).astype(np.float32, copy=False)
    depth = 4
    nc = _get_nc(("fused",), build_fused)
    C32, S32 = rope_tables_host(0, SEQ)
    shared = {"ropeC": C32, "ropeS": S32, "mask_in": dilated_mask_table(), "g_fin": _g16(A(final_norm_g)), "perm_in": perm_table()}
    for l in range(depth):
        i = l // 2
        shared["g_mix_%d" % l] = _g16(A(norm_mix_g)[l])
        shared["g_xa_%d" % l] = _g16(A(norm_xa_g)[l])
        shared["g_mem_%d" % l] = _g16(A(norm_mem_g)[l])
        shared["g_ffn_%d" % l] = _g16(A(norm_ffn_g)[l])
        shared["wq_%d" % l] = A(xa_wq)[l]
        shared["wkv_%d" % l] = A(xa_wkv)[l]
        shared["wo_%d" % l] = A(xa_wo)[l]
        shared["w13_%d" % l] = A(ffn_w13)[l]
        shared["w2_%d" % l] = A(ffn_w2)[l]
        if l % 2 == 0:
            W = A(ab_w_in)[i]
            shared["w_in_%d" % l] = W
            shared["w_sw_%d" % l] = swap_cols(W, [1024 + hc * 128 for hc in range(8)] + [2048 + hc * 128 for hc in range(8)])
            shared["w_out_%d" % l] = A(ab_w_out)[i]
            shared["glu_w_%d" % l] = A(s5_glu_w)[i]
            shared["glu_b_%d" % l] = np.ascontiguousarray(A(s5_glu_b)[i].reshape(8, 128).T)
            shared["lam_in_%d" % l] = np.ascontiguousarray(np.broadcast_to(A(diff_lambda)[i].astype(np.float32), (128, 4, 128)))
            shared["sg_in_%d" % l] = np.ascontiguousarray(A(diff_subln_g)[i].astype(np.float32).reshape(2, 128).T)
            pk = pack_s5(A(s5_lambda_re)[i], A(s5_lambda_im)[i], A(s5_log_step)[i], A(s5_b_re)[i], A(s5_b_im)[i],
                         A(s5_c_re)[i], A(s5_c_im)[i], A(s5_d)[i], 0, 64)
            for k, v in pk.items():
                shared["%s_%d" % (k, l)] = v
        else:
            W = A(c_w_qkv)[i]
            shared["w_in_%d" % l] = W
            shared["w_sw_%d" % l] = swap_cols(W, [hc * 128 for hc in range(16)] + [2048 + hc * 128 for hc in range(16)])
            shared["w_out_%d" % l] = A(c_w_out)[i]
    in_maps = []
    for b in range(NB):
        m = dict(shared)
        m["x_hT"] = np.ascontiguousarray(x[b].T)
        m["memT"] = np.ascontiguousarray(mem[b].T)
        in_maps.append(m)
    res = run_bass_kernel_spmd(nc, in_maps, core_ids=list(range(NB))).results
    out = np.empty((NB, SEQ, D), np.float32)
    for b in range(NB):
        out[b] = res[b]["outT"].T
    return out


def kernel(**inputs):
    return kernel_fused(**inputs)
```

```python
import contextlib
import math
import numpy as np
import ml_dtypes
import concourse.bass as bass
import concourse.mybir as mybir
from concourse.bass_utils import run_bass_kernel_spmd

F32 = mybir.dt.float32
BF16 = mybir.dt.bfloat16
I32 = mybir.dt.int32
ALU = mybir.AluOpType
AF = mybir.ActivationFunctionType
NPBF = ml_dtypes.bfloat16

D = 2048
SEQ = 8192
NB = 2
NCORE = 8
TOK = 2048
TG = 512
FFN = 5632
EPS = 1e-6
SKIP = set()
NGRP_OVERRIDE = 0


class Prog:
    DMA_K = 6

    def __init__(self, nc):
        self.nc = nc
        self.ops = []
        self.last_w = {}
        self.readers = {}

    def add(self, eng, fn, R=(), W=(), dma=False):
        i = len(self.ops)
        deps = set()
        for k in list(R) + list(W):
            if k in self.last_w:
                deps.add(self.last_w[k])
        for k in W:
            for r in self.readers.get(k, ()):
                deps.add(r)
        for k in R:
            lst = self.readers.setdefault(k, [])
            if not dma:
                lst[:] = [r for r in lst if self.ops[r]["dma"] or self.ops[r]["eng"] != eng]
            lst.append(i)
        for k in W:
            self.last_w[k] = i
            self.readers[k] = []
        self.ops.append(dict(eng=eng, fn=fn, deps=deps, dma=dma, sig=dma, signal=None))
        return i

    def dma(self, eng, out, in_, R=(), W=()):
        return self.add(eng, lambda e: e.dma_start(out=out, in_=in_), R, W, dma=True)

    def emit(self):
        nc = self.nc
        ops = self.ops
        for i, o in enumerate(ops):
            nd = set()
            for d in o["deps"]:
                if ops[d]["eng"] == "tensor" and o["eng"] == "tensor" and not ops[d]["dma"] and not o["dma"]:
                    continue
                nd.add(d)
            o["deps"] = nd
            for d in nd:
                ops[d]["sig"] = True
        engs = ["tensor", "vector", "scalar", "gpsimd", "sync"]
        pfx = getattr(self, "pfx", "")
        self.sems = []

        def mk(name):
            h = nc.alloc_semaphore(name=pfx + name)
            self.sems.append(h)
            return h
        csem = {e: mk("c_" + e) for e in engs}
        dsem = {e: [mk("d_%s%d" % (e, k)) for k in range(self.DMA_K)] for e in ["sync", "gpsimd"]}
        ccount = {e: 0 for e in engs}
        dcount = {e: 0 for e in dsem}
        streams = {e: [] for e in engs}
        waited = {e: {} for e in engs}
        final = {}
        for i, o in enumerate(ops):
            E = o["eng"]
            st = streams[E]
            w = waited[E]
            for d in sorted(o["deps"]):
                sem, val = ops[d]["signal"]
                if w.get(id(sem), 0) < val:
                    w[id(sem)] = val
                    st.append((lambda e, sem=sem, val=val: e.wait_ge(sem, val)))
            if o["dma"]:
                n = dcount[E]
                dcount[E] += 1
                sem = dsem[E][n % self.DMA_K]
                val = 16 * (n // self.DMA_K + 1)
                if n >= self.DMA_K and w.get(id(sem), 0) < val - 16:
                    w[id(sem)] = val - 16
                    st.append((lambda e, sem=sem, val=val: e.wait_ge(sem, val - 16)))
                o["signal"] = (sem, val)
                final[id(sem)] = (sem, val)
                st.append((lambda e, fn=o["fn"], sem=sem: fn(e).then_inc(sem, 16)))
            elif o["sig"]:
                ccount[E] += 1
                sem = csem[E]
                val = ccount[E]
                o["signal"] = (sem, val)
                st.append((lambda e, fn=o["fn"], sem=sem: fn(e).then_inc(sem, 1)))
            else:
                st.append((lambda e, fn=o["fn"]: fn(e)))
        for sem, val in final.values():
            if waited["sync"].get(id(sem), 0) < val:
                streams["sync"].append((lambda e, sem=sem, val=val: e.wait_ge(sem, val)))
        with nc.Block() as block:
            @block.tensor
            def _(e):
                for f in streams["tensor"]:
                    f(e)

            @block.vector
            def _(e):
                for f in streams["vector"]:
                    f(e)

            @block.scalar
            def _(e):
                for f in streams["scalar"]:
                    f(e)

            @block.gpsimd
            def _(e):
                for f in streams["gpsimd"]:
                    f(e)

            @block.sync
            def _(e):
                for f in streams["sync"]:
                    f(e)
        self.stats = {e: len(streams[e]) for e in engs}
        self.counts = (dict(ccount), dict(dcount))


class Ctx:
    count = 0

    def __init__(self, nc=None, io=None):
        self.io = io
        Ctx.count += 1
        self.pfx = "p%d_" % Ctx.count
        self.nc = nc if nc is not None else bass.Bass("TRN2", target_bir_lowering=False)
        self.P = Prog(self.nc)
        self.S = contextlib.ExitStack()
        self.P.stack = self.S
        self.P.pfx = self.pfx
        self.nps = 0
        self.ps = []
        self.rot = {}
        self.cast_i = 0

    def din(self, name, shape, dt=F32):
        if self.io is not None:
            ap = self.io[name]
            assert list(ap.shape) == list(shape), (name, ap.shape, shape)
            return ap
        return self.nc.dram_tensor(name, list(shape), dt, kind="ExternalInput").ap()

    def dout(self, name, shape, dt=F32):
        if self.io is not None:
            ap = self.io[name]
            assert list(ap.shape) == list(shape), (name, ap.shape, shape)
            return ap
        return self.nc.dram_tensor(name, list(shape), dt, kind="ExternalOutput").ap()

    def finish(self):
        self.P.emit()
        if self.io is not None:
            self.nc.all_engine_barrier()
            self.nc.clear_and_free_semaphores(self.P.sems)
            self.nc.all_engine_barrier()
        self.S.close()
        return self.nc

    def sb(self, name, shape, dt):
        return self.S.enter_context(self.nc.sbuf_tensor(self.pfx + name, list(shape), dt))

    def alloc_psum(self, n=8):
        self.ps = [self.S.enter_context(self.nc.psum_tensor(self.pfx + "ps%d" % i, [128, 512], F32)) for i in range(n)]

    def next_ps(self, group="main", banks=None):
        banks = banks if banks is not None else list(range(len(self.ps)))
        i = self.rot.get(group, 0)
        self.rot[group] = i + 1
        b = banks[i % len(banks)]
        return self.ps[b], ("ps", b)

    def nxt(self, group, n):
        i = self.rot.get(group, 0)
        self.rot[group] = i + 1
        return i % n


class BgW:
    count = 0

    def __init__(self, nc, specs):
        BgW.count += 1
        self.nc = nc
        self.jobs = []
        self.out = {}
        self.pos = 0
        for key, W, blk in specs:
            K_, N_ = W.shape
            nci, nb = K_ // 128, N_ // blk
            Wb = nc.dram_tensor("bgw%d_%s" % (BgW.count, key), [nb, 128, nci, blk], BF16).ap()
            for j in range(nb):
                for ci0 in range(0, nci, 16):
                    n = min(16, nci - ci0)
                    src = W[ci0 * 128:(ci0 + n) * 128, j * blk:(j + 1) * blk].rearrange("(c p) o -> p c o", p=128)
                    self.jobs.append((src, Wb[j, :, ci0:ci0 + n, :], n, blk))
            self.out[key] = (Wb, blk)

    def remaining(self):
        return len(self.jobs) - self.pos

    def emit(self, C, count=1, cast_engines=("scalar",)):
        if not hasattr(C, "bg_st"):
            C.bg_st = [C.sb("bgst%d" % i, [128, 16, 128], F32) for i in range(2)]
            C.bg_bf = [C.sb("bgbf%d" % i, [128, 16, 128], BF16) for i in range(2)]
            C.bg_k = 0
        P = C.P
        for _ in range(count):
            if self.pos >= len(self.jobs):
                return
            src, dst, n, blk = self.jobs[self.pos]
            self.pos += 1
            k = C.bg_k
            C.bg_k += 1
            s = k % 2
            st, bf = C.bg_st[s], C.bg_bf[s]
            P.dma("sync", st[:, 0:n, 0:blk], src, W=[("bgst", s)])
            ce = cast_engines[k % len(cast_engines)]
            if ce == "scalar":
                P.add("scalar", lambda e, st=st, bf=bf, n=n, blk=blk: e.activation(out=bf[:, 0:n, 0:blk], in_=st[:, 0:n, 0:blk], func=AF.Copy),
                      R=[("bgst", s)], W=[("bgbf", s)])
            else:
                P.add(ce, lambda e, st=st, bf=bf, n=n, blk=blk: e.tensor_copy(out=bf[:, 0:n, 0:blk], in_=st[:, 0:n, 0:blk]),
                      R=[("bgst", s)], W=[("bgbf", s)])
            P.dma("gpsimd", dst, bf[:, 0:n, 0:blk], R=[("bgbf", s)])

    def flush(self):
        if self.remaining() == 0:
            return
        C = Ctx(self.nc, io={})
        self.emit(C, self.remaining(), cast_engines=("vector", "scalar"))
        C.finish()


def emit_W(nc, specs):
    C = Ctx(nc, io={})
    P = C.P
    NS = 4
    wst = [C.sb("wst%d" % i, [128, 16, 128], F32) for i in range(NS)]
    wbf = [C.sb("wbf%d" % i, [128, 16, 128], BF16) for i in range(NS)]
    out = {}
    k = 0
    for key, W, blk in specs:
        K_, N_ = W.shape
        nci, nb = K_ // 128, N_ // blk
        Wb = nc.dram_tensor(C.pfx + "wb_" + key, [nb, 128, nci, blk], BF16).ap()
        for j in range(nb):
            for ci0 in range(0, nci, 16):
                n = min(16, nci - ci0)
                s = k % NS
                src = W[ci0 * 128:(ci0 + n) * 128, j * blk:(j + 1) * blk].rearrange("(c p) o -> p c o", p=128)
                P.dma("sync", wst[s][:, 0:n, 0:blk], src, W=[("wst", s)])
                ce = ["vector", "scalar"][k % 2]
                if ce == "scalar":
                    P.add("scalar", lambda e, s=s, n=n, blk=blk: e.activation(out=wbf[s][:, 0:n, 0:blk], in_=wst[s][:, 0:n, 0:blk], func=AF.Copy),
                          R=[("wst", s)], W=[("wbf", s)])
                else:
                    P.add(ce, lambda e, s=s, n=n, blk=blk: e.tensor_copy(out=wbf[s][:, 0:n, 0:blk], in_=wst[s][:, 0:n, 0:blk]),
                          R=[("wst", s)], W=[("wbf", s)])
                P.dma("sync" if k % 2 else "gpsimd", Wb[j, :, ci0:ci0 + n, :], wbf[s][:, 0:n, 0:blk], R=[("wbf", s)])
                k += 1
        out[key] = (Wb, blk)
    C.finish()
    return out


def t_weight_specs(prev, nxt, io):
    specs = []
    if prev is not None:
        if prev == "even":
            specs.append(("glu_w", io["glu_w"], 128))
        for nm in ("w_out", "wq", "wkv", "wo", "w13", "w2"):
            specs.append((nm, io[nm], 128))
    if nxt is not None:
        specs.append(("w_in", io["w_in"], 128))
    return specs


def build_T(prev, nxt, final, nc=None, io=None, wb=None, ntok=None):
    C = Ctx(nc, io)
    nc, P = C.nc, C.P
    add, dma = P.add, P.dma
    ntok = ntok or TOK
    ngrp = NGRP_OVERRIDE or (ntok // TG)

    hT_in = C.din("hT_in", [D, ntok])
    if prev is not None:
        if prev == "even":
            ys5T = C.din("ys5T", [1024, ntok])
            ydiffT = C.din("ydiffT", [1024, ntok], BF16)
            glu_w = C.din("glu_w", [1024, 1024])
            glu_b = C.din("glu_b", [128, 8])
        else:
            oT_in = C.din("oT_in", [D, ntok], BF16)
        w_out = C.din("w_out", [D, D])
        memT = C.din("memT", [D, 256])
        g_xa = C.din("g_xa", [128, 16])
        g_mem = C.din("g_mem", [128, 16])
        wq = C.din("wq", [D, D])
        wkv = C.din("wkv", [D, 2 * D])
        wo = C.din("wo", [D, D])
        g_ffn = C.din("g_ffn", [128, 16])
        w13 = C.din("w13", [D, 2 * FFN])
        w2 = C.din("w2", [FFN, D])
    if nxt is not None:
        g_mix = C.din("g_mix", [128, 16])
        ropeC = C.din("ropeC", [32, ntok])
        ropeS = C.din("ropeS", [32, ntok])
        perm_in = C.din("perm_in", [32, 32], BF16)
        if nxt == "even":
            w_in = C.din("w_in", [D, 4096])
            w_sw = C.din("w_sw", [D, 16 * 32])
            uT_out = C.dout("uT_out", [1024, ntok], BF16)
            qT_out = C.dout("qT_out", [8, 128, ntok], BF16)
            kT_out = C.dout("kT_out", [8, 128, ntok], BF16)
            v_out = C.dout("v_out", [ntok, 1024], BF16)
        else:
            w_in = C.din("w_in", [D, 6144])
            w_sw = C.din("w_sw", [D, 32 * 32])
            qT_out = C.dout("qT_out", [16, 128, ntok], BF16)
            kT_out = C.dout("kT_out", [16, 128, ntok], BF16)
            v_out = C.dout("v_out", [ntok, 2048], BF16)
        hT_out = C.dout("hT_out", [D, ntok])
    if final:
        g_fin = C.din("g_fin", [128, 16])
        outT = C.dout("outT", [D, ntok])

    if wb is None:
        wio = {}
        if prev is not None:
            wio.update({"w_out": w_out, "wq": wq, "wkv": wkv, "wo": wo, "w13": w13, "w2": w2})
            if prev == "even":
                wio["glu_w"] = glu_w
        if nxt is not None:
            wio.update({"w_in": w_in, "w_sw": w_sw})
        wb = emit_W(nc, t_weight_specs(prev, nxt, wio))
    wbmap = {}
    if prev is not None:
        for nm, ap in (("w_out", w_out), ("wq", wq), ("wkv", wkv), ("wo", wo), ("w13", w13), ("w2", w2)):
            wbmap[id(ap)] = wb[nm]
        if prev == "even":
            wbmap[id(glu_w)] = wb["glu_w"]
    if nxt is not None:
        wbmap[id(w_in)] = wb["w_in"]

    C.alloc_psum(8)
    hT = C.sb("hT", [128, 16, TG], F32)
    hn = C.sb("hn", [128, 16, TG], BF16)
    sq = [C.sb("sq%d" % i, [128, TG], BF16) for i in range(2)]
    rstd = C.sb("rstd", [128, TG], F32)
    ones = C.sb("ones", [128, 128], BF16)
    epsc = C.sb("epsc", [128, 1], F32)
    NW = 6
    wbf = [C.sb("wbf%d" % i, [128, 16, 128], BF16) for i in range(NW)]
    gvec = C.sb("gvec", [128, 5, 16], F32)
    add("gpsimd", lambda e: e.memset(ones[:], 1.0), W=["ones"])
    add("gpsimd", lambda e: e.memset(epsc[:], EPS), W=["epsc"])

    def load_g(slot, src):
        dma("sync", gvec[:, slot, :], src[:, :], W=[("gvec", slot)])

    def load_w(Wd, ci0, n, c0, ncol=128):
        Wb, blk = wbmap[id(Wd)]
        assert ncol == blk and c0 % blk == 0
        s = C.nxt("w", NW)
        dma("sync", wbf[s][:, 0:n, 0:ncol], Wb[c0 // blk, :, ci0:ci0 + n, :], W=[("wbf", s)])
        return wbf[s], ("wbf", s)

    def linear(Wd, n_ci, cols, x_tile, x_keys, evac, ntok=TG, banks=None):
        for c0 in cols:
            ps, pk = C.next_ps("main", banks)
            blocks = []
            ci = 0
            while ci < n_ci:
                n = min(16, n_ci - ci)
                blocks.append((ci, n))
                ci += n
            first = True
            for (ci0, n) in blocks:
                wt, wk = load_w(Wd, ci0, n, c0)
                for j in range(n):
                    last = (ci0 + j == n_ci - 1)
                    xk = [("hn", ci0 + j)] if list(x_keys) == ["hn"] else list(x_keys)
                    add("tensor", lambda e, wt=wt, j=j, cj=ci0 + j, ps=ps, first=first, last=last:
                        e.matmul(ps[:, 0:ntok], lhsT=wt[:, j, :], rhs=x_tile[:, cj, 0:ntok], start=first, stop=last),
                        R=[wk] + xk, W=[pk])
                    first = False
            evac(c0, ps, pk)

    def rmsnorm(src_tile, src_keys, gslot, dst_tile, dst_key, ntok=TG, nchunk=16):
        ps, pk = C.next_ps("main")
        for c in range(nchunk):
            s = C.nxt("sq", 2)
            add("scalar", lambda e, s=s, c=c: e.activation(out=sq[s][:, 0:ntok], in_=src_tile[:, c, 0:ntok], func=AF.Square),
                R=list(src_keys), W=[("sq", s)])
            add("tensor", lambda e, s=s, c=c, ps=ps: e.matmul(ps[:, 0:ntok], lhsT=ones[:], rhs=sq[s][:, 0:ntok],
                                                               start=(c == 0), stop=(c == nchunk - 1)),
                R=[("sq", s), "ones"], W=[pk])
        add("scalar", lambda e, ps=ps: e.activation(out=rstd[:, 0:ntok], in_=ps[:, 0:ntok], func=AF.Sqrt,
                                                     scale=1.0 / D, bias=epsc[:, 0:1]), R=[pk, "epsc"], W=["rstd"])
        add("vector", lambda e: e.reciprocal(out=rstd[:, 0:ntok], in_=rstd[:, 0:ntok]), R=["rstd"], W=["rstd"])
        for c in range(nchunk):
            add("vector", lambda e, c=c: e.scalar_tensor_tensor(out=dst_tile[:, c, 0:ntok], in0=src_tile[:, c, 0:ntok],
                                                                 scalar=gvec[:, gslot, c:c + 1], in1=rstd[:, 0:ntok],
                                                                 op0=ALU.mult, op1=ALU.mult),
                R=list(src_keys) + ["rstd", ("gvec", gslot)], W=[(dst_key, c) if dst_key == "hn" else dst_key])

    def resid_evac(c0, ps, pk):
        c = c0 // 128
        add("vector", lambda e: e.tensor_tensor(out=hT[:, c, :], in0=hT[:, c, :], in1=ps[:, :], op=ALU.add),
            R=[pk, "hT"], W=["hT"])

    if prev is not None:
        load_g(0, g_xa)
        load_g(1, g_mem)
        load_g(2, g_ffn)
        memt = hT
        memn = hn
        kmT = C.sb("kmT", [128, 16, 256], BF16)
        vm = C.sb("vm", [128, 2, 2048], BF16)
        dma("sync", memt[:, :, 0:256], memT.rearrange("(c p) t -> p c t", p=128), W=["hT"])
        rmsnorm(memt, ["hT"], 1, memn, "hn", ntok=256)

        def k_evac(c0, ps, pk):
            c = c0 // 128
            add("vector", lambda e: e.tensor_copy(out=kmT[:, c, :], in_=ps[:, 0:256]), R=[pk], W=["kmT"])
        linear(wkv, 16, [c * 128 for c in range(16)], memn, ["hn"], k_evac, ntok=256)
        for c in range(16):
            wt, wk = load_w(wkv, 0, 16, D + c * 128)
            for kt in range(2):
                ps, pk = C.next_ps("main")
                for j in range(16):
                    add("tensor", lambda e, wt=wt, j=j, ps=ps, kt=kt: e.matmul(ps[:, 0:128], lhsT=memn[:, j, kt * 128:(kt + 1) * 128],
                                                                                rhs=wt[:, j, :], start=(j == 0), stop=(j == 15)),
                        R=[wk, ("hn", j)], W=[pk])
                add("vector", lambda e, ps=ps, kt=kt, c=c: e.tensor_copy(out=vm[:, kt, c * 128:(c + 1) * 128], in_=ps[:, 0:128]),
                    R=[pk], W=["vm"])
        if prev == "even":
            glub = C.sb("glub", [128, 8], F32)
            dma("sync", glub[:], glu_b[:, :], W=["glub"])
    if nxt is not None:
        load_g(3, g_mix)
    if final:
        load_g(4, g_fin)

    if prev is not None:
        xin = C.sb("xin", [128, 16, TG], BF16)
        et = [C.sb("et%d" % i, [128, TG], BF16) for i in range(2)]
        rl = C.sb("rl", [128, TG], F32)
        gT = C.sb("gT", [128, 44, TG], BF16)
        qx = gT[:, 24:40, :]
        sa = [C.sb("sa%d" % i, [128, TG], F32) for i in range(2)]
        if prev == "even":
            ysf = [C.sb("ysf%d" % i, [128, TG], F32) for i in range(2)]
            gg = gT[:, 16:24, :]
            sg = C.sb("sg", [128, TG], F32)
            g2 = C.sb("g2", [128, TG], F32)
    if nxt is not None:
        rc = C.sb("rc", [32, TG], F32)
        rs = C.sb("rs", [32, TG], F32)
        qhi = [C.sb("qhi%d" % i, [32, TG], BF16) for i in range(2)]
        qlo = [C.sb("qlo%d" % i, [32, TG], BF16) for i in range(2)]
        permt = C.sb("permt", [32, 32], BF16)
        dma("sync", permt[:], perm_in[:, :], W=["permt"])
        t1 = [C.sb("t1_%d" % i, [32, TG], F32) for i in range(1)]
        t2 = [C.sb("t2_%d" % i, [32, TG], F32) for i in range(1)]
        ob = [C.sb("ob%d" % i, [128, TG], BF16) for i in range(3)]
        nvc = 1024 if nxt == "even" else 2048
        if prev is not None:
            vtok = gT[:, 0:16, :].rearrange("p (a b) t -> p a (b t)", a=4)[:, :, 0:nvc]
        else:
            vtok = C.sb("vtok", [128, 4, nvc], BF16)[:, :, :]
    if final:
        of = [C.sb("of%d" % i, [128, TG], F32) for i in range(2)]

    def emit_gg(gi):
        tg0 = gi * TG
        for c in range(8):
            y = C.nxt("ysf", 2)
            dma("gpsimd", ysf[y][:], ys5T[c * 128:(c + 1) * 128, tg0:tg0 + TG], W=[("ysf", y)])
            add("scalar", lambda e, y=y: e.activation(out=g2[:], in_=ysf[y][:], func=AF.Square), R=[("ysf", y)], W=["g2"])
            add("vector", lambda e: e.tensor_scalar(out=g2[:], in0=g2[:], scalar1=0.044715, scalar2=1.0, op0=ALU.mult, op1=ALU.add),
                R=["g2"], W=["g2"])
            add("vector", lambda e, y=y: e.tensor_tensor(out=g2[:], in0=g2[:], in1=ysf[y][:], op=ALU.mult), R=["g2", ("ysf", y)], W=["g2"])
            add("scalar", lambda e: e.activation(out=g2[:], in_=g2[:], func=AF.Sigmoid, scale=1.5957691216057308), R=["g2"], W=["g2"])
            add("vector", lambda e, y=y, c=c: e.tensor_tensor(out=gg[:, c, :], in0=g2[:], in1=ysf[y][:], op=ALU.mult),
                R=["g2", ("ysf", y)], W=["gT"])

    for g in range(ngrp):
        t0 = g * TG
        dma("gpsimd", hT[:], hT_in[:, t0:t0 + TG].rearrange("(c p) t -> p c t", p=128), W=["hT"])
        if prev is not None:
            if prev == "even":
                dma("gpsimd", xin[:, 8:16, :], ydiffT[:, t0:t0 + TG].rearrange("(c p) t -> p c t", p=128), W=["xin_hi"])
                if g == 0:
                    emit_gg(0)

                def glu_evac(c0, ps, pk):
                    c = c0 // 128
                    add("scalar", lambda e: e.activation(out=sg[:], in_=ps[:, :], func=AF.Sigmoid, bias=glub[:, c:c + 1]),
                        R=[pk, "glub"], W=["sg"])
                    add("vector", lambda e: e.tensor_tensor(out=xin[:, c, :], in0=sg[:], in1=gg[:, c, :], op=ALU.mult),
                        R=["sg", "gT"], W=["xin_lo"])
                linear(glu_w, 8, [c * 128 for c in range(8)], gg, ["gT"], glu_evac)
                xkeys = ["xin_lo", "xin_hi"]
            else:
                dma("gpsimd", xin[:], oT_in[:, t0:t0 + TG].rearrange("(c p) t -> p c t", p=128), W=["xin_lo", "xin_hi"])
                xkeys = ["xin_lo", "xin_hi"]
            if "mix" not in SKIP:
                linear(w_out, 16, [c * 128 for c in range(16)], xin, xkeys, resid_evac)

            if "xa" in SKIP:
                continue
            rmsnorm(hT, ["hT"], 0, hn, "hn")

            def q_evac(c0, ps, pk):
                c = c0 // 128
                add("vector", lambda e: e.tensor_copy(out=qx[:, c, :], in_=ps[:, :]), R=[pk], W=["gT"])
            linear(wq, 16, [c * 128 for c in range(16)], hn, ["hn"], q_evac)
            for h in range(4):
                eks = []
                for kt in range(2):
                    ps, pk = C.next_ps("main")
                    for c in range(4):
                        add("tensor", lambda e, ps=ps, c=c, kt=kt, h=h: e.matmul(ps[:, :], lhsT=kmT[:, 4 * h + c, kt * 128:(kt + 1) * 128],
                                                                                   rhs=qx[:, 4 * h + c, :], start=(c == 0), stop=(c == 3)),
                            R=["kmT", "gT"], W=[pk])
                    add("scalar", lambda e, ps=ps, kt=kt: e.activation(out=et[kt][:], in_=ps[:, :], func=AF.Exp, scale=512 ** -0.5),
                        R=[pk], W=[("et", kt)])
                psl, pkl = C.next_ps("main")
                for kt in range(2):
                    add("tensor", lambda e, psl=psl, kt=kt: e.matmul(psl[:, :], lhsT=ones[:], rhs=et[kt][:], start=(kt == 0), stop=(kt == 1)),
                        R=[("et", kt), "ones"], W=[pkl])
                add("vector", lambda e, psl=psl: e.reciprocal(out=rl[:], in_=psl[:, :]), R=[pkl], W=["rl"])
                for dc in range(4):
                    ps, pk = C.next_ps("main")
                    for kt in range(2):
                        add("tensor", lambda e, ps=ps, kt=kt, dc=dc, h=h: e.matmul(ps[:, :], lhsT=vm[:, kt, h * 512 + dc * 128:h * 512 + (dc + 1) * 128],
                                                                                     rhs=et[kt][:], start=(kt == 0), stop=(kt == 1)),
                            R=["vm", ("et", kt)], W=[pk])
                    add("vector", lambda e, ps=ps, dc=dc, h=h: e.tensor_tensor(out=xin[:, 4 * h + dc, :], in0=ps[:, :], in1=rl[:], op=ALU.mult),
                        R=[pk, "rl"], W=["xin_lo", "xin_hi"])
            linear(wo, 16, [c * 128 for c in range(16)], xin, ["xin_lo", "xin_hi"], resid_evac)

            if "ffn" in SKIP:
                continue
            rmsnorm(hT, ["hT"], 2, hn, "hn")
            for j in range(44):
                holder = {}

                def a_evac(c0, ps, pk, holder=holder):
                    s = C.nxt("sa", 2)
                    holder["s"] = s
                    add("scalar", lambda e: e.activation(out=sa[s][:], in_=ps[:, :], func=AF.Silu), R=[pk], W=[("sa", s)])

                def b_evac(c0, ps, pk, holder=holder, j=j):
                    s = holder["s"]
                    add("vector", lambda e: e.tensor_tensor(out=gT[:, j, :], in0=ps[:, :], in1=sa[s][:], op=ALU.mult),
                        R=[pk, ("sa", s)], W=["gT"])
                linear(w13, 16, [j * 128], hn, ["hn"], a_evac)
                linear(w13, 16, [FFN + j * 128], hn, ["hn"], b_evac)
            linear(w2, 44, [c * 128 for c in range(16)], gT, ["gT"], resid_evac)
            if prev == "even" and g + 1 < ngrp:
                emit_gg(g + 1)

        if nxt is not None:
            dma("gpsimd", hT_out[:, t0:t0 + TG].rearrange("(c p) t -> p c t", p=128), hT[:], R=["hT"])
            rmsnorm(hT, ["hT"], 3, hn, "hn")
            dma("sync", rc[:], ropeC[:, t0:t0 + TG], W=["rc"])
            dma("sync", rs[:], ropeS[:, t0:t0 + TG], W=["rs"])

            def plain_out(dst):
                def ev(c0, ps, pk):
                    s = C.nxt("ob", 3)
                    add("vector", lambda e: e.tensor_copy(out=ob[s][:], in_=ps[:, :]), R=[pk], W=[("ob", s)])
                    dma("gpsimd", dst, ob[s][:], R=[("ob", s)])
                return ev

            def rope_A(col0, dst):
                hold = {}

                def ev_main(c0, ps, pk):
                    hold["ps"], hold["pk"] = ps, pk
                linear(w_in, 16, [col0], hn, ["hn"], ev_main)
                ps, pk = hold["ps"], hold["pk"]
                h = C.nxt("qh", 2)
                add("scalar", lambda e: e.activation(out=qhi[h][:], in_=ps[0:32, :], func=AF.Copy), R=[pk], W=[("qhi", h)])
                add("vector", lambda e: e.tensor_tensor(out=qlo[h][:], in0=ps[0:32, :], in1=qhi[h][:], op=ALU.subtract),
                    R=[pk, ("qhi", h)], W=[("qlo", h)])
                return ps, pk, h, dst

            def rope_B(ps, pk, h, dst):
                ps2, pk2 = C.next_ps("main")
                add("tensor", lambda e: e.matmul(ps2[0:32, :], lhsT=permt[:, :], rhs=qhi[h][:], start=True, stop=False),
                    R=["permt", ("qhi", h)], W=[pk2])
                add("tensor", lambda e: e.matmul(ps2[0:32, :], lhsT=permt[:, :], rhs=qlo[h][:], start=False, stop=True),
                    R=["permt", ("qlo", h)], W=[pk2])
                s = C.nxt("ob", 3)
                r = 0
                add("vector", lambda e: e.tensor_tensor(out=t1[r][:], in0=ps[0:32, :], in1=rc[:], op=ALU.mult), R=[pk, "rc"], W=[("t1", r)])
                add("vector", lambda e: e.tensor_tensor(out=t2[r][:], in0=ps2[0:32, :], in1=rs[:], op=ALU.mult), R=[pk2, "rs"], W=[("t2", r)])
                add("vector", lambda e: e.tensor_tensor(out=ob[s][0:32, :], in0=t1[r][:], in1=t2[r][:], op=ALU.add),
                    R=[("t1", r), ("t2", r)], W=[("ob", s)])
                add("vector", lambda e: e.tensor_copy(out=ob[s][32:64, :], in_=ps[32:64, :]), R=[pk], W=[("ob", s)])
                add("vector", lambda e: e.tensor_copy(out=ob[s][64:128, :], in_=ps[64:128, :]), R=[pk], W=[("ob", s)])
                dma("gpsimd", dst, ob[s][:], R=[("ob", s)])

            def rope_chain(items):
                pend = None
                for col0, dst in items:
                    cur = rope_A(col0, dst)
                    if pend is not None:
                        rope_B(*pend)
                    pend = cur
                rope_B(*pend)

            if nxt == "even":
                for c in range(8):
                    linear(w_in, 16, [c * 128], hn, ["hn"], plain_out(uT_out[c * 128:(c + 1) * 128, t0:t0 + TG]))
                rope_chain([(1024 + hc * 128, qT_out[hc, :, t0:t0 + TG]) for hc in range(8)] +
                           [(2048 + hc * 128, kT_out[hc, :, t0:t0 + TG]) for hc in range(8)])
                vcol0 = 3072
            else:
                rope_chain([(hc * 128, qT_out[hc, :, t0:t0 + TG]) for hc in range(16)] +
                           [(2048 + hc * 128, kT_out[hc, :, t0:t0 + TG]) for hc in range(16)])
                vcol0 = 4096
            for c in range(nvc // 128):
                wt, wk = load_w(w_in, 0, 16, vcol0 + c * 128)
                for tt in range(4):
                    ps, pk = C.next_ps("main")
                    for j in range(16):
                        add("tensor", lambda e, wt=wt, j=j, ps=ps, tt=tt: e.matmul(ps[:, 0:128], lhsT=hn[:, j, tt * 128:(tt + 1) * 128],
                                                                                    rhs=wt[:, j, :], start=(j == 0), stop=(j == 15)),
                            R=[wk, ("hn", j)], W=[pk])
                    add("vector", lambda e, ps=ps, tt=tt, c=c: e.tensor_copy(out=vtok[:, tt, c * 128:(c + 1) * 128], in_=ps[:, 0:128]),
                        R=[pk], W=["gT"])
            dma("gpsimd", v_out[t0:t0 + TG, :].rearrange("(t p) c -> p t c", p=128), vtok, R=["gT"])
        if final:
            rmsnorm(hT, ["hT"], 4, hn, "hn")
            for c in range(16):
                s = C.nxt("of", 2)
                add("vector", lambda e, c=c, s=s: e.scalar_tensor_tensor(out=of[s][:], in0=hT[:, c, :], scalar=gvec[:, 4, c:c + 1],
                                                                          in1=rstd[:], op0=ALU.mult, op1=ALU.mult),
                    R=["hT", "rstd", ("gvec", 4)], W=[("of", s)])
                dma("gpsimd", outT[c * 128:(c + 1) * 128, t0:t0 + TG], of[s][:], R=[("of", s)])
    return C.finish()


def build_H_odd(nc=None, io=None, bg=None, bg_total=0):
    C = Ctx(nc, io)
    nc, P = C.nc, C.P
    add, dma = P.add, P.dma
    qT_in = C.din("qT_in", [4, 128, SEQ], BF16)
    kT_in = C.din("kT_in", [4, 128, SEQ], BF16)
    v_in = C.din("v_in", [SEQ, 4, 128], BF16)
    mask_in = C.din("mask_in", [128, 20, 512], BF16)
    oT_out = C.dout("oT_out", [512, SEQ], BF16)
    C.alloc_psum(8)
    qT = C.sb("qT", [128, SEQ], BF16)
    kT = C.sb("kT", [128, SEQ], BF16)
    vv = C.sb("vv", [128, 64, 128], BF16)
    mk = C.sb("mk", [128, 20, 512], BF16)
    ones = C.sb("ones", [128, 128], BF16)
    et = [C.sb("et%d" % i, [128, 512], BF16) for i in range(3)]
    em = [C.sb("em%d" % i, [128, 512], BF16) for i in range(5)]
    rl = C.sb("rl", [128, 512], F32)
    ob = [C.sb("ob%d" % i, [128, 512], BF16) for i in range(2)]
    esum = [C.sb("esum%d" % i, [128, 512], F32) for i in range(2)]
    esb = [C.sb("esb%d" % i, [128, 512], BF16) for i in range(2)]
    add("gpsimd", lambda e: e.memset(ones[:], 1.0), W=["ones"])
    dma("sync", mk[:], mask_in[:, :, :], W=["mk"])
    scale = 128 ** -0.5
    LOOK = 3
    for hh in range(4):
        dma("sync", qT[:], qT_in[hh], W=["qT"])
        dma("sync", kT[:], kT_in[hh], W=["kT"])
        dma("gpsimd", vv[:], v_in[:, hh, :].rearrange("(t p) c -> p t c", p=128), W=["vv"])
        steps = []
        for qg in range(16):
            kts = [kt for kt in range(4 * qg - 8, 4 * qg + 12) if 0 <= kt < 64]
            for n, kt in enumerate(kts):
                steps.append((qg, kt, n, len(kts)))
        acc = {}

        def front(st, gi):
            qg, kt, n, nk = st
            idx = kt - (4 * qg - 8)
            ps, pk = C.next_ps("S", [0, 1, 2, 3])
            add("tensor", lambda e, ps=ps, kt=kt, qg=qg: e.matmul(ps[:, :], lhsT=kT[:, kt * 128:(kt + 1) * 128],
                                                                   rhs=qT[:, qg * 512:(qg + 1) * 512], start=True, stop=True),
                R=["kT", "qT"], W=[pk])
            s_ = C.nxt("et", 3)
            add("scalar", lambda e, ps=ps, s_=s_: e.activation(out=et[s_][:], in_=ps[:, :], func=AF.Exp, scale=scale),
                R=[pk], W=[("et", s_)])
            m = C.nxt("em", 5)
            me = ["vector", "gpsimd"][gi % 2]
            add(me, lambda e, s_=s_, m=m, idx=idx: e.tensor_tensor(out=em[m][:], in0=et[s_][:], in1=mk[:, idx, :], op=ALU.mult),
                R=[("et", s_), "mk"], W=[("em", m)])
            return m

        def back(st, m):
            qg, kt, n, nk = st
            if n == 0:
                acc["o"] = C.next_ps("O", [4, 5])
                acc["l"] = C.next_ps("L", [6, 7])
            (po, pko), (pl, pkl) = acc["o"], acc["l"]
            last = (n == nk - 1)
            add("tensor", lambda e, po=po, kt=kt, m=m, n=n, last=last: e.matmul(
                po[:, :], lhsT=vv[:, kt, :], rhs=em[m][:], start=(n == 0), stop=last), R=["vv", ("em", m)], W=[pko])
            add("tensor", lambda e, pl=pl, m=m, n=n, last=last: e.matmul(
                pl[:, :], lhsT=ones[:], rhs=em[m][:], start=(n == 0), stop=last), R=["ones", ("em", m)], W=[pkl])
            if last:
                add("vector", lambda e, pl=pl: e.reciprocal(out=rl[:], in_=pl[:, :]), R=[pkl], W=["rl"])
                o = C.nxt("ob", 2)
                add("vector", lambda e, po=po, o=o: e.tensor_tensor(out=ob[o][:], in0=po[:, :], in1=rl[:], op=ALU.mult),
                    R=[pko, "rl"], W=[("ob", o)])
                dma("gpsimd", oT_out[hh * 128:(hh + 1) * 128, qg * 512:(qg + 1) * 512], ob[o][:], R=[("ob", o)])

        queue = []
        bg_every = max(1, (4 * len(steps)) // max(1, bg_total)) if bg is not None else 0
        for gi, st in enumerate(steps):
            queue.append((st, front(st, gi)))
            if len(queue) > LOOK:
                back(*queue.pop(0))
            if bg is not None and gi % bg_every == 0:
                bg.emit(C, 1, cast_engines=("scalar", "vector"))
        while queue:
            back(*queue.pop(0))
    return C.finish()


def dilated_mask_table():
    kl = np.arange(128)[:, None, None]
    idx = np.arange(20)[None, :, None]
    ql = np.arange(512)[None, None, :]
    off = (idx - 8) * 128 + kl - ql
    m = np.zeros(off.shape, np.float32)
    for window, dil in ((128, 1), (512, 4), (2048, 16)):
        half = window // (2 * dil)
        m += ((off % dil == 0) & (np.abs(off) <= half * dil)).astype(np.float32)
    return m.astype(NPBF)


def build_H_even(lambda_init, do_s5=True, do_attn=True, nc=None, io=None, bg=None, bg_total=0):
    C = Ctx(nc, io)
    nc, P = C.nc, C.P
    add, dma = P.add, P.dma
    qT_in = C.din("qT_in", [2, 128, SEQ], BF16)
    kT_in = C.din("kT_in", [2, 128, SEQ], BF16)
    v_in = C.din("v_in", [SEQ, 256], BF16)
    lam_in = C.din("lam_in", [128, 4, 128])
    sg_in = C.din("sg_in", [128, 2])
    ydT_out = C.dout("ydT_out", [256, SEQ], BF16)
    uT_in = C.din("uT_in", [256, SEQ], BF16)
    s5v = C.din("s5v", [8, 2, 128, 3])
    s5B = C.din("s5B", [8, 2, 2, 32, 128], BF16)
    s5C = C.din("s5C", [8, 2, 2, 128, 32], BF16)
    s5d = C.din("s5d", [8, 32, 1])
    ysT_out = C.dout("ysT_out", [256, SEQ])
    C.alloc_psum(8)
    ones = C.sb("ones", [128, 128], BF16)
    epsc = C.sb("epsc", [128, 1], F32)
    add("gpsimd", lambda e: e.memset(ones[:], 1.0), W=["ones"])
    add("gpsimd", lambda e: e.memset(epsc[:], EPS), W=["epsc"])

    if do_attn:
        qTg = [C.sb("qTg%d" % i, [128, 2, 512], BF16) for i in range(2)]
        kT = C.sb("kT", [128, 2, SEQ], BF16)
        vv = C.sb("vv", [128, 64, 256], BF16)
        et = [C.sb("et%d" % i, [128, 512], BF16) for i in range(3)]
        rl = C.sb("rl", [128, 512], F32)
        oc = C.sb("oc", [128, 2, 2, 512], F32)
        od = C.sb("od", [128, 2, 512], F32)
        sq = C.sb("sq", [128, 512], BF16)
        rstd = C.sb("rstd", [128, 512], F32)
        ob = [C.sb("ob%d" % i, [128, 512], BF16) for i in range(2)]
        lamt = C.sb("lamt", [128, 4, 128], F32)
        lprod = C.sb("lprod", [128, 128], F32)
        lsc = C.sb("lsc", [128, 4], F32)
        sgt = C.sb("sgt", [128, 2], F32)
        for c in range(2):
            dma("sync", kT[:, c, :], kT_in[c], W=["kT"])
        dma("gpsimd", vv[:], v_in.rearrange("(t p) c -> p t c", p=128), W=["vv"])
        dma("sync", lamt[:], lam_in[:, :, :], W=["lamt"])
        dma("sync", sgt[:], sg_in[:, :], W=["sgt"])
        for i in range(2):
            add("vector", lambda e, i=i: e.tensor_tensor(out=lprod[:], in0=lamt[:, 2 * i, :], in1=lamt[:, 2 * i + 1, :], op=ALU.mult),
                R=["lamt"], W=["lprod"])
            add("vector", lambda e, i=i: e.reduce_sum(out=lsc[:, i:i + 1], in_=lprod[:], axis=mybir.AxisListType.X), R=["lprod"], W=["lsc"])
        add("scalar", lambda e: e.activation(out=lsc[:, 0:2], in_=lsc[:, 0:2], func=AF.Exp), R=["lsc"], W=["lsc"])
        add("vector", lambda e: e.tensor_tensor(out=lsc[:, 2:3], in0=lsc[:, 1:2], in1=lsc[:, 0:1], op=ALU.subtract), R=["lsc"], W=["lsc"])
        add("vector", lambda e: e.tensor_scalar(out=lsc[:, 2:3], in0=lsc[:, 2:3], scalar1=-float(lambda_init), scalar2=None, op0=ALU.add),
            R=["lsc"], W=["lsc"])
        add("vector", lambda e: e.tensor_scalar(out=sgt[:], in0=sgt[:], scalar1=float(1.0 - lambda_init), scalar2=None, op0=ALU.mult),
            R=["sgt"], W=["sgt"])
        scale = 128 ** -0.5

    def attn_gen():
        LOOK = 1
        SB, OB, LB = [0, 1], [2, 3], [4]
        steps = [(qg, comp, kt) for qg in range(16) for comp in range(2) for kt in range(64)]
        st8 = {}

        def front(stp):
            qg, comp, kt = stp
            if comp == 0 and kt == 0:
                qs = C.nxt("qTg", 2)
                st8["qs"] = qs
                dma("sync", qTg[qs][:], qT_in[:, :, qg * 512:(qg + 1) * 512].rearrange("c p t -> p c t"), W=[("qTg", qs)])
            qs = st8["qs"]
            ps, pk = C.next_ps("S", SB)
            add("tensor", lambda e, ps=ps, kt=kt, qs=qs, comp=comp: e.matmul(
                ps[:, :], lhsT=kT[:, comp, kt * 128:(kt + 1) * 128], rhs=qTg[qs][:, comp, :],
                start=True, stop=True), R=["kT", ("qTg", qs)], W=[pk])
            s_ = C.nxt("et", 3)
            add("scalar", lambda e, ps=ps, s_=s_: e.activation(out=et[s_][:], in_=ps[:, :], func=AF.Exp, scale=scale),
                R=[pk], W=[("et", s_)])
            return s_

        def back(stp, s_):
            qg, comp, kt = stp
            if kt == 0:
                st8["po"] = [C.next_ps("O", OB), C.next_ps("O", OB)]
                st8["pl"] = C.next_ps("L", LB)
            po = st8["po"]
            pl, pkl = st8["pl"]
            for dc in range(2):
                add("tensor", lambda e, dc=dc, kt=kt, s_=s_, pp=po[dc][0]: e.matmul(
                    pp[:, :], lhsT=vv[:, kt, dc * 128:(dc + 1) * 128], rhs=et[s_][:], start=(kt == 0), stop=(kt == 63)),
                    R=["vv", ("et", s_)], W=[po[dc][1]])
            add("tensor", lambda e, pl=pl, s_=s_, kt=kt: e.matmul(pl[:, :], lhsT=ones[:], rhs=et[s_][:], start=(kt == 0), stop=(kt == 63)),
                R=["ones", ("et", s_)], W=[pkl])
            if kt != 63:
                return
            add("vector", lambda e, pl=pl: e.reciprocal(out=rl[:], in_=pl[:, :]), R=[pkl], W=["rl"])
            for dc in range(2):
                add("vector", lambda e, dc=dc, comp=comp, pp=po[dc][0]: e.tensor_tensor(out=oc[:, comp, dc, :], in0=pp[:, :], in1=rl[:], op=ALU.mult),
                    R=[po[dc][1], "rl"], W=["oc"])
            if comp != 1:
                return
            psn, pkn = C.next_ps("N", [7])
            for dc in range(2):
                add("vector", lambda e, dc=dc: e.scalar_tensor_tensor(out=od[:, dc, :], in0=oc[:, 1, dc, :], scalar=lsc[:, 2:3], in1=oc[:, 0, dc, :],
                                                                       op0=ALU.mult, op1=ALU.add), R=["oc", "lsc"], W=["od"])
                add("scalar", lambda e, dc=dc: e.activation(out=sq[:], in_=od[:, dc, :], func=AF.Square), R=["od"], W=["sq"])
                add("tensor", lambda e, dc=dc, psn=psn: e.matmul(psn[:, :], lhsT=ones[:], rhs=sq[:], start=(dc == 0), stop=(dc == 1)),
                    R=["sq", "ones"], W=[pkn])
            add("scalar", lambda e, psn=psn: e.activation(out=rstd[:], in_=psn[:, :], func=AF.Sqrt, scale=1.0 / 256, bias=epsc[:, 0:1]),
                R=[pkn, "epsc"], W=["rstd"])
            add("vector", lambda e: e.reciprocal(out=rstd[:], in_=rstd[:]), R=["rstd"], W=["rstd"])
            for dc in range(2):
                o = C.nxt("ob", 2)
                add("vector", lambda e, dc=dc, o=o: e.scalar_tensor_tensor(out=ob[o][:], in0=od[:, dc, :], scalar=sgt[:, dc:dc + 1], in1=rstd[:],
                                                                            op0=ALU.mult, op1=ALU.mult), R=["od", "sgt", "rstd"], W=[("ob", o)])
                dma("gpsimd", ydT_out[dc * 128:(dc + 1) * 128, qg * 512:(qg + 1) * 512], ob[o][:], R=[("ob", o)])

        queue = []
        for stp in steps:
            queue.append((stp, front(stp)))
            if len(queue) > LOOK:
                back(*queue.pop(0))
                yield
        while queue:
            back(*queue.pop(0))
            yield

    def s5_gen():
        BL = 512
        NBLK = SEQ // BL
        ug = C.sb("ug", [32, SEQ], BF16)
        yf = C.sb("yf", [32, SEQ], F32)
        pv = C.sb("pv", [128, 3], F32)
        sc = C.sb("sc", [128, 24], F32)
        sci = C.sb("sci", [128, 2], I32)
        Tr = C.sb("Tr", [128, BL + 1], F32)
        Ti = C.sb("Ti", [128, BL + 1], F32)
        tmp = C.sb("tmp", [128, BL], F32)
        Ar = C.sb("Ar", [128, BL], F32)
        Ai = C.sb("Ai", [128, BL], F32)
        Bb = C.sb("Bb", [32, 2, 128], BF16)
        Cb = C.sb("Cb", [128, 2, 32], BF16)
        dsk = C.sb("dsk", [32, 1], F32)
        m = [C.sb("m%d" % i, [128, BL], F32) for i in range(4)]
        wr = C.sb("wr", [128, BL], F32)
        wi = C.sb("wi", [128, BL], F32)
        sr = C.sb("sr", [128, BL], F32)
        si = C.sb("si", [128, BL], F32)
        xr = [C.sb("xr%d" % i, [128, BL], BF16) for i in range(2)]
        xi = [C.sb("xi%d" % i, [128, BL], BF16) for i in range(2)]
        init = C.sb("init", [128, 2], F32)
        yo = [C.sb("yo%d" % i, [32, BL], F32) for i in range(2)]
        TWO_PI = 2.0 * math.pi
        V = "vector"
        G = "gpsimd"

        def col(i):
            return sc[:, i:i + 1]

        def cmul_scalar(out_r, out_i, in_r, in_i, s_r, s_i, keys_in, keys_out, n):
            add(V, lambda e: e.tensor_scalar(out=tmp[:, 0:n], in0=in_i, scalar1=s_i, scalar2=None, op0=ALU.mult), R=keys_in, W=["tmp"])
            add(V, lambda e: e.scalar_tensor_tensor(out=out_r, in0=in_r, scalar=s_r, in1=tmp[:, 0:n], op0=ALU.mult, op1=ALU.subtract),
                R=keys_in + ["tmp"], W=keys_out)
            add(V, lambda e: e.tensor_scalar(out=tmp[:, 0:n], in0=in_i, scalar1=s_r, scalar2=None, op0=ALU.mult), R=keys_in, W=["tmp"])
            add(V, lambda e: e.scalar_tensor_tensor(out=out_i, in0=in_r, scalar=s_i, in1=tmp[:, 0:n], op0=ALU.mult, op1=ALU.add),
                R=keys_in + ["tmp"], W=keys_out)

        for gp in range(8):
            dma("sync", ug[:], uT_in[gp * 32:(gp + 1) * 32, :], W=["ug"])
            dma("sync", dsk[:], s5d[gp], W=["dsk"])
            for dr in range(2):
                dma("sync", pv[:], s5v[gp, dr], W=["pv"])
                dma("sync", Bb[:, 0, :], s5B[gp, dr, 0], W=["Bb"])
                dma("sync", Bb[:, 1, :], s5B[gp, dr, 1], W=["Bb"])
                dma("sync", Cb[:, 0, :], s5C[gp, dr, 0], W=["Cb"])
                dma("sync", Cb[:, 1, :], s5C[gp, dr, 1], W=["Cb"])
                K = ["sc"]
                add("scalar", lambda e: e.activation(out=col(0), in_=pv[:, 2:3], func=AF.Exp), R=["pv"], W=K)
                add("scalar", lambda e: e.activation(out=col(1), in_=pv[:, 0:1], func=AF.Exp, scale=col(0)), R=["pv"] + K, W=K)
                add(V, lambda e: e.tensor_tensor(out=col(2), in0=pv[:, 1:2], in1=col(0), op=ALU.mult), R=["pv"] + K, W=K)
                add(V, lambda e: e.tensor_scalar(out=col(3), in0=col(2), scalar1=0.5 * math.pi, scalar2=None, op0=ALU.add), R=K, W=K)
                add(V, lambda e: e.tensor_scalar(out=sc[:, 4:6], in0=sc[:, 2:4], scalar1=1.0 / TWO_PI, scalar2=None, op0=ALU.mult), R=K, W=K)
                add(V, lambda e: e.tensor_copy(out=sci[:, 0:2], in_=sc[:, 4:6]), R=K, W=["sci"])
                add(V, lambda e: e.tensor_copy(out=sc[:, 4:6], in_=sci[:, 0:2]), R=["sci"], W=K)
                add(V, lambda e: e.scalar_tensor_tensor(out=sc[:, 6:8], in0=sc[:, 4:6], scalar=-TWO_PI, in1=sc[:, 2:4], op0=ALU.mult, op1=ALU.add),
                    R=K, W=K)
                add(V, lambda e: e.tensor_scalar(out=sc[:, 6:8], in0=sc[:, 6:8], scalar1=-math.pi, scalar2=math.pi, op0=ALU.max, op1=ALU.min), R=K, W=K)
                add("scalar", lambda e: e.activation(out=sc[:, 8:10], in_=sc[:, 6:8], func=AF.Sin), R=K, W=K)
                add(G, lambda e: e.memset(Tr[:, 0:1], 1.0), W=["T"])
                add(G, lambda e: e.memset(Ti[:, 0:1], 0.0), W=["T"])
                add(V, lambda e: e.tensor_copy(out=Tr[:, 1:2], in_=col(9)), R=K, W=["T"])
                add(V, lambda e: e.tensor_copy(out=Ti[:, 1:2], in_=col(8)), R=K, W=["T"])
                n = 1
                while n < BL:
                    cmul_scalar(Tr[:, n + 1:2 * n + 1], Ti[:, n + 1:2 * n + 1], Tr[:, 1:n + 1], Ti[:, 1:n + 1],
                                Tr[:, n:n + 1], Ti[:, n:n + 1], ["T"], ["T"], n)
                    n *= 2
                add(V, lambda e: e.tensor_tensor(out=col(10), in0=col(1), in1=col(9), op=ALU.mult), R=K, W=K)
                add(V, lambda e: e.tensor_tensor(out=col(11), in0=col(1), in1=col(8), op=ALU.mult), R=K, W=K)
                add(V, lambda e: e.tensor_scalar(out=col(10), in0=col(10), scalar1=-1.0, scalar2=None, op0=ALU.add), R=K, W=K)
                add(V, lambda e: e.tensor_tensor(out=col(12), in0=col(11), in1=pv[:, 1:2], op=ALU.mult), R=K + ["pv"], W=K)
                add(V, lambda e: e.scalar_tensor_tensor(out=col(13), in0=col(10), scalar=pv[:, 0:1], in1=col(12), op0=ALU.mult, op1=ALU.add),
                    R=K + ["pv"], W=K)
                add(V, lambda e: e.tensor_tensor(out=col(12), in0=col(10), in1=pv[:, 1:2], op=ALU.mult), R=K + ["pv"], W=K)
                add(V, lambda e: e.scalar_tensor_tensor(out=col(14), in0=col(11), scalar=pv[:, 0:1], in1=col(12), op0=ALU.mult, op1=ALU.subtract),
                    R=K + ["pv"], W=K)
                add(V, lambda e: e.tensor_tensor(out=col(15), in0=pv[:, 0:1], in1=pv[:, 0:1], op=ALU.mult), R=["pv"], W=K)
                add(V, lambda e: e.scalar_tensor_tensor(out=col(15), in0=pv[:, 1:2], scalar=pv[:, 1:2], in1=col(15), op0=ALU.mult, op1=ALU.add),
                    R=K + ["pv"], W=K)
                add(V, lambda e: e.reciprocal(out=col(15), in_=col(15)), R=K, W=K)
                add(V, lambda e: e.tensor_tensor(out=col(16), in0=col(13), in1=col(15), op=ALU.mult), R=K, W=K)
                add(V, lambda e: e.tensor_tensor(out=col(17), in0=col(14), in1=col(15), op=ALU.mult), R=K, W=K)
                add(V, lambda e: e.tensor_scalar(out=col(18), in0=col(17), scalar1=-1.0, scalar2=None, op0=ALU.mult), R=K, W=K)
                add(V, lambda e: e.tensor_scalar(out=tmp[:], in0=Ti[:, 0:BL], scalar1=col(17), scalar2=None, op0=ALU.mult), R=["T"] + K, W=["tmp"])
                add(V, lambda e: e.scalar_tensor_tensor(out=Ar[:], in0=Tr[:, 0:BL], scalar=col(16), in1=tmp[:], op0=ALU.mult, op1=ALU.add),
                    R=["T", "tmp"] + K, W=["A"])
                add(V, lambda e: e.tensor_scalar(out=tmp[:], in0=Ti[:, 0:BL], scalar1=col(16), scalar2=None, op0=ALU.mult), R=["T"] + K, W=["tmp"])
                add(V, lambda e: e.scalar_tensor_tensor(out=Ai[:], in0=Tr[:, 0:BL], scalar=col(17), in1=tmp[:], op0=ALU.mult, op1=ALU.subtract),
                    R=["T", "tmp"] + K, W=["A"])
                add(G, lambda e: e.memset(init[:], 0.0), W=["init"])
                def emit_bu(bi):
                    blk = bi if dr == 0 else NBLK - 1 - bi
                    t0 = blk * BL
                    pa, pka = C.next_ps("S5a", [5])
                    pb, pkb = C.next_ps("S5b", [6])
                    add("tensor", lambda e, pa=pa, t0=t0: e.matmul(pa[:, :], lhsT=Bb[:, 0, :], rhs=ug[:, t0:t0 + BL], start=True, stop=True),
                        R=["Bb", "ug"], W=[pka])
                    add("tensor", lambda e, pb=pb, t0=t0: e.matmul(pb[:, :], lhsT=Bb[:, 1, :], rhs=ug[:, t0:t0 + BL], start=True, stop=True),
                        R=["Bb", "ug"], W=[pkb])
                    return pa, pka, pb, pkb
                nxt_bu = emit_bu(0)
                for bi in range(NBLK):
                    blk = bi if dr == 0 else NBLK - 1 - bi
                    t0 = blk * BL
                    pa, pka, pb, pkb = nxt_bu
                    if dr == 0:
                        bur, bui = pa[:, :], pb[:, :]
                    else:
                        bur, bui = pa[:, ::-1], pb[:, ::-1]
                    add(V, lambda e, bur=bur: e.tensor_tensor(out=m[0][:], in0=bur, in1=Ar[:], op=ALU.mult), R=[pka, "A"], W=["m0"])
                    add(V, lambda e, bui=bui: e.tensor_tensor(out=m[1][:], in0=bui, in1=Ai[:], op=ALU.mult), R=[pkb, "A"], W=["m1"])
                    add(V, lambda e, bui=bui: e.tensor_tensor(out=m[2][:], in0=bui, in1=Ar[:], op=ALU.mult), R=[pkb, "A"], W=["m2"])
                    add(V, lambda e, bur=bur: e.tensor_tensor(out=m[3][:], in0=bur, in1=Ai[:], op=ALU.mult), R=[pka, "A"], W=["m3"])
                    if bi + 1 < NBLK:
                        nxt_bu = emit_bu(bi + 1)
                    add(G, lambda e: e.tensor_tensor(out=wr[:], in0=m[0][:], in1=m[1][:], op=ALU.subtract), R=["m0", "m1"], W=["wr"])
                    add(G, lambda e: e.tensor_tensor(out=wi[:], in0=m[2][:], in1=m[3][:], op=ALU.add), R=["m2", "m3"], W=["wi"])
                    add(V, lambda e: e.tensor_tensor_scan(out=sr[:], data0=col(1).to_broadcast([128, BL]), data1=wr[:], initial=init[:, 0:1],
                                                          op0=ALU.mult, op1=ALU.add), R=["wr", "init"] + K, W=["sr"])
                    add(V, lambda e: e.tensor_tensor_scan(out=si[:], data0=col(1).to_broadcast([128, BL]), data1=wi[:], initial=init[:, 1:2],
                                                          op0=ALU.mult, op1=ALU.add), R=["wi", "init"] + K, W=["si"])
                    cmul_scalar(init[:, 0:1], init[:, 1:2], sr[:, BL - 1:BL], si[:, BL - 1:BL], Tr[:, BL:BL + 1], Ti[:, BL:BL + 1],
                                ["sr", "si", "T"], ["init"], 1)
                    add(V, lambda e: e.tensor_tensor(out=m[0][:], in0=sr[:], in1=Tr[:, 0:BL], op=ALU.mult), R=["sr", "T"], W=["m0"])
                    add(V, lambda e: e.tensor_tensor(out=m[1][:], in0=si[:], in1=Ti[:, 0:BL], op=ALU.mult), R=["si", "T"], W=["m1"])
                    add(V, lambda e: e.tensor_tensor(out=m[2][:], in0=si[:], in1=Tr[:, 0:BL], op=ALU.mult), R=["si", "T"], W=["m2"])
                    add(V, lambda e: e.tensor_tensor(out=m[3][:], in0=sr[:], in1=Ti[:, 0:BL], op=ALU.mult), R=["sr", "T"], W=["m3"])
                    xs = C.nxt("xs", 2)
                    if dr == 0:
                        a0, a1, a2, a3 = m[0][:], m[1][:], m[2][:], m[3][:]
                    else:
                        a0, a1, a2, a3 = m[0][:, ::-1], m[1][:, ::-1], m[2][:, ::-1], m[3][:, ::-1]
                    add(V, lambda e, xs=xs, a0=a0, a1=a1: e.tensor_tensor(out=xr[xs][:], in0=a0, in1=a1, op=ALU.subtract),
                        R=["m0", "m1"], W=[("xr", xs)])
                    add(V, lambda e, xs=xs, a2=a2, a3=a3: e.scalar_tensor_tensor(out=xi[xs][:], in0=a2, scalar=-1.0, in1=a3, op0=ALU.mult, op1=ALU.subtract),
                        R=["m2", "m3"], W=[("xi", xs)])
                    py, pky = C.next_ps("Y", [7])
                    add("tensor", lambda e, py=py, xs=xs: e.matmul(py[0:32, :], lhsT=Cb[:, 0, :], rhs=xr[xs][:], start=True, stop=False),
                        R=["Cb", ("xr", xs)], W=[pky])
                    add("tensor", lambda e, py=py, xs=xs: e.matmul(py[0:32, :], lhsT=Cb[:, 1, :], rhs=xi[xs][:], start=False, stop=True),
                        R=["Cb", ("xi", xs)], W=[pky])
                    if dr == 0:
                        add(V, lambda e, py=py, t0=t0: e.tensor_copy(out=yf[:, t0:t0 + BL], in_=py[0:32, :]), R=[pky], W=["yf"])
                    else:
                        y = C.nxt("yo", 2)
                        add(V, lambda e, py=py, t0=t0: e.tensor_tensor(out=yf[:, t0:t0 + BL], in0=py[0:32, :], in1=yf[:, t0:t0 + BL], op=ALU.add),
                            R=[pky, "yf"], W=["yf"])
                        add(V, lambda e, y=y, t0=t0: e.scalar_tensor_tensor(out=yo[y][:], in0=ug[:, t0:t0 + BL], scalar=dsk[:, 0:1], in1=yf[:, t0:t0 + BL],
                                                                             op0=ALU.mult, op1=ALU.add), R=["ug", "dsk", "yf"], W=[("yo", y)])
                        dma("gpsimd", ysT_out[gp * 32:(gp + 1) * 32, t0:t0 + BL], yo[y][:], R=[("yo", y)])
                    yield
    gens = []
    if do_attn:
        gens.append((attn_gen(), 8))
    if do_s5:
        gens.append((s5_gen(), 1))
    alive = [True] * len(gens)
    it = 0
    bg_every = max(1, 256 // max(1, bg_total)) if bg is not None else 0
    bg_per = max(1, -(-bg_total // 256)) if bg is not None else 0
    while any(alive):
        it += 1
        if bg is not None and it % bg_every == 0:
            bg.emit(C, bg_per, cast_engines=("scalar",))
        for gi, (g, reps) in enumerate(gens):
            if not alive[gi]:
                continue
            for _ in range(reps):
                try:
                    next(g)
                except StopIteration:
                    alive[gi] = False
                    break
    return C.finish()


def pack_s5(lam_re, lam_im, log_step, b_re, b_im, c_re, c_im, d_skip, g0, ng):
    ngp = ng // 2
    s5v = np.zeros((ngp, 2, 128, 3), np.float32)
    s5B = np.zeros((ngp, 2, 2, 32, 128), NPBF)
    s5C = np.zeros((ngp, 2, 2, 128, 32), NPBF)
    s5d = np.zeros((ngp, 32, 1), np.float32)
    for gp in range(ngp):
        for j in range(2):
            g = g0 + 2 * gp + j
            s5d[gp, j * 16:(j + 1) * 16, 0] = d_skip[g * 16:(g + 1) * 16]
            for dr in range(2):
                s5v[gp, dr, j * 64:(j + 1) * 64, 0] = lam_re[dr, g]
                s5v[gp, dr, j * 64:(j + 1) * 64, 1] = lam_im[dr, g]
                s5v[gp, dr, j * 64:(j + 1) * 64, 2] = log_step[dr, g]
                s5B[gp, dr, 0, j * 16:(j + 1) * 16, j * 64:(j + 1) * 64] = b_re[dr, g].T.astype(NPBF)
                s5B[gp, dr, 1, j * 16:(j + 1) * 16, j * 64:(j + 1) * 64] = b_im[dr, g].T.astype(NPBF)
                s5C[gp, dr, 0, j * 64:(j + 1) * 64, j * 16:(j + 1) * 16] = c_re[dr, g].T.astype(NPBF)
                s5C[gp, dr, 1, j * 64:(j + 1) * 64, j * 16:(j + 1) * 16] = c_im[dr, g].T.astype(NPBF)
    return {"s5v": s5v, "s5B": s5B, "s5C": s5C, "s5d": s5d}


def _g16(g):
    return np.ascontiguousarray(np.asarray(g, np.float32).reshape(16, 128).T)


def rope_tables_host(p0, n):
    f32 = np.float32
    pos = np.arange(p0, p0 + n, dtype=f32)
    inv = (f32(500000.0) ** (-(np.arange(0, 32, 2, dtype=f32)) / f32(32))).astype(f32)
    ang = (pos[:, None] * inv[None, :]).astype(f32)
    cos, sin = np.cos(ang).astype(f32), np.sin(ang).astype(f32)
    C32 = np.concatenate([cos.T, cos.T], axis=0)
    S32 = np.concatenate([-sin.T, sin.T], axis=0)
    return np.ascontiguousarray(C32), np.ascontiguousarray(S32)


def perm_table():
    p = np.zeros((32, 32), np.float32)
    for m in range(32):
        p[(m + 16) % 32, m] = 1.0
    return p.astype(NPBF)


def swap_cols(W, col0s):
    parts = []
    for c0 in col0s:
        parts.append(W[:, c0 + 16:c0 + 32])
        parts.append(W[:, c0:c0 + 16])
    return np.ascontiguousarray(np.concatenate(parts, axis=1))


_NC_CACHE = {}


def _get_nc(key, fn):
    if key not in _NC_CACHE:
        _NC_CACHE[key] = fn()
    return _NC_CACHE[key]


def _run(nc, in_maps):
    res = run_bass_kernel_spmd(nc, in_maps, core_ids=list(range(NCORE)))
    return res.results


def kernel_unfused(x, mem, norm_mix_g, norm_xa_g, norm_mem_g, xa_wq, xa_wkv, xa_wo, norm_ffn_g,
           ffn_w13, ffn_w2, ab_w_in, ab_w_out, s5_lambda_re, s5_lambda_im, s5_log_step,
           s5_b_re, s5_b_im, s5_c_re, s5_c_im, s5_d, s5_glu_w, s5_glu_b, diff_lambda,
           diff_subln_g, c_w_qkv, c_w_out, final_norm_g):
    A = lambda a: np.asarray(a)
    x = A(x).astype(np.float32, copy=False)
    mem = A(mem).astype(np.float32, copy=False)
    depth = 4
    cores = [(c // 4, (c % 4) * TOK) for c in range(NCORE)]
    hT = [np.ascontiguousarray(x[b, t0:t0 + TOK, :].T) for (b, t0) in cores]
    memT = [np.ascontiguousarray(mem[b].T) for b in range(NB)]
    ropes = [rope_tables_host(t0, TOK) for (b, t0) in cores]
    mask_tab = dilated_mask_table()

    def nxt_inputs(l):
        i = l // 2
        d = {"g_mix": _g16(A(norm_mix_g)[l])}
        if l % 2 == 0:
            W = A(ab_w_in)[i]
            d["w_in"] = W
            d["w_sw"] = swap_cols(W, [1024 + hc * 128 for hc in range(8)] + [2048 + hc * 128 for hc in range(8)])
        else:
            W = A(c_w_qkv)[i]
            d["w_in"] = W
            d["w_sw"] = swap_cols(W, [hc * 128 for hc in range(16)] + [2048 + hc * 128 for hc in range(16)])
        return d

    def prev_inputs(l):
        i = l // 2
        d = {"g_xa": _g16(A(norm_xa_g)[l]), "g_mem": _g16(A(norm_mem_g)[l]), "g_ffn": _g16(A(norm_ffn_g)[l]),
             "wq": A(xa_wq)[l], "wkv": A(xa_wkv)[l], "wo": A(xa_wo)[l], "w13": A(ffn_w13)[l], "w2": A(ffn_w2)[l]}
        if l % 2 == 0:
            d["w_out"] = A(ab_w_out)[i]
            d["glu_w"] = A(s5_glu_w)[i]
            d["glu_b"] = np.ascontiguousarray(A(s5_glu_b)[i].reshape(8, 128).T)
        else:
            d["w_out"] = A(c_w_out)[i]
        return d

    typ = lambda l: "even" if l % 2 == 0 else "odd"
    mix = None
    out = None
    for l in range(depth + 1):
        prev = typ(l - 1) if l > 0 else None
        nxt = typ(l) if l < depth else None
        final = (l == depth)
        nc = _get_nc(("T", prev, nxt, final), lambda: build_T(prev, nxt, final))
        shared = {}
        if prev is not None:
            shared.update(prev_inputs(l - 1))
        if nxt is not None:
            shared.update(nxt_inputs(l))
        if final:
            shared["g_fin"] = _g16(A(final_norm_g))
        in_maps = []
        for c, (b, t0) in enumerate(cores):
            m = dict(shared)
            m["hT_in"] = hT[c]
            if prev is not None:
                m["memT"] = memT[b]
                m.update(mix[c])
            if nxt is not None:
                m["ropeC"], m["ropeS"] = ropes[c]
                m["perm_in"] = perm_table()
            in_maps.append(m)
        res = _run(nc, in_maps)
        if final:
            out = np.empty((NB, SEQ, D), np.float32)
            for c, (b, t0) in enumerate(cores):
                out[b, t0:t0 + TOK, :] = res[c]["outT"].T
            break
        hT = [res[c]["hT_out"] for c in range(NCORE)]
        i = l // 2
        cat = lambda name, b, sl: np.ascontiguousarray(np.concatenate([res[4 * b + r][name][sl] for r in range(4)], axis=-1))
        if nxt == "even":
            lambda_init = 0.8 - 0.6 * math.exp(-0.3 * l)
            nch = _get_nc(("He", l), lambda: build_H_even(lambda_init))
            in_maps = []
            lam_rep = np.ascontiguousarray(np.broadcast_to(A(diff_lambda)[i].astype(np.float32), (128, 4, 128)))
            sgi = np.ascontiguousarray(A(diff_subln_g)[i].astype(np.float32).reshape(2, 128).T)
            for c in range(NCORE):
                b, hd = c // 4, c % 4
                m = {"qT_in": cat("qT_out", b, slice(2 * hd, 2 * hd + 2)),
                     "kT_in": cat("kT_out", b, slice(2 * hd, 2 * hd + 2)),
                     "v_in": np.ascontiguousarray(np.concatenate([res[4 * b + r]["v_out"][:, hd * 256:(hd + 1) * 256] for r in range(4)], axis=0)),
                     "uT_in": cat("uT_out", b, slice(hd * 256, (hd + 1) * 256)),
                     "lam_in": lam_rep, "sg_in": sgi}
                m.update(pack_s5(A(s5_lambda_re)[i], A(s5_lambda_im)[i], A(s5_log_step)[i], A(s5_b_re)[i], A(s5_b_im)[i],
                                 A(s5_c_re)[i], A(s5_c_im)[i], A(s5_d)[i], 16 * hd, 16))
                in_maps.append(m)
            rh = _run(nch, in_maps)
            mix = []
            for c, (b, t0) in enumerate(cores):
                mix.append({"ys5T": np.ascontiguousarray(np.concatenate([rh[4 * b + hd]["ysT_out"][:, t0:t0 + TOK] for hd in range(4)], axis=0)),
                            "ydiffT": np.ascontiguousarray(np.concatenate([rh[4 * b + hd]["ydT_out"][:, t0:t0 + TOK] for hd in range(4)], axis=0))})
        else:
            nch = _get_nc(("Ho",), build_H_odd)
            in_maps = []
            for c in range(NCORE):
                b, hq = c // 4, c % 4
                vv = np.concatenate([res[4 * b + r]["v_out"][:, hq * 512:(hq + 1) * 512] for r in range(4)], axis=0)
                in_maps.append({"qT_in": cat("qT_out", b, slice(4 * hq, 4 * hq + 4)),
                                "kT_in": cat("kT_out", b, slice(4 * hq, 4 * hq + 4)),
                                "v_in": np.ascontiguousarray(vv.reshape(SEQ, 4, 128)),
                                "mask_in": mask_tab})
            rh = _run(nch, in_maps)
            mix = []
            for c, (b, t0) in enumerate(cores):
                mix.append({"oT_in": np.ascontiguousarray(np.concatenate([rh[4 * b + hq]["oT_out"][:, t0:t0 + TOK] for hq in range(4)], axis=0))})
    return out


def build_fused(depth=4):
    nc = bass.Bass("TRN2", target_bir_lowering=False)

    def ext(name, shape, dt=F32):
        return nc.dram_tensor(name, list(shape), dt, kind="ExternalInput").ap()

    def internal(name, shape, dt=F32):
        return nc.dram_tensor(name, list(shape), dt).ap()

    x_hT = ext("x_hT", [D, SEQ])
    memT = ext("memT", [D, 256])
    ropeC = ext("ropeC", [32, SEQ])
    ropeS = ext("ropeS", [32, SEQ])
    mask_in = ext("mask_in", [128, 20, 512], BF16)
    perm_in = ext("perm_in", [32, 32], BF16)
    g_fin = ext("g_fin", [128, 16])
    outT = nc.dram_tensor("outT", [D, SEQ], F32, kind="ExternalOutput").ap()
    L = []
    for l in range(depth):
        d = {}
        for nm in ("g_mix", "g_xa", "g_mem", "g_ffn"):
            d[nm] = ext("%s_%d" % (nm, l), [128, 16])
        d["wq"] = ext("wq_%d" % l, [D, D])
        d["wkv"] = ext("wkv_%d" % l, [D, 2 * D])
        d["wo"] = ext("wo_%d" % l, [D, D])
        d["w13"] = ext("w13_%d" % l, [D, 2 * FFN])
        d["w2"] = ext("w2_%d" % l, [FFN, D])
        d["w_out"] = ext("w_out_%d" % l, [D, D])
        if l % 2 == 0:
            d["w_in"] = ext("w_in_%d" % l, [D, 4096])
            d["w_sw"] = ext("w_sw_%d" % l, [D, 512])
            d["glu_w"] = ext("glu_w_%d" % l, [1024, 1024])
            d["glu_b"] = ext("glu_b_%d" % l, [128, 8])
            d["lam_in"] = ext("lam_in_%d" % l, [128, 4, 128])
            d["sg_in"] = ext("sg_in_%d" % l, [128, 2])
            d["s5v"] = ext("s5v_%d" % l, [32, 2, 128, 3])
            d["s5B"] = ext("s5B_%d" % l, [32, 2, 2, 32, 128], BF16)
            d["s5C"] = ext("s5C_%d" % l, [32, 2, 2, 128, 32], BF16)
            d["s5d"] = ext("s5d_%d" % l, [32, 32, 1])
        else:
            d["w_in"] = ext("w_in_%d" % l, [D, 6144])
            d["w_sw"] = ext("w_sw_%d" % l, [D, 1024])
        L.append(d)
    hT = internal("hT", [D, SEQ])
    uT = internal("uT", [1024, SEQ], BF16)
    qTe = internal("qTe", [8, 128, SEQ], BF16)
    kTe = internal("kTe", [8, 128, SEQ], BF16)
    ve = internal("ve", [SEQ, 1024], BF16)
    qTo = internal("qTo", [16, 128, SEQ], BF16)
    kTo = internal("kTo", [16, 128, SEQ], BF16)
    vo = internal("vo", [SEQ, 2048], BF16)
    ys5T = internal("ys5T", [1024, SEQ])
    ydiffT = internal("ydiffT", [1024, SEQ], BF16)
    oT = internal("oT", [2048, SEQ], BF16)
    typ = lambda l: "even" if l % 2 == 0 else "odd"
    bgs = {}

    def launch_wio(l):
        prev = typ(l - 1) if l > 0 else None
        nxt = typ(l) if l < depth else None
        wio = {}
        if prev is not None:
            wio.update({nm: L[l - 1][nm] for nm in ("w_out", "wq", "wkv", "wo", "w13", "w2")})
            if prev == "even":
                wio["glu_w"] = L[l - 1]["glu_w"]
        if nxt is not None:
            wio.update({"w_in": L[l]["w_in"], "w_sw": L[l]["w_sw"]})
        return t_weight_specs(prev, nxt, wio)

    for l in range(depth + 1):
        prev = typ(l - 1) if l > 0 else None
        nxt = typ(l) if l < depth else None
        final = (l == depth)
        wio = {}
        if prev is not None:
            wio.update({nm: L[l - 1][nm] for nm in ("w_out", "wq", "wkv", "wo", "w13", "w2")})
            if prev == "even":
                wio["glu_w"] = L[l - 1]["glu_w"]
        if nxt is not None:
            wio.update({"w_in": L[l]["w_in"], "w_sw": L[l]["w_sw"]})
        if bgs.get(l) is None:
            wb = emit_W(nc, t_weight_specs(prev, nxt, wio))
        else:
            bgs[l].flush()
            wb = bgs[l].out
        TSUB = 4096
        for r in range(SEQ // TSUB):
            ts = slice(r * TSUB, (r + 1) * TSUB)
            io = {"hT_in": (x_hT if l == 0 else hT)[:, ts]}
            if prev is not None:
                pl = L[l - 1]
                for nm in ("w_out", "g_xa", "g_mem", "wq", "wkv", "wo", "g_ffn", "w13", "w2"):
                    io[nm] = pl[nm]
                io["memT"] = memT
                if prev == "even":
                    io["ys5T"] = ys5T[:, ts]
                    io["ydiffT"] = ydiffT[:, ts]
                    io["glu_w"] = pl["glu_w"]
                    io["glu_b"] = pl["glu_b"]
                else:
                    io["oT_in"] = oT[:, ts]
            if nxt is not None:
                nl = L[l]
                io["g_mix"] = nl["g_mix"]
                io["ropeC"] = ropeC[:, ts]
                io["ropeS"] = ropeS[:, ts]
                io["perm_in"] = perm_in
                io["w_in"] = nl["w_in"]
                io["w_sw"] = nl["w_sw"]
                io["hT_out"] = hT[:, ts]
                if nxt == "even":
                    io["uT_out"] = uT[:, ts]
                    io["qT_out"] = qTe[:, :, ts]
                    io["kT_out"] = kTe[:, :, ts]
                    io["v_out"] = ve[ts, :]
                else:
                    io["qT_out"] = qTo[:, :, ts]
                    io["kT_out"] = kTo[:, :, ts]
                    io["v_out"] = vo[ts, :]
            if final:
                io["g_fin"] = g_fin
                io["outT"] = outT[:, ts]
            build_T(prev, nxt, final, nc=nc, io=io, wb=wb, ntok=TSUB)
        if final:
            break
        bg = BgW(nc, launch_wio(l + 1))
        bgs[l + 1] = bg
        bg_total = -(-len(bg.jobs) // 4)
        if nxt == "even":
            lambda_init = 0.8 - 0.6 * math.exp(-0.3 * l)
            nl = L[l]
            for hd in range(4):
                cs = slice(hd * 256, (hd + 1) * 256)
                io = {"qT_in": qTe[2 * hd:2 * hd + 2], "kT_in": kTe[2 * hd:2 * hd + 2], "v_in": ve[:, cs],
                      "lam_in": nl["lam_in"], "sg_in": nl["sg_in"], "ydT_out": ydiffT[cs, :], "uT_in": uT[cs, :],
                      "s5v": nl["s5v"][8 * hd:8 * hd + 8], "s5B": nl["s5B"][8 * hd:8 * hd + 8], "s5C": nl["s5C"][8 * hd:8 * hd + 8],
                      "s5d": nl["s5d"][8 * hd:8 * hd + 8], "ysT_out": ys5T[cs, :]}
                build_H_even(lambda_init, nc=nc, io=io, bg=bg, bg_total=bg_total)
        else:
            for hq in range(4):
                io = {"qT_in": qTo[4 * hq:4 * hq + 4], "kT_in": kTo[4 * hq:4 * hq + 4],
                      "v_in": vo[:, hq * 512:(hq + 1) * 512].rearrange("s (h e) -> s h e", h=4),
                      "mask_in": mask_in, "oT_out": oT[hq * 512:(hq + 1) * 512, :]}
                build_H_odd(nc=nc, io=io, bg=bg, bg_total=bg_total)
    return nc


def kernel_fused(x, mem, norm_mix_g, norm_xa_g, norm_mem_g, xa_wq, xa_wkv, xa_wo, norm_ffn_g,
                 ffn_w13, ffn_w2, ab_w_in, ab_w_out, s5_lambda_re, s5_lambda_im, s5_log_step,
                 s5_b_re, s5_b_im, s5_c_re, s5_c_im, s5_d, s5_glu_w, s5_glu_b, diff_lambda,
                 diff_subln_g, c_w_qkv, c_w_out, final_norm_g):
    A = lambda a: np.asarray(a)
    x = A(x).astype(np.float32, copy=False)
    mem = A(mem).astype(np.float32, copy=False)
    depth = 4
    nc = _get_nc(("fused",), build_fused)
    C32, S32 = rope_tables_host(0, SEQ)
    shared = {"ropeC": C32, "ropeS": S32, "mask_in": dilated_mask_table(), "g_fin": _g16(A(final_norm_g)), "perm_in": perm_table()}
    for l in range(depth):
        i = l // 2
        shared["g_mix_%d" % l] = _g16(A(norm_mix_g)[l])
        shared["g_xa_%d" % l] = _g16(A(norm_xa_g)[l])
        shared["g_mem_%d" % l] = _g16(A(norm_mem_g)[l])
        shared["g_ffn_%d" % l] = _g16(A(norm_ffn_g)[l])
        shared["wq_%d" % l] = A(xa_wq)[l]
        shared["wkv_%d" % l] = A(xa_wkv)[l]
        shared["wo_%d" % l] = A(xa_wo)[l]
        shared["w13_%d" % l] = A(ffn_w13)[l]
        shared["w2_%d" % l] = A(ffn_w2)[l]
        if l % 2 == 0:
            W = A(ab_w_in)[i]
            shared["w_in_%d" % l] = W
            shared["w_sw_%d" % l] = swap_cols(W, [1024 + hc * 128 for hc in range(8)] + [2048 + hc * 128 for hc in range(8)])
            shared["w_out_%d" % l] = A(ab_w_out)[i]
            shared["glu_w_%d" % l] = A(s5_glu_w)[i]
            shared["glu_b_%d" % l] = np.ascontiguousarray(A(s5_glu_b)[i].reshape(8, 128).T)
            shared["lam_in_%d" % l] = np.ascontiguousarray(np.broadcast_to(A(diff_lambda)[i].astype(np.float32), (128, 4, 128)))
            shared["sg_in_%d" % l] = np.ascontiguousarray(A(diff_subln_g)[i].astype(np.float32).reshape(2, 128).T)
            pk = pack_s5(A(s5_lambda_re)[i], A(s5_lambda_im)[i], A(s5_log_step)[i], A(s5_b_re)[i], A(s5_b_im)[i],
                         A(s5_c_re)[i], A(s5_c_im)[i], A(s5_d)[i], 0, 64)
            for k, v in pk.items():
                shared["%s_%d" % (k, l)] = v
        else:
            W = A(c_w_qkv)[i]
            shared["w_in_%d" % l] = W
            shared["w_sw_%d" % l] = swap_cols(W, [hc * 128 for hc in range(16)] + [2048 + hc * 128 for hc in range(16)])
            shared["w_out_%d" % l] = A(c_w_out)[i]
    in_maps = []
    for b in range(NB):
        m = dict(shared)
        m["x_hT"] = np.ascontiguousarray(x[b].T)
        m["memT"] = np.ascontiguousarray(mem[b].T)
        in_maps.append(m)
    res = run_bass_kernel_spmd(nc, in_maps, core_ids=list(range(NB))).results
    out = np.empty((NB, SEQ, D), np.float32)
    for b in range(NB):
        out[b] = res[b]["outT"].T
    return out


def kernel(**inputs):
    return kernel_unfused(**inputs)
```
